# Optimizing a Trainium2 kernel written in Bass

```python
import math
import jax, jax.numpy as jnp
from jax import lax
import numpy as np

D_MODEL = 1024
BATCH = 8
SEQ = 4096
DEPTH = 1

MEM_LEN = 256
EPS = 1e-6
N_HEADS = 16
HEAD_DIM = 64
N_KV = 4
GROUP = N_HEADS // N_KV
KV_W = N_KV * HEAD_DIM
CMP_LEN = 32
CMP_STRIDE = 16
CMP_HIDDEN = 256
SEL_LEN = 64
N_SEL = 16
WINDOW = 512
Q_CHUNK = 32
CONV_CH = 512
CONV_K = 31
N_BUCKETS = 32
MAX_EXACT = N_BUCKETS // 2
MAX_DIST = 128
X_HEADS = 4
X_HEAD_DIM = D_MODEL // X_HEADS
D_FF = 2816
FFN_K = 3
IN_SIZES = (2 * CONV_CH, N_HEADS * HEAD_DIM, KV_W, KV_W, KV_W, KV_W, KV_W, KV_W, 3 * N_HEADS, 2 * D_MODEL)
IN_W = sum(IN_SIZES)
IN_SPLITS = tuple(int(v) for v in np.cumsum(IN_SIZES)[:-1])
NEG = -1e30
FORCE = 1e4

kernel_name = 'hybrid_conformer_nsa_gated_block'


def _rmsnorm(x, g):
    xf = x.astype(jnp.float32)
    y = xf * lax.rsqrt(jnp.mean(xf * xf, axis=-1, keepdims=True) + EPS)
    return y.astype(x.dtype) * g


def _layernorm(x, g, b):
    xf = x.astype(jnp.float32)
    mu = jnp.mean(xf, axis=-1, keepdims=True)
    xc = xf - mu
    y = xc * lax.rsqrt(jnp.mean(xc * xc, axis=-1, keepdims=True) + EPS)
    return y.astype(x.dtype) * g + b


def _t5_bucket(dist):
    n = jnp.maximum(dist, 0)
    nf = jnp.maximum(n, 1).astype(jnp.float32)
    large = MAX_EXACT + (jnp.log(nf / MAX_EXACT) / math.log(MAX_DIST / MAX_EXACT) * (N_BUCKETS - MAX_EXACT)).astype(jnp.int32)
    large = jnp.minimum(large, N_BUCKETS - 1)
    return jnp.where(n < MAX_EXACT, n, large)


def _causal_dwconv(x, w, b):
    k = w.shape[0]
    y = lax.conv_general_dilated(x, w[:, None, :], window_strides=(1,), padding=[(k - 1, 0)],
                                 dimension_numbers=('NWC', 'WIO', 'NWC'), feature_group_count=x.shape[-1])
    return y + b


def _conformer_conv(u, dw_w, dw_b, ln_g, ln_b, w_pw):
    a, gt = jnp.split(u, 2, axis=-1)
    h = a * jax.nn.sigmoid(gt)
    h = _causal_dwconv(h, dw_w, dw_b)
    h = jax.nn.silu(_layernorm(h, ln_g, ln_b))
    return h @ w_pw


def _compress(k, pe, w1, b1, w2):
    bsz, s = k.shape[:2]
    n_cmp = (s - CMP_LEN) // CMP_STRIDE + 1
    idx = jnp.arange(n_cmp)[:, None] * CMP_STRIDE + jnp.arange(CMP_LEN)[None, :]
    blk = k[:, idx] + pe[:, None, :]
    blk = blk.transpose(0, 1, 3, 2, 4).reshape(bsz, n_cmp, N_KV, CMP_LEN * HEAD_DIM)
    return jax.nn.gelu(blk @ w1 + b1) @ w2


def _nsa(q, kc, vc, ks, vs, kw, vw, gates, rel_bias):
    bsz, s = q.shape[:2]
    n_cmp = kc.shape[1]
    n_slc = s // SEL_LEN
    n_top = min(N_SEL, n_slc)
    qg = q.reshape(bsz, s, N_KV, GROUP, HEAD_DIM)
    gg = gates.reshape(bsz, s, N_KV, GROUP, 3)
    ks_b = ks.reshape(bsz, n_slc, SEL_LEN, N_KV, HEAD_DIM).transpose(0, 3, 1, 2, 4)
    vs_b = vs.reshape(bsz, n_slc, SEL_LEN, N_KV, HEAD_DIM).transpose(0, 3, 1, 2, 4)
    kw_p = jnp.pad(kw, ((0, 0), (WINDOW, 0), (0, 0), (0, 0)))
    vw_p = jnp.pad(vw, ((0, 0), (WINDOW, 0), (0, 0), (0, 0)))
    cmp_start = jnp.arange(n_cmp) * CMP_STRIDE
    cmp_end = cmp_start + CMP_LEN - 1
    slc_start = jnp.arange(n_slc) * SEL_LEN
    overlap = jnp.clip(jnp.minimum(cmp_end[:, None] + 1, slc_start[None, :] + SEL_LEN)
                       - jnp.maximum(cmp_start[:, None], slc_start[None, :]), 0, None).astype(jnp.float32) / CMP_LEN
    tab = rel_bias.astype(jnp.float32).reshape(N_BUCKETS, N_KV, GROUP)
    tab_g = tab.transpose(1, 0, 2)
    b_ix = jnp.arange(bsz)[:, None, None, None]
    g_ix = jnp.arange(N_KV)[None, :, None, None]
    blk_ids = jnp.arange(n_slc)
    tok_in_blk = jnp.arange(SEL_LEN)
    win_off = jnp.arange(WINDOW + Q_CHUNK)

    def chunk(c):
        t0 = c * Q_CHUNK
        t = t0 + jnp.arange(Q_CHUNK)
        qc = lax.dynamic_slice_in_dim(qg, t0, Q_CHUNK, axis=1)
        gc = lax.dynamic_slice_in_dim(gg, t0, Q_CHUNK, axis=1)
        d1 = t[:, None] - cmp_end[None, :]
        m1 = d1 >= 0
        s1 = jnp.einsum('bqghd,bngd->bghqn', qc, kc).astype(jnp.float32) + tab[_t5_bucket(d1)].transpose(2, 3, 0, 1)
        p1 = jax.nn.softmax(jnp.where(m1, s1, NEG), axis=-1) * m1
        o1 = jnp.einsum('bghqn,bngd->bqghd', p1.astype(vc.dtype), vc)
        imp = jnp.einsum('bghqn,nj->bgqj', p1, overlap)
        t_blk = t // SEL_LEN
        causal_blk = blk_ids[None, :] <= t_blk[:, None]
        forced = (blk_ids[None, :] == 0) | (blk_ids[None, :] >= t_blk[:, None] - 1)
        score = jnp.where(causal_blk, jnp.where(forced, FORCE, imp), -1.0)
        _, sel = lax.top_k(score, n_top)
        kg = ks_b[b_ix, g_ix, sel]
        vg = vs_b[b_ix, g_ix, sel]
        pos = sel[..., None] * SEL_LEN + tok_in_blk
        d2 = t[:, None, None] - pos
        m2 = d2 >= 0
        b2 = jnp.moveaxis(tab_g[g_ix[..., None], _t5_bucket(d2)], -1, 2)
        s2 = jnp.einsum('bqghd,bgqkld->bghqkl', qc, kg).astype(jnp.float32) + b2
        s2 = jnp.where(m2[:, :, None], s2, NEG)
        p2 = jax.nn.softmax(s2.reshape(s2.shape[:4] + (-1,)), axis=-1).reshape(s2.shape)
        o2 = jnp.einsum('bghqkl,bgqkld->bqghd', p2.astype(vg.dtype), vg)
        kwc = lax.dynamic_slice_in_dim(kw_p, t0, WINDOW + Q_CHUNK, axis=1)
        vwc = lax.dynamic_slice_in_dim(vw_p, t0, WINDOW + Q_CHUNK, axis=1)
        kp = t0 - WINDOW + win_off
        d3 = t[:, None] - kp[None, :]
        m3 = (d3 >= 0) & (d3 < WINDOW) & (kp[None, :] >= 0)
        s3 = jnp.einsum('bqghd,bkgd->bghqk', qc, kwc).astype(jnp.float32) + tab[_t5_bucket(d3)].transpose(2, 3, 0, 1)
        p3 = jax.nn.softmax(jnp.where(m3, s3, NEG), axis=-1)
        o3 = jnp.einsum('bghqk,bkgd->bqghd', p3.astype(vwc.dtype), vwc)
        return gc[..., 0:1] * o1 + gc[..., 1:2] * o2 + gc[..., 2:3] * o3

    out = lax.map(chunk, jnp.arange(s // Q_CHUNK))
    return jnp.moveaxis(out, 0, 1).reshape(bsz, s, N_HEADS * HEAD_DIM)


def _cross_attn(hn, memn, wq, wkv, wo):
    bsz, s, _ = hn.shape
    m = memn.shape[1]
    q = (hn @ wq).reshape(bsz, s, X_HEADS, X_HEAD_DIM) * (X_HEAD_DIM ** -0.5)
    k, v = jnp.split(memn @ wkv, 2, axis=-1)
    k = k.reshape(bsz, m, X_HEADS, X_HEAD_DIM)
    v = v.reshape(bsz, m, X_HEADS, X_HEAD_DIM)
    p = jax.nn.softmax(jnp.einsum('bshd,bmhd->bhsm', q, k).astype(jnp.float32), axis=-1)
    o = jnp.einsum('bhsm,bmhd->bshd', p.astype(v.dtype), v).reshape(bsz, s, D_MODEL)
    return o @ wo


def _conv_ffn(hn, w_up, dw_w, dw_b, w_down):
    u = _causal_dwconv(hn @ w_up, dw_w, dw_b)
    g, v = jnp.split(u, 2, axis=-1)
    return (jax.nn.silu(g) * v) @ w_down


def setup_inputs(seed: int = 0) -> dict:
    key = jax.random.key(seed)
    ks = jax.random.split(key, 32)
    f32 = jnp.float32
    L = DEPTH

    def nrm(k, shape, scale):
        return jax.random.normal(k, shape, f32) * scale

    return {
        'x': nrm(ks[0], (BATCH, SEQ, D_MODEL), 1.0),
        'mem': nrm(ks[1], (BATCH, MEM_LEN, D_MODEL), 1.0),
        'norm1_g': 1.0 + nrm(ks[2], (L, D_MODEL), 0.02),
        'w_in': nrm(ks[3], (L, D_MODEL, IN_W), D_MODEL ** -0.5),
        'conv_dw_w': nrm(ks[4], (L, CONV_K, CONV_CH), CONV_K ** -0.5),
        'conv_dw_b': nrm(ks[5], (L, CONV_CH), 0.02),
        'conv_ln_g': 1.0 + nrm(ks[6], (L, CONV_CH), 0.02),
        'conv_ln_b': nrm(ks[7], (L, CONV_CH), 0.02),
        'conv_w_pw': nrm(ks[8], (L, CONV_CH, D_MODEL), CONV_CH ** -0.5),
        'cmp_pe': nrm(ks[9], (L, 2, CMP_LEN, HEAD_DIM), 0.1),
        'cmp_w1': nrm(ks[10], (L, 2, CMP_LEN * HEAD_DIM, CMP_HIDDEN), (CMP_LEN * HEAD_DIM) ** -0.5),
        'cmp_b1': nrm(ks[11], (L, 2, CMP_HIDDEN), 0.02),
        'cmp_w2': nrm(ks[12], (L, 2, CMP_HIDDEN, HEAD_DIM), CMP_HIDDEN ** -0.5),
        'nsa_w_o': nrm(ks[13], (L, N_HEADS * HEAD_DIM, D_MODEL), (N_HEADS * HEAD_DIM) ** -0.5),
        'w_out': nrm(ks[14], (L, D_MODEL, D_MODEL), D_MODEL ** -0.5),
        'norm2_g': 1.0 + nrm(ks[15], (L, D_MODEL), 0.02),
        'mem_norm_g': 1.0 + nrm(ks[16], (L, D_MODEL), 0.02),
        'xa_wq': nrm(ks[17], (L, D_MODEL, D_MODEL), D_MODEL ** -0.5),
        'xa_wkv': nrm(ks[18], (L, D_MODEL, 2 * D_MODEL), D_MODEL ** -0.5),
        'xa_wo': nrm(ks[19], (L, D_MODEL, D_MODEL), D_MODEL ** -0.5),
        'norm3_g': 1.0 + nrm(ks[20], (L, D_MODEL), 0.02),
        'ffn_w_up': nrm(ks[21], (L, D_MODEL, 2 * D_FF), D_MODEL ** -0.5),
        'ffn_dw_w': nrm(ks[22], (L, FFN_K, 2 * D_FF), FFN_K ** -0.5),
        'ffn_dw_b': nrm(ks[23], (L, 2 * D_FF), 0.02),
        'ffn_w_down': nrm(ks[24], (L, D_FF, D_MODEL), D_FF ** -0.5),
        'rel_bias': nrm(ks[25], (N_BUCKETS, N_HEADS), 0.5),
        'final_g': 1.0 + nrm(ks[26], (D_MODEL,), 0.02),
    }


def reference(x, mem, norm1_g, w_in, conv_dw_w, conv_dw_b, conv_ln_g, conv_ln_b, conv_w_pw,
              cmp_pe, cmp_w1, cmp_b1, cmp_w2, nsa_w_o, w_out, norm2_g, mem_norm_g,
              xa_wq, xa_wkv, xa_wo, norm3_g, ffn_w_up, ffn_dw_w, ffn_dw_b, ffn_w_down,
              rel_bias, final_g):
    bsz, s, _ = x.shape
    h = x
    for l in range(DEPTH):
        n = _rmsnorm(h, norm1_g[l])
        parts = jnp.split(n @ w_in[l], IN_SPLITS, axis=-1)
        conv_in, q, k_c, v_c, k_s, v_s, k_w, v_w, nsa_g, merge_g = parts
        hd = lambda a, nh: a.reshape(bsz, s, nh, HEAD_DIM)
        q = hd(q, N_HEADS) * (HEAD_DIM ** -0.5)
        kc = _compress(hd(k_c, N_KV), cmp_pe[l, 0], cmp_w1[l, 0], cmp_b1[l, 0], cmp_w2[l, 0])
        vc = _compress(hd(v_c, N_KV), cmp_pe[l, 1], cmp_w1[l, 1], cmp_b1[l, 1], cmp_w2[l, 1])
        gates = jax.nn.sigmoid(nsa_g).reshape(bsz, s, N_HEADS, 3)
        y_attn = _nsa(q, kc, vc, hd(k_s, N_KV), hd(v_s, N_KV), hd(k_w, N_KV), hd(v_w, N_KV), gates, rel_bias) @ nsa_w_o[l]
        y_conv = _conformer_conv(conv_in, conv_dw_w[l], conv_dw_b[l], conv_ln_g[l], conv_ln_b[l], conv_w_pw[l])
        g_a, g_b = jnp.split(jax.nn.sigmoid(merge_g), 2, axis=-1)
        h = h + (g_a * y_conv + g_b * y_attn) @ w_out[l]
        h = h + _cross_attn(_rmsnorm(h, norm2_g[l]), _rmsnorm(mem, mem_norm_g[l]), xa_wq[l], xa_wkv[l], xa_wo[l])
        h = h + _conv_ffn(_rmsnorm(h, norm3_g[l]), ffn_w_up[l], ffn_dw_w[l], ffn_dw_b[l], ffn_w_down[l])
    return _rmsnorm(h, final_g)
```

```python
import numpy as np
from contextlib import ExitStack
import concourse.bass as bass
import concourse.mybir as mybir
from concourse.bass_utils import run_bass_kernel_spmd

F32 = mybir.dt.float32
BF16 = mybir.dt.bfloat16
AF = mybir.ActivationFunctionType
ALU = mybir.AluOpType
AX = mybir.AxisListType

D = 1024
SEQ = 4096
MEM = 256
IN_W = 5680
D_FF = 2816
MASKV = -30000.0
EPS = 1e-6

ENGS = ['pe', 'act', 'dve', 'pool', 'sp']


class Buf:
    __slots__ = ('w', 'r')

    def __init__(self):
        self.w = None
        self.r = {}


class Tl:
    def __init__(self, t, sem=None):
        self.t = t
        self.b = Buf()
        self.sem = sem


def _b(x):
    return x.b if isinstance(x, Tl) else x


class Sched:
    def __init__(self, nc, es):
        self.nc = nc
        self.es = es
        self.q = {e: [] for e in ENGS}
        self.sem = {}
        self.cnt = {}
        self.known = {e: {} for e in ENGS}
        self.ndma = 0

    def semh(self, key):
        if key not in self.sem:
            self.sem[key] = self.es.enter_context(self.nc.semaphore('s_' + key))
            self.cnt[key] = 0
        return self.sem[key]

    def newdma(self, name=None):
        self.ndma += 1
        key = 'd%d' % self.ndma
        self.semh(key)
        return key

    def _deps(self, eng, reads, writes):
        need = {}
        for b in reads:
            b = _b(b)
            if b.w:
                k, v = b.w
                need[k] = max(need.get(k, 0), v)
        for b in writes:
            b = _b(b)
            if b.w:
                k, v = b.w
                need[k] = max(need.get(k, 0), v)
            for k, v in b.r.items():
                need[k] = max(need.get(k, 0), v)
        out = []
        kn = self.known[eng]
        for k, v in need.items():
            if kn.get(k, 0) < v:
                kn[k] = v
                out.append((k, v))
        return out

    def _post(self, key, v, reads, writes):
        for b in reads:
            b = _b(b)
            b.r[key] = max(b.r.get(key, 0), v)
        for b in writes:
            b = _b(b)
            b.w = (key, v)
            b.r = {}

    def op(self, eng, meth, kw, reads=(), writes=()):
        self.semh(eng)
        waits = self._deps(eng, reads, writes)
        self.cnt[eng] += 1
        v = self.cnt[eng]
        self.q[eng].append((waits, meth, kw, eng, 1))
        self._post(eng, v, reads, writes)

    def dma(self, eng, semkey, kw, reads=(), writes=()):
        self.semh(semkey)
        waits = self._deps(eng, reads, writes)
        self.cnt[semkey] += 16
        v = self.cnt[semkey]
        self.q[eng].append((waits, 'dma_start', kw, semkey, 16))
        self._post(semkey, v, reads, writes)

    def barrier(self):
        for e in ENGS:
            kn = self.known[e]
            waits = []
            for k, v in self.cnt.items():
                if v > 0 and kn.get(k, 0) < v:
                    kn[k] = v
                    waits.append((k, v))
            if waits:
                self.q[e].append((waits, None, None, None, 0))

    def flush(self):
        self.barrier()
        nc = self.nc
        q = self.q
        sem = self.sem

        def run(engobj, items):
            for waits, meth, kw, key, inc in items:
                for k, v in waits:
                    engobj.wait_ge(sem[k], v)
                if meth is not None:
                    getattr(engobj, meth)(**kw).then_inc(sem[key], inc)

        with nc.Block() as block:
            @block.tensor
            def _(e):
                run(e, q['pe'])

            @block.scalar
            def _(e):
                run(e, q['act'])

            @block.vector
            def _(e):
                run(e, q['dve'])

            @block.gpsimd
            def _(e):
                run(e, q['pool'])

            @block.sync
            def _(e):
                run(e, q['sp'])
        self.q = {e: [] for e in ENGS}


class Ctx:
    pass


def _mm(S, out, lhsT, rhs, start, stop, reads, writes, skip=False):
    kw = dict(out=out, lhsT=lhsT, rhs=rhs, start=start, stop=stop)
    if skip:
        kw['skip_group_check'] = True
    S.op('pe', 'matmul', kw, reads, writes)


def _tp(S, out, in_, ident, reads, writes):
    S.op('pe', 'transpose', dict(out=out, in_=in_, identity=ident), reads, writes)


def _act(S, out, in_, func, reads, writes, **kw):
    S.op('act', 'activation', dict(out=out, in_=in_, func=func, **kw), reads, writes)


def _tt(S, eng, out, in0, in1, op, reads, writes):
    S.op(eng, 'tensor_tensor', dict(out=out, in0=in0, in1=in1, op=op), reads, writes)


def _ts(S, eng, out, in0, s1, op0, reads, writes, s2=None, op1=None):
    kw = dict(out=out, in0=in0, scalar1=s1, scalar2=s2, op0=op0)
    if op1 is not None:
        kw['op1'] = op1
    S.op(eng, 'tensor_scalar', kw, reads, writes)


def _stt(S, out, in0, scalar, in1, op0, op1, reads, writes, accum_out=None):
    kw = dict(out=out, in0=in0, scalar=scalar, in1=in1, op0=op0, op1=op1)
    if accum_out is not None:
        kw['accum_out'] = accum_out
    S.op('dve', 'scalar_tensor_tensor', kw, reads, writes)


def _cp(S, eng, out, in_, reads, writes):
    if eng == 'act':
        S.op('act', 'copy', dict(out=out, in_=in_), reads, writes)
    else:
        S.op(eng, 'tensor_copy', dict(out=out, in_=in_), reads, writes)


def _memset(S, eng, ap, val, writes):
    S.op(eng, 'memset', dict(ap=ap, constant=val), (), writes)


def load_w_cast(S, dst_tl, dst_ap_fn, src, kc_n, ncols, rows_per=128):
    for kc in range(kc_n):
        c0 = 0
        while c0 < ncols:
            c1 = min(ncols, c0 + 2048)
            S.dma('pool', dst_tl.sem, dict(out=dst_ap_fn(kc, c0, c1),
                                           in_=src[kc * 128:(kc + 1) * 128, c0:c1]),
                  writes=[dst_tl])
            c0 = c1


def cols_from_rows(S, C, src_rows, R, ncol, dst_tl, dst_fn):
    stage = C.stage
    assert ncol <= stage.t.shape[1] and R <= 32
    S.dma('sp', stage.sem, dict(out=stage.t[0:R, 0:ncol], in_=src_rows), writes=[stage])
    for ch in range(ncol // 128):
        _tp(S, C.pst.t[:, 0:R], stage.t[0:R, ch * 128:(ch + 1) * 128], C.identF.t[0:R, 0:R],
            [stage, C.identF], [C.pst])
        _cp(S, 'dve', dst_fn(ch), C.pst.t[:, 0:R], [C.pst], [dst_tl])


def phase_A(nc, S, Sq, T):
    NSUP = Sq // 512
    with ExitStack() as es:
        def sb(name, shape, dt, sem=False):
            t = es.enter_context(nc.sbuf_tensor(name, shape, dt))
            return Tl(t, S.newdma() if sem else None)

        def ps(name, shape, dt=F32):
            return Tl(es.enter_context(nc.psum_tensor(name, shape, dt)))

        C = Ctx()
        win = sb('a_win', [128, 8, IN_W], BF16, sem=True)
        dg = sb('a_dg', [128, 4, 31, 128], BF16)
        identF = sb('a_identF', [128, 128], F32, sem=True)
        identB = sb('a_identB', [128, 128], BF16, sem=True)
        onesF = sb('a_onesF', [128, 128], F32)
        C.identF = identF
        C.stage = sb('a_stage', [32, 1024], F32, sem=True)
        g1T = sb('a_g1T', [128, 8, 1], F32)
        dwT = sb('a_dwT', [128, 4, 31], F32)
        dwb = sb('a_dwb', [128, 4, 1], F32)
        lng = sb('a_lng', [128, 4, 1], F32)
        lnb = sb('a_lnb', [128, 4, 1], F32)
        xs = [sb('a_x%d' % i, [128, D], F32, sem=True) for i in range(4)]
        ss = sb('a_ss', [128, 4], F32)
        rt = sb('a_rt', [128, 4], F32)
        rstd = sb('a_rstd', [128, 4], F32)
        junk = sb('a_junk', [128, D], BF16)
        ntok = [sb('a_ntok%d' % i, [128, D], BF16) for i in range(2)]
        nT = sb('a_nT', [128, 8, 512], BF16)
        hglu = sb('a_hglu', [128, 4, 542], BF16)
        stg = [sb('a_stg%d' % i, [128, 512], BF16, sem=True) for i in range(6)]
        sg = [sb('a_sg%d' % i, [128, 512], F32) for i in range(2)]
        ycs = sb('a_ycs', [128, 4, 512], F32)
        ysq = sb('a_ysq', [128, 4, 512], F32)
        mean = sb('a_mean', [128, 512], F32)
        msq = sb('a_msq', [128, 512], F32)
        rs = sb('a_rs', [128, 512], F32)
        dtl = [sb('a_d%d' % i, [128, 512], F32) for i in range(2)]
        vstg = [sb('a_vstg%d' % i, [128, 2, 4, 65], BF16, sem=True) for i in range(2)]
        gstg = [sb('a_gstg%d' % i, [128, 48], F32, sem=True) for i in range(2)]
        epsT = sb('a_eps', [128, 1], F32)

        ptr = ps('a_ptr', [128, 1024], BF16)
        pf = [ps('a_pf%d' % i, [128, 512]) for i in range(3)]
        pv = ps('a_pv', [128, 512])
        pg = ps('a_pg', [128, 48])
        C.pst = pg
        pc = [ps('a_pc%d' % i, [128, 512]) for i in range(2)]

        S.dma('sp', identF.sem, dict(out=identF.t[:, :], in_=T['identF']), writes=[identF])
        S.dma('pool', identB.sem, dict(out=identB.t[:, :], in_=T['identF']), writes=[identB])
        _memset(S, 'dve', onesF.t[:, :], 1.0 / 512.0, [onesF])
        _memset(S, 'dve', epsT.t[:, :], EPS, [epsT])
        _memset(S, 'dve', hglu.t[:, :, :], 0.0, [hglu])
        for v in vstg:
            _memset(S, 'dve', v.t[:, :, :, :], 1.0, [v])
        load_w_cast(S, win, lambda kc, c0, c1: win.t[:, kc, c0:c1], T['w_in'], 8, IN_W)
        cols_from_rows(S, C, T['norm1_g'], 1, 1024, g1T, lambda ch: g1T.t[:, ch, :])
        cols_from_rows(S, C, T['conv_dw_w'], 31, 512, dwT, lambda ch: dwT.t[:, ch, :])
        cols_from_rows(S, C, T['conv_dw_b'], 1, 512, dwb, lambda ch: dwb.t[:, ch, :])
        cols_from_rows(S, C, T['conv_ln_g'], 1, 512, lng, lambda ch: lng.t[:, ch, :])
        cols_from_rows(S, C, T['conv_ln_b'], 1, 512, lnb, lambda ch: lnb.t[:, ch, :])
        for kc in range(8):
            _ts(S, 'dve', win.t[:, kc, :], win.t[:, kc, :], g1T.t[:, kc, :], ALU.mult, [win, g1T], [win])
        for ch in range(4):
            for j in range(31):
                _ts(S, 'dve', dg.t[:, ch, j, :], identF.t[:, :], dwT.t[:, ch, j:j + 1], ALU.mult,
                    [identF, dwT], [dg])

        x = T['x']
        fchunks = []
        for i in range(4):
            fchunks.append((512 + 128 * i, 'gate', i))
            fchunks.append((128 * i, 'a', i))
        for i in range(8):
            fchunks.append((1024 + 128 * i, 'q', i))
        for nm, c0 in (('kc', 2048), ('vc', 2304), ('ks', 2560), ('kw', 3072)):
            for i in range(2):
                fchunks.append((c0 + 128 * i, nm, i))
        for i in range(16):
            fchunks.append((3632 + 128 * i, 'gm', i))

        import os
        STG = int(os.environ.get('STG', '9'))
        nstg = 0
        npf = 0
        for st in range(NSUP if STG >= 1 else 0):
            for t in range(4):
                tt = st * 4 + t
                xb = xs[tt % 4]
                S.dma('sp', xb.sem, dict(out=xb.t[:, :], in_=x[tt * 128:(tt + 1) * 128, :]), writes=[xb])
                _stt(S, junk.t[:, :], xb.t[:, :], 1.0, xb.t[:, :], ALU.mult, ALU.mult, [xb], [junk, ss],
                     accum_out=ss.t[:, t:t + 1])
            _act(S, rt.t[:, :], ss.t[:, :], AF.Sqrt, [ss, epsT], [rt], scale=1.0 / D, bias=epsT.t[:, :])
            S.op('dve', 'reciprocal', dict(out=rstd.t[:, :], in_=rt.t[:, :]), [rt], [rstd])
            for t in range(4):
                tt = st * 4 + t
                xb = xs[tt % 4]
                nk = ntok[t % 2]
                _ts(S, 'dve', nk.t[:, :], xb.t[:, :], rstd.t[:, t:t + 1], ALU.mult, [xb, rstd], [nk])
                for kc in range(8):
                    _tp(S, ptr.t[:, kc * 128:(kc + 1) * 128], nk.t[:, kc * 128:(kc + 1) * 128], identB.t[:, :],
                        [nk, identB], [ptr])
                _cp(S, 'act', nT.t[:, :, t * 128:(t + 1) * 128],
                    ptr.t[:, :].rearrange('p (k q) -> p k q', k=8), [ptr], [nT])
            for t in range(4 if STG >= 2 else 0):
                tt = st * 4 + t
                for half, c0 in ((0, 2816), (1, 3328)):
                    for kc in range(8):
                        _mm(S, pv.t[:, half * 256:(half + 1) * 256], nT.t[:, kc, t * 128:(t + 1) * 128],
                            win.t[:, kc, c0:c0 + 256], kc == 0, kc == 7, [nT, win], [pv])
                for kc in range(8):
                    _mm(S, pg.t[:, :], nT.t[:, kc, t * 128:(t + 1) * 128], win.t[:, kc, 3584:3632],
                        kc == 0, kc == 7, [nT, win], [pg])
                vs = vstg[tt % 2]
                _cp(S, 'dve', vs.t[:, :, :, 0:64], pv.t[:, :].rearrange('p (a g d) -> p a g d', a=2, g=4),
                    [pv], [vs])
                S.dma('sp', vs.sem, dict(out=T['VS1'][tt * 128:(tt + 1) * 128, :, :], in_=vs.t[:, 0, :, :]),
                      reads=[vs])
                S.dma('sp', vs.sem, dict(out=T['VW1'][tt * 128:(tt + 1) * 128, :, :], in_=vs.t[:, 1, :, :]),
                      reads=[vs])
                gs = gstg[tt % 2]
                _act(S, gs.t[:, :], pg.t[:, :], AF.Sigmoid, [pg], [gs])
                S.dma('sp', gs.sem, dict(out=T['G'][tt * 128:(tt + 1) * 128, :], in_=gs.t[:, :]), reads=[gs])
            cs = slice(st * 512, (st + 1) * 512)
            for (c0, kind, idx) in (fchunks if STG >= 3 else []):
                p = pf[npf % 3]
                npf += 1
                for kc in range(8):
                    _mm(S, p.t[:, :], win.t[:, kc, c0:c0 + 128], nT.t[:, kc, :], kc == 0, kc == 7, [win, nT], [p])
                if kind == 'gate':
                    sgt = sg[idx % 2]
                    _act(S, sgt.t[:, :], p.t[:, :], AF.Sigmoid, [p], [sgt])
                elif kind == 'a':
                    sgt = sg[idx % 2]
                    _tt(S, 'dve', hglu.t[:, idx, 30:542], p.t[:, :], sgt.t[:, :], ALU.mult, [p, sgt], [hglu])
                else:
                    sl = stg[nstg % 6]
                    nstg += 1
                    if kind == 'q':
                        _act(S, sl.t[:, :], p.t[:, :], AF.Copy, [p], [sl], scale=0.125)
                        dst = T['QT'][2 * idx:2 * idx + 2, :, cs].rearrange('h d s -> (h d) s')
                    elif kind == 'gm':
                        _act(S, sl.t[:, :], p.t[:, :], AF.Sigmoid, [p], [sl])
                        dst = T['GM'][idx, :, cs]
                    else:
                        _cp(S, 'dve', sl.t[:, :], p.t[:, :], [p], [sl])
                        dst = T[{'kc': 'KC', 'vc': 'VC', 'ks': 'KS', 'kw': 'KW'}[kind]][2 * idx:2 * idx + 2, :, cs] \
                            .rearrange('g d s -> (g d) s')
                    S.dma('sp', sl.sem, dict(out=dst, in_=sl.t[:, :]), reads=[sl])
            if STG < 4:
                continue
            for ch in range(4):
                p = pc[ch % 2]
                for j in range(0, 31, int(os.environ.get('JSTEP', '1'))):
                    _mm(S, p.t[:, :], dg.t[:, ch, j, :], hglu.t[:, ch, j:j + 512], j == 0, j == 30, [dg, hglu], [p])
                CP = int(os.environ.get('CP', '15'))
                if CP & 2:
                    _ts(S, 'dve', ycs.t[:, ch, :], p.t[:, :], dwb.t[:, ch, :], ALU.add, [p, dwb], [ycs])
                if CP & 4:
                    _tt(S, 'pool', ysq.t[:, ch, :], ycs.t[:, ch, :], ycs.t[:, ch, :], ALU.mult, [ycs], [ysq])
            if CP & 8:
                _cp(S, 'dve', hglu.t[:, :, 0:30], hglu.t[:, :, 512:542], [hglu], [hglu])
            if STG < 5:
                continue
            pm = pf[npf % 3]
            npf += 1
            pq = pf[npf % 3]
            npf += 1
            for ch in range(4):
                _mm(S, pm.t[:, :], onesF.t[:, :], ycs.t[:, ch, :], ch == 0, ch == 3, [onesF, ycs], [pm])
            for ch in range(4):
                _mm(S, pq.t[:, :], onesF.t[:, :], ysq.t[:, ch, :], ch == 0, ch == 3, [onesF, ysq], [pq])
            _cp(S, 'act', mean.t[:, :], pm.t[:, :], [pm], [mean])
            _tt(S, 'dve', msq.t[:, :], mean.t[:, :], mean.t[:, :], ALU.mult, [mean], [msq])
            _tt(S, 'dve', msq.t[:, :], pq.t[:, :], msq.t[:, :], ALU.subtract, [pq, msq], [msq])
            _act(S, msq.t[:, :], msq.t[:, :], AF.Sqrt, [msq, epsT], [msq], bias=epsT.t[:, :])
            S.op('dve', 'reciprocal', dict(out=rs.t[:, :], in_=msq.t[:, :]), [msq], [rs])
            for ch in range(4):
                d_ = dtl[ch % 2]
                _tt(S, 'dve', d_.t[:, :], ycs.t[:, ch, :], mean.t[:, :], ALU.subtract, [ycs, mean], [d_])
                _tt(S, 'dve', d_.t[:, :], d_.t[:, :], rs.t[:, :], ALU.mult, [d_, rs], [d_])
                _ts(S, 'dve', d_.t[:, :], d_.t[:, :], lng.t[:, ch, :], ALU.mult, [d_, lng, lnb], [d_],
                    s2=lnb.t[:, ch, :], op1=ALU.add)
                sl = stg[nstg % 6]
                nstg += 1
                _act(S, sl.t[:, :], d_.t[:, :], AF.Silu, [d_], [sl])
                S.dma('sp', sl.sem, dict(out=T['HC'][ch, :, cs], in_=sl.t[:, :]), reads=[sl])
        S.flush()


def _mk(nc, es, S):
    def sb(name, shape, dt, sem=False):
        t = es.enter_context(nc.sbuf_tensor(name, shape, dt))
        return Tl(t, S.newdma() if sem else None)

    def ps(name, shape, dt=F32):
        return Tl(es.enter_context(nc.psum_tensor(name, shape, dt)))
    return sb, ps


def cmp_chunks(Sq):
    NC = Sq // 16 - 1
    out = []
    n0 = 0
    while n0 < NC:
        out.append((n0, min(128, NC - n0)))
        n0 += 128
    return NC, out


def phase_B(nc, S, Sq, T, kcT, vco):
    NC, chunks = cmp_chunks(Sq)
    with ExitStack() as es:
        sb, ps = _mk(nc, es, S)
        C = Ctx()
        identF = sb('b_identF', [128, 128], F32, sem=True)
        C.identF = identF
        C.stage = sb('b_stage', [32, 1024], F32, sem=True)
        pst = ps('b_pst', [128, 48])
        C.pst = pst
        kin = [sb('b_kin%d' % i, [64, 4, Sq], BF16, sem=True) for i in range(2)]
        w1s = sb('b_w1s', [64, 2, 32, 256], BF16, sem=True)
        w2s = sb('b_w2s', [128, 2, 2, 64], BF16, sem=True)
        peT = sb('b_peT', [64, 2, 32], BF16)
        b1T = sb('b_b1T', [128, 4, 1], F32)
        biasc = sb('b_biasc', [128, 4, 1], F32)
        hT = [sb('b_hT%d' % i, [128, 2, 512], BF16) for i in range(2)]
        xb = sb('b_xb', [128, 512], F32)
        x2 = sb('b_x2', [128, 512], F32)
        u = sb('b_u', [128, 512], F32)
        sgm = sb('b_sgm', [128, 512], F32)
        ovs = sb('b_ovs', [128, 2, 64], F32, sem=True)
        pA = [ps('b_pA%d' % i, [128, 512]) for i in range(2)]
        pB = ps('b_pB', [128, 512])

        S.dma('sp', identF.sem, dict(out=identF.t[:, :], in_=T['identF']), writes=[identF])
        S.dma('sp', kin[0].sem, dict(out=kin[0].t[:, :, :], in_=T['KC'].rearrange('g d s -> d g s')), writes=[kin[0]])
        S.dma('sp', kin[1].sem, dict(out=kin[1].t[:, :, :], in_=T['VC'].rearrange('g d s -> d g s')), writes=[kin[1]])
        for kv in range(2):
            for l0 in range(0, 32, 8):
                S.dma('pool', w1s.sem, dict(out=w1s.t[:, kv, l0:l0 + 8, :],
                                            in_=T['cmp_w1'][kv, l0 * 64:(l0 + 8) * 64, :].rearrange('(l d) c -> d l c', d=64)),
                      writes=[w1s])
            S.dma('pool', w2s.sem, dict(out=w2s.t[:, kv, :, :],
                                        in_=T['cmp_w2'][kv].rearrange('(h p) d -> p h d', p=128)), writes=[w2s])
            S.dma('sp', C.stage.sem, dict(out=C.stage.t[0:32, 0:64], in_=T['cmp_pe'][kv]), writes=[C.stage])
            _tp(S, pst.t[0:64, 0:32], C.stage.t[0:32, 0:64], identF.t[0:32, 0:32], [C.stage, identF], [pst])
            _cp(S, 'dve', peT.t[:, kv, :], pst.t[0:64, 0:32], [pst], [peT])
        cols_from_rows(S, C, T['cmp_b1'], 1, 512, b1T, lambda ch: b1T.t[:, ch, :])
        _memset(S, 'dve', vco.t[:, :, :, :], 0.0, [vco])
        S.dma('sp', ovs.sem, dict(out=ovs.t[:, 0:len(chunks), :], in_=T['ovl'].rearrange('(c p) j -> p c j', p=128)),
              writes=[ovs])
        for ci in range(len(chunks)):
            for g in range(4):
                _cp(S, 'dve', vco.t[:, ci, g, 64:128], ovs.t[:, ci, :], [ovs], [vco])
        for kv in range(2):
            for half in range(2):
                for l in range(32):
                    _mm(S, pB.t[:, 0:1], w1s.t[0:64, kv, l, half * 128:(half + 1) * 128], peT.t[0:64, kv, l:l + 1],
                        l == 0, l == 31, [w1s, peT], [pB])
                _tt(S, 'dve', biasc.t[:, kv * 2 + half, :], pB.t[:, 0:1], b1T.t[:, kv * 2 + half, :], ALU.add,
                    [pB, b1T], [biasc])
        npa = 0
        for kv in range(2):
            src = kin[kv]
            for g in range(4):
                h_ = hT[(kv * 4 + g) % 2]
                for half in range(2):
                    p = pA[npa % 2]
                    npa += 1
                    for l in range(32):
                        _mm(S, p.t[:, 0:NC], w1s.t[0:64, kv, l, half * 128:(half + 1) * 128],
                            src.t[0:64, g, l:l + 16 * (NC - 1) + 1:16], l == 0, l == 31, [w1s, src], [p])
                    _ts(S, 'dve', xb.t[:, 0:NC], p.t[:, 0:NC], biasc.t[:, kv * 2 + half, :], ALU.add, [p, biasc], [xb])
                    _tt(S, 'pool', x2.t[:, 0:NC], xb.t[:, 0:NC], xb.t[:, 0:NC], ALU.mult, [xb], [x2])
                    _ts(S, 'dve', x2.t[:, 0:NC], x2.t[:, 0:NC], 0.044715, ALU.mult, [x2], [x2], s2=1.0, op1=ALU.add)
                    _tt(S, 'dve', u.t[:, 0:NC], x2.t[:, 0:NC], xb.t[:, 0:NC], ALU.mult, [x2, xb], [u])
                    _act(S, sgm.t[:, 0:NC], u.t[:, 0:NC], AF.Sigmoid, [u], [sgm], scale=1.5957691216057308)
                    _tt(S, 'dve', h_.t[:, half, 0:NC], xb.t[:, 0:NC], sgm.t[:, 0:NC], ALU.mult, [xb, sgm], [h_])
                if kv == 0:
                    for half in range(2):
                        _mm(S, pB.t[0:64, 0:NC], w2s.t[:, 0, half, :], h_.t[:, half, 0:NC], half == 0, half == 1,
                            [w2s, h_], [pB])
                    _cp(S, 'act', kcT.t[0:64, g, 0:NC], pB.t[0:64, 0:NC], [pB], [kcT])
                else:
                    for ci, (n0, sz) in enumerate(chunks):
                        for half in range(2):
                            _mm(S, pB.t[0:sz, 0:64], h_.t[:, half, n0:n0 + sz], w2s.t[:, 1, half, :], half == 0,
                                half == 1, [w2s, h_], [pB])
                        _cp(S, 'act', vco.t[0:sz, ci, g, 0:64], pB.t[0:sz, 0:64], [pB], [vco])
        S.flush()


def phase_C(nc, S, Sq, T, kcT, vco):
    NT = Sq // 128
    NC, chunks = cmp_chunks(Sq)
    OFFS = 8 * (NT - 1)
    with ExitStack() as es:
        sb, ps = _mk(nc, es, S)
        identB = sb('c_identB', [128, 128], BF16, sem=True)
        KE = sb('c_KE', [128, 4, Sq], BF16, sem=True)
        KWt = sb('c_KW', [64, 4, Sq], BF16, sem=True)
        KEe = Buf()
        KEe_sem = S.newdma()
        VS = sb('c_VS', [128, NT, 4, 65], BF16, sem=True)
        VW = sb('c_VW', [128, NT, 4, 65], BF16, sem=True)
        biasT = sb('c_biasT', [128, 2, 16, 128], BF16)
        tmpA = sb('c_tmpA', [128, 2048], F32, sem=True)
        tmpB = sb('c_tmpB', [128, 2048], F32, sem=True)
        m512 = sb('c_m512', [128, 512], BF16, sem=True)
        Bband = sb('c_Bband', [32, 16, 128], BF16)
        SelW = sb('c_SelW', [32, OFFS + 128 * len(chunks)], BF16, sem=True)
        mulB = sb('c_mulB', [128, 128], F32, sem=True)
        addB = sb('c_addB', [128, 128], F32, sem=True)
        QM = [sb('c_QM%d' % i, [128, 4, 4, 128], BF16, sem=True) for i in range(2)]
        QMq = [Buf() for _ in range(2)]
        QMm = [[Buf() for _ in range(4)] for _ in range(2)]
        gt = [sb('c_gt%d' % i, [128, 16, 3], F32, sem=True) for i in range(2)]
        NE = 4
        Et = [sb('c_E%d' % i, [128, 512], BF16) for i in range(NE)]
        o1s = [sb('c_o1s%d' % i, [128, 4, 128], F32) for i in range(4)]
        o2s = sb('c_o2s', [128, 4, 65], F32)
        o3s = [sb('c_o3s%d' % i, [128, 4, 65], F32) for i in range(4)]
        coef1 = [sb('c_coef1_%d' % i, [128, 4], F32) for i in range(4)]
        den = sb('c_den', [128, 4], F32)
        rden = sb('c_rden', [128, 4], F32)
        c2 = sb('c_c2', [128, 4], F32)
        c3 = sb('c_c3', [128, 4], F32)
        imp = sb('c_imp', [128, 64], F32)
        score = sb('c_score', [128, 64], F32)
        score2 = sb('c_score2', [128, 64], F32)
        m8a = sb('c_m8a', [128, 8], F32)
        m8b = sb('c_m8b', [128, 8], F32)
        nm = sb('c_nm', [128, 128], BF16)
        acc = sb('c_acc', [128, 4, 64], F32)
        ntk = sb('c_ntk', [128, 1024], BF16)
        nst = [sb('c_nst%d' % i, [128, 8, 128], BF16, sem=True) for i in range(2)]

        scp = [ps('c_sc%d' % i, [128, 512]) for i in range(3)]
        o1U = ps('c_o1U', [128, 512])
        o3p = ps('c_o3', [128, 4, 65])
        o2p = [ps('c_o2_%d' % i, [128, 4, 65]) for i in range(2)]
        ptr = ps('c_ptr', [128, 1024], BF16)

        S.dma('pool', identB.sem, dict(out=identB.t[:, :], in_=T['identF']), writes=[identB])
        S.dma('sp', KE.sem, dict(out=KE.t[0:64, :, :], in_=T['KS'].rearrange('g d s -> d g s')), writes=[KE])
        for g in range(4):
            for c0 in range(0, Sq, 2048):
                c1 = min(Sq, c0 + 2048)
                S.dma('pool', KEe_sem, dict(out=KE.t[64:128, g, c0:c1], in_=T['Econst'][:, c0:c1]), writes=[KEe])
        S.dma('sp', KWt.sem, dict(out=KWt.t[:, :, :], in_=T['KW'].rearrange('g d s -> d g s')), writes=[KWt])
        for k0 in range(0, NT, 8):
            k1 = min(NT, k0 + 8)
            S.dma('sp', VS.sem, dict(out=VS.t[:, k0:k1, :, :],
                                     in_=T['VS1'][k0 * 128:k1 * 128].rearrange('(k p) g d -> p k g d', p=128)),
                  writes=[VS])
            S.dma('sp', VW.sem, dict(out=VW.t[:, k0:k1, :, :],
                                     in_=T['VW1'][k0 * 128:k1 * 128].rearrange('(k p) g d -> p k g d', p=128)),
                  writes=[VW])
        S.dma('pool', m512.sem, dict(out=m512.t[:, :], in_=T['m512']), writes=[m512])
        S.dma('pool', SelW.sem, dict(out=SelW.t[:, :], in_=T['SelW']), writes=[SelW])
        S.dma('sp', mulB.sem, dict(out=mulB.t[:, :], in_=T['mulB']), writes=[mulB])
        S.dma('sp', addB.sem, dict(out=addB.t[:, :], in_=T['addB']), writes=[addB])
        for dl in range(2):
            S.dma('sp', tmpA.sem, dict(out=tmpA.t[:, :], in_=T['tz1'][dl].rearrange('k h q -> k (h q)')), writes=[tmpA])
            S.dma('sp', tmpB.sem, dict(out=tmpB.t[:, :], in_=T['tz31'][dl].rearrange('k h q -> k (h q)')), writes=[tmpB])
            _tt(S, 'dve', tmpA.t[:, :], tmpA.t[:, :], tmpB.t[:, :], ALU.subtract, [tmpA, tmpB], [tmpA])
            S.dma('sp', tmpB.sem, dict(out=tmpB.t[:, :], in_=T['tzm'][dl].rearrange('k h q -> k (h q)')), writes=[tmpB])
            _tt(S, 'dve', biasT.t[:, dl, :, :].rearrange('k h q -> k (h q)'), tmpA.t[:, :], tmpB.t[:, :], ALU.add,
                [tmpA, tmpB], [biasT])
        S.dma('sp', tmpA.sem, dict(out=tmpA.t[0:32, :], in_=T['cb1'].rearrange('k h q -> k (h q)')), writes=[tmpA])
        S.dma('sp', tmpB.sem, dict(out=tmpB.t[0:32, :], in_=T['cb31'].rearrange('k h q -> k (h q)')), writes=[tmpB])
        _tt(S, 'dve', tmpA.t[0:32, :], tmpA.t[0:32, :], tmpB.t[0:32, :], ALU.subtract, [tmpA, tmpB], [tmpA])
        S.dma('sp', tmpB.sem, dict(out=tmpB.t[0:32, :], in_=T['cbm'].rearrange('k h q -> k (h q)')), writes=[tmpB])
        _tt(S, 'dve', Bband.t[:, :, :].rearrange('k h q -> k (h q)'), tmpA.t[0:32, :], tmpB.t[0:32, :], ALU.add,
            [tmpA, tmpB], [Bband])
        _memset(S, 'dve', nm.t[:, :], 0.0, [nm])

        steps = []
        for qt in range(NT):
            for g in range(4):
                cl = [(ci, n0, sz) for ci, (n0, sz) in enumerate(chunks) if n0 <= 8 * qt + 6]
                for i, (ci, n0, sz) in enumerate(cl):
                    steps.append(dict(kind='cmp', qt=qt, g=g, ci=ci, n0=n0, sz=sz, first=i == 0, last=i == len(cl) - 1))
            for g in range(4):
                kl = list(range(max(0, qt - 4), qt + 1))
                for i, kt in enumerate(kl):
                    steps.append(dict(kind='win', qt=qt, g=g, kt=kt, sz=128, first=i == 0, last=i == len(kl) - 1))
            for g in range(4):
                for kt in range(qt + 1):
                    steps.append(dict(kind='sel', qt=qt, g=g, kt=kt, sz=128, first=kt == 0, last=kt == qt))
        cnt = dict(sc=0, e=0, o2=0)

        def load_q(qt):
            sl = qt % 2
            qs = slice(qt * 128, (qt + 1) * 128)
            S.dma('sp', QM[sl].sem, dict(out=QM[sl].t[0:64, :, :, :].rearrange('d g h q -> d (g h) q'),
                                         in_=T['QT'][:, :, qs].rearrange('h d q -> d h q')), writes=[QMq[sl]])
            S.dma('sp', gt[sl].sem, dict(out=gt[sl].t[:, :, :].rearrange('p h b -> p (h b)'), in_=T['G'][qs, :]),
                  writes=[gt[sl]])

        def emit_scores(st):
            qt, g, sz = st['qt'], st['g'], st['sz']
            sl = qt % 2
            sc = scp[cnt['sc'] % 3]
            cnt['sc'] += 1
            st['sc'] = sc
            qrow = QM[sl].t[0:64, g, :, :].rearrange('d h q -> d (h q)')
            if st['kind'] == 'cmp':
                n0 = st['n0']
                a = n0 - 8 * qt + OFFS
                _mm(S, sc.t[0:sz, :], kcT.t[0:64, g, n0:n0 + sz], qrow, True, False, [kcT, QMq[sl]], [sc])
                _mm(S, sc.t[0:sz, :], SelW.t[0:32, a:a + sz],
                    Bband.t[0:32, 4 * g:4 * g + 4, :].rearrange('k h q -> k (h q)'), False, True, [SelW, Bband], [sc])
                return
            kt = st['kt']
            dl = (qt - kt)
            extra = None
            if dl in (0, 1):
                extra = (biasT.t[:, dl, 4 * g:4 * g + 4, :].rearrange('k h q -> k (h q)'), biasT)
            elif dl == 4 and st['kind'] == 'win':
                extra = (m512.t[:, :], m512)
            ks = slice(kt * 128, (kt + 1) * 128)
            if st['kind'] == 'win':
                _mm(S, sc.t[:, :], KWt.t[0:64, g, ks], qrow, True, extra is None, [KWt, QMq[sl]], [sc])
            else:
                _mm(S, sc.t[:, :], KE.t[:, g, ks], QM[sl].t[:, g, :, :].rearrange('d h q -> d (h q)'), True,
                    extra is None, [KE, KEe, QMq[sl], QMm[sl][g]], [sc])
            if extra is not None:
                _mm(S, sc.t[:, :], identB.t[:, :], extra[0], False, True, [identB, extra[1]], [sc])

        def emit_exp(st):
            sz = st['sz']
            E = Et[cnt['e'] % NE]
            cnt['e'] += 1
            st['E'] = E
            _act(S, E.t[0:sz, :], st['sc'].t[0:sz, :], AF.Exp, [st['sc']], [E])

        def emit_pv(st):
            qt, g, sz, E = st['qt'], st['g'], st['sz'], st['E']
            if st['kind'] == 'cmp':
                for h in range(4):
                    _mm(S, o1U.t[:, h * 128:(h + 1) * 128], E.t[0:sz, h * 128:(h + 1) * 128], vco.t[0:sz, st['ci'], g, :],
                        st['first'] and h == 0, st['last'] and h == 3, [E, vco], [o1U], skip=True)
                if st['last']:
                    fin_cmp(qt, g)
                return
            kt = st['kt']
            if st['kind'] == 'win':
                op_, V = o3p, VW
            else:
                if st['first']:
                    st['o2'] = o2p[cnt['o2'] % 2]
                    cnt['o2'] += 1
                    cur['o2'] = st['o2']
                op_, V = cur['o2'], VS
            for h in range(4):
                _mm(S, op_.t[:, h, :], E.t[:, h * 128:(h + 1) * 128], V.t[:, kt, g, :], st['first'] and h == 0,
                    st['last'] and h == 3, [E, V], [op_], skip=True)
            if st['last']:
                if st['kind'] == 'win':
                    _cp(S, 'dve', o3s[g].t[:, :, :], o3p.t[:, :, :], [o3p], [o3s[g]])
                else:
                    fin_sel(qt, g, op_)

        cur = {}

        def fin_cmp(qt, g):
            sl = qt % 2
            o1 = o1s[g]
            _cp(S, 'dve', o1.t[:, :, :], o1U.t[:, :].rearrange('p (h c) -> p h c', h=4), [o1U], [o1])
            S.op('dve', 'tensor_reduce', dict(out=den.t[:, :], in_=o1.t[:, :, 64:128], axis=AX.X, op=ALU.add), [o1], [den])
            _ts(S, 'dve', den.t[:, :], den.t[:, :], 1e-30, ALU.max, [den], [den])
            S.op('dve', 'reciprocal', dict(out=rden.t[:, :], in_=den.t[:, :]), [den], [rden])
            _ts(S, 'dve', imp.t[:, :], o1.t[:, 0, 64:128], rden.t[:, 0:1], ALU.mult, [o1, rden], [imp])
            for h in range(1, 4):
                _stt(S, imp.t[:, :], o1.t[:, h, 64:128], rden.t[:, h:h + 1], imp.t[:, :], ALU.mult, ALU.add,
                     [o1, rden, imp], [imp])
            a = 62 - 2 * qt
            _tt(S, 'dve', score.t[:, :], imp.t[:, :], mulB.t[:, a:a + 64], ALU.mult, [imp, mulB], [score])
            _tt(S, 'dve', score.t[:, :], score.t[:, :], addB.t[:, a:a + 64], ALU.add, [score, addB], [score])
            _memset(S, 'dve', score.t[:, 0:1], 50.0, [score])
            S.op('dve', 'max', dict(out=m8a.t[:, :], in_=score.t[:, :]), [score], [m8a])
            S.op('dve', 'match_replace', dict(out=score2.t[:, :], in_to_replace=m8a.t[:, :], in_values=score.t[:, :],
                                              imm_value=-1e9), [score, m8a], [score2])
            S.op('dve', 'max', dict(out=m8b.t[:, :], in_=score2.t[:, :]), [score2], [m8b])
            _ts(S, 'dve', nm.t[:, 64:128], score.t[:, :], m8b.t[:, 7:8], ALU.is_lt, [score, m8b], [nm], s2=MASKV,
                op1=ALU.mult)
            _tp(S, ptr.t[:, 0:128], nm.t[:, :], identB.t[:, :], [nm, identB], [ptr])
            for h in range(4):
                _cp(S, 'dve', QM[sl].t[64:128, g, h, :], ptr.t[64:128, 0:128], [ptr], [QMm[sl][g]])
            _tt(S, 'dve', coef1[g].t[:, :], rden.t[:, :], gt[sl].t[:, 4 * g:4 * g + 4, 0], ALU.mult, [rden, gt[sl]],
                [coef1[g]])

        def fin_sel(qt, g, o2):
            sl = qt % 2
            _cp(S, 'dve', o2s.t[:, :, :], o2.t[:, :, :], [o2], [o2s])
            for (osrc, cf, br) in ((o2s, c2, 1), (o3s[g], c3, 2)):
                _ts(S, 'dve', den.t[:, :], osrc.t[:, :, 64], 1e-30, ALU.max, [osrc], [den])
                S.op('dve', 'reciprocal', dict(out=rden.t[:, :], in_=den.t[:, :]), [den], [rden])
                _tt(S, 'dve', cf.t[:, :], rden.t[:, :], gt[sl].t[:, 4 * g:4 * g + 4, br], ALU.mult, [rden, gt[sl]], [cf])
            o1 = o1s[g]
            for h in range(4):
                _ts(S, 'dve', acc.t[:, h, :], o1.t[:, h, 0:64], coef1[g].t[:, h:h + 1], ALU.mult, [o1, coef1[g]], [acc])
                _stt(S, acc.t[:, h, :], o3s[g].t[:, h, 0:64], c3.t[:, h:h + 1], acc.t[:, h, :], ALU.mult, ALU.add,
                     [o3s[g], c3, acc], [acc])
                c0 = (4 * g + h) * 64
                _stt(S, ntk.t[:, c0:c0 + 64], o2s.t[:, h, 0:64], c2.t[:, h:h + 1], acc.t[:, h, :], ALU.mult, ALU.add,
                     [o2s, c2, acc], [ntk])
            if g == 3:
                for kc in range(8):
                    _tp(S, ptr.t[:, kc * 128:(kc + 1) * 128], ntk.t[:, kc * 128:(kc + 1) * 128], identB.t[:, :],
                        [ntk, identB], [ptr])
                ns = nst[qt % 2]
                _cp(S, 'dve', ns.t[:, :, :], ptr.t[:, :].rearrange('p (k q) -> p k q', k=8), [ptr], [ns])
                S.dma('sp', ns.sem, dict(out=T['NSAT'][:, :, qt * 128:(qt + 1) * 128].rearrange('k p q -> p k q'),
                                         in_=ns.t[:, :, :]), reads=[ns])

        prev = None
        lastq = -1
        for st in steps:
            if st['qt'] != lastq:
                lastq = st['qt']
                load_q(lastq)
            emit_scores(st)
            emit_exp(st)
            if prev is not None:
                emit_pv(prev)
            prev = st
        emit_pv(prev)
        S.flush()


def phase_D(nc, S, Sq, T):
    NSUP = Sq // 512
    with ExitStack() as es:
        sb, ps = _mk(nc, es, S)
        C = Ctx()
        identF = sb('d_identF', [128, 128], F32, sem=True)
        identB = sb('d_identB', [128, 128], BF16, sem=True)
        onesB = sb('d_onesB', [128, 128], BF16)
        epsT = sb('d_eps', [128, 1], F32)
        C.identF = identF
        C.stage = sb('d_stage', [32, 1024], F32, sem=True)
        wno = sb('d_wno', [128, 8, D], BF16, sem=True)
        wpw = sb('d_wpw', [128, 4, D], BF16, sem=True)
        wout = sb('d_wout', [128, 8, D], BF16, sem=True)
        wq = sb('d_wq', [128, 8, D], BF16, sem=True)
        wo = sb('d_wo', [128, 8, D], BF16, sem=True)
        kmT = sb('d_kmT', [128, 8, 256], BF16)
        vm = sb('d_vm', [128, 2, D], BF16)
        g2T = sb('d_g2T', [128, 8, 1], F32)
        gmT = sb('d_gmT', [128, 8, 1], F32)
        ptr = ps('d_ptr', [128, 1024], BF16)
        pf = [ps('d_pf%d' % i, [128, 512]) for i in range(4)]
        ph = [ps('d_ph%d' % i, [128, 512]) for i in range(2)]
        pst = ps('d_pst', [128, 48])
        C.pst = pst
        ss = sb('d_ss', [128, 4], F32)
        rt = sb('d_rt', [128, 4], F32)
        rstd = sb('d_rstd', [128, 4], F32)
        junk = sb('d_junk', [128, D], BF16)
        ntok = [sb('d_ntok%d' % i, [128, D], BF16) for i in range(2)]

        S.dma('sp', identF.sem, dict(out=identF.t[:, :], in_=T['identF']), writes=[identF])
        S.dma('pool', identB.sem, dict(out=identB.t[:, :], in_=T['identF']), writes=[identB])
        _memset(S, 'dve', onesB.t[:, :], 1.0, [onesB])
        _memset(S, 'dve', epsT.t[:, :], EPS, [epsT])
        load_w_cast(S, wno, lambda kc, c0, c1: wno.t[:, kc, c0:c1], T['nsa_w_o'], 8, D)
        load_w_cast(S, wpw, lambda kc, c0, c1: wpw.t[:, kc, c0:c1], T['conv_w_pw'], 4, D)
        load_w_cast(S, wout, lambda kc, c0, c1: wout.t[:, kc, c0:c1], T['w_out'], 8, D)
        load_w_cast(S, wq, lambda kc, c0, c1: wq.t[:, kc, c0:c1], T['xa_wq'], 8, D)
        load_w_cast(S, wo, lambda kc, c0, c1: wo.t[:, kc, c0:c1], T['xa_wo'], 8, D)
        cols_from_rows(S, C, T['norm2_g'], 1, 1024, g2T, lambda ch: g2T.t[:, ch, :])
        cols_from_rows(S, C, T['mem_norm_g'], 1, 1024, gmT, lambda ch: gmT.t[:, ch, :])
        for kc in range(8):
            _ts(S, 'dve', wq.t[:, kc, :], wq.t[:, kc, :], g2T.t[:, kc, :], ALU.mult, [wq, g2T], [wq])

        def rms_T(xtiles, nT_tl, ncols_off):
            n = len(xtiles)
            for i, xb in enumerate(xtiles):
                _stt(S, junk.t[:, :], xb.t[:, :], 1.0, xb.t[:, :], ALU.mult, ALU.mult, [xb], [junk, ss],
                     accum_out=ss.t[:, i:i + 1])
            _act(S, rt.t[:, 0:n], ss.t[:, 0:n], AF.Sqrt, [ss, epsT], [rt], scale=1.0 / D, bias=epsT.t[:, :])
            S.op('dve', 'reciprocal', dict(out=rstd.t[:, 0:n], in_=rt.t[:, 0:n]), [rt], [rstd])
            for i, xb in enumerate(xtiles):
                nk = ntok[i % 2]
                _ts(S, 'dve', nk.t[:, :], xb.t[:, :], rstd.t[:, i:i + 1], ALU.mult, [xb, rstd], [nk])
                for kc in range(8):
                    _tp(S, ptr.t[:, kc * 128:(kc + 1) * 128], nk.t[:, kc * 128:(kc + 1) * 128], identB.t[:, :],
                        [nk, identB], [ptr])
                yield i, ptr

        with ExitStack() as es2:
            sb2, _ = _mk(nc, es2, S)
            wkv = sb2('d_wkv', [128, 8, 2 * D], BF16, sem=True)
            memx = [sb2('d_memx%d' % i, [128, D], F32, sem=True) for i in range(2)]
            memnT = sb2('d_memnT', [128, 8, 256], BF16)
            load_w_cast(S, wkv, lambda kc, c0, c1: wkv.t[:, kc, c0:c1], T['xa_wkv'], 8, 2 * D)
            for kc in range(8):
                _ts(S, 'dve', wkv.t[:, kc, :], wkv.t[:, kc, :], gmT.t[:, kc, :], ALU.mult, [wkv, gmT], [wkv])
            for mc in range(2):
                S.dma('sp', memx[mc].sem, dict(out=memx[mc].t[:, :], in_=T['mem'][mc * 128:(mc + 1) * 128, :]),
                      writes=[memx[mc]])
            for i, p_ in rms_T(memx, memnT, 0):
                _cp(S, 'act', memnT.t[:, :, i * 128:(i + 1) * 128], p_.t[:, :].rearrange('p (k q) -> p k q', k=8),
                    [p_], [memnT])
            for c in range(8):
                p = pf[c % 4]
                for kc in range(8):
                    _mm(S, p.t[:, 0:256], wkv.t[:, kc, c * 128:(c + 1) * 128], memnT.t[:, kc, :], kc == 0, kc == 7,
                        [wkv, memnT], [p])
                _cp(S, 'dve', kmT.t[:, c, :], p.t[:, 0:256], [p], [kmT])
            for mc in range(2):
                for half in range(2):
                    p = pf[(mc * 2 + half) % 4]
                    for kc in range(8):
                        _mm(S, p.t[:, :], memnT.t[:, kc, mc * 128:(mc + 1) * 128],
                            wkv.t[:, kc, D + half * 512:D + (half + 1) * 512], kc == 0, kc == 7, [wkv, memnT], [p])
                    _cp(S, 'dve', vm.t[:, mc, half * 512:(half + 1) * 512], p.t[:, :], [p], [vm])
            S.flush()

        nsa_s = sb('d_nsa', [128, 8, 512], BF16, sem=True)
        hc_s = sb('d_hc', [128, 4, 512], BF16, sem=True)
        gm_s = sb('d_gm', [128, 16, 512], BF16, sem=True)
        xs = [sb('d_x%d' % i, [128, D], F32, sem=True) for i in range(4)]
        mrg = sb('d_mrg', [128, 8, 512], BF16)
        n2T = sb('d_n2T', [128, 8, 512], BF16)
        qxT = sb('d_qxT', [128, 8, 512], BF16)
        PT = sb('d_PT', [128, 2, 512], BF16)
        oTn = sb('d_oTn', [128, 8, 512], BF16)
        t1 = sb('d_t1', [128, 512], F32)
        t2 = sb('d_t2', [128, 512], F32)
        rdn = sb('d_rdn', [128, 512], F32)
        n3s = [sb('d_n3s%d' % i, [128, 8, 128], BF16, sem=True) for i in range(2)]
        npf = 0
        for st in range(NSUP):
            cs = slice(st * 512, (st + 1) * 512)
            S.dma('sp', nsa_s.sem, dict(out=nsa_s.t[:, :, :], in_=T['NSAT'][:, :, cs].rearrange('k p s -> p k s')),
                  writes=[nsa_s])
            S.dma('sp', hc_s.sem, dict(out=hc_s.t[:, :, :], in_=T['HC'][:, :, cs].rearrange('k p s -> p k s')),
                  writes=[hc_s])
            S.dma('sp', gm_s.sem, dict(out=gm_s.t[:, :, :], in_=T['GM'][:, :, cs].rearrange('k p s -> p k s')),
                  writes=[gm_s])
            for t in range(4):
                tt = st * 4 + t
                S.dma('sp', xs[t].sem, dict(out=xs[t].t[:, :], in_=T['x'][tt * 128:(tt + 1) * 128, :]), writes=[xs[t]])
            for f in range(8):
                pa = pf[npf % 4]
                pcv = pf[(npf + 1) % 4]
                npf += 2
                for kc in range(8):
                    _mm(S, pa.t[:, :], wno.t[:, kc, f * 128:(f + 1) * 128], nsa_s.t[:, kc, :], kc == 0, kc == 7,
                        [wno, nsa_s], [pa])
                for c in range(4):
                    _mm(S, pcv.t[:, :], wpw.t[:, c, f * 128:(f + 1) * 128], hc_s.t[:, c, :], c == 0, c == 3,
                        [wpw, hc_s], [pcv])
                _tt(S, 'dve', t1.t[:, :], pa.t[:, :], gm_s.t[:, 8 + f, :], ALU.mult, [pa, gm_s], [t1])
                _tt(S, 'dve', t2.t[:, :], pcv.t[:, :], gm_s.t[:, f, :], ALU.mult, [pcv, gm_s], [t2])
                _tt(S, 'pool', mrg.t[:, f, :], t1.t[:, :], t2.t[:, :], ALU.add, [t1, t2], [mrg])
            for t in range(4):
                for half in range(2):
                    p = ph[half]
                    for f in range(8):
                        _mm(S, p.t[:, :], mrg.t[:, f, t * 128:(t + 1) * 128], wout.t[:, f, half * 512:(half + 1) * 512],
                            f == 0, f == 7, [mrg, wout], [p])
                    _tt(S, 'dve', xs[t].t[:, half * 512:(half + 1) * 512], p.t[:, :],
                        xs[t].t[:, half * 512:(half + 1) * 512], ALU.add, [p, xs[t]], [xs[t]])
            for i, p_ in rms_T(xs, n2T, 0):
                _cp(S, 'act', n2T.t[:, :, i * 128:(i + 1) * 128], p_.t[:, :].rearrange('p (k q) -> p k q', k=8),
                    [p_], [n2T])
            for c in range(8):
                p = pf[npf % 4]
                npf += 1
                for kc in range(8):
                    _mm(S, p.t[:, :], wq.t[:, kc, c * 128:(c + 1) * 128], n2T.t[:, kc, :], kc == 0, kc == 7,
                        [wq, n2T], [p])
                _act(S, qxT.t[:, c, :], p.t[:, :], AF.Copy, [p], [qxT], scale=1.0 / 16.0)
            for hd in range(4):
                for mc in range(2):
                    p = pf[npf % 4]
                    npf += 1
                    for dc in range(2):
                        _mm(S, p.t[:, :], kmT.t[:, hd * 2 + dc, mc * 128:(mc + 1) * 128], qxT.t[:, hd * 2 + dc, :],
                            dc == 0, dc == 1, [kmT, qxT], [p])
                    _act(S, PT.t[:, mc, :], p.t[:, :], AF.Exp, [p], [PT])
                pd = pf[npf % 4]
                npf += 1
                for mc in range(2):
                    _mm(S, pd.t[:, :], onesB.t[:, :], PT.t[:, mc, :], mc == 0, mc == 1, [onesB, PT], [pd])
                S.op('dve', 'reciprocal', dict(out=rdn.t[:, :], in_=pd.t[:, :]), [pd], [rdn])
                for dc in range(2):
                    po = pf[npf % 4]
                    npf += 1
                    for mc in range(2):
                        _mm(S, po.t[:, :], vm.t[:, mc, hd * 256 + dc * 128:hd * 256 + (dc + 1) * 128], PT.t[:, mc, :],
                            mc == 0, mc == 1, [vm, PT], [po])
                    _tt(S, 'dve', oTn.t[:, hd * 2 + dc, :], po.t[:, :], rdn.t[:, :], ALU.mult, [po, rdn], [oTn])
            for t in range(4):
                tt = st * 4 + t
                for half in range(2):
                    p = ph[half]
                    for c in range(8):
                        _mm(S, p.t[:, :], oTn.t[:, c, t * 128:(t + 1) * 128], wo.t[:, c, half * 512:(half + 1) * 512],
                            c == 0, c == 7, [oTn, wo], [p])
                    _tt(S, 'dve', xs[t].t[:, half * 512:(half + 1) * 512], p.t[:, :],
                        xs[t].t[:, half * 512:(half + 1) * 512], ALU.add, [p, xs[t]], [xs[t]])
                S.dma('sp', xs[t].sem, dict(out=T['H2'][tt * 128:(tt + 1) * 128, :], in_=xs[t].t[:, :]), reads=[xs[t]])
            for i, p_ in rms_T(xs, None, 0):
                tt = st * 4 + i
                ns = n3s[i % 2]
                _cp(S, 'act', ns.t[:, :, :], p_.t[:, :].rearrange('p (k q) -> p k q', k=8), [p_], [ns])
                S.dma('sp', ns.sem, dict(out=T['N3T'][:, :, tt * 128:(tt + 1) * 128].rearrange('k p q -> p k q'),
                                         in_=ns.t[:, :, :]), reads=[ns])
        S.flush()


def phase_E(nc, S, Sq, T):
    NSUP = Sq // 512
    NP = D_FF // 128
    with ExitStack() as es:
        sb, ps = _mk(nc, es, S)
        C = Ctx()
        identF = sb('e_identF', [128, 128], F32, sem=True)
        C.identF = identF
        C.stage = sb('e_stage', [32, 1024], F32, sem=True)
        pst = ps('e_pst', [128, 48])
        C.pst = pst
        wup = sb('e_wup', [128, 8, 2 * D_FF], BF16, sem=True)
        wdn = sb('e_wdn', [128, NP, D], BF16, sem=True)
        g3T = sb('e_g3T', [128, 8, 1], F32)
        fw = sb('e_fw', [128, 2 * NP, 3], F32)
        fb = sb('e_fb', [128, 2 * NP, 1], F32)
        fgB = sb('e_fgB', [128, D], F32)
        onesF = sb('e_onesF', [1, 128], F32)
        epsT = sb('e_eps', [128, 1], F32)
        halo = sb('e_halo', [128, 2 * NP, 2], F32)
        n3 = sb('e_n3', [128, 8, 512], BF16, sem=True)
        actT = sb('e_actT', [128, NP, 512], BF16)
        ub = [sb('e_ub%d' % i, [128, 514], F32) for i in range(3)]
        tb = [sb('e_tb%d' % i, [128, 512], F32) for i in range(3)]
        sgl = sb('e_sgl', [128, 512], F32)
        h2 = [sb('e_h2_%d' % i, [128, D], F32, sem=True) for i in range(2)]
        junk = sb('e_junk', [128, D], BF16)
        ss = sb('e_ss', [128, 1], F32)
        rt = sb('e_rt', [128, 1], F32)
        rstd = sb('e_rstd', [128, 1], F32)
        pu = [ps('e_pu%d' % i, [128, 512]) for i in range(3)]
        pd = [ps('e_pd%d' % i, [128, 512]) for i in range(2)]

        S.dma('sp', identF.sem, dict(out=identF.t[:, :], in_=T['identF']), writes=[identF])
        _memset(S, 'dve', onesF.t[:, :], 1.0, [onesF])
        _memset(S, 'dve', epsT.t[:, :], EPS, [epsT])
        _memset(S, 'dve', halo.t[:, :, :], 0.0, [halo])
        load_w_cast(S, wup, lambda kc, c0, c1: wup.t[:, kc, c0:c1], T['ffn_w_up'], 8, 2 * D_FF)
        load_w_cast(S, wdn, lambda kc, c0, c1: wdn.t[:, kc, c0:c1], T['ffn_w_down'], NP, D)
        cols_from_rows(S, C, T['norm3_g'], 1, 1024, g3T, lambda ch: g3T.t[:, ch, :])
        for kc in range(8):
            _ts(S, 'dve', wup.t[:, kc, :], wup.t[:, kc, :], g3T.t[:, kc, :], ALU.mult, [wup, g3T], [wup])
        for blk in range(0, 2 * D_FF, 1024):
            w = min(1024, 2 * D_FF - blk)
            c0 = blk // 128
            cols_from_rows(S, C, T['ffn_dw_w'][:, blk:blk + w], 3, w, fw, lambda ch, c0=c0: fw.t[:, c0 + ch, :])
            cols_from_rows(S, C, T['ffn_dw_b'][:, blk:blk + w], 1, w, fb, lambda ch, c0=c0: fb.t[:, c0 + ch, :])
        S.dma('sp', C.stage.sem, dict(out=C.stage.t[0:1, 0:1024], in_=T['final_g']), writes=[C.stage])
        for half in range(2):
            _mm(S, pd[half].t[:, :], onesF.t[0:1, :], C.stage.t[0:1, half * 512:(half + 1) * 512], True, True,
                [onesF, C.stage], [pd[half]])
            _cp(S, 'dve', fgB.t[:, half * 512:(half + 1) * 512], pd[half].t[:, :], [pd[half]], [fgB])

        npu = 0
        nub = 0
        for st in range(NSUP):
            cs = slice(st * 512, (st + 1) * 512)
            S.dma('sp', n3.sem, dict(out=n3.t[:, :, :], in_=T['N3T'][:, :, cs].rearrange('k p s -> p k s')), writes=[n3])
            for j in range(NP):
                tpair = []
                for c in (j, j + NP):
                    p = pu[npu % 3]
                    npu += 1
                    u_ = ub[nub % 3]
                    t_ = tb[nub % 3]
                    nub += 1
                    for kc in range(8):
                        _mm(S, p.t[:, :], wup.t[:, kc, c * 128:(c + 1) * 128], n3.t[:, kc, :], kc == 0, kc == 7,
                            [wup, n3], [p])
                    _cp(S, 'act', u_.t[:, 2:514], p.t[:, :], [p], [u_])
                    _cp(S, 'pool', u_.t[:, 0:2], halo.t[:, c, :], [halo], [u_])
                    _act(S, t_.t[:, :], p.t[:, :], AF.Identity, [p, fw, fb], [t_], scale=fw.t[:, c, 2:3], bias=fb.t[:, c, :])
                    _stt(S, t_.t[:, :], u_.t[:, 0:512], fw.t[:, c, 0:1], t_.t[:, :], ALU.mult, ALU.add, [u_, fw, t_], [t_])
                    _stt(S, t_.t[:, :], u_.t[:, 1:513], fw.t[:, c, 1:2], t_.t[:, :], ALU.mult, ALU.add, [u_, fw, t_], [t_])
                    _cp(S, 'pool', halo.t[:, c, :], u_.t[:, 512:514], [u_], [halo])
                    tpair.append(t_)
                _act(S, sgl.t[:, :], tpair[0].t[:, :], AF.Silu, [tpair[0]], [sgl])
                _tt(S, 'dve', actT.t[:, j, :], sgl.t[:, :], tpair[1].t[:, :], ALU.mult, [sgl, tpair[1]], [actT])
            for t in range(4):
                tt = st * 4 + t
                hb = h2[tt % 2]
                ob = hb
                S.dma('sp', hb.sem, dict(out=hb.t[:, :], in_=T['H2'][tt * 128:(tt + 1) * 128, :]), writes=[hb])
                for half in range(2):
                    p = pd[half]
                    for j in range(NP):
                        _mm(S, p.t[:, :], actT.t[:, j, t * 128:(t + 1) * 128], wdn.t[:, j, half * 512:(half + 1) * 512],
                            j == 0, j == NP - 1, [actT, wdn], [p])
                    _tt(S, 'dve', hb.t[:, half * 512:(half + 1) * 512], p.t[:, :], hb.t[:, half * 512:(half + 1) * 512],
                        ALU.add, [p, hb], [hb])
                _stt(S, junk.t[:, :], hb.t[:, :], 1.0, hb.t[:, :], ALU.mult, ALU.mult, [hb], [junk, ss], accum_out=ss.t[:, 0:1])
                _act(S, rt.t[:, :], ss.t[:, :], AF.Sqrt, [ss, epsT], [rt], scale=1.0 / D, bias=epsT.t[:, :])
                S.op('dve', 'reciprocal', dict(out=rstd.t[:, :], in_=rt.t[:, :]), [rt], [rstd])
                _stt(S, ob.t[:, :], hb.t[:, :], rstd.t[:, 0:1], fgB.t[:, :], ALU.mult, ALU.mult, [hb, rstd, fgB], [hb])
                S.dma('sp', hb.sem, dict(out=T['y'][tt * 128:(tt + 1) * 128, :], in_=hb.t[:, :]), reads=[hb])
        S.flush()


def t5_bucket_np(d):
    n = np.maximum(d, 0)
    nf = np.maximum(n, 1).astype(np.float32)
    large = 16 + (np.log(nf / np.float32(16)) / np.float32(np.log(8.0)) * np.float32(16)).astype(np.int32)
    large = np.minimum(large, 31)
    return np.where(n < 16, n, large)


def scratch_spec(Sq):
    return {
        'QT': ([16, 64, Sq], BF16), 'KC': ([4, 64, Sq], BF16), 'VC': ([4, 64, Sq], BF16),
        'KS': ([4, 64, Sq], BF16), 'KW': ([4, 64, Sq], BF16), 'GM': ([16, 128, Sq], BF16),
        'HC': ([4, 128, Sq], BF16), 'VS1': ([Sq, 4, 65], BF16), 'VW1': ([Sq, 4, 65], BF16),
        'G': ([Sq, 48], F32), 'NSAT': ([8, 128, Sq], BF16), 'H2': ([Sq, D], F32), 'N3T': ([8, 128, Sq], BF16),
    }


INPUT_SHAPES = {
    'norm1_g': [1, D], 'w_in': [D, IN_W], 'conv_dw_w': [31, 512], 'conv_dw_b': [1, 512],
    'conv_ln_g': [1, 512], 'conv_ln_b': [1, 512], 'conv_w_pw': [512, D],
    'cmp_pe': [2, 32, 64], 'cmp_w1': [2, 2048, 256], 'cmp_b1': [1, 512], 'cmp_w2': [2, 256, 64],
    'nsa_w_o': [D, D], 'w_out': [D, D], 'norm2_g': [1, D], 'mem_norm_g': [1, D],
    'xa_wq': [D, D], 'xa_wkv': [D, 2 * D], 'xa_wo': [D, D], 'norm3_g': [1, D],
    'ffn_w_up': [D, 2 * D_FF], 'ffn_dw_w': [3, 2 * D_FF], 'ffn_dw_b': [1, 2 * D_FF],
    'ffn_w_down': [D_FF, D], 'final_g': [1, D],
}


def const_shapes(Sq):
    NT = Sq // 128
    NC, chunks = cmp_chunks(Sq)
    return {
        'identF': [128, 128], 'Econst': [64, Sq], 'm512': [128, 512],
        'SelW': [32, 8 * (NT - 1) + 128 * len(chunks)], 'mulB': [128, 128], 'addB': [128, 128],
        'ovl': [128 * len(chunks), 64],
        'tz1': [2, 128, 16, 128], 'tz31': [2, 128, 16, 128], 'tzm': [2, 128, 16, 128],
        'cb1': [32, 16, 128], 'cb31': [32, 16, 128], 'cbm': [32, 16, 128],
    }


def build(Sq, debug=(), phases='ABCDE'):
    nc = bass.Bass("TRN2", target_bir_lowering=False)
    T = {}
    T['x'] = nc.dram_tensor('x', [Sq, D], F32, kind='ExternalInput').ap()
    T['mem'] = nc.dram_tensor('mem', [MEM, D], F32, kind='ExternalInput').ap()
    for k, shp in list(INPUT_SHAPES.items()) + list(const_shapes(Sq).items()):
        T[k] = nc.dram_tensor(k, shp, F32, kind='ExternalInput').ap()
    for k, (shp, dt) in scratch_spec(Sq).items():
        kind = 'ExternalOutput' if k in debug else 'Internal'
        T[k] = nc.dram_tensor(k, shp, dt, kind=kind).ap()
    T['y'] = nc.dram_tensor('y', [Sq, D], F32, kind='ExternalOutput').ap()
    NC, chunks = cmp_chunks(Sq)
    with ExitStack() as es:
        S = Sched(nc, es)
        if 'A' in phases:
            phase_A(nc, S, Sq, T)
        kcT = Tl(es.enter_context(nc.sbuf_tensor('kcT', [64, 4, 128 * len(chunks)], BF16)))
        vco = Tl(es.enter_context(nc.sbuf_tensor('vco', [128, len(chunks), 4, 128], BF16)))
        if 'B' in phases:
            phase_B(nc, S, Sq, T, kcT, vco)
        if 'C' in phases:
            phase_C(nc, S, Sq, T, kcT, vco)
        if 'D' in phases:
            phase_D(nc, S, Sq, T)
        if 'E' in phases:
            phase_E(nc, S, Sq, T)
    return nc


def host_consts(rel_bias, Sq):
    NT = Sq // 128
    NC, chunks = cmp_chunks(Sq)
    rb = np.asarray(rel_bias, dtype=np.float32)
    c = {}
    c['identF'] = np.eye(128, dtype=np.float32)
    E = np.zeros((64, Sq), np.float32)
    kk = np.arange(Sq)
    valid = kk // 64 < 64
    E[(kk // 64)[valid], kk[valid]] = 1.0
    c['Econst'] = E
    ki = np.arange(128)[:, None]
    qi = np.arange(128)[None, :]
    c['m512'] = np.tile(np.where(qi >= ki, MASKV, 0.0).astype(np.float32), (1, 4))
    OFFS = 8 * (NT - 1)
    W = np.zeros((32, OFFS + 128 * len(chunks)), np.float32)
    m = np.arange(W.shape[1]) - OFFS
    for r in range(17):
        W[r, m == r - 10] = 1.0
    W[31, m >= 7] = 1.0
    c['SelW'] = W
    r = np.arange(128)[None, :] - 62
    p = np.arange(128)[:, None]
    hi = (p >= 64).astype(np.int64)
    rel = r - hi
    free = rel <= -2
    forced = (rel == -1) | (rel == 0)
    c['mulB'] = np.where(free, 1.0, 0.0).astype(np.float32) * np.ones((128, 1), np.float32)
    c['addB'] = np.where(free, 0.0, np.where(forced, 10.0 + (rel + 2), -1.0 - 0.001 * np.maximum(rel, 0))).astype(np.float32)
    n = np.arange(128 * len(chunks))[:, None]
    j = np.arange(64)[None, :]
    ov = np.clip(np.minimum(16 * n + 32, 64 * j + 64) - np.maximum(16 * n, 64 * j), 0, None).astype(np.float32) / 32.0
    ov[NC:] = 0.0
    c['ovl'] = ov.astype(np.float32)
    tz1 = np.zeros((2, 128, 16, 128), np.float32)
    tz31 = np.zeros_like(tz1)
    tzm = np.zeros_like(tz1)
    for dl in range(2):
        d = dl * 128 + qi - ki
        ok = d >= 0
        g1 = rb[t5_bucket_np(d)]
        g31 = rb[np.full_like(d, 31)]
        tz1[dl] = np.where(ok[:, :, None], g1, 0.0).transpose(0, 2, 1)
        tz31[dl] = np.where(ok[:, :, None], g31, 0.0).transpose(0, 2, 1)
        tzm[dl] = np.where(ok[:, :, None], 0.0, MASKV).transpose(0, 2, 1) * np.ones((1, 16, 1), np.float32)
    c['tz1'], c['tz31'], c['tzm'] = tz1, tz31, tzm
    cb1 = np.zeros((32, 16, 128), np.float32)
    cb31 = np.zeros_like(cb1)
    cbm = np.zeros_like(cb1)
    rr = np.arange(17)[:, None]
    d1 = np.arange(128)[None, :] - 16 * (rr - 10) - 31
    ok = d1 >= 0
    cb1[:17] = np.where(ok[:, :, None], rb[t5_bucket_np(d1)], 0.0).transpose(0, 2, 1)
    cb31[:17] = np.where(ok[:, :, None], rb[np.full_like(d1, 31)], 0.0).transpose(0, 2, 1)
    cbm[:17] = (np.where(ok, 0.0, MASKV)[:, None, :] * np.ones((1, 16, 1))).astype(np.float32)
    cbm[31] = MASKV
    c['cb1'], c['cb31'], c['cbm'] = cb1, cb31, cbm
    return {k: np.ascontiguousarray(v, dtype=np.float32) for k, v in c.items()}


def host_inputs(inp, Sq):
    shared = {}
    for k, shp in INPUT_SHAPES.items():
        shared[k] = np.ascontiguousarray(np.asarray(inp[k], dtype=np.float32).reshape(shp))
    shared.update(host_consts(inp['rel_bias'], Sq))
    return shared


def kernel(**inp):
    x = np.asarray(inp['x'], dtype=np.float32)
    mem = np.asarray(inp['mem'], dtype=np.float32)
    B, Sq, _ = x.shape
    nc = build(Sq)
    shared = host_inputs(inp, Sq)
    in_maps = []
    for b in range(B):
        m = dict(shared)
        m['x'] = np.ascontiguousarray(x[b])
        m['mem'] = np.ascontiguousarray(mem[b])
        in_maps.append(m)
    res = run_bass_kernel_spmd(nc, in_maps, core_ids=list(range(B)))
    return np.stack([np.asarray(r['y'], dtype=np.float32) for r in res.results], axis=0)
```

```python
import os
import numpy as np
from contextlib import ExitStack
import concourse.bass as bass
import concourse.mybir as mybir
from concourse.bass_utils import run_bass_kernel_spmd

F32 = mybir.dt.float32
BF16 = mybir.dt.bfloat16
AF = mybir.ActivationFunctionType
ALU = mybir.AluOpType
AX = mybir.AxisListType

D = 1024
SEQ = 4096
MEM = 256
IN_W = 5680
D_FF = 2816
MASKV = -30000.0
EPS = 1e-6

ENGS = ['pe', 'act', 'dve', 'pool', 'sp']


class Buf:
    __slots__ = ('w', 'r')

    def __init__(self):
        self.w = None
        self.r = {}


class Tl:
    def __init__(self, t, sem=None):
        self.t = t
        self.b = Buf()
        self.sem = sem


def _b(x):
    return x.b if isinstance(x, Tl) else x


class Sched:
    def __init__(self, nc, es):
        self.nc = nc
        self.es = es
        self.q = {e: [] for e in ENGS}
        self.sem = {}
        self.cnt = {}
        self.known = {e: {} for e in ENGS}
        self.ndma = 0

    def semh(self, key):
        if key not in self.sem:
            self.sem[key] = self.es.enter_context(self.nc.semaphore('s_' + key))
            self.cnt[key] = 0
        return self.sem[key]

    def newdma(self, name=None):
        self.ndma += 1
        key = 'd%d' % self.ndma
        self.semh(key)
        return key

    def _deps(self, eng, reads, writes):
        need = {}
        for b in reads:
            b = _b(b)
            if b.w:
                k, v = b.w
                need[k] = max(need.get(k, 0), v)
        for b in writes:
            b = _b(b)
            if b.w:
                k, v = b.w
                need[k] = max(need.get(k, 0), v)
            for k, v in b.r.items():
                need[k] = max(need.get(k, 0), v)
        out = []
        kn = self.known[eng]
        for k, v in need.items():
            if eng == 'pe' and k == 'pe':
                continue
            if kn.get(k, 0) < v:
                kn[k] = v
                out.append((k, v))
        return out

    def _post(self, key, v, reads, writes):
        for b in reads:
            b = _b(b)
            b.r[key] = max(b.r.get(key, 0), v)
        for b in writes:
            b = _b(b)
            b.w = (key, v)
            b.r = {}

    def op(self, eng, meth, kw, reads=(), writes=()):
        self.semh(eng)
        waits = self._deps(eng, reads, writes)
        self.cnt[eng] += 1
        v = self.cnt[eng]
        self.q[eng].append((waits, meth, kw, eng, 1))
        self._post(eng, v, reads, writes)

    def dma(self, eng, semkey, kw, reads=(), writes=()):
        self.semh(semkey)
        waits = self._deps(eng, reads, writes)
        self.cnt[semkey] += 16
        v = self.cnt[semkey]
        self.q[eng].append((waits, 'dma_start', kw, semkey, 16))
        self._post(semkey, v, reads, writes)

    def barrier(self):
        for e in ENGS:
            kn = self.known[e]
            waits = []
            for k, v in self.cnt.items():
                if v > 0 and kn.get(k, 0) < v:
                    kn[k] = v
                    waits.append((k, v))
            if waits:
                self.q[e].append((waits, None, None, None, 0))

    def flush(self):
        self.barrier()
        nc = self.nc
        q = self.q
        sem = self.sem

        def run(engobj, items):
            for waits, meth, kw, key, inc in items:
                for k, v in waits:
                    engobj.wait_ge(sem[k], v)
                if meth is not None:
                    getattr(engobj, meth)(**kw).then_inc(sem[key], inc)

        with nc.Block() as block:
            @block.tensor
            def _(e):
                run(e, q['pe'])

            @block.scalar
            def _(e):
                run(e, q['act'])

            @block.vector
            def _(e):
                run(e, q['dve'])

            @block.gpsimd
            def _(e):
                run(e, q['pool'])

            @block.sync
            def _(e):
                run(e, q['sp'])
        self.q = {e: [] for e in ENGS}


class Ctx:
    pass


def _mm(S, out, lhsT, rhs, start, stop, reads, writes, skip=False):
    kw = dict(out=out, lhsT=lhsT, rhs=rhs, start=start, stop=stop)
    if skip:
        kw['skip_group_check'] = True
    S.op('pe', 'matmul', kw, reads, writes)


def _tp(S, out, in_, ident, reads, writes):
    S.op('pe', 'transpose', dict(out=out, in_=in_, identity=ident), reads, writes)


def _act(S, out, in_, func, reads, writes, **kw):
    S.op('act', 'activation', dict(out=out, in_=in_, func=func, **kw), reads, writes)


def _tt(S, eng, out, in0, in1, op, reads, writes):
    S.op(eng, 'tensor_tensor', dict(out=out, in0=in0, in1=in1, op=op), reads, writes)


def _ts(S, eng, out, in0, s1, op0, reads, writes, s2=None, op1=None):
    kw = dict(out=out, in0=in0, scalar1=s1, scalar2=s2, op0=op0)
    if op1 is not None:
        kw['op1'] = op1
    S.op(eng, 'tensor_scalar', kw, reads, writes)


def _stt(S, out, in0, scalar, in1, op0, op1, reads, writes, accum_out=None):
    kw = dict(out=out, in0=in0, scalar=scalar, in1=in1, op0=op0, op1=op1)
    if accum_out is not None:
        kw['accum_out'] = accum_out
    S.op('dve', 'scalar_tensor_tensor', kw, reads, writes)


def _cp(S, eng, out, in_, reads, writes):
    if eng == 'act':
        S.op('act', 'copy', dict(out=out, in_=in_), reads, writes)
    else:
        S.op(eng, 'tensor_copy', dict(out=out, in_=in_), reads, writes)


def _memset(S, eng, ap, val, writes):
    S.op(eng, 'memset', dict(ap=ap, constant=val), (), writes)


def load_w_cast(S, dst_tl, dst_ap_fn, src, kc_n, ncols, rows_per=128):
    for kc in range(kc_n):
        c0 = 0
        while c0 < ncols:
            c1 = min(ncols, c0 + 2048)
            S.dma('pool', dst_tl.sem, dict(out=dst_ap_fn(kc, c0, c1),
                                           in_=src[kc * 128:(kc + 1) * 128, c0:c1]),
                  writes=[dst_tl])
            c0 = c1


def cols_from_rows(S, C, src_rows, R, ncol, dst_tl, dst_fn):
    stage = C.stage
    assert ncol <= stage.t.shape[1] and R <= 32
    S.dma('sp', stage.sem, dict(out=stage.t[0:R, 0:ncol], in_=src_rows), writes=[stage])
    for ch in range(ncol // 128):
        _tp(S, C.pst.t[:, 0:R], stage.t[0:R, ch * 128:(ch + 1) * 128], C.identF.t[0:R, 0:R],
            [stage, C.identF], [C.pst])
        _cp(S, 'dve', dst_fn(ch), C.pst.t[:, 0:R], [C.pst], [dst_tl])


def phase_A(nc, S, Sq, T):
    NSUP = Sq // 512
    with ExitStack() as es:
        def sb(name, shape, dt, sem=False):
            t = es.enter_context(nc.sbuf_tensor(name, shape, dt))
            return Tl(t, S.newdma() if sem else None)

        def ps(name, shape, dt=F32):
            return Tl(es.enter_context(nc.psum_tensor(name, shape, dt)))

        C = Ctx()
        win = sb('a_win', [128, 8, IN_W], BF16, sem=True)
        dg = sb('a_dg', [128, 4, 31, 128], BF16)
        identF = sb('a_identF', [128, 128], F32, sem=True)
        identB = sb('a_identB', [128, 128], BF16, sem=True)
        onesF = sb('a_onesF', [128, 128], F32)
        C.identF = identF
        C.stage = sb('a_stage', [32, 1024], F32, sem=True)
        g1T = sb('a_g1T', [128, 8, 1], F32)
        dwT = sb('a_dwT', [128, 4, 31], F32)
        dwb = sb('a_dwb', [128, 4, 1], F32)
        lng = sb('a_lng', [128, 4, 1], F32)
        lnb = sb('a_lnb', [128, 4, 1], F32)
        xs = [sb('a_x%d' % i, [128, D], F32, sem=True) for i in range(4)]
        ss = sb('a_ss', [128, 4], F32)
        rt = sb('a_rt', [128, 4], F32)
        rstd = sb('a_rstd', [128, 4], F32)
        junk = sb('a_junk', [128, D], BF16)
        ntok = [sb('a_ntok%d' % i, [128, D], BF16) for i in range(2)]
        nT = sb('a_nT', [128, 8, 512], BF16)
        hglu = sb('a_hglu', [128, 4, 542], BF16)
        stg = [sb('a_stg%d' % i, [128, 512], BF16, sem=True) for i in range(6)]
        sg = [sb('a_sg%d' % i, [128, 512], F32) for i in range(2)]
        ycs = sb('a_ycs', [128, 4, 512], F32)
        ysq = sb('a_ysq', [128, 4, 512], F32)
        mean = sb('a_mean', [128, 512], F32)
        msq = sb('a_msq', [128, 512], F32)
        rs = sb('a_rs', [128, 512], F32)
        dtl = [sb('a_d%d' % i, [128, 512], F32) for i in range(2)]
        vstg = [sb('a_vstg%d' % i, [128, 2, 4, 65], BF16, sem=True) for i in range(2)]
        gstg = [sb('a_gstg%d' % i, [128, 48], F32, sem=True) for i in range(2)]
        epsT = sb('a_eps', [128, 1], F32)

        ptr = ps('a_ptr', [128, 1024], BF16)
        pf = [ps('a_pf%d' % i, [128, 512]) for i in range(3)]
        pv = ps('a_pv', [128, 512])
        pg = ps('a_pg', [128, 48])
        C.pst = pg
        pc = [ps('a_pc%d' % i, [128, 512]) for i in range(2)]

        S.dma('sp', identF.sem, dict(out=identF.t[:, :], in_=T['identF']), writes=[identF])
        S.dma('pool', identB.sem, dict(out=identB.t[:, :], in_=T['identF']), writes=[identB])
        _memset(S, 'dve', onesF.t[:, :], 1.0 / 512.0, [onesF])
        _memset(S, 'dve', epsT.t[:, :], EPS, [epsT])
        _memset(S, 'dve', hglu.t[:, :, :], 0.0, [hglu])
        for v in vstg:
            _memset(S, 'dve', v.t[:, :, :, :], 1.0, [v])
        load_w_cast(S, win, lambda kc, c0, c1: win.t[:, kc, c0:c1], T['w_in'], 8, IN_W)
        cols_from_rows(S, C, T['norm1_g'], 1, 1024, g1T, lambda ch: g1T.t[:, ch, :])
        cols_from_rows(S, C, T['conv_dw_w'], 31, 512, dwT, lambda ch: dwT.t[:, ch, :])
        cols_from_rows(S, C, T['conv_dw_b'], 1, 512, dwb, lambda ch: dwb.t[:, ch, :])
        cols_from_rows(S, C, T['conv_ln_g'], 1, 512, lng, lambda ch: lng.t[:, ch, :])
        cols_from_rows(S, C, T['conv_ln_b'], 1, 512, lnb, lambda ch: lnb.t[:, ch, :])
        for kc in range(8):
            _ts(S, 'dve', win.t[:, kc, :], win.t[:, kc, :], g1T.t[:, kc, :], ALU.mult, [win, g1T], [win])
        for ch in range(4):
            for j in range(31):
                _ts(S, 'dve', dg.t[:, ch, j, :], identF.t[:, :], dwT.t[:, ch, j:j + 1], ALU.mult,
                    [identF, dwT], [dg])

        x = T['x']
        fchunks = []
        for i in range(4):
            fchunks.append((512 + 128 * i, 'gate', i))
            fchunks.append((128 * i, 'a', i))
        for i in range(8):
            fchunks.append((1024 + 128 * i, 'q', i))
        for nm, c0 in (('kc', 2048), ('vc', 2304), ('ks', 2560), ('kw', 3072)):
            for i in range(2):
                fchunks.append((c0 + 128 * i, nm, i))
        for i in range(16):
            fchunks.append((3632 + 128 * i, 'gm', i))

        import os
        STG = int(os.environ.get('STG', '9'))
        nstg = 0
        npf = 0
        for st in range(NSUP if STG >= 1 else 0):
            for t in range(4):
                tt = st * 4 + t
                xb = xs[tt % 4]
                S.dma('sp', xb.sem, dict(out=xb.t[:, :], in_=x[tt * 128:(tt + 1) * 128, :]), writes=[xb])
                _stt(S, junk.t[:, :], xb.t[:, :], 1.0, xb.t[:, :], ALU.mult, ALU.mult, [xb], [junk, ss],
                     accum_out=ss.t[:, t:t + 1])
            _act(S, rt.t[:, :], ss.t[:, :], AF.Sqrt, [ss, epsT], [rt], scale=1.0 / D, bias=epsT.t[:, :])
            S.op('dve', 'reciprocal', dict(out=rstd.t[:, :], in_=rt.t[:, :]), [rt], [rstd])
            for t in range(4):
                tt = st * 4 + t
                xb = xs[tt % 4]
                nk = ntok[t % 2]
                _ts(S, 'dve', nk.t[:, :], xb.t[:, :], rstd.t[:, t:t + 1], ALU.mult, [xb, rstd], [nk])
                for kc in range(8):
                    _tp(S, ptr.t[:, kc * 128:(kc + 1) * 128], nk.t[:, kc * 128:(kc + 1) * 128], identB.t[:, :],
                        [nk, identB], [ptr])
                _cp(S, 'act', nT.t[:, :, t * 128:(t + 1) * 128],
                    ptr.t[:, :].rearrange('p (k q) -> p k q', k=8), [ptr], [nT])
            for t in range(4 if STG >= 2 else 0):
                tt = st * 4 + t
                for half, c0 in ((0, 2816), (1, 3328)):
                    for kc in range(8):
                        _mm(S, pv.t[:, half * 256:(half + 1) * 256], nT.t[:, kc, t * 128:(t + 1) * 128],
                            win.t[:, kc, c0:c0 + 256], kc == 0, kc == 7, [nT, win], [pv])
                for kc in range(8):
                    _mm(S, pg.t[:, :], nT.t[:, kc, t * 128:(t + 1) * 128], win.t[:, kc, 3584:3632],
                        kc == 0, kc == 7, [nT, win], [pg])
                vs = vstg[tt % 2]
                _cp(S, 'dve', vs.t[:, :, :, 0:64], pv.t[:, :].rearrange('p (a g d) -> p a g d', a=2, g=4),
                    [pv], [vs])
                S.dma('sp', vs.sem, dict(out=T['VS1'][tt * 128:(tt + 1) * 128, :, :], in_=vs.t[:, 0, :, :]),
                      reads=[vs])
                S.dma('sp', vs.sem, dict(out=T['VW1'][tt * 128:(tt + 1) * 128, :, :], in_=vs.t[:, 1, :, :]),
                      reads=[vs])
                gs = gstg[tt % 2]
                _act(S, gs.t[:, :], pg.t[:, :], AF.Sigmoid, [pg], [gs])
                S.dma('sp', gs.sem, dict(out=T['G'][tt * 128:(tt + 1) * 128, :], in_=gs.t[:, :]), reads=[gs])
            cs = slice(st * 512, (st + 1) * 512)
            for (c0, kind, idx) in (fchunks if STG >= 3 else []):
                p = pf[npf % 3]
                npf += 1
                for kc in range(8):
                    _mm(S, p.t[:, :], win.t[:, kc, c0:c0 + 128], nT.t[:, kc, :], kc == 0, kc == 7, [win, nT], [p])
                if kind == 'gate':
                    sgt = sg[idx % 2]
                    _act(S, sgt.t[:, :], p.t[:, :], AF.Sigmoid, [p], [sgt])
                elif kind == 'a':
                    sgt = sg[idx % 2]
                    _tt(S, 'dve', hglu.t[:, idx, 30:542], p.t[:, :], sgt.t[:, :], ALU.mult, [p, sgt], [hglu])
                else:
                    sl = stg[nstg % 6]
                    nstg += 1
                    if kind == 'q':
                        _act(S, sl.t[:, :], p.t[:, :], AF.Copy, [p], [sl], scale=0.125)
                        dst = T['QT'][2 * idx:2 * idx + 2, :, cs].rearrange('h d s -> (h d) s')
                    elif kind == 'gm':
                        _act(S, sl.t[:, :], p.t[:, :], AF.Sigmoid, [p], [sl])
                        dst = T['GM'][idx, :, cs]
                    else:
                        _cp(S, 'dve', sl.t[:, :], p.t[:, :], [p], [sl])
                        dst = T[{'kc': 'KC', 'vc': 'VC', 'ks': 'KS', 'kw': 'KW'}[kind]][2 * idx:2 * idx + 2, :, cs] \
                            .rearrange('g d s -> (g d) s')
                    S.dma('sp', sl.sem, dict(out=dst, in_=sl.t[:, :]), reads=[sl])
            if STG < 4:
                continue
            for ch in range(4):
                p = pc[ch % 2]
                for j in range(0, 31, int(os.environ.get('JSTEP', '1'))):
                    _mm(S, p.t[:, :], dg.t[:, ch, j, :], hglu.t[:, ch, j:j + 512], j == 0, j == 30, [dg, hglu], [p])
                CP = int(os.environ.get('CP', '15'))
                if CP & 2:
                    _ts(S, 'dve', ycs.t[:, ch, :], p.t[:, :], dwb.t[:, ch, :], ALU.add, [p, dwb], [ycs])
                if CP & 4:
                    _tt(S, 'pool', ysq.t[:, ch, :], ycs.t[:, ch, :], ycs.t[:, ch, :], ALU.mult, [ycs], [ysq])
            if CP & 8:
                _cp(S, 'dve', hglu.t[:, :, 0:30], hglu.t[:, :, 512:542], [hglu], [hglu])
            if STG < 5:
                continue
            pm = pf[npf % 3]
            npf += 1
            pq = pf[npf % 3]
            npf += 1
            for ch in range(4):
                _mm(S, pm.t[:, :], onesF.t[:, :], ycs.t[:, ch, :], ch == 0, ch == 3, [onesF, ycs], [pm])
            for ch in range(4):
                _mm(S, pq.t[:, :], onesF.t[:, :], ysq.t[:, ch, :], ch == 0, ch == 3, [onesF, ysq], [pq])
            _cp(S, 'act', mean.t[:, :], pm.t[:, :], [pm], [mean])
            _tt(S, 'dve', msq.t[:, :], mean.t[:, :], mean.t[:, :], ALU.mult, [mean], [msq])
            _tt(S, 'dve', msq.t[:, :], pq.t[:, :], msq.t[:, :], ALU.subtract, [pq, msq], [msq])
            _act(S, msq.t[:, :], msq.t[:, :], AF.Sqrt, [msq, epsT], [msq], bias=epsT.t[:, :])
            S.op('dve', 'reciprocal', dict(out=rs.t[:, :], in_=msq.t[:, :]), [msq], [rs])
            for ch in range(4):
                d_ = dtl[ch % 2]
                _tt(S, 'dve', d_.t[:, :], ycs.t[:, ch, :], mean.t[:, :], ALU.subtract, [ycs, mean], [d_])
                _tt(S, 'dve', d_.t[:, :], d_.t[:, :], rs.t[:, :], ALU.mult, [d_, rs], [d_])
                _ts(S, 'dve', d_.t[:, :], d_.t[:, :], lng.t[:, ch, :], ALU.mult, [d_, lng, lnb], [d_],
                    s2=lnb.t[:, ch, :], op1=ALU.add)
                sl = stg[nstg % 6]
                nstg += 1
                _act(S, sl.t[:, :], d_.t[:, :], AF.Silu, [d_], [sl])
                S.dma('sp', sl.sem, dict(out=T['HC'][ch, :, cs], in_=sl.t[:, :]), reads=[sl])
        S.flush()


def _mk(nc, es, S):
    def sb(name, shape, dt, sem=False):
        t = es.enter_context(nc.sbuf_tensor(name, shape, dt))
        return Tl(t, S.newdma() if sem else None)

    def ps(name, shape, dt=F32):
        return Tl(es.enter_context(nc.psum_tensor(name, shape, dt)))
    return sb, ps


def cmp_chunks(Sq):
    NC = Sq // 16 - 1
    out = []
    n0 = 0
    while n0 < NC:
        out.append((n0, min(128, NC - n0)))
        n0 += 128
    return NC, out


def phase_B(nc, S, Sq, T, kcT, vco):
    NC, chunks = cmp_chunks(Sq)
    with ExitStack() as es:
        sb, ps = _mk(nc, es, S)
        C = Ctx()
        identF = sb('b_identF', [128, 128], F32, sem=True)
        C.identF = identF
        C.stage = sb('b_stage', [32, 1024], F32, sem=True)
        pst = ps('b_pst', [128, 48])
        C.pst = pst
        kin = [sb('b_kin%d' % i, [64, 4, Sq], BF16, sem=True) for i in range(2)]
        w1s = sb('b_w1s', [64, 2, 32, 256], BF16, sem=True)
        w2s = sb('b_w2s', [128, 2, 2, 64], BF16, sem=True)
        peT = sb('b_peT', [64, 2, 32], BF16)
        b1T = sb('b_b1T', [128, 4, 1], F32)
        biasc = sb('b_biasc', [128, 4, 1], F32)
        hT = [sb('b_hT%d' % i, [128, 2, 512], BF16) for i in range(2)]
        xb = sb('b_xb', [128, 512], F32)
        x2 = sb('b_x2', [128, 512], F32)
        u = sb('b_u', [128, 512], F32)
        sgm = sb('b_sgm', [128, 512], F32)
        ovs = sb('b_ovs', [128, 2, 64], F32, sem=True)
        pA = [ps('b_pA%d' % i, [128, 512]) for i in range(2)]
        pB = ps('b_pB', [128, 512])

        S.dma('sp', identF.sem, dict(out=identF.t[:, :], in_=T['identF']), writes=[identF])
        S.dma('sp', kin[0].sem, dict(out=kin[0].t[:, :, :], in_=T['KC'].rearrange('g d s -> d g s')), writes=[kin[0]])
        S.dma('sp', kin[1].sem, dict(out=kin[1].t[:, :, :], in_=T['VC'].rearrange('g d s -> d g s')), writes=[kin[1]])
        for kv in range(2):
            for l0 in range(0, 32, 8):
                S.dma('pool', w1s.sem, dict(out=w1s.t[:, kv, l0:l0 + 8, :],
                                            in_=T['cmp_w1'][kv, l0 * 64:(l0 + 8) * 64, :].rearrange('(l d) c -> d l c', d=64)),
                      writes=[w1s])
            S.dma('pool', w2s.sem, dict(out=w2s.t[:, kv, :, :],
                                        in_=T['cmp_w2'][kv].rearrange('(h p) d -> p h d', p=128)), writes=[w2s])
            S.dma('sp', C.stage.sem, dict(out=C.stage.t[0:32, 0:64], in_=T['cmp_pe'][kv]), writes=[C.stage])
            _tp(S, pst.t[0:64, 0:32], C.stage.t[0:32, 0:64], identF.t[0:32, 0:32], [C.stage, identF], [pst])
            _cp(S, 'dve', peT.t[:, kv, :], pst.t[0:64, 0:32], [pst], [peT])
        cols_from_rows(S, C, T['cmp_b1'], 1, 512, b1T, lambda ch: b1T.t[:, ch, :])
        _memset(S, 'dve', vco.t[:, :, :, :], 0.0, [vco])
        _memset(S, 'dve', kcT.t[:, :, :], 0.0, [kcT])
        S.dma('sp', ovs.sem, dict(out=ovs.t[:, 0:len(chunks), :], in_=T['ovl'].rearrange('(c p) j -> p c j', p=128)),
              writes=[ovs])
        for ci in range(len(chunks)):
            for g in range(4):
                _cp(S, 'dve', vco.t[:, ci, g, 64:128], ovs.t[:, ci, :], [ovs], [vco])
        for kv in range(2):
            for half in range(2):
                for l in range(32):
                    _mm(S, pB.t[:, 0:1], w1s.t[0:64, kv, l, half * 128:(half + 1) * 128], peT.t[0:64, kv, l:l + 1],
                        l == 0, l == 31, [w1s, peT], [pB])
                _tt(S, 'dve', biasc.t[:, kv * 2 + half, :], pB.t[:, 0:1], b1T.t[:, kv * 2 + half, :], ALU.add,
                    [pB, b1T], [biasc])
        npa = 0
        for kv in range(2):
            src = kin[kv]
            for g in range(4):
                h_ = hT[(kv * 4 + g) % 2]
                for half in range(2):
                    p = pA[npa % 2]
                    npa += 1
                    for l in range(32):
                        _mm(S, p.t[:, 0:NC], w1s.t[0:64, kv, l, half * 128:(half + 1) * 128],
                            src.t[0:64, g, l:l + 16 * (NC - 1) + 1:16], l == 0, l == 31, [w1s, src], [p])
                    _ts(S, 'dve', xb.t[:, 0:NC], p.t[:, 0:NC], biasc.t[:, kv * 2 + half, :], ALU.add, [p, biasc], [xb])
                    _tt(S, 'pool', x2.t[:, 0:NC], xb.t[:, 0:NC], xb.t[:, 0:NC], ALU.mult, [xb], [x2])
                    _ts(S, 'dve', x2.t[:, 0:NC], x2.t[:, 0:NC], 0.044715, ALU.mult, [x2], [x2], s2=1.0, op1=ALU.add)
                    _tt(S, 'dve', u.t[:, 0:NC], x2.t[:, 0:NC], xb.t[:, 0:NC], ALU.mult, [x2, xb], [u])
                    _act(S, sgm.t[:, 0:NC], u.t[:, 0:NC], AF.Sigmoid, [u], [sgm], scale=1.5957691216057308)
                    _tt(S, 'dve', h_.t[:, half, 0:NC], xb.t[:, 0:NC], sgm.t[:, 0:NC], ALU.mult, [xb, sgm], [h_])
                if kv == 0:
                    for half in range(2):
                        _mm(S, pB.t[0:64, 0:NC], w2s.t[:, 0, half, :], h_.t[:, half, 0:NC], half == 0, half == 1,
                            [w2s, h_], [pB])
                    _cp(S, 'act', kcT.t[0:64, g, 0:NC], pB.t[0:64, 0:NC], [pB], [kcT])
                else:
                    for ci, (n0, sz) in enumerate(chunks):
                        for half in range(2):
                            _mm(S, pB.t[0:sz, 0:64], h_.t[:, half, n0:n0 + sz], w2s.t[:, 1, half, :], half == 0,
                                half == 1, [w2s, h_], [pB])
                        _cp(S, 'act', vco.t[0:sz, ci, g, 0:64], pB.t[0:sz, 0:64], [pB], [vco])
        S.flush()


def phase_C(nc, S, Sq, T, kcT, vco):
    NT = Sq // 128
    NC, chunks = cmp_chunks(Sq)
    OFFS = 8 * (NT - 1)
    with ExitStack() as es:
        sb, ps = _mk(nc, es, S)
        identB = sb('c_identB', [128, 128], BF16, sem=True)
        KE = sb('c_KE', [128, 4, Sq], BF16, sem=True)
        KWt = sb('c_KW', [128, 4, Sq], BF16, sem=True)
        KWz = Buf()
        KEe = Buf()
        KEe_sem = S.newdma()
        VS = sb('c_VS', [128, NT, 4, 65], BF16, sem=True)
        VW = sb('c_VW', [128, NT, 4, 65], BF16, sem=True)
        biasT = sb('c_biasT', [128, 2, 16, 128], BF16)
        tmpA = sb('c_tmpA', [128, 2048], F32, sem=True)
        tmpB = sb('c_tmpB', [128, 2048], F32, sem=True)
        m512 = sb('c_m512', [128, 512], BF16, sem=True)
        Bband = sb('c_Bband', [32, 16, 128], BF16)
        SelW = sb('c_SelW', [32, OFFS + 128 * len(chunks)], BF16, sem=True)
        mulB = sb('c_mulB', [128, 128], F32, sem=True)
        addB = sb('c_addB', [128, 128], F32, sem=True)
        QM = [sb('c_QM%d' % i, [128, 4, 4, 128], BF16, sem=True) for i in range(2)]
        QMq = [Buf() for _ in range(2)]
        QMm = [[Buf() for _ in range(4)] for _ in range(2)]
        gt = [sb('c_gt%d' % i, [128, 16, 3], F32, sem=True) for i in range(3)]
        NE = 4
        Et = [sb('c_E%d' % i, [128, 512], BF16) for i in range(NE)]
        o1s = [[sb('c_o1s%d_%d' % (j, i), [128, 4, 128], F32) for i in range(4)] for j in range(2)]
        o2s = sb('c_o2s', [128, 4, 65], F32)
        o3s = [[sb('c_o3s%d_%d' % (j, i), [128, 4, 65], F32) for i in range(4)] for j in range(2)]
        coef1 = [[sb('c_coef1_%d_%d' % (j, i), [128, 4], F32) for i in range(4)] for j in range(2)]
        den = sb('c_den', [128, 4], F32)
        rden = sb('c_rden', [128, 4], F32)
        c2 = sb('c_c2', [128, 4], F32)
        c3 = sb('c_c3', [128, 4], F32)
        imp = sb('c_imp', [128, 64], F32)
        score = sb('c_score', [128, 64], F32)
        score2 = sb('c_score2', [128, 64], F32)
        m8a = sb('c_m8a', [128, 8], F32)
        m8b = sb('c_m8b', [128, 8], F32)
        nms = [sb('c_nm%d' % i, [128, 128], BF16) for i in range(4)]
        acc = sb('c_acc', [128, 4, 64], F32)
        ntk = sb('c_ntk', [128, 1024], BF16)
        nst = [sb('c_nst%d' % i, [128, 8, 128], BF16, sem=True) for i in range(2)]

        scp = [ps('c_sc%d' % i, [128, 512]) for i in range(3)]
        o1U = ps('c_o1U', [128, 512])
        o3p = ps('c_o3', [128, 4, 65])
        o2p = [ps('c_o2_%d' % i, [128, 4, 65]) for i in range(2)]
        ptr = ps('c_ptr', [128, 1024], BF16)

        S.dma('pool', identB.sem, dict(out=identB.t[:, :], in_=T['identF']), writes=[identB])
        S.dma('sp', KE.sem, dict(out=KE.t[0:64, :, :], in_=T['KS'].rearrange('g d s -> d g s')), writes=[KE])
        for g in range(4):
            for c0 in range(0, Sq, 2048):
                c1 = min(Sq, c0 + 2048)
                S.dma('pool', KEe_sem, dict(out=KE.t[64:128, g, c0:c1], in_=T['Econst'][:, c0:c1]), writes=[KEe])
        S.dma('sp', KWt.sem, dict(out=KWt.t[0:64, :, :], in_=T['KW'].rearrange('g d s -> d g s')), writes=[KWt])
        _memset(S, 'pool', KWt.t[64:128, :, :], 0.0, [KWz])
        for i_, q_ in enumerate(QM):
            _memset(S, 'pool', q_.t[:, :, :, :], 0.0, [q_, QMq[i_]] + QMm[i_])
        for k0 in range(0, NT, 8):
            k1 = min(NT, k0 + 8)
            S.dma('sp', VS.sem, dict(out=VS.t[:, k0:k1, :, :],
                                     in_=T['VS1'][k0 * 128:k1 * 128].rearrange('(k p) g d -> p k g d', p=128)),
                  writes=[VS])
            S.dma('sp', VW.sem, dict(out=VW.t[:, k0:k1, :, :],
                                     in_=T['VW1'][k0 * 128:k1 * 128].rearrange('(k p) g d -> p k g d', p=128)),
                  writes=[VW])
        S.dma('pool', m512.sem, dict(out=m512.t[:, :], in_=T['m512']), writes=[m512])
        S.dma('pool', SelW.sem, dict(out=SelW.t[:, :], in_=T['SelW']), writes=[SelW])
        S.dma('sp', mulB.sem, dict(out=mulB.t[:, :], in_=T['mulB']), writes=[mulB])
        S.dma('sp', addB.sem, dict(out=addB.t[:, :], in_=T['addB']), writes=[addB])
        for dl in range(2):
            S.dma('sp', tmpA.sem, dict(out=tmpA.t[:, :], in_=T['tz1'][dl].rearrange('k h q -> k (h q)')), writes=[tmpA])
            S.dma('sp', tmpB.sem, dict(out=tmpB.t[:, :], in_=T['tz31'][dl].rearrange('k h q -> k (h q)')), writes=[tmpB])
            _tt(S, 'dve', tmpA.t[:, :], tmpA.t[:, :], tmpB.t[:, :], ALU.subtract, [tmpA, tmpB], [tmpA])
            S.dma('sp', tmpB.sem, dict(out=tmpB.t[:, :], in_=T['tzm'][dl].rearrange('k h q -> k (h q)')), writes=[tmpB])
            _tt(S, 'dve', biasT.t[:, dl, :, :].rearrange('k h q -> k (h q)'), tmpA.t[:, :], tmpB.t[:, :], ALU.add,
                [tmpA, tmpB], [biasT])
        S.dma('sp', tmpA.sem, dict(out=tmpA.t[0:32, :], in_=T['cb1'].rearrange('k h q -> k (h q)')), writes=[tmpA])
        S.dma('sp', tmpB.sem, dict(out=tmpB.t[0:32, :], in_=T['cb31'].rearrange('k h q -> k (h q)')), writes=[tmpB])
        _tt(S, 'dve', tmpA.t[0:32, :], tmpA.t[0:32, :], tmpB.t[0:32, :], ALU.subtract, [tmpA, tmpB], [tmpA])
        S.dma('sp', tmpB.sem, dict(out=tmpB.t[0:32, :], in_=T['cbm'].rearrange('k h q -> k (h q)')), writes=[tmpB])
        _tt(S, 'dve', Bband.t[:, :, :].rearrange('k h q -> k (h q)'), tmpA.t[0:32, :], tmpB.t[0:32, :], ALU.add,
            [tmpA, tmpB], [Bband])
        for nm_ in nms:
            _memset(S, 'dve', nm_.t[:, :], 0.0, [nm_])

        def cw_steps(qt):
            out = []
            for g in range(4):
                cl = [(ci, n0, sz) for ci, (n0, sz) in enumerate(chunks) if n0 <= 8 * qt + 6]
                for i, (ci, n0, sz) in enumerate(cl):
                    out.append(dict(kind='cmp', qt=qt, g=g, ci=ci, n0=n0, sz=sz, first=i == 0, last=i == len(cl) - 1))
                kl = list(range(max(0, qt - 4), qt + 1))
                for i, kt in enumerate(kl):
                    out.append(dict(kind='win', qt=qt, g=g, kt=kt, sz=128, first=i == 0, last=i == len(kl) - 1))
            out[0]['loadq'] = qt
            out[-1]['flush_def'] = True
            return out

        def sel_steps(qt):
            out = []
            for g in range(4):
                for kt in range(qt + 1):
                    out.append(dict(kind='sel', qt=qt, g=g, kt=kt, sz=128, first=kt == 0, last=kt == qt))
            return out

        if os.environ.get('PIPE', '1') == '1':
            steps = cw_steps(0)
            for qt in range(NT):
                if qt + 1 < NT:
                    steps += cw_steps(qt + 1)
                steps += sel_steps(qt)
        else:
            steps = []
            for qt in range(NT):
                steps += cw_steps(qt) + sel_steps(qt)
        cnt = dict(sc=0, e=0, o2=0)

        def load_q(qt):
            sl = qt % 2
            qs = slice(qt * 128, (qt + 1) * 128)
            S.dma('sp', QM[sl].sem, dict(out=QM[sl].t[0:64, :, :, :].rearrange('d g h q -> d (g h) q'),
                                         in_=T['QT'][:, :, qs].rearrange('h d q -> d h q')), writes=[QMq[sl]])
            S.dma('sp', gt[qt % 3].sem, dict(out=gt[qt % 3].t[:, :, :].rearrange('p h b -> p (h b)'), in_=T['G'][qs, :]),
                  writes=[gt[qt % 3]])

        def emit_scores(st):
            qt, g, sz = st['qt'], st['g'], st['sz']
            sl = qt % 2
            sc = scp[cnt['sc'] % 3]
            cnt['sc'] += 1
            st['sc'] = sc
            qrow = QM[sl].t[:, g, :, :].rearrange('d h q -> d (h q)')
            if st['kind'] == 'cmp':
                n0 = st['n0']
                a = n0 - 8 * qt + OFFS
                _mm(S, sc.t[0:sz, :], kcT.t[:, g, n0:n0 + sz], qrow, True, False, [kcT, QMq[sl], QM[sl], QMm[sl][g]], [sc])
                _mm(S, sc.t[0:sz, :], SelW.t[0:32, a:a + sz],
                    Bband.t[0:32, 4 * g:4 * g + 4, :].rearrange('k h q -> k (h q)'), False, True, [SelW, Bband], [sc])
                return
            kt = st['kt']
            dl = (qt - kt)
            extra = None
            if dl in (0, 1):
                extra = (biasT.t[:, dl, 4 * g:4 * g + 4, :].rearrange('k h q -> k (h q)'), biasT)
            elif dl == 4 and st['kind'] == 'win':
                extra = (m512.t[:, :], m512)
            ks = slice(kt * 128, (kt + 1) * 128)
            if st['kind'] == 'win':
                _mm(S, sc.t[:, :], KWt.t[:, g, ks], qrow, True, extra is None, [KWt, KWz, QMq[sl], QM[sl], QMm[sl][g]], [sc])
            else:
                _mm(S, sc.t[:, :], KE.t[:, g, ks], QM[sl].t[:, g, :, :].rearrange('d h q -> d (h q)'), True,
                    extra is None, [KE, KEe, QMq[sl], QMm[sl][g]], [sc])
            if extra is not None:
                _mm(S, sc.t[:, :], identB.t[:, :], extra[0], False, True, [identB, extra[1]], [sc])

        def emit_exp(st):
            sz = st['sz']
            E = Et[cnt['e'] % NE]
            cnt['e'] += 1
            st['E'] = E
            _act(S, E.t[0:sz, :], st['sc'].t[0:sz, :], AF.Exp, [st['sc']], [E])

        def emit_pv(st):
            qt, g, sz, E = st['qt'], st['g'], st['sz'], st['E']
            if st['kind'] == 'cmp':
                for h in range(4):
                    _mm(S, o1U.t[:, h * 128:(h + 1) * 128], E.t[0:sz, h * 128:(h + 1) * 128], vco.t[0:sz, st['ci'], g, :],
                        st['first'] and h == 0, st['last'] and h == 3, [E, vco], [o1U], skip=True)
                if st['last']:
                    fin_cmp(qt, g)
                return
            kt = st['kt']
            if st['kind'] == 'win':
                op_, V = o3p, VW
            else:
                if st['first']:
                    st['o2'] = o2p[cnt['o2'] % 2]
                    cnt['o2'] += 1
                    cur['o2'] = st['o2']
                op_, V = cur['o2'], VS
            for h in range(4):
                _mm(S, op_.t[:, h, :], E.t[:, h * 128:(h + 1) * 128], V.t[:, kt, g, :], st['first'] and h == 0,
                    st['last'] and h == 3, [E, V], [op_], skip=True)
            if st['last']:
                if st['kind'] == 'win':
                    _cp(S, EV, o3s[qt % 2][g].t[:, :, :], o3p.t[:, :, :], [o3p], [o3s[qt % 2][g]])
                else:
                    fin_sel(qt, g, op_)

        cur = {}

        def fin_cmp(qt, g):
            sl = qt % 2
            nm = nms[g]
            o1 = o1s[sl][g]
            _cp(S, EV, o1.t[:, :, :], o1U.t[:, :].rearrange('p (h c) -> p h c', h=4), [o1U], [o1])
            S.op('dve', 'tensor_reduce', dict(out=den.t[:, :], in_=o1.t[:, :, 64:128], axis=AX.X, op=ALU.add), [o1], [den])
            _ts(S, 'dve', den.t[:, :], den.t[:, :], 1e-30, ALU.max, [den], [den])
            S.op('dve', 'reciprocal', dict(out=rden.t[:, :], in_=den.t[:, :]), [den], [rden])
            _ts(S, 'dve', imp.t[:, :], o1.t[:, 0, 64:128], rden.t[:, 0:1], ALU.mult, [o1, rden], [imp])
            for h in range(1, 4):
                _stt(S, imp.t[:, :], o1.t[:, h, 64:128], rden.t[:, h:h + 1], imp.t[:, :], ALU.mult, ALU.add,
                     [o1, rden, imp], [imp])
            a = 62 - 2 * qt
            _tt(S, 'dve', score.t[:, :], imp.t[:, :], mulB.t[:, a:a + 64], ALU.mult, [imp, mulB], [score])
            _tt(S, 'dve', score.t[:, :], score.t[:, :], addB.t[:, a:a + 64], ALU.add, [score, addB], [score])
            _memset(S, 'dve', score.t[:, 0:1], 50.0, [score])
            S.op('dve', 'max', dict(out=m8a.t[:, :], in_=score.t[:, :]), [score], [m8a])
            S.op('dve', 'match_replace', dict(out=score2.t[:, :], in_to_replace=m8a.t[:, :], in_values=score.t[:, :],
                                              imm_value=-1e9), [score, m8a], [score2])
            S.op('dve', 'max', dict(out=m8b.t[:, :], in_=score2.t[:, :]), [score2], [m8b])
            _ts(S, 'dve', nm.t[:, 64:128], score.t[:, :], m8b.t[:, 7:8], ALU.is_lt, [score, m8b], [nm], s2=MASKV,
                op1=ALU.mult)

            def part2(sl=sl, g=g, nm=nm):
                _tp(S, ptr.t[:, 0:128], nm.t[:, :], identB.t[:, :], [nm, identB], [ptr])
                for h in range(4):
                    _cp(S, 'dve', QM[sl].t[64:128, g, h, :], ptr.t[64:128, 0:128], [ptr], [QMm[sl][g]])
            deferred.append([DEFER, part2])
            _tt(S, 'dve', coef1[sl][g].t[:, :], rden.t[:, :], gt[qt % 3].t[:, 4 * g:4 * g + 4, 0], ALU.mult, [rden, gt[qt % 3]],
                [coef1[sl][g]])

        def fin_sel(qt, g, o2):
            sl = qt % 2
            _cp(S, EV, o2s.t[:, :, :], o2.t[:, :, :], [o2], [o2s])
            for (osrc, cf, br) in ((o2s, c2, 1), (o3s[sl][g], c3, 2)):
                _ts(S, 'dve', den.t[:, :], osrc.t[:, :, 64], 1e-30, ALU.max, [osrc], [den])
                S.op('dve', 'reciprocal', dict(out=rden.t[:, :], in_=den.t[:, :]), [den], [rden])
                _tt(S, 'dve', cf.t[:, :], rden.t[:, :], gt[qt % 3].t[:, 4 * g:4 * g + 4, br], ALU.mult, [rden, gt[qt % 3]], [cf])
            o1 = o1s[sl][g]
            for h in range(4):
                _ts(S, 'dve', acc.t[:, h, :], o1.t[:, h, 0:64], coef1[sl][g].t[:, h:h + 1], ALU.mult, [o1, coef1[sl][g]], [acc])
                _stt(S, acc.t[:, h, :], o3s[sl][g].t[:, h, 0:64], c3.t[:, h:h + 1], acc.t[:, h, :], ALU.mult, ALU.add,
                     [o3s[sl][g], c3, acc], [acc])
                c0 = (4 * g + h) * 64
                _stt(S, ntk.t[:, c0:c0 + 64], o2s.t[:, h, 0:64], c2.t[:, h:h + 1], acc.t[:, h, :], ALU.mult, ALU.add,
                     [o2s, c2, acc], [ntk])
            if g == 3:
                def part2(qt=qt):
                    for kc in range(8):
                        _tp(S, ptr.t[:, kc * 128:(kc + 1) * 128], ntk.t[:, kc * 128:(kc + 1) * 128], identB.t[:, :],
                            [ntk, identB], [ptr])
                    ns = nst[qt % 2]
                    _cp(S, 'dve', ns.t[:, :, :], ptr.t[:, :].rearrange('p (k q) -> p k q', k=8), [ptr], [ns])
                    S.dma('sp', ns.sem, dict(out=T['NSAT'][:, :, qt * 128:(qt + 1) * 128].rearrange('k p q -> p k q'),
                                             in_=ns.t[:, :, :]), reads=[ns])
                deferred.append([DEFER, part2])

        LOOK = int(os.environ.get('LOOK', '2'))
        EV = os.environ.get('EV', 'act')
        DEFER = int(os.environ.get('DEFER', '8'))
        deferred = []
        pend = []

        def run_deferred(force=False):
            while deferred and (force or deferred[0][0] <= 0):
                deferred.pop(0)[1]()

        for st in steps:
            if 'loadq' in st:
                load_q(st['loadq'])
            emit_scores(st)
            emit_exp(st)
            pend.append(st)
            if len(pend) > LOOK:
                emit_pv(pend.pop(0))
            for d_ in deferred:
                d_[0] -= 1
            run_deferred()
            if st.get('flush_def'):
                while pend:
                    emit_pv(pend.pop(0))
                run_deferred(force=True)
        while pend:
            emit_pv(pend.pop(0))
        run_deferred(force=True)
        S.flush()


def phase_D(nc, S, Sq, T):
    NSUP = Sq // 512
    with ExitStack() as es:
        sb, ps = _mk(nc, es, S)
        C = Ctx()
        identF = sb('d_identF', [128, 128], F32, sem=True)
        identB = sb('d_identB', [128, 128], BF16, sem=True)
        onesB = sb('d_onesB', [128, 128], BF16)
        epsT = sb('d_eps', [128, 1], F32)
        C.identF = identF
        C.stage = sb('d_stage', [32, 1024], F32, sem=True)
        wno = sb('d_wno', [128, 8, D], BF16, sem=True)
        wpw = sb('d_wpw', [128, 4, D], BF16, sem=True)
        wout = sb('d_wout', [128, 8, D], BF16, sem=True)
        wq = sb('d_wq', [128, 8, D], BF16, sem=True)
        wo = sb('d_wo', [128, 8, D], BF16, sem=True)
        kmT = sb('d_kmT', [128, 8, 256], BF16)
        vm = sb('d_vm', [128, 2, D], BF16)
        g2T = sb('d_g2T', [128, 8, 1], F32)
        gmT = sb('d_gmT', [128, 8, 1], F32)
        ptr = ps('d_ptr', [128, 1024], BF16)
        pf = [ps('d_pf%d' % i, [128, 512]) for i in range(4)]
        ph = [ps('d_ph%d' % i, [128, 512]) for i in range(2)]
        pst = ps('d_pst', [128, 48])
        C.pst = pst
        ss = sb('d_ss', [128, 4], F32)
        rt = sb('d_rt', [128, 4], F32)
        rstd = sb('d_rstd', [128, 4], F32)
        junk = sb('d_junk', [128, D], BF16)
        ntok = [sb('d_ntok%d' % i, [128, D], BF16) for i in range(2)]

        S.dma('sp', identF.sem, dict(out=identF.t[:, :], in_=T['identF']), writes=[identF])
        S.dma('pool', identB.sem, dict(out=identB.t[:, :], in_=T['identF']), writes=[identB])
        _memset(S, 'dve', onesB.t[:, :], 1.0, [onesB])
        _memset(S, 'dve', epsT.t[:, :], EPS, [epsT])
        load_w_cast(S, wno, lambda kc, c0, c1: wno.t[:, kc, c0:c1], T['nsa_w_o'], 8, D)
        load_w_cast(S, wpw, lambda kc, c0, c1: wpw.t[:, kc, c0:c1], T['conv_w_pw'], 4, D)
        load_w_cast(S, wout, lambda kc, c0, c1: wout.t[:, kc, c0:c1], T['w_out'], 8, D)
        load_w_cast(S, wq, lambda kc, c0, c1: wq.t[:, kc, c0:c1], T['xa_wq'], 8, D)
        load_w_cast(S, wo, lambda kc, c0, c1: wo.t[:, kc, c0:c1], T['xa_wo'], 8, D)
        cols_from_rows(S, C, T['norm2_g'], 1, 1024, g2T, lambda ch: g2T.t[:, ch, :])
        cols_from_rows(S, C, T['mem_norm_g'], 1, 1024, gmT, lambda ch: gmT.t[:, ch, :])
        for kc in range(8):
            _ts(S, 'dve', wq.t[:, kc, :], wq.t[:, kc, :], g2T.t[:, kc, :], ALU.mult, [wq, g2T], [wq])

        def rms_T(xtiles, nT_tl, ncols_off):
            n = len(xtiles)
            for i, xb in enumerate(xtiles):
                _stt(S, junk.t[:, :], xb.t[:, :], 1.0, xb.t[:, :], ALU.mult, ALU.mult, [xb], [junk, ss],
                     accum_out=ss.t[:, i:i + 1])
            _act(S, rt.t[:, 0:n], ss.t[:, 0:n], AF.Sqrt, [ss, epsT], [rt], scale=1.0 / D, bias=epsT.t[:, :])
            S.op('dve', 'reciprocal', dict(out=rstd.t[:, 0:n], in_=rt.t[:, 0:n]), [rt], [rstd])
            for i, xb in enumerate(xtiles):
                nk = ntok[i % 2]
                _ts(S, 'dve', nk.t[:, :], xb.t[:, :], rstd.t[:, i:i + 1], ALU.mult, [xb, rstd], [nk])
                for kc in range(8):
                    _tp(S, ptr.t[:, kc * 128:(kc + 1) * 128], nk.t[:, kc * 128:(kc + 1) * 128], identB.t[:, :],
                        [nk, identB], [ptr])
                yield i, ptr

        with ExitStack() as es2:
            sb2, _ = _mk(nc, es2, S)
            wkv = sb2('d_wkv', [128, 8, 2 * D], BF16, sem=True)
            memx = [sb2('d_memx%d' % i, [128, D], F32, sem=True) for i in range(2)]
            memnT = sb2('d_memnT', [128, 8, 256], BF16)
            load_w_cast(S, wkv, lambda kc, c0, c1: wkv.t[:, kc, c0:c1], T['xa_wkv'], 8, 2 * D)
            for kc in range(8):
                _ts(S, 'dve', wkv.t[:, kc, :], wkv.t[:, kc, :], gmT.t[:, kc, :], ALU.mult, [wkv, gmT], [wkv])
            for mc in range(2):
                S.dma('sp', memx[mc].sem, dict(out=memx[mc].t[:, :], in_=T['mem'][mc * 128:(mc + 1) * 128, :]),
                      writes=[memx[mc]])
            for i, p_ in rms_T(memx, memnT, 0):
                _cp(S, 'act', memnT.t[:, :, i * 128:(i + 1) * 128], p_.t[:, :].rearrange('p (k q) -> p k q', k=8),
                    [p_], [memnT])
            for c in range(8):
                p = pf[c % 4]
                for kc in range(8):
                    _mm(S, p.t[:, 0:256], wkv.t[:, kc, c * 128:(c + 1) * 128], memnT.t[:, kc, :], kc == 0, kc == 7,
                        [wkv, memnT], [p])
                _cp(S, 'dve', kmT.t[:, c, :], p.t[:, 0:256], [p], [kmT])
            for mc in range(2):
                for half in range(2):
                    p = pf[(mc * 2 + half) % 4]
                    for kc in range(8):
                        _mm(S, p.t[:, :], memnT.t[:, kc, mc * 128:(mc + 1) * 128],
                            wkv.t[:, kc, D + half * 512:D + (half + 1) * 512], kc == 0, kc == 7, [wkv, memnT], [p])
                    _cp(S, 'dve', vm.t[:, mc, half * 512:(half + 1) * 512], p.t[:, :], [p], [vm])
            S.flush()

        nsa_s = sb('d_nsa', [128, 8, 512], BF16, sem=True)
        hc_s = sb('d_hc', [128, 4, 512], BF16, sem=True)
        gm_s = sb('d_gm', [128, 16, 512], BF16, sem=True)
        xs = [sb('d_x%d' % i, [128, D], F32, sem=True) for i in range(4)]
        mrg = sb('d_mrg', [128, 8, 512], BF16)
        n2T = sb('d_n2T', [128, 8, 512], BF16)
        qxT = sb('d_qxT', [128, 8, 512], BF16)
        PT = sb('d_PT', [128, 2, 512], BF16)
        oTn = sb('d_oTn', [128, 8, 512], BF16)
        t1 = sb('d_t1', [128, 512], F32)
        t2 = sb('d_t2', [128, 512], F32)
        rdn = sb('d_rdn', [128, 512], F32)
        n3s = [sb('d_n3s%d' % i, [128, 8, 128], BF16, sem=True) for i in range(2)]
        npf = 0
        for st in range(NSUP):
            cs = slice(st * 512, (st + 1) * 512)
            S.dma('sp', nsa_s.sem, dict(out=nsa_s.t[:, :, :], in_=T['NSAT'][:, :, cs].rearrange('k p s -> p k s')),
                  writes=[nsa_s])
            S.dma('sp', hc_s.sem, dict(out=hc_s.t[:, :, :], in_=T['HC'][:, :, cs].rearrange('k p s -> p k s')),
                  writes=[hc_s])
            S.dma('sp', gm_s.sem, dict(out=gm_s.t[:, :, :], in_=T['GM'][:, :, cs].rearrange('k p s -> p k s')),
                  writes=[gm_s])
            for t in range(4):
                tt = st * 4 + t
                S.dma('sp', xs[t].sem, dict(out=xs[t].t[:, :], in_=T['x'][tt * 128:(tt + 1) * 128, :]), writes=[xs[t]])
            for f in range(8):
                pa = pf[npf % 4]
                pcv = pf[(npf + 1) % 4]
                npf += 2
                for kc in range(8):
                    _mm(S, pa.t[:, :], wno.t[:, kc, f * 128:(f + 1) * 128], nsa_s.t[:, kc, :], kc == 0, kc == 7,
                        [wno, nsa_s], [pa])
                for c in range(4):
                    _mm(S, pcv.t[:, :], wpw.t[:, c, f * 128:(f + 1) * 128], hc_s.t[:, c, :], c == 0, c == 3,
                        [wpw, hc_s], [pcv])
                _tt(S, 'dve', t1.t[:, :], pa.t[:, :], gm_s.t[:, 8 + f, :], ALU.mult, [pa, gm_s], [t1])
                _tt(S, 'dve', t2.t[:, :], pcv.t[:, :], gm_s.t[:, f, :], ALU.mult, [pcv, gm_s], [t2])
                _tt(S, 'pool', mrg.t[:, f, :], t1.t[:, :], t2.t[:, :], ALU.add, [t1, t2], [mrg])
            for t in range(4):
                for half in range(2):
                    p = ph[half]
                    for f in range(8):
                        _mm(S, p.t[:, :], mrg.t[:, f, t * 128:(t + 1) * 128], wout.t[:, f, half * 512:(half + 1) * 512],
                            f == 0, f == 7, [mrg, wout], [p])
                    _tt(S, 'dve', xs[t].t[:, half * 512:(half + 1) * 512], p.t[:, :],
                        xs[t].t[:, half * 512:(half + 1) * 512], ALU.add, [p, xs[t]], [xs[t]])
            for i, p_ in rms_T(xs, n2T, 0):
                _cp(S, 'act', n2T.t[:, :, i * 128:(i + 1) * 128], p_.t[:, :].rearrange('p (k q) -> p k q', k=8),
                    [p_], [n2T])
            for c in range(8):
                p = pf[npf % 4]
                npf += 1
                for kc in range(8):
                    _mm(S, p.t[:, :], wq.t[:, kc, c * 128:(c + 1) * 128], n2T.t[:, kc, :], kc == 0, kc == 7,
                        [wq, n2T], [p])
                _act(S, qxT.t[:, c, :], p.t[:, :], AF.Copy, [p], [qxT], scale=1.0 / 16.0)
            for hd in range(4):
                for mc in range(2):
                    p = pf[npf % 4]
                    npf += 1
                    for dc in range(2):
                        _mm(S, p.t[:, :], kmT.t[:, hd * 2 + dc, mc * 128:(mc + 1) * 128], qxT.t[:, hd * 2 + dc, :],
                            dc == 0, dc == 1, [kmT, qxT], [p])
                    _act(S, PT.t[:, mc, :], p.t[:, :], AF.Exp, [p], [PT])
                pd = pf[npf % 4]
                npf += 1
                for mc in range(2):
                    _mm(S, pd.t[:, :], onesB.t[:, :], PT.t[:, mc, :], mc == 0, mc == 1, [onesB, PT], [pd])
                S.op('dve', 'reciprocal', dict(out=rdn.t[:, :], in_=pd.t[:, :]), [pd], [rdn])
                for dc in range(2):
                    po = pf[npf % 4]
                    npf += 1
                    for mc in range(2):
                        _mm(S, po.t[:, :], vm.t[:, mc, hd * 256 + dc * 128:hd * 256 + (dc + 1) * 128], PT.t[:, mc, :],
                            mc == 0, mc == 1, [vm, PT], [po])
                    _tt(S, 'dve', oTn.t[:, hd * 2 + dc, :], po.t[:, :], rdn.t[:, :], ALU.mult, [po, rdn], [oTn])
            for t in range(4):
                tt = st * 4 + t
                for half in range(2):
                    p = ph[half]
                    for c in range(8):
                        _mm(S, p.t[:, :], oTn.t[:, c, t * 128:(t + 1) * 128], wo.t[:, c, half * 512:(half + 1) * 512],
                            c == 0, c == 7, [oTn, wo], [p])
                    _tt(S, 'dve', xs[t].t[:, half * 512:(half + 1) * 512], p.t[:, :],
                        xs[t].t[:, half * 512:(half + 1) * 512], ALU.add, [p, xs[t]], [xs[t]])
                S.dma('sp', xs[t].sem, dict(out=T['H2'][tt * 128:(tt + 1) * 128, :], in_=xs[t].t[:, :]), reads=[xs[t]])
            for i, p_ in rms_T(xs, None, 0):
                tt = st * 4 + i
                ns = n3s[i % 2]
                _cp(S, 'act', ns.t[:, :, :], p_.t[:, :].rearrange('p (k q) -> p k q', k=8), [p_], [ns])
                S.dma('sp', ns.sem, dict(out=T['N3T'][:, :, tt * 128:(tt + 1) * 128].rearrange('k p q -> p k q'),
                                         in_=ns.t[:, :, :]), reads=[ns])
        S.flush()


def phase_E(nc, S, Sq, T):
    NSUP = Sq // 512
    NP = D_FF // 128
    with ExitStack() as es:
        sb, ps = _mk(nc, es, S)
        C = Ctx()
        identF = sb('e_identF', [128, 128], F32, sem=True)
        C.identF = identF
        C.stage = sb('e_stage', [32, 1024], F32, sem=True)
        pst = ps('e_pst', [128, 48])
        C.pst = pst
        wup = sb('e_wup', [128, 8, 2 * D_FF], BF16, sem=True)
        wdn = sb('e_wdn', [128, NP, D], BF16, sem=True)
        g3T = sb('e_g3T', [128, 8, 1], F32)
        fw = sb('e_fw', [128, 2 * NP, 3], F32)
        fb = sb('e_fb', [128, 2 * NP, 1], F32)
        fgB = sb('e_fgB', [128, D], F32)
        onesF = sb('e_onesF', [1, 128], F32)
        epsT = sb('e_eps', [128, 1], F32)
        halo = sb('e_halo', [128, 2 * NP, 2], F32)
        n3 = sb('e_n3', [128, 8, 512], BF16, sem=True)
        actT = sb('e_actT', [128, NP, 512], BF16)
        ub = [sb('e_ub%d' % i, [128, 514], F32) for i in range(3)]
        tb = [sb('e_tb%d' % i, [128, 512], F32) for i in range(3)]
        sgl = sb('e_sgl', [128, 512], F32)
        h2 = [sb('e_h2_%d' % i, [128, D], F32, sem=True) for i in range(2)]
        junk = sb('e_junk', [128, D], BF16)
        ss = sb('e_ss', [128, 1], F32)
        rt = sb('e_rt', [128, 1], F32)
        rstd = sb('e_rstd', [128, 1], F32)
        pu = [ps('e_pu%d' % i, [128, 512]) for i in range(3)]
        pd = [ps('e_pd%d' % i, [128, 512]) for i in range(2)]

        S.dma('sp', identF.sem, dict(out=identF.t[:, :], in_=T['identF']), writes=[identF])
        _memset(S, 'dve', onesF.t[:, :], 1.0, [onesF])
        _memset(S, 'dve', epsT.t[:, :], EPS, [epsT])
        _memset(S, 'dve', halo.t[:, :, :], 0.0, [halo])
        load_w_cast(S, wup, lambda kc, c0, c1: wup.t[:, kc, c0:c1], T['ffn_w_up'], 8, 2 * D_FF)
        load_w_cast(S, wdn, lambda kc, c0, c1: wdn.t[:, kc, c0:c1], T['ffn_w_down'], NP, D)
        cols_from_rows(S, C, T['norm3_g'], 1, 1024, g3T, lambda ch: g3T.t[:, ch, :])
        for kc in range(8):
            _ts(S, 'dve', wup.t[:, kc, :], wup.t[:, kc, :], g3T.t[:, kc, :], ALU.mult, [wup, g3T], [wup])
        for blk in range(0, 2 * D_FF, 1024):
            w = min(1024, 2 * D_FF - blk)
            c0 = blk // 128
            cols_from_rows(S, C, T['ffn_dw_w'][:, blk:blk + w], 3, w, fw, lambda ch, c0=c0: fw.t[:, c0 + ch, :])
            cols_from_rows(S, C, T['ffn_dw_b'][:, blk:blk + w], 1, w, fb, lambda ch, c0=c0: fb.t[:, c0 + ch, :])
        S.dma('sp', C.stage.sem, dict(out=C.stage.t[0:1, 0:1024], in_=T['final_g']), writes=[C.stage])
        for half in range(2):
            _mm(S, pd[half].t[:, :], onesF.t[0:1, :], C.stage.t[0:1, half * 512:(half + 1) * 512], True, True,
                [onesF, C.stage], [pd[half]])
            _cp(S, 'dve', fgB.t[:, half * 512:(half + 1) * 512], pd[half].t[:, :], [pd[half]], [fgB])

        npu = 0
        nub = 0
        for st in range(NSUP):
            cs = slice(st * 512, (st + 1) * 512)
            S.dma('sp', n3.sem, dict(out=n3.t[:, :, :], in_=T['N3T'][:, :, cs].rearrange('k p s -> p k s')), writes=[n3])
            for j in range(NP):
                tpair = []
                for c in (j, j + NP):
                    p = pu[npu % 3]
                    npu += 1
                    u_ = ub[nub % 3]
                    t_ = tb[nub % 3]
                    nub += 1
                    for kc in range(8):
                        _mm(S, p.t[:, :], wup.t[:, kc, c * 128:(c + 1) * 128], n3.t[:, kc, :], kc == 0, kc == 7,
                            [wup, n3], [p])
                    _cp(S, 'act', u_.t[:, 2:514], p.t[:, :], [p], [u_])
                    _cp(S, 'pool', u_.t[:, 0:2], halo.t[:, c, :], [halo], [u_])
                    _act(S, t_.t[:, :], p.t[:, :], AF.Identity, [p, fw, fb], [t_], scale=fw.t[:, c, 2:3], bias=fb.t[:, c, :])
                    _stt(S, t_.t[:, :], u_.t[:, 0:512], fw.t[:, c, 0:1], t_.t[:, :], ALU.mult, ALU.add, [u_, fw, t_], [t_])
                    _stt(S, t_.t[:, :], u_.t[:, 1:513], fw.t[:, c, 1:2], t_.t[:, :], ALU.mult, ALU.add, [u_, fw, t_], [t_])
                    _cp(S, 'pool', halo.t[:, c, :], u_.t[:, 512:514], [u_], [halo])
                    tpair.append(t_)
                _act(S, sgl.t[:, :], tpair[0].t[:, :], AF.Silu, [tpair[0]], [sgl])
                _tt(S, 'dve', actT.t[:, j, :], sgl.t[:, :], tpair[1].t[:, :], ALU.mult, [sgl, tpair[1]], [actT])
            for t in range(4):
                tt = st * 4 + t
                hb = h2[tt % 2]
                ob = hb
                S.dma('sp', hb.sem, dict(out=hb.t[:, :], in_=T['H2'][tt * 128:(tt + 1) * 128, :]), writes=[hb])
                for half in range(2):
                    p = pd[half]
                    for j in range(NP):
                        _mm(S, p.t[:, :], actT.t[:, j, t * 128:(t + 1) * 128], wdn.t[:, j, half * 512:(half + 1) * 512],
                            j == 0, j == NP - 1, [actT, wdn], [p])
                    _tt(S, 'dve', hb.t[:, half * 512:(half + 1) * 512], p.t[:, :], hb.t[:, half * 512:(half + 1) * 512],
                        ALU.add, [p, hb], [hb])
                _stt(S, junk.t[:, :], hb.t[:, :], 1.0, hb.t[:, :], ALU.mult, ALU.mult, [hb], [junk, ss], accum_out=ss.t[:, 0:1])
                _act(S, rt.t[:, :], ss.t[:, :], AF.Sqrt, [ss, epsT], [rt], scale=1.0 / D, bias=epsT.t[:, :])
                S.op('dve', 'reciprocal', dict(out=rstd.t[:, :], in_=rt.t[:, :]), [rt], [rstd])
                _stt(S, ob.t[:, :], hb.t[:, :], rstd.t[:, 0:1], fgB.t[:, :], ALU.mult, ALU.mult, [hb, rstd, fgB], [hb])
                S.dma('sp', hb.sem, dict(out=T['y'][tt * 128:(tt + 1) * 128, :], in_=hb.t[:, :]), reads=[hb])
        S.flush()


def t5_bucket_np(d):
    n = np.maximum(d, 0)
    nf = np.maximum(n, 1).astype(np.float32)
    large = 16 + (np.log(nf / np.float32(16)) / np.float32(np.log(8.0)) * np.float32(16)).astype(np.int32)
    large = np.minimum(large, 31)
    return np.where(n < 16, n, large)


def scratch_spec(Sq):
    return {
        'QT': ([16, 64, Sq], BF16), 'KC': ([4, 64, Sq], BF16), 'VC': ([4, 64, Sq], BF16),
        'KS': ([4, 64, Sq], BF16), 'KW': ([4, 64, Sq], BF16), 'GM': ([16, 128, Sq], BF16),
        'HC': ([4, 128, Sq], BF16), 'VS1': ([Sq, 4, 65], BF16), 'VW1': ([Sq, 4, 65], BF16),
        'G': ([Sq, 48], F32), 'NSAT': ([8, 128, Sq], BF16), 'H2': ([Sq, D], F32), 'N3T': ([8, 128, Sq], BF16),
    }


INPUT_SHAPES = {
    'norm1_g': [1, D], 'w_in': [D, IN_W], 'conv_dw_w': [31, 512], 'conv_dw_b': [1, 512],
    'conv_ln_g': [1, 512], 'conv_ln_b': [1, 512], 'conv_w_pw': [512, D],
    'cmp_pe': [2, 32, 64], 'cmp_w1': [2, 2048, 256], 'cmp_b1': [1, 512], 'cmp_w2': [2, 256, 64],
    'nsa_w_o': [D, D], 'w_out': [D, D], 'norm2_g': [1, D], 'mem_norm_g': [1, D],
    'xa_wq': [D, D], 'xa_wkv': [D, 2 * D], 'xa_wo': [D, D], 'norm3_g': [1, D],
    'ffn_w_up': [D, 2 * D_FF], 'ffn_dw_w': [3, 2 * D_FF], 'ffn_dw_b': [1, 2 * D_FF],
    'ffn_w_down': [D_FF, D], 'final_g': [1, D],
}


def const_shapes(Sq):
    NT = Sq // 128
    NC, chunks = cmp_chunks(Sq)
    return {
        'identF': [128, 128], 'Econst': [64, Sq], 'm512': [128, 512],
        'SelW': [32, 8 * (NT - 1) + 128 * len(chunks)], 'mulB': [128, 128], 'addB': [128, 128],
        'ovl': [128 * len(chunks), 64],
        'tz1': [2, 128, 16, 128], 'tz31': [2, 128, 16, 128], 'tzm': [2, 128, 16, 128],
        'cb1': [32, 16, 128], 'cb31': [32, 16, 128], 'cbm': [32, 16, 128],
    }


def build(Sq, debug=(), phases='ABCDE'):
    nc = bass.Bass("TRN2", target_bir_lowering=False)
    T = {}
    T['x'] = nc.dram_tensor('x', [Sq, D], F32, kind='ExternalInput').ap()
    T['mem'] = nc.dram_tensor('mem', [MEM, D], F32, kind='ExternalInput').ap()
    for k, shp in list(INPUT_SHAPES.items()) + list(const_shapes(Sq).items()):
        T[k] = nc.dram_tensor(k, shp, F32, kind='ExternalInput').ap()
    for k, (shp, dt) in scratch_spec(Sq).items():
        kind = 'ExternalOutput' if k in debug else 'Internal'
        T[k] = nc.dram_tensor(k, shp, dt, kind=kind).ap()
    T['y'] = nc.dram_tensor('y', [Sq, D], F32, kind='ExternalOutput').ap()
    NC, chunks = cmp_chunks(Sq)
    with ExitStack() as es:
        S = Sched(nc, es)
        if 'A' in phases:
            phase_A(nc, S, Sq, T)
        kcT = Tl(es.enter_context(nc.sbuf_tensor('kcT', [128, 4, 128 * len(chunks)], BF16)))
        vco = Tl(es.enter_context(nc.sbuf_tensor('vco', [128, len(chunks), 4, 128], BF16)))
        if 'B' in phases:
            phase_B(nc, S, Sq, T, kcT, vco)
        if 'C' in phases:
            phase_C(nc, S, Sq, T, kcT, vco)
        if 'D' in phases:
            phase_D(nc, S, Sq, T)
        if 'E' in phases:
            phase_E(nc, S, Sq, T)
    return nc


def host_consts(rel_bias, Sq):
    NT = Sq // 128
    NC, chunks = cmp_chunks(Sq)
    rb = np.asarray(rel_bias, dtype=np.float32)
    c = {}
    c['identF'] = np.eye(128, dtype=np.float32)
    E = np.zeros((64, Sq), np.float32)
    kk = np.arange(Sq)
    valid = kk // 64 < 64
    E[(kk // 64)[valid], kk[valid]] = 1.0
    c['Econst'] = E
    ki = np.arange(128)[:, None]
    qi = np.arange(128)[None, :]
    c['m512'] = np.tile(np.where(qi >= ki, MASKV, 0.0).astype(np.float32), (1, 4))
    OFFS = 8 * (NT - 1)
    W = np.zeros((32, OFFS + 128 * len(chunks)), np.float32)
    m = np.arange(W.shape[1]) - OFFS
    for r in range(17):
        W[r, m == r - 10] = 1.0
    W[31, m >= 7] = 1.0
    c['SelW'] = W
    r = np.arange(128)[None, :] - 62
    p = np.arange(128)[:, None]
    hi = (p >= 64).astype(np.int64)
    rel = r - hi
    free = rel <= -2
    forced = (rel == -1) | (rel == 0)
    c['mulB'] = np.where(free, 1.0, 0.0).astype(np.float32) * np.ones((128, 1), np.float32)
    c['addB'] = np.where(free, 0.0, np.where(forced, 10.0 + (rel + 2), -1.0 - 0.001 * np.maximum(rel, 0))).astype(np.float32)
    n = np.arange(128 * len(chunks))[:, None]
    j = np.arange(64)[None, :]
    ov = np.clip(np.minimum(16 * n + 32, 64 * j + 64) - np.maximum(16 * n, 64 * j), 0, None).astype(np.float32) / 32.0
    ov[NC:] = 0.0
    c['ovl'] = ov.astype(np.float32)
    tz1 = np.zeros((2, 128, 16, 128), np.float32)
    tz31 = np.zeros_like(tz1)
    tzm = np.zeros_like(tz1)
    for dl in range(2):
        d = dl * 128 + qi - ki
        ok = d >= 0
        g1 = rb[t5_bucket_np(d)]
        g31 = rb[np.full_like(d, 31)]
        tz1[dl] = np.where(ok[:, :, None], g1, 0.0).transpose(0, 2, 1)
        tz31[dl] = np.where(ok[:, :, None], g31, 0.0).transpose(0, 2, 1)
        tzm[dl] = np.where(ok[:, :, None], 0.0, MASKV).transpose(0, 2, 1) * np.ones((1, 16, 1), np.float32)
    c['tz1'], c['tz31'], c['tzm'] = tz1, tz31, tzm
    cb1 = np.zeros((32, 16, 128), np.float32)
    cb31 = np.zeros_like(cb1)
    cbm = np.zeros_like(cb1)
    rr = np.arange(17)[:, None]
    d1 = np.arange(128)[None, :] - 16 * (rr - 10) - 31
    ok = d1 >= 0
    cb1[:17] = np.where(ok[:, :, None], rb[t5_bucket_np(d1)], 0.0).transpose(0, 2, 1)
    cb31[:17] = np.where(ok[:, :, None], rb[np.full_like(d1, 31)], 0.0).transpose(0, 2, 1)
    cbm[:17] = (np.where(ok, 0.0, MASKV)[:, None, :] * np.ones((1, 16, 1))).astype(np.float32)
    cbm[31] = MASKV
    c['cb1'], c['cb31'], c['cbm'] = cb1, cb31, cbm
    return {k: np.ascontiguousarray(v, dtype=np.float32) for k, v in c.items()}


def host_inputs(inp, Sq):
    shared = {}
    for k, shp in INPUT_SHAPES.items():
        shared[k] = np.ascontiguousarray(np.asarray(inp[k], dtype=np.float32).reshape(shp))
    shared.update(host_consts(inp['rel_bias'], Sq))
    return shared


def kernel(**inp):
    x = np.asarray(inp['x'], dtype=np.float32)
    mem = np.asarray(inp['mem'], dtype=np.float32)
    B, Sq, _ = x.shape
    nc = build(Sq)
    shared = host_inputs(inp, Sq)
    in_maps = []
    for b in range(B):
        m = dict(shared)
        m['x'] = np.ascontiguousarray(x[b])
        m['mem'] = np.ascontiguousarray(mem[b])
        in_maps.append(m)
    res = run_bass_kernel_spmd(nc, in_maps, core_ids=list(range(B)))
    return np.stack([np.asarray(r['y'], dtype=np.float32) for r in res.results], axis=0)
```

```python
import os
import numpy as np
from contextlib import ExitStack
import concourse.bass as bass
import concourse.mybir as mybir
from concourse.bass_utils import run_bass_kernel_spmd

F32 = mybir.dt.float32
BF16 = mybir.dt.bfloat16
AF = mybir.ActivationFunctionType
ALU = mybir.AluOpType
AX = mybir.AxisListType

D = 1024
SEQ = 4096
MEM = 256
IN_W = 5680
D_FF = 2816
MASKV = -30000.0
EPS = 1e-6

ENGS = ['pe', 'act', 'dve', 'pool', 'sp']


class Buf:
    __slots__ = ('w', 'r')

    def __init__(self):
        self.w = None
        self.r = {}


class Tl:
    def __init__(self, t, sem=None):
        self.t = t
        self.b = Buf()
        self.sem = sem


def _b(x):
    return x.b if isinstance(x, Tl) else x


class Sched:
    def __init__(self, nc, es):
        self.nc = nc
        self.es = es
        self.q = {e: [] for e in ENGS}
        self.sem = {}
        self.cnt = {}
        self.known = {e: {} for e in ENGS}
        self.ndma = 0

    def semh(self, key):
        if key not in self.sem:
            self.sem[key] = self.es.enter_context(self.nc.semaphore('s_' + key))
            self.cnt[key] = 0
        return self.sem[key]

    def newdma(self, name=None):
        self.ndma += 1
        key = 'd%d' % self.ndma
        self.semh(key)
        return key

    def _deps(self, eng, reads, writes):
        need = {}
        for b in reads:
            b = _b(b)
            if b.w:
                k, v = b.w
                need[k] = max(need.get(k, 0), v)
        for b in writes:
            b = _b(b)
            if b.w:
                k, v = b.w
                need[k] = max(need.get(k, 0), v)
            for k, v in b.r.items():
                need[k] = max(need.get(k, 0), v)
        out = []
        kn = self.known[eng]
        for k, v in need.items():
            if eng == 'pe' and k == 'pe':
                continue
            if kn.get(k, 0) < v:
                kn[k] = v
                out.append((k, v))
        return out

    def _post(self, key, v, reads, writes):
        for b in reads:
            b = _b(b)
            b.r[key] = max(b.r.get(key, 0), v)
        for b in writes:
            b = _b(b)
            b.w = (key, v)
            b.r = {}

    def op(self, eng, meth, kw, reads=(), writes=()):
        self.semh(eng)
        waits = self._deps(eng, reads, writes)
        self.cnt[eng] += 1
        v = self.cnt[eng]
        self.q[eng].append((waits, meth, kw, eng, 1))
        self._post(eng, v, reads, writes)

    def dma(self, eng, semkey, kw, reads=(), writes=()):
        self.semh(semkey)
        waits = self._deps(eng, reads, writes)
        self.cnt[semkey] += 16
        v = self.cnt[semkey]
        self.q[eng].append((waits, 'dma_start', kw, semkey, 16))
        self._post(semkey, v, reads, writes)

    def barrier(self):
        for e in ENGS:
            kn = self.known[e]
            waits = []
            for k, v in self.cnt.items():
                if v > 0 and kn.get(k, 0) < v:
                    kn[k] = v
                    waits.append((k, v))
            if waits:
                self.q[e].append((waits, None, None, None, 0))

    def flush(self):
        self.barrier()
        nc = self.nc
        q = self.q
        sem = self.sem

        def run(engobj, items):
            for waits, meth, kw, key, inc in items:
                for k, v in waits:
                    engobj.wait_ge(sem[k], v)
                if meth is not None:
                    getattr(engobj, meth)(**kw).then_inc(sem[key], inc)

        with nc.Block() as block:
            @block.tensor
            def _(e):
                run(e, q['pe'])

            @block.scalar
            def _(e):
                run(e, q['act'])

            @block.vector
            def _(e):
                run(e, q['dve'])

            @block.gpsimd
            def _(e):
                run(e, q['pool'])

            @block.sync
            def _(e):
                run(e, q['sp'])
        self.q = {e: [] for e in ENGS}


class Ctx:
    pass


def _mm(S, out, lhsT, rhs, start, stop, reads, writes, skip=False):
    kw = dict(out=out, lhsT=lhsT, rhs=rhs, start=start, stop=stop)
    if skip:
        kw['skip_group_check'] = True
    S.op('pe', 'matmul', kw, reads, writes)


def _tp(S, out, in_, ident, reads, writes):
    S.op('pe', 'transpose', dict(out=out, in_=in_, identity=ident), reads, writes)


def _act(S, out, in_, func, reads, writes, **kw):
    S.op('act', 'activation', dict(out=out, in_=in_, func=func, **kw), reads, writes)


def _tt(S, eng, out, in0, in1, op, reads, writes):
    S.op(eng, 'tensor_tensor', dict(out=out, in0=in0, in1=in1, op=op), reads, writes)


def _ts(S, eng, out, in0, s1, op0, reads, writes, s2=None, op1=None):
    kw = dict(out=out, in0=in0, scalar1=s1, scalar2=s2, op0=op0)
    if op1 is not None:
        kw['op1'] = op1
    S.op(eng, 'tensor_scalar', kw, reads, writes)


def _stt(S, out, in0, scalar, in1, op0, op1, reads, writes, accum_out=None):
    kw = dict(out=out, in0=in0, scalar=scalar, in1=in1, op0=op0, op1=op1)
    if accum_out is not None:
        kw['accum_out'] = accum_out
    S.op('dve', 'scalar_tensor_tensor', kw, reads, writes)


def _cp(S, eng, out, in_, reads, writes):
    if eng == 'act':
        S.op('act', 'copy', dict(out=out, in_=in_), reads, writes)
    else:
        S.op(eng, 'tensor_copy', dict(out=out, in_=in_), reads, writes)


def _memset(S, eng, ap, val, writes):
    S.op(eng, 'memset', dict(ap=ap, constant=val), (), writes)


def load_w_cast(S, dst_tl, dst_ap_fn, src, kc_n, ncols, rows_per=128):
    for kc in range(kc_n):
        c0 = 0
        while c0 < ncols:
            c1 = min(ncols, c0 + 2048)
            S.dma('pool', dst_tl.sem, dict(out=dst_ap_fn(kc, c0, c1),
                                           in_=src[kc * 128:(kc + 1) * 128, c0:c1]),
                  writes=[dst_tl])
            c0 = c1


def cols_from_rows(S, C, src_rows, R, ncol, dst_tl, dst_fn):
    stage = C.stage
    assert ncol <= stage.t.shape[1] and R <= 32
    S.dma('sp', stage.sem, dict(out=stage.t[0:R, 0:ncol], in_=src_rows), writes=[stage])
    for ch in range(ncol // 128):
        _tp(S, C.pst.t[:, 0:R], stage.t[0:R, ch * 128:(ch + 1) * 128], C.identF.t[0:R, 0:R],
            [stage, C.identF], [C.pst])
        _cp(S, 'dve', dst_fn(ch), C.pst.t[:, 0:R], [C.pst], [dst_tl])


def phase_A(nc, S, Sq, T):
    NSUP = Sq // 512
    with ExitStack() as es:
        def sb(name, shape, dt, sem=False):
            t = es.enter_context(nc.sbuf_tensor(name, shape, dt))
            return Tl(t, S.newdma() if sem else None)

        def ps(name, shape, dt=F32):
            return Tl(es.enter_context(nc.psum_tensor(name, shape, dt)))

        C = Ctx()
        win = sb('a_win', [128, 8, IN_W], BF16, sem=True)
        dg = sb('a_dg', [128, 4, 31, 128], BF16)
        identF = sb('a_identF', [128, 128], F32, sem=True)
        identB = sb('a_identB', [128, 128], BF16, sem=True)
        onesF = sb('a_onesF', [128, 128], F32)
        C.identF = identF
        C.stage = sb('a_stage', [32, 1024], F32, sem=True)
        g1T = sb('a_g1T', [128, 8, 1], F32)
        dwT = sb('a_dwT', [128, 4, 31], F32)
        dwb = sb('a_dwb', [128, 4, 1], F32)
        lng = sb('a_lng', [128, 4, 1], F32)
        lnb = sb('a_lnb', [128, 4, 1], F32)
        xs = [sb('a_x%d' % i, [128, D], F32, sem=True) for i in range(4)]
        ss = sb('a_ss', [128, 4], F32)
        rt = sb('a_rt', [128, 4], F32)
        rstd = sb('a_rstd', [128, 4], F32)
        junk = sb('a_junk', [128, D], BF16)
        ntok = [sb('a_ntok%d' % i, [128, D], BF16) for i in range(2)]
        nT = sb('a_nT', [128, 8, 512], BF16)
        hglu = sb('a_hglu', [128, 4, 542], BF16)
        stg = [sb('a_stg%d' % i, [128, 512], BF16, sem=True) for i in range(6)]
        sg = [sb('a_sg%d' % i, [128, 512], F32) for i in range(2)]
        ycs = sb('a_ycs', [128, 4, 512], F32)
        ysq = sb('a_ysq', [128, 4, 512], F32)
        mean = sb('a_mean', [128, 512], F32)
        msq = sb('a_msq', [128, 512], F32)
        rs = sb('a_rs', [128, 512], F32)
        dtl = [sb('a_d%d' % i, [128, 512], F32) for i in range(2)]
        vstg = [sb('a_vstg%d' % i, [128, 2, 4, 65], BF16, sem=True) for i in range(2)]
        gstg = [sb('a_gstg%d' % i, [128, 48], F32, sem=True) for i in range(2)]
        epsT = sb('a_eps', [128, 1], F32)

        ptr = ps('a_ptr', [128, 1024], BF16)
        pf = [ps('a_pf%d' % i, [128, 512]) for i in range(3)]
        pv = ps('a_pv', [128, 512])
        pg = ps('a_pg', [128, 48])
        C.pst = pg
        pc = [ps('a_pc%d' % i, [128, 512]) for i in range(2)]

        S.dma('sp', identF.sem, dict(out=identF.t[:, :], in_=T['identF']), writes=[identF])
        S.dma('pool', identB.sem, dict(out=identB.t[:, :], in_=T['identF']), writes=[identB])
        _memset(S, 'dve', onesF.t[:, :], 1.0 / 512.0, [onesF])
        _memset(S, 'dve', epsT.t[:, :], EPS, [epsT])
        _memset(S, 'dve', hglu.t[:, :, :], 0.0, [hglu])
        for v in vstg:
            _memset(S, 'dve', v.t[:, :, :, :], 1.0, [v])
        load_w_cast(S, win, lambda kc, c0, c1: win.t[:, kc, c0:c1], T['w_in'], 8, IN_W)
        precast_weights(S, T)
        cols_from_rows(S, C, T['norm1_g'], 1, 1024, g1T, lambda ch: g1T.t[:, ch, :])
        cols_from_rows(S, C, T['conv_dw_w'], 31, 512, dwT, lambda ch: dwT.t[:, ch, :])
        cols_from_rows(S, C, T['conv_dw_b'], 1, 512, dwb, lambda ch: dwb.t[:, ch, :])
        cols_from_rows(S, C, T['conv_ln_g'], 1, 512, lng, lambda ch: lng.t[:, ch, :])
        cols_from_rows(S, C, T['conv_ln_b'], 1, 512, lnb, lambda ch: lnb.t[:, ch, :])
        for kc in range(8):
            _ts(S, 'dve', win.t[:, kc, :], win.t[:, kc, :], g1T.t[:, kc, :], ALU.mult, [win, g1T], [win])
        for ch in range(4):
            for j in range(31):
                _ts(S, 'dve', dg.t[:, ch, j, :], identF.t[:, :], dwT.t[:, ch, j:j + 1], ALU.mult,
                    [identF, dwT], [dg])

        x = T['x']
        fchunks = []
        for i in range(4):
            fchunks.append((512 + 128 * i, 'gate', i))
            fchunks.append((128 * i, 'a', i))
        for i in range(8):
            fchunks.append((1024 + 128 * i, 'q', i))
        for nm, c0 in (('kc', 2048), ('vc', 2304), ('ks', 2560), ('kw', 3072)):
            for i in range(2):
                fchunks.append((c0 + 128 * i, nm, i))
        for i in range(16):
            fchunks.append((3632 + 128 * i, 'gm', i))

        import os
        STG = int(os.environ.get('STG', '9'))
        nstg = 0
        npf = 0
        for st in range(NSUP if STG >= 1 else 0):
            for t in range(4):
                tt = st * 4 + t
                xb = xs[tt % 4]
                S.dma('sp', xb.sem, dict(out=xb.t[:, :], in_=x[tt * 128:(tt + 1) * 128, :]), writes=[xb])
                _stt(S, junk.t[:, :], xb.t[:, :], 1.0, xb.t[:, :], ALU.mult, ALU.mult, [xb], [junk, ss],
                     accum_out=ss.t[:, t:t + 1])
            _act(S, rt.t[:, :], ss.t[:, :], AF.Sqrt, [ss, epsT], [rt], scale=1.0 / D, bias=epsT.t[:, :])
            S.op('dve', 'reciprocal', dict(out=rstd.t[:, :], in_=rt.t[:, :]), [rt], [rstd])
            for t in range(4):
                tt = st * 4 + t
                xb = xs[tt % 4]
                nk = ntok[t % 2]
                _ts(S, 'dve', nk.t[:, :], xb.t[:, :], rstd.t[:, t:t + 1], ALU.mult, [xb, rstd], [nk])
                for kc in range(8):
                    _tp(S, ptr.t[:, kc * 128:(kc + 1) * 128], nk.t[:, kc * 128:(kc + 1) * 128], identB.t[:, :],
                        [nk, identB], [ptr])
                _cp(S, 'act', nT.t[:, :, t * 128:(t + 1) * 128],
                    ptr.t[:, :].rearrange('p (k q) -> p k q', k=8), [ptr], [nT])
            for t in range(4 if STG >= 2 else 0):
                tt = st * 4 + t
                for half, c0 in ((0, 2816), (1, 3328)):
                    for kc in range(8):
                        _mm(S, pv.t[:, half * 256:(half + 1) * 256], nT.t[:, kc, t * 128:(t + 1) * 128],
                            win.t[:, kc, c0:c0 + 256], kc == 0, kc == 7, [nT, win], [pv])
                for kc in range(8):
                    _mm(S, pg.t[:, :], nT.t[:, kc, t * 128:(t + 1) * 128], win.t[:, kc, 3584:3632],
                        kc == 0, kc == 7, [nT, win], [pg])
                vs = vstg[tt % 2]
                _cp(S, 'dve', vs.t[:, :, :, 0:64], pv.t[:, :].rearrange('p (a g d) -> p a g d', a=2, g=4),
                    [pv], [vs])
                S.dma('sp', vs.sem, dict(out=T['VS1'][tt * 128:(tt + 1) * 128, :, :], in_=vs.t[:, 0, :, :]),
                      reads=[vs])
                S.dma('sp', vs.sem, dict(out=T['VW1'][tt * 128:(tt + 1) * 128, :, :], in_=vs.t[:, 1, :, :]),
                      reads=[vs])
                gs = gstg[tt % 2]
                _act(S, gs.t[:, :], pg.t[:, :], AF.Sigmoid, [pg], [gs])
                S.dma('sp', gs.sem, dict(out=T['G'][tt * 128:(tt + 1) * 128, :], in_=gs.t[:, :]), reads=[gs])
            cs = slice(st * 512, (st + 1) * 512)
            for (c0, kind, idx) in (fchunks if STG >= 3 else []):
                p = pf[npf % 3]
                npf += 1
                for kc in range(8):
                    _mm(S, p.t[:, :], win.t[:, kc, c0:c0 + 128], nT.t[:, kc, :], kc == 0, kc == 7, [win, nT], [p])
                if kind == 'gate':
                    sgt = sg[idx % 2]
                    _act(S, sgt.t[:, :], p.t[:, :], AF.Sigmoid, [p], [sgt])
                elif kind == 'a':
                    sgt = sg[idx % 2]
                    _tt(S, 'dve', hglu.t[:, idx, 30:542], p.t[:, :], sgt.t[:, :], ALU.mult, [p, sgt], [hglu])
                else:
                    sl = stg[nstg % 6]
                    nstg += 1
                    if kind == 'q':
                        _act(S, sl.t[:, :], p.t[:, :], AF.Copy, [p], [sl], scale=0.125)
                        dst = T['QT'][2 * idx:2 * idx + 2, :, cs].rearrange('h d s -> (h d) s')
                    elif kind == 'gm':
                        _act(S, sl.t[:, :], p.t[:, :], AF.Sigmoid, [p], [sl])
                        dst = T['GM'][idx, :, cs]
                    else:
                        _cp(S, 'dve', sl.t[:, :], p.t[:, :], [p], [sl])
                        dst = T[{'kc': 'KC', 'vc': 'VC', 'ks': 'KS', 'kw': 'KW'}[kind]][2 * idx:2 * idx + 2, :, cs] \
                            .rearrange('g d s -> (g d) s')
                    S.dma('sp', sl.sem, dict(out=dst, in_=sl.t[:, :]), reads=[sl])
            if STG < 4:
                continue
            for ch in range(4):
                p = pc[ch % 2]
                for j in range(0, 31, int(os.environ.get('JSTEP', '1'))):
                    _mm(S, p.t[:, :], dg.t[:, ch, j, :], hglu.t[:, ch, j:j + 512], j == 0, j == 30, [dg, hglu], [p])
                CP = int(os.environ.get('CP', '15'))
                if CP & 2:
                    _ts(S, 'dve', ycs.t[:, ch, :], p.t[:, :], dwb.t[:, ch, :], ALU.add, [p, dwb], [ycs])
                if CP & 4:
                    _tt(S, 'pool', ysq.t[:, ch, :], ycs.t[:, ch, :], ycs.t[:, ch, :], ALU.mult, [ycs], [ysq])
            if CP & 8:
                _cp(S, 'dve', hglu.t[:, :, 0:30], hglu.t[:, :, 512:542], [hglu], [hglu])
            if STG < 5:
                continue
            pm = pf[npf % 3]
            npf += 1
            pq = pf[npf % 3]
            npf += 1
            for ch in range(4):
                _mm(S, pm.t[:, :], onesF.t[:, :], ycs.t[:, ch, :], ch == 0, ch == 3, [onesF, ycs], [pm])
            for ch in range(4):
                _mm(S, pq.t[:, :], onesF.t[:, :], ysq.t[:, ch, :], ch == 0, ch == 3, [onesF, ysq], [pq])
            _cp(S, 'act', mean.t[:, :], pm.t[:, :], [pm], [mean])
            _tt(S, 'dve', msq.t[:, :], mean.t[:, :], mean.t[:, :], ALU.mult, [mean], [msq])
            _tt(S, 'dve', msq.t[:, :], pq.t[:, :], msq.t[:, :], ALU.subtract, [pq, msq], [msq])
            _act(S, msq.t[:, :], msq.t[:, :], AF.Sqrt, [msq, epsT], [msq], bias=epsT.t[:, :])
            S.op('dve', 'reciprocal', dict(out=rs.t[:, :], in_=msq.t[:, :]), [msq], [rs])
            for ch in range(4):
                d_ = dtl[ch % 2]
                _tt(S, 'dve', d_.t[:, :], ycs.t[:, ch, :], mean.t[:, :], ALU.subtract, [ycs, mean], [d_])
                _tt(S, 'dve', d_.t[:, :], d_.t[:, :], rs.t[:, :], ALU.mult, [d_, rs], [d_])
                _ts(S, 'dve', d_.t[:, :], d_.t[:, :], lng.t[:, ch, :], ALU.mult, [d_, lng, lnb], [d_],
                    s2=lnb.t[:, ch, :], op1=ALU.add)
                sl = stg[nstg % 6]
                nstg += 1
                _act(S, sl.t[:, :], d_.t[:, :], AF.Silu, [d_], [sl])
                S.dma('sp', sl.sem, dict(out=T['HC'][ch, :, cs], in_=sl.t[:, :]), reads=[sl])
        S.flush()


def bcast_row(S, C, src_row, dst_tl, pbanks, onesF):
    S.dma('sp', C.stage.sem, dict(out=C.stage.t[0:1, 0:1024], in_=src_row), writes=[C.stage])
    for half in range(2):
        p = pbanks[half]
        _mm(S, p.t[:, :], onesF.t[0:1, :], C.stage.t[0:1, half * 512:(half + 1) * 512], True, True,
            [onesF, C.stage], [p])
        _cp(S, 'dve', dst_tl.t[:, half * 512:(half + 1) * 512], p.t[:, :], [p], [dst_tl])


def rms_T(S, R, xtiles, gB):
    n = len(xtiles)
    for i, xb in enumerate(xtiles):
        _stt(S, R.junk.t[:, :], xb.t[:, :], 1.0, xb.t[:, :], ALU.mult, ALU.mult, [xb], [R.junk, R.ss],
             accum_out=R.ss.t[:, i:i + 1])
    _act(S, R.rt.t[:, 0:n], R.ss.t[:, 0:n], AF.Sqrt, [R.ss, R.epsT], [R.rt], scale=1.0 / D, bias=R.epsT.t[:, :])
    S.op('dve', 'reciprocal', dict(out=R.rstd.t[:, 0:n], in_=R.rt.t[:, 0:n]), [R.rt], [R.rstd])
    for i, xb in enumerate(xtiles):
        nk = R.ntok[i % 2]
        _stt(S, nk.t[:, :], xb.t[:, :], R.rstd.t[:, i:i + 1], gB.t[:, :], ALU.mult, ALU.mult, [xb, R.rstd, gB], [nk])
        for kc in range(8):
            _tp(S, R.ptr.t[:, kc * 128:(kc + 1) * 128], nk.t[:, kc * 128:(kc + 1) * 128], R.identB.t[:, :],
                [nk, R.identB], [R.ptr])
        yield i, R.ptr


WB_SPEC = {'nsa_w_o': (D, D), 'conv_w_pw': (512, D), 'w_out': (D, D), 'xa_wq': (D, D), 'xa_wo': (D, D),
           'ffn_w_up': (D, 2 * D_FF), 'ffn_w_down': (D_FF, D)}


def precast_weights(S, T):
    key = S.newdma()
    for name, (K_, N_) in WB_SPEC.items():
        for r0 in range(0, K_, 128):
            for c0 in range(0, N_, 2048):
                c1 = min(N_, c0 + 2048)
                S.dma('pool', key, dict(out=T['WB_' + name][r0:r0 + 128, c0:c1], in_=T[name][r0:r0 + 128, c0:c1]))


def _mk(nc, es, S):
    def sb(name, shape, dt, sem=False):
        t = es.enter_context(nc.sbuf_tensor(name, shape, dt))
        return Tl(t, S.newdma() if sem else None)

    def ps(name, shape, dt=F32):
        return Tl(es.enter_context(nc.psum_tensor(name, shape, dt)))
    return sb, ps


def cmp_chunks(Sq):
    NC = Sq // 16 - 1
    out = []
    n0 = 0
    while n0 < NC:
        out.append((n0, min(128, NC - n0)))
        n0 += 128
    return NC, out


def phase_B(nc, S, Sq, T, kcT, vco, P):
    NC, chunks = cmp_chunks(Sq)
    with ExitStack() as es:
        sb, ps = _mk(nc, es, S)
        C = Ctx()
        identF = sb('b_identF', [128, 128], F32, sem=True)
        C.identF = identF
        C.stage = sb('b_stage', [32, 1024], F32, sem=True)
        pst = ps('b_pst', [128, 48])
        C.pst = pst
        kin1 = sb('b_kin', [64, 4, Sq], BF16, sem=True)
        kin = [kin1, kin1]
        w1s = sb('b_w1s', [64, 2, 32, 256], BF16, sem=True)
        w2s = sb('b_w2s', [128, 2, 2, 64], BF16, sem=True)
        peT = sb('b_peT', [64, 2, 32], BF16)
        b1T = sb('b_b1T', [128, 4, 1], F32)
        biasc = sb('b_biasc', [128, 4, 1], F32)
        hT = [sb('b_hT%d' % i, [128, 2, 512], BF16) for i in range(2)]
        xb = sb('b_xb', [128, 512], F32)
        x2 = sb('b_x2', [128, 512], F32)
        u = sb('b_u', [128, 512], F32)
        sgm = sb('b_sgm', [128, 512], F32)
        ovs = sb('b_ovs', [128, 2, 64], F32, sem=True)
        pA = [ps('b_pA%d' % i, [128, 512]) for i in range(2)]
        pB = ps('b_pB', [128, 512])

        S.dma('sp', identF.sem, dict(out=identF.t[:, :], in_=T['identF']), writes=[identF])
        for kv in range(2):
            for l0 in range(0, 32, 8):
                S.dma('pool', w1s.sem, dict(out=w1s.t[:, kv, l0:l0 + 8, :],
                                            in_=T['cmp_w1'][kv, l0 * 64:(l0 + 8) * 64, :].rearrange('(l d) c -> d l c', d=64)),
                      writes=[w1s])
            S.dma('pool', w2s.sem, dict(out=w2s.t[:, kv, :, :],
                                        in_=T['cmp_w2'][kv].rearrange('(h p) d -> p h d', p=128)), writes=[w2s])
            S.dma('sp', C.stage.sem, dict(out=C.stage.t[0:32, 0:64], in_=T['cmp_pe'][kv]), writes=[C.stage])
            _tp(S, pst.t[0:64, 0:32], C.stage.t[0:32, 0:64], identF.t[0:32, 0:32], [C.stage, identF], [pst])
            _cp(S, 'dve', peT.t[:, kv, :], pst.t[0:64, 0:32], [pst], [peT])
        cols_from_rows(S, C, T['cmp_b1'], 1, 512, b1T, lambda ch: b1T.t[:, ch, :])
        _memset(S, 'dve', vco.t[:, :, :, :], 0.0, [vco])
        _memset(S, 'dve', kcT.t[:, :, :], 0.0, [kcT])
        S.dma('sp', ovs.sem, dict(out=ovs.t[:, 0:len(chunks), :], in_=T['ovl'].rearrange('(c p) j -> p c j', p=128)),
              writes=[ovs])
        for ci in range(len(chunks)):
            for g in range(4):
                _cp(S, 'dve', vco.t[:, ci, g, 64:128], ovs.t[:, ci, :], [ovs], [vco])
        for kv in range(2):
            for half in range(2):
                for l in range(32):
                    _mm(S, pB.t[:, 0:1], w1s.t[0:64, kv, l, half * 128:(half + 1) * 128], peT.t[0:64, kv, l:l + 1],
                        l == 0, l == 31, [w1s, peT], [pB])
                _tt(S, 'dve', biasc.t[:, kv * 2 + half, :], pB.t[:, 0:1], b1T.t[:, kv * 2 + half, :], ALU.add,
                    [pB, b1T], [biasc])

        tmpA = sb('b_tmpA', [128, 2048], F32, sem=True)
        tmpB = sb('b_tmpB', [128, 2048], F32, sem=True)
        biasT, Bband = P.biasT, P.Bband
        for dl in range(2):
            S.dma('sp', tmpA.sem, dict(out=tmpA.t[:, :], in_=T['tz1'][dl].rearrange('k h q -> k (h q)')), writes=[tmpA])
            S.dma('sp', tmpB.sem, dict(out=tmpB.t[:, :], in_=T['tz31'][dl].rearrange('k h q -> k (h q)')), writes=[tmpB])
            _tt(S, 'pool', tmpA.t[:, :], tmpA.t[:, :], tmpB.t[:, :], ALU.subtract, [tmpA, tmpB], [tmpA])
            S.dma('sp', tmpB.sem, dict(out=tmpB.t[:, :], in_=T['tzm'][dl].rearrange('k h q -> k (h q)')), writes=[tmpB])
            _tt(S, 'pool', biasT.t[:, dl, :, :].rearrange('k h q -> k (h q)'), tmpA.t[:, :], tmpB.t[:, :], ALU.add,
                [tmpA, tmpB], [biasT])
        S.dma('sp', tmpA.sem, dict(out=tmpA.t[0:32, :], in_=T['cb1'].rearrange('k h q -> k (h q)')), writes=[tmpA])
        S.dma('sp', tmpB.sem, dict(out=tmpB.t[0:32, :], in_=T['cb31'].rearrange('k h q -> k (h q)')), writes=[tmpB])
        _tt(S, 'pool', tmpA.t[0:32, :], tmpA.t[0:32, :], tmpB.t[0:32, :], ALU.subtract, [tmpA, tmpB], [tmpA])
        S.dma('sp', tmpB.sem, dict(out=tmpB.t[0:32, :], in_=T['cbm'].rearrange('k h q -> k (h q)')), writes=[tmpB])
        _tt(S, 'pool', Bband.t[:, :, :].rearrange('k h q -> k (h q)'), tmpA.t[0:32, :], tmpB.t[0:32, :], ALU.add,
            [tmpA, tmpB], [Bband])

        R = Ctx()
        R.identB = sb('b_identB', [128, 128], BF16, sem=True)
        R.epsT = sb('b_eps', [128, 1], F32)
        R.ss = sb('b_ss', [128, 4], F32)
        R.rt = sb('b_rt', [128, 4], F32)
        R.rstd = sb('b_rstd', [128, 4], F32)
        R.junk = sb('b_junk', [128, D], BF16)
        R.ntok = [sb('b_ntok%d' % i, [128, D], BF16) for i in range(2)]
        R.ptr = ps('b_ptr', [128, 1024], BF16)
        onesF = sb('b_onesF', [1, 128], F32)
        gmB = sb('b_gmB', [128, D], F32)
        wkv = sb('b_wkv', [128, 8, 2 * D], BF16, sem=True)
        memx = [sb('b_memx%d' % i, [128, D], F32, sem=True) for i in range(2)]
        memnT = sb('b_memnT', [128, 8, 256], BF16)
        kmT, vm = P.kmT, P.vm
        S.dma('pool', R.identB.sem, dict(out=R.identB.t[:, :], in_=T['identF']), writes=[R.identB])
        _memset(S, 'dve', R.epsT.t[:, :], EPS, [R.epsT])
        _memset(S, 'dve', onesF.t[:, :], 1.0, [onesF])
        load_w_cast(S, wkv, lambda kc, c0, c1: wkv.t[:, kc, c0:c1], T['xa_wkv'], 8, 2 * D)
        bcast_row(S, C, T['mem_norm_g'], gmB, pA, onesF)
        for mc in range(2):
            S.dma('sp', memx[mc].sem, dict(out=memx[mc].t[:, :], in_=T['mem'][mc * 128:(mc + 1) * 128, :]),
                  writes=[memx[mc]])
        for i, p_ in rms_T(S, R, memx, gmB):
            _cp(S, 'act', memnT.t[:, :, i * 128:(i + 1) * 128], p_.t[:, :].rearrange('p (k q) -> p k q', k=8),
                [p_], [memnT])
        for c in range(8):
            p = pA[c % 2]
            for kc in range(8):
                _mm(S, p.t[:, 0:256], wkv.t[:, kc, c * 128:(c + 1) * 128], memnT.t[:, kc, :], kc == 0, kc == 7,
                    [wkv, memnT], [p])
            _cp(S, 'dve', kmT.t[:, c, :], p.t[:, 0:256], [p], [kmT])
        for mc in range(2):
            for half in range(2):
                p = pA[(mc * 2 + half) % 2]
                for kc in range(8):
                    _mm(S, p.t[:, :], memnT.t[:, kc, mc * 128:(mc + 1) * 128],
                        wkv.t[:, kc, D + half * 512:D + (half + 1) * 512], kc == 0, kc == 7, [wkv, memnT], [p])
                _cp(S, 'dve', vm.t[:, mc, half * 512:(half + 1) * 512], p.t[:, :], [p], [vm])
        npa = 0
        for kv in range(2):
            src = kin[kv]
            S.dma('sp', src.sem, dict(out=src.t[:, :, :], in_=T['KC' if kv == 0 else 'VC'].rearrange('g d s -> d g s')), writes=[src])
            for g in range(4):
                h_ = hT[(kv * 4 + g) % 2]
                for half in range(2):
                    p = pA[npa % 2]
                    npa += 1
                    for l in range(32):
                        _mm(S, p.t[:, 0:NC], w1s.t[0:64, kv, l, half * 128:(half + 1) * 128],
                            src.t[0:64, g, l:l + 16 * (NC - 1) + 1:16], l == 0, l == 31, [w1s, src], [p])
                    _ts(S, 'dve', xb.t[:, 0:NC], p.t[:, 0:NC], biasc.t[:, kv * 2 + half, :], ALU.add, [p, biasc], [xb])
                    _tt(S, 'pool', x2.t[:, 0:NC], xb.t[:, 0:NC], xb.t[:, 0:NC], ALU.mult, [xb], [x2])
                    _ts(S, 'dve', x2.t[:, 0:NC], x2.t[:, 0:NC], 0.044715, ALU.mult, [x2], [x2], s2=1.0, op1=ALU.add)
                    _tt(S, 'dve', u.t[:, 0:NC], x2.t[:, 0:NC], xb.t[:, 0:NC], ALU.mult, [x2, xb], [u])
                    _act(S, sgm.t[:, 0:NC], u.t[:, 0:NC], AF.Sigmoid, [u], [sgm], scale=1.5957691216057308)
                    _tt(S, 'dve', h_.t[:, half, 0:NC], xb.t[:, 0:NC], sgm.t[:, 0:NC], ALU.mult, [xb, sgm], [h_])
                if kv == 0:
                    for half in range(2):
                        _mm(S, pB.t[0:64, 0:NC], w2s.t[:, 0, half, :], h_.t[:, half, 0:NC], half == 0, half == 1,
                            [w2s, h_], [pB])
                    _cp(S, 'act', kcT.t[0:64, g, 0:NC], pB.t[0:64, 0:NC], [pB], [kcT])
                else:
                    for ci, (n0, sz) in enumerate(chunks):
                        for half in range(2):
                            _mm(S, pB.t[0:sz, 0:64], h_.t[:, half, n0:n0 + sz], w2s.t[:, 1, half, :], half == 0,
                                half == 1, [w2s, h_], [pB])
                        _cp(S, 'act', vco.t[0:sz, ci, g, 0:64], pB.t[0:sz, 0:64], [pB], [vco])
        S.flush()


def phase_C(nc, S, Sq, T, kcT, vco, P):
    NT = Sq // 128
    NC, chunks = cmp_chunks(Sq)
    OFFS = 8 * (NT - 1)
    with ExitStack() as es:
        sb, ps = _mk(nc, es, S)
        identB = sb('c_identB', [128, 128], BF16, sem=True)
        KE = sb('c_KE', [128, 4, Sq], BF16, sem=True)
        KWt = sb('c_KW', [128, 4, Sq], BF16, sem=True)
        KWz = Buf()
        KEe = Buf()
        KEe_sem = S.newdma()
        VS = sb('c_VS', [128, NT, 4, 65], BF16, sem=True)
        VW = sb('c_VW', [128, NT, 4, 65], BF16, sem=True)
        biasT, Bband = P.biasT, P.Bband
        m512 = sb('c_m512', [128, 512], BF16, sem=True)
        SelW = sb('c_SelW', [32, OFFS + 128 * len(chunks)], BF16, sem=True)
        mulB = sb('c_mulB', [128, 128], F32, sem=True)
        addB = sb('c_addB', [128, 128], F32, sem=True)
        QM = [sb('c_QM%d' % i, [128, 4, 4, 128], BF16, sem=True) for i in range(2)]
        QMq = [Buf() for _ in range(2)]
        QMm = [[Buf() for _ in range(4)] for _ in range(2)]
        gt = [sb('c_gt%d' % i, [128, 16, 3], F32, sem=True) for i in range(3)]
        NE = 4
        Et = [sb('c_E%d' % i, [128, 512], BF16) for i in range(NE)]
        o1s = [[sb('c_o1s%d_%d' % (j, i), [128, 4, 128], F32) for i in range(4)] for j in range(2)]
        o2s = sb('c_o2s', [128, 4, 65], F32)
        o3s = [[sb('c_o3s%d_%d' % (j, i), [128, 4, 65], F32) for i in range(4)] for j in range(2)]
        coef1 = [[sb('c_coef1_%d_%d' % (j, i), [128, 4], F32) for i in range(4)] for j in range(2)]
        den = sb('c_den', [128, 4], F32)
        rden = sb('c_rden', [128, 4], F32)
        c2 = sb('c_c2', [128, 4], F32)
        c3 = sb('c_c3', [128, 4], F32)
        imp = sb('c_imp', [128, 64], F32)
        score = sb('c_score', [128, 64], F32)
        score2 = sb('c_score2', [128, 64], F32)
        m8a = sb('c_m8a', [128, 8], F32)
        m8b = sb('c_m8b', [128, 8], F32)
        nms = [sb('c_nm%d' % i, [128, 128], BF16) for i in range(4)]
        acc = sb('c_acc', [128, 4, 64], F32)
        ntk = sb('c_ntk', [128, 1024], BF16)
        nst = [sb('c_nst%d' % i, [128, 8, 128], BF16, sem=True) for i in range(2)]

        scp = [ps('c_sc%d' % i, [128, 512]) for i in range(3)]
        o1U = ps('c_o1U', [128, 512])
        o3p = ps('c_o3', [128, 4, 65])
        o2p = [ps('c_o2_%d' % i, [128, 4, 65]) for i in range(2)]
        ptr = ps('c_ptr', [128, 1024], BF16)

        S.dma('pool', identB.sem, dict(out=identB.t[:, :], in_=T['identF']), writes=[identB])
        S.dma('sp', KE.sem, dict(out=KE.t[0:64, :, :], in_=T['KS'].rearrange('g d s -> d g s')), writes=[KE])
        for g in range(4):
            for c0 in range(0, Sq, 2048):
                c1 = min(Sq, c0 + 2048)
                S.dma('pool', KEe_sem, dict(out=KE.t[64:128, g, c0:c1], in_=T['Econst'][:, c0:c1]), writes=[KEe])
        S.dma('sp', KWt.sem, dict(out=KWt.t[0:64, :, :], in_=T['KW'].rearrange('g d s -> d g s')), writes=[KWt])
        _memset(S, 'pool', KWt.t[64:128, :, :], 0.0, [KWz])
        for i_, q_ in enumerate(QM):
            _memset(S, 'pool', q_.t[:, :, :, :], 0.0, [q_, QMq[i_]] + QMm[i_])
        for k0 in range(0, NT, 8):
            k1 = min(NT, k0 + 8)
            S.dma('sp', VS.sem, dict(out=VS.t[:, k0:k1, :, :],
                                     in_=T['VS1'][k0 * 128:k1 * 128].rearrange('(k p) g d -> p k g d', p=128)),
                  writes=[VS])
            S.dma('sp', VW.sem, dict(out=VW.t[:, k0:k1, :, :],
                                     in_=T['VW1'][k0 * 128:k1 * 128].rearrange('(k p) g d -> p k g d', p=128)),
                  writes=[VW])
        S.dma('pool', m512.sem, dict(out=m512.t[:, :], in_=T['m512']), writes=[m512])
        S.dma('pool', SelW.sem, dict(out=SelW.t[:, :], in_=T['SelW']), writes=[SelW])
        S.dma('sp', mulB.sem, dict(out=mulB.t[:, :], in_=T['mulB']), writes=[mulB])
        S.dma('sp', addB.sem, dict(out=addB.t[:, :], in_=T['addB']), writes=[addB])
        for nm_ in nms:
            _memset(S, 'dve', nm_.t[:, :], 0.0, [nm_])

        def cw_steps(qt):
            out = []
            for g in range(4):
                cl = [(ci, n0, sz) for ci, (n0, sz) in enumerate(chunks) if n0 <= 8 * qt + 6]
                for i, (ci, n0, sz) in enumerate(cl):
                    out.append(dict(kind='cmp', qt=qt, g=g, ci=ci, n0=n0, sz=sz, first=i == 0, last=i == len(cl) - 1))
                kl = list(range(max(0, qt - 4), qt + 1))
                for i, kt in enumerate(kl):
                    out.append(dict(kind='win', qt=qt, g=g, kt=kt, sz=128, first=i == 0, last=i == len(kl) - 1))
            out[0]['loadq'] = qt
            out[-1]['flush_def'] = True
            return out

        def sel_steps(qt):
            out = []
            for g in range(4):
                for kt in range(qt + 1):
                    out.append(dict(kind='sel', qt=qt, g=g, kt=kt, sz=128, first=kt == 0, last=kt == qt))
            return out

        if os.environ.get('PIPE', '1') == '1':
            steps = cw_steps(0)
            for qt in range(NT):
                if qt + 1 < NT:
                    steps += cw_steps(qt + 1)
                steps += sel_steps(qt)
        else:
            steps = []
            for qt in range(NT):
                steps += cw_steps(qt) + sel_steps(qt)
        cnt = dict(sc=0, e=0, o2=0)

        def load_q(qt):
            sl = qt % 2
            qs = slice(qt * 128, (qt + 1) * 128)
            S.dma('sp', QM[sl].sem, dict(out=QM[sl].t[0:64, :, :, :].rearrange('d g h q -> d (g h) q'),
                                         in_=T['QT'][:, :, qs].rearrange('h d q -> d h q')), writes=[QMq[sl]])
            S.dma('sp', gt[qt % 3].sem, dict(out=gt[qt % 3].t[:, :, :].rearrange('p h b -> p (h b)'), in_=T['G'][qs, :]),
                  writes=[gt[qt % 3]])

        def emit_scores(st):
            qt, g, sz = st['qt'], st['g'], st['sz']
            sl = qt % 2
            sc = scp[cnt['sc'] % 3]
            cnt['sc'] += 1
            st['sc'] = sc
            qrow = QM[sl].t[:, g, :, :].rearrange('d h q -> d (h q)')
            if st['kind'] == 'cmp':
                n0 = st['n0']
                a = n0 - 8 * qt + OFFS
                _mm(S, sc.t[0:sz, :], kcT.t[:, g, n0:n0 + sz], qrow, True, False, [kcT, QMq[sl], QM[sl], QMm[sl][g]], [sc])
                _mm(S, sc.t[0:sz, :], SelW.t[0:32, a:a + sz],
                    Bband.t[0:32, 4 * g:4 * g + 4, :].rearrange('k h q -> k (h q)'), False, True, [SelW, Bband], [sc])
                return
            kt = st['kt']
            dl = (qt - kt)
            extra = None
            if dl in (0, 1):
                extra = (biasT.t[:, dl, 4 * g:4 * g + 4, :].rearrange('k h q -> k (h q)'), biasT)
            elif dl == 4 and st['kind'] == 'win':
                extra = (m512.t[:, :], m512)
            ks = slice(kt * 128, (kt + 1) * 128)
            if st['kind'] == 'win':
                _mm(S, sc.t[:, :], KWt.t[:, g, ks], qrow, True, extra is None, [KWt, KWz, QMq[sl], QM[sl], QMm[sl][g]], [sc])
            else:
                _mm(S, sc.t[:, :], KE.t[:, g, ks], QM[sl].t[:, g, :, :].rearrange('d h q -> d (h q)'), True,
                    extra is None, [KE, KEe, QMq[sl], QMm[sl][g]], [sc])
            if extra is not None:
                _mm(S, sc.t[:, :], identB.t[:, :], extra[0], False, True, [identB, extra[1]], [sc])

        def emit_exp(st):
            sz = st['sz']
            E = Et[cnt['e'] % NE]
            cnt['e'] += 1
            st['E'] = E
            _act(S, E.t[0:sz, :], st['sc'].t[0:sz, :], AF.Exp, [st['sc']], [E])

        def emit_pv(st):
            qt, g, sz, E = st['qt'], st['g'], st['sz'], st['E']
            if st['kind'] == 'cmp':
                for h in range(4):
                    _mm(S, o1U.t[:, h * 128:(h + 1) * 128], E.t[0:sz, h * 128:(h + 1) * 128], vco.t[0:sz, st['ci'], g, :],
                        st['first'] and h == 0, st['last'] and h == 3, [E, vco], [o1U], skip=True)
                if st['last']:
                    fin_cmp(qt, g)
                return
            kt = st['kt']
            if st['kind'] == 'win':
                op_, V = o3p, VW
            else:
                if st['first']:
                    st['o2'] = o2p[cnt['o2'] % 2]
                    cnt['o2'] += 1
                    cur['o2'] = st['o2']
                op_, V = cur['o2'], VS
            for h in range(4):
                _mm(S, op_.t[:, h, :], E.t[:, h * 128:(h + 1) * 128], V.t[:, kt, g, :], st['first'] and h == 0,
                    st['last'] and h == 3, [E, V], [op_], skip=True)
            if st['last']:
                if st['kind'] == 'win':
                    _cp(S, EV, o3s[qt % 2][g].t[:, :, :], o3p.t[:, :, :], [o3p], [o3s[qt % 2][g]])
                else:
                    fin_sel(qt, g, op_)

        cur = {}

        def fin_cmp(qt, g):
            sl = qt % 2
            nm = nms[g]
            o1 = o1s[sl][g]
            _cp(S, EV, o1.t[:, :, :], o1U.t[:, :].rearrange('p (h c) -> p h c', h=4), [o1U], [o1])
            S.op('dve', 'tensor_reduce', dict(out=den.t[:, :], in_=o1.t[:, :, 64:128], axis=AX.X, op=ALU.add), [o1], [den])
            _ts(S, 'dve', den.t[:, :], den.t[:, :], 1e-30, ALU.max, [den], [den])
            S.op('dve', 'reciprocal', dict(out=rden.t[:, :], in_=den.t[:, :]), [den], [rden])
            _ts(S, 'dve', imp.t[:, :], o1.t[:, 0, 64:128], rden.t[:, 0:1], ALU.mult, [o1, rden], [imp])
            for h in range(1, 4):
                _stt(S, imp.t[:, :], o1.t[:, h, 64:128], rden.t[:, h:h + 1], imp.t[:, :], ALU.mult, ALU.add,
                     [o1, rden, imp], [imp])
            a = 62 - 2 * qt
            _tt(S, 'dve', score.t[:, :], imp.t[:, :], mulB.t[:, a:a + 64], ALU.mult, [imp, mulB], [score])
            _tt(S, 'dve', score.t[:, :], score.t[:, :], addB.t[:, a:a + 64], ALU.add, [score, addB], [score])
            _memset(S, 'dve', score.t[:, 0:1], 50.0, [score])
            S.op('dve', 'max', dict(out=m8a.t[:, :], in_=score.t[:, :]), [score], [m8a])
            S.op('dve', 'match_replace', dict(out=score2.t[:, :], in_to_replace=m8a.t[:, :], in_values=score.t[:, :],
                                              imm_value=-1e9), [score, m8a], [score2])
            S.op('dve', 'max', dict(out=m8b.t[:, :], in_=score2.t[:, :]), [score2], [m8b])
            _ts(S, 'dve', nm.t[:, 64:128], score.t[:, :], m8b.t[:, 7:8], ALU.is_lt, [score, m8b], [nm], s2=MASKV,
                op1=ALU.mult)

            def part2(sl=sl, g=g, nm=nm):
                _tp(S, ptr.t[:, 0:128], nm.t[:, :], identB.t[:, :], [nm, identB], [ptr])
                for h in range(4):
                    _cp(S, 'dve', QM[sl].t[64:128, g, h, :], ptr.t[64:128, 0:128], [ptr], [QMm[sl][g]])
            deferred.append([DEFER, part2])
            _tt(S, 'dve', coef1[sl][g].t[:, :], rden.t[:, :], gt[qt % 3].t[:, 4 * g:4 * g + 4, 0], ALU.mult, [rden, gt[qt % 3]],
                [coef1[sl][g]])

        def fin_sel(qt, g, o2):
            sl = qt % 2
            _cp(S, EV, o2s.t[:, :, :], o2.t[:, :, :], [o2], [o2s])
            for (osrc, cf, br) in ((o2s, c2, 1), (o3s[sl][g], c3, 2)):
                _ts(S, 'dve', den.t[:, :], osrc.t[:, :, 64], 1e-30, ALU.max, [osrc], [den])
                S.op('dve', 'reciprocal', dict(out=rden.t[:, :], in_=den.t[:, :]), [den], [rden])
                _tt(S, 'dve', cf.t[:, :], rden.t[:, :], gt[qt % 3].t[:, 4 * g:4 * g + 4, br], ALU.mult, [rden, gt[qt % 3]], [cf])
            o1 = o1s[sl][g]
            for h in range(4):
                _ts(S, 'dve', acc.t[:, h, :], o1.t[:, h, 0:64], coef1[sl][g].t[:, h:h + 1], ALU.mult, [o1, coef1[sl][g]], [acc])
                _stt(S, acc.t[:, h, :], o3s[sl][g].t[:, h, 0:64], c3.t[:, h:h + 1], acc.t[:, h, :], ALU.mult, ALU.add,
                     [o3s[sl][g], c3, acc], [acc])
                c0 = (4 * g + h) * 64
                _stt(S, ntk.t[:, c0:c0 + 64], o2s.t[:, h, 0:64], c2.t[:, h:h + 1], acc.t[:, h, :], ALU.mult, ALU.add,
                     [o2s, c2, acc], [ntk])
            if g == 3:
                def part2(qt=qt):
                    for kc in range(8):
                        _tp(S, ptr.t[:, kc * 128:(kc + 1) * 128], ntk.t[:, kc * 128:(kc + 1) * 128], identB.t[:, :],
                            [ntk, identB], [ptr])
                    ns = nst[qt % 2]
                    _cp(S, 'dve', ns.t[:, :, :], ptr.t[:, :].rearrange('p (k q) -> p k q', k=8), [ptr], [ns])
                    S.dma('sp', ns.sem, dict(out=T['NSAT'][:, :, qt * 128:(qt + 1) * 128].rearrange('k p q -> p k q'),
                                             in_=ns.t[:, :, :]), reads=[ns])
                deferred.append([DEFER, part2])

        LOOK = int(os.environ.get('LOOK', '2'))
        EV = os.environ.get('EV', 'act')
        DEFER = int(os.environ.get('DEFER', '8'))
        deferred = []
        pend = []

        def run_deferred(force=False):
            while deferred and (force or deferred[0][0] <= 0):
                deferred.pop(0)[1]()

        for st in steps:
            if 'loadq' in st:
                load_q(st['loadq'])
            emit_scores(st)
            emit_exp(st)
            pend.append(st)
            if len(pend) > LOOK:
                emit_pv(pend.pop(0))
            for d_ in deferred:
                d_[0] -= 1
            run_deferred()
            if st.get('flush_def'):
                while pend:
                    emit_pv(pend.pop(0))
                run_deferred(force=True)
        while pend:
            emit_pv(pend.pop(0))
        run_deferred(force=True)
        S.flush()


def phase_D(nc, S, Sq, T, P):
    NSUP = Sq // 512
    with ExitStack() as es:
        sb, ps = _mk(nc, es, S)
        C = Ctx()
        R = Ctx()
        identB = sb('d_identB', [128, 128], BF16, sem=True)
        R.identB = identB
        onesB = sb('d_onesB', [128, 128], BF16)
        onesF = sb('d_onesF', [1, 128], F32)
        R.epsT = sb('d_eps', [128, 1], F32)
        C.stage = sb('d_stage', [32, 1024], F32, sem=True)
        wno = sb('d_wno', [128, 8, D], BF16, sem=True)
        wpw = sb('d_wpw', [128, 4, D], BF16, sem=True)
        wout = sb('d_wout', [128, 8, D], BF16, sem=True)
        wq = sb('d_wq', [128, 8, D], BF16, sem=True)
        wo = sb('d_wo', [128, 8, D], BF16, sem=True)
        g2B = sb('d_g2B', [128, D], F32)
        g3B = sb('d_g3B', [128, D], F32)
        kmT, vm = P.kmT, P.vm
        R.ptr = ps('d_ptr', [128, 1024], BF16)
        ptr = R.ptr
        pf = [ps('d_pf%d' % i, [128, 512]) for i in range(4)]
        ph = [ps('d_ph%d' % i, [128, 512]) for i in range(2)]
        R.ss = sb('d_ss', [128, 4], F32)
        R.rt = sb('d_rt', [128, 4], F32)
        R.rstd = sb('d_rstd', [128, 4], F32)
        R.junk = sb('d_junk', [128, D], BF16)
        R.ntok = [sb('d_ntok%d' % i, [128, D], BF16) for i in range(2)]

        S.dma('pool', identB.sem, dict(out=identB.t[:, :], in_=T['identF']), writes=[identB])
        _memset(S, 'dve', onesB.t[:, :], 1.0, [onesB])
        _memset(S, 'dve', onesF.t[:, :], 1.0, [onesF])
        _memset(S, 'dve', R.epsT.t[:, :], EPS, [R.epsT])
        for w_, nm_ in ((wno, 'nsa_w_o'), (wpw, 'conv_w_pw'), (wout, 'w_out'), (wq, 'xa_wq'), (wo, 'xa_wo')):
            S.dma('sp', w_.sem, dict(out=w_.t[:, :, :], in_=T['WB_' + nm_].rearrange('(k p) n -> p k n', p=128)),
                  writes=[w_])
        bcast_row(S, C, T['norm2_g'], g2B, ph, onesF)
        bcast_row(S, C, T['norm3_g'], g3B, ph, onesF)

        nsa_s = sb('d_nsa', [128, 8, 512], BF16, sem=True)
        hc_s = sb('d_hc', [128, 4, 512], BF16, sem=True)
        gm_s = sb('d_gm', [128, 16, 512], BF16, sem=True)
        xs = [sb('d_x%d' % i, [128, D], F32, sem=True) for i in range(4)]
        mrg = sb('d_mrg', [128, 8, 512], BF16)
        n2T = sb('d_n2T', [128, 8, 512], BF16)
        qxT = sb('d_qxT', [128, 8, 512], BF16)
        PT = sb('d_PT', [128, 2, 512], BF16)
        oTn = sb('d_oTn', [128, 8, 512], BF16)
        t1 = sb('d_t1', [128, 512], F32)
        t2 = sb('d_t2', [128, 512], F32)
        rdn = sb('d_rdn', [128, 512], F32)
        n3s = [sb('d_n3s%d' % i, [128, 8, 128], BF16, sem=True) for i in range(2)]
        npf = 0
        for st in range(NSUP):
            cs = slice(st * 512, (st + 1) * 512)
            S.dma('sp', nsa_s.sem, dict(out=nsa_s.t[:, :, :], in_=T['NSAT'][:, :, cs].rearrange('k p s -> p k s')),
                  writes=[nsa_s])
            S.dma('sp', hc_s.sem, dict(out=hc_s.t[:, :, :], in_=T['HC'][:, :, cs].rearrange('k p s -> p k s')),
                  writes=[hc_s])
            S.dma('sp', gm_s.sem, dict(out=gm_s.t[:, :, :], in_=T['GM'][:, :, cs].rearrange('k p s -> p k s')),
                  writes=[gm_s])
            for t in range(4):
                tt = st * 4 + t
                S.dma('sp', xs[t].sem, dict(out=xs[t].t[:, :], in_=T['x'][tt * 128:(tt + 1) * 128, :]), writes=[xs[t]])
            for f in range(8):
                pa = pf[npf % 4]
                pcv = pf[(npf + 1) % 4]
                npf += 2
                for kc in range(8):
                    _mm(S, pa.t[:, :], wno.t[:, kc, f * 128:(f + 1) * 128], nsa_s.t[:, kc, :], kc == 0, kc == 7,
                        [wno, nsa_s], [pa])
                for c in range(4):
                    _mm(S, pcv.t[:, :], wpw.t[:, c, f * 128:(f + 1) * 128], hc_s.t[:, c, :], c == 0, c == 3,
                        [wpw, hc_s], [pcv])
                _tt(S, 'dve', t1.t[:, :], pa.t[:, :], gm_s.t[:, 8 + f, :], ALU.mult, [pa, gm_s], [t1])
                _tt(S, 'dve', t2.t[:, :], pcv.t[:, :], gm_s.t[:, f, :], ALU.mult, [pcv, gm_s], [t2])
                _tt(S, 'pool', mrg.t[:, f, :], t1.t[:, :], t2.t[:, :], ALU.add, [t1, t2], [mrg])
            for t in range(4):
                for half in range(2):
                    p = ph[half]
                    for f in range(8):
                        _mm(S, p.t[:, :], mrg.t[:, f, t * 128:(t + 1) * 128], wout.t[:, f, half * 512:(half + 1) * 512],
                            f == 0, f == 7, [mrg, wout], [p])
                    _tt(S, 'dve', xs[t].t[:, half * 512:(half + 1) * 512], p.t[:, :],
                        xs[t].t[:, half * 512:(half + 1) * 512], ALU.add, [p, xs[t]], [xs[t]])
            for i, p_ in rms_T(S, R, xs, g2B):
                _cp(S, 'act', n2T.t[:, :, i * 128:(i + 1) * 128], p_.t[:, :].rearrange('p (k q) -> p k q', k=8),
                    [p_], [n2T])
            for c in range(8):
                p = pf[npf % 4]
                npf += 1
                for kc in range(8):
                    _mm(S, p.t[:, :], wq.t[:, kc, c * 128:(c + 1) * 128], n2T.t[:, kc, :], kc == 0, kc == 7,
                        [wq, n2T], [p])
                _act(S, qxT.t[:, c, :], p.t[:, :], AF.Copy, [p], [qxT], scale=1.0 / 16.0)
            for hd in range(4):
                for mc in range(2):
                    p = pf[npf % 4]
                    npf += 1
                    for dc in range(2):
                        _mm(S, p.t[:, :], kmT.t[:, hd * 2 + dc, mc * 128:(mc + 1) * 128], qxT.t[:, hd * 2 + dc, :],
                            dc == 0, dc == 1, [kmT, qxT], [p])
                    _act(S, PT.t[:, mc, :], p.t[:, :], AF.Exp, [p], [PT])
                pd = pf[npf % 4]
                npf += 1
                for mc in range(2):
                    _mm(S, pd.t[:, :], onesB.t[:, :], PT.t[:, mc, :], mc == 0, mc == 1, [onesB, PT], [pd])
                S.op('dve', 'reciprocal', dict(out=rdn.t[:, :], in_=pd.t[:, :]), [pd], [rdn])
                for dc in range(2):
                    po = pf[npf % 4]
                    npf += 1
                    for mc in range(2):
                        _mm(S, po.t[:, :], vm.t[:, mc, hd * 256 + dc * 128:hd * 256 + (dc + 1) * 128], PT.t[:, mc, :],
                            mc == 0, mc == 1, [vm, PT], [po])
                    _tt(S, 'dve', oTn.t[:, hd * 2 + dc, :], po.t[:, :], rdn.t[:, :], ALU.mult, [po, rdn], [oTn])
            for t in range(4):
                tt = st * 4 + t
                for half in range(2):
                    p = ph[half]
                    for c in range(8):
                        _mm(S, p.t[:, :], oTn.t[:, c, t * 128:(t + 1) * 128], wo.t[:, c, half * 512:(half + 1) * 512],
                            c == 0, c == 7, [oTn, wo], [p])
                    _tt(S, 'dve', xs[t].t[:, half * 512:(half + 1) * 512], p.t[:, :],
                        xs[t].t[:, half * 512:(half + 1) * 512], ALU.add, [p, xs[t]], [xs[t]])
                S.dma('sp', xs[t].sem, dict(out=T['H2'][tt * 128:(tt + 1) * 128, :], in_=xs[t].t[:, :]), reads=[xs[t]])
            for i, p_ in rms_T(S, R, xs, g3B):
                tt = st * 4 + i
                ns = n3s[i % 2]
                _cp(S, 'act', ns.t[:, :, :], p_.t[:, :].rearrange('p (k q) -> p k q', k=8), [p_], [ns])
                S.dma('sp', ns.sem, dict(out=T['N3T'][:, :, tt * 128:(tt + 1) * 128].rearrange('k p q -> p k q'),
                                         in_=ns.t[:, :, :]), reads=[ns])
        S.flush()


def phase_E(nc, S, Sq, T, P):
    NSUP = Sq // 512
    NP = D_FF // 128
    with ExitStack() as es:
        sb, ps = _mk(nc, es, S)
        C = Ctx()
        identF = sb('e_identF', [128, 128], F32, sem=True)
        C.identF = identF
        C.stage = sb('e_stage', [32, 1024], F32, sem=True)
        pst = ps('e_pst', [128, 48])
        C.pst = pst
        wupb = [sb('e_wup%d' % i, [128, 8, 512], BF16, sem=True) for i in range(11)]
        wdn = sb('e_wdn', [128, NP, D], BF16, sem=True)
        fw = sb('e_fw', [128, 2 * NP, 3], F32)
        fb = sb('e_fb', [128, 2 * NP, 1], F32)
        fgB = sb('e_fgB', [128, D], F32)
        onesF = sb('e_onesF', [1, 128], F32)
        epsT = sb('e_eps', [128, 1], F32)
        halo = sb('e_halo', [128, 2 * NP, 2], F32)
        n3 = sb('e_n3', [128, 8, 512], BF16, sem=True)
        actT = sb('e_actT', [128, NP, 512], BF16)
        ub = [sb('e_ub%d' % i, [128, 514], F32) for i in range(3)]
        tb = [sb('e_tb%d' % i, [128, 512], F32) for i in range(3)]
        sgl = sb('e_sgl', [128, 512], F32)
        h2 = [sb('e_h2_%d' % i, [128, D], F32, sem=True) for i in range(2)]
        junk = sb('e_junk', [128, D], BF16)
        ss = sb('e_ss', [128, 1], F32)
        rt = sb('e_rt', [128, 1], F32)
        rstd = sb('e_rstd', [128, 1], F32)
        pu = [ps('e_pu%d' % i, [128, 512]) for i in range(3)]
        pd = [ps('e_pd%d' % i, [128, 512]) for i in range(2)]

        S.dma('sp', identF.sem, dict(out=identF.t[:, :], in_=T['identF']), writes=[identF])
        _memset(S, 'dve', onesF.t[:, :], 1.0, [onesF])
        _memset(S, 'dve', epsT.t[:, :], EPS, [epsT])
        _memset(S, 'dve', halo.t[:, :, :], 0.0, [halo])
        order = []
        for j in range(NP):
            for c in (j, j + NP):
                if c // 4 not in order:
                    order.append(c // 4)
        for bi in order[:2]:
            S.dma('sp', wupb[bi].sem, dict(out=wupb[bi].t[:, :, :],
                                           in_=T['WB_ffn_w_up'][:, bi * 512:(bi + 1) * 512].rearrange('(k p) n -> p k n', p=128)),
                  writes=[wupb[bi]])
        for blk in range(0, 2 * D_FF, 1024):
            w = min(1024, 2 * D_FF - blk)
            c0 = blk // 128
            cols_from_rows(S, C, T['ffn_dw_w'][:, blk:blk + w], 3, w, fw, lambda ch, c0=c0: fw.t[:, c0 + ch, :])
            cols_from_rows(S, C, T['ffn_dw_b'][:, blk:blk + w], 1, w, fb, lambda ch, c0=c0: fb.t[:, c0 + ch, :])
        for bi in order[2:]:
            S.dma('sp', wupb[bi].sem, dict(out=wupb[bi].t[:, :, :],
                                           in_=T['WB_ffn_w_up'][:, bi * 512:(bi + 1) * 512].rearrange('(k p) n -> p k n', p=128)),
                  writes=[wupb[bi]])
        S.dma('sp', wdn.sem, dict(out=wdn.t[:, :, :], in_=T['WB_ffn_w_down'].rearrange('(k p) n -> p k n', p=128)),
              writes=[wdn])
        S.dma('sp', C.stage.sem, dict(out=C.stage.t[0:1, 0:1024], in_=T['final_g']), writes=[C.stage])
        for half in range(2):
            _mm(S, pd[half].t[:, :], onesF.t[0:1, :], C.stage.t[0:1, half * 512:(half + 1) * 512], True, True,
                [onesF, C.stage], [pd[half]])
            _cp(S, 'dve', fgB.t[:, half * 512:(half + 1) * 512], pd[half].t[:, :], [pd[half]], [fgB])

        npu = 0
        nub = 0
        for st in range(NSUP):
            cs = slice(st * 512, (st + 1) * 512)
            S.dma('sp', n3.sem, dict(out=n3.t[:, :, :], in_=T['N3T'][:, :, cs].rearrange('k p s -> p k s')), writes=[n3])
            for j in range(NP):
                tpair = []
                for c in (j, j + NP):
                    p = pu[npu % 3]
                    npu += 1
                    u_ = ub[nub % 3]
                    t_ = tb[nub % 3]
                    nub += 1
                    for kc in range(8):
                        _mm(S, p.t[:, :], wupb[c // 4].t[:, kc, (c % 4) * 128:(c % 4 + 1) * 128], n3.t[:, kc, :], kc == 0, kc == 7,
                            [wupb[c // 4], n3], [p])
                    _cp(S, 'act', u_.t[:, 2:514], p.t[:, :], [p], [u_])
                    _cp(S, 'pool', u_.t[:, 0:2], halo.t[:, c, :], [halo], [u_])
                    _act(S, t_.t[:, :], p.t[:, :], AF.Identity, [p, fw, fb], [t_], scale=fw.t[:, c, 2:3], bias=fb.t[:, c, :])
                    _stt(S, t_.t[:, :], u_.t[:, 0:512], fw.t[:, c, 0:1], t_.t[:, :], ALU.mult, ALU.add, [u_, fw, t_], [t_])
                    _stt(S, t_.t[:, :], u_.t[:, 1:513], fw.t[:, c, 1:2], t_.t[:, :], ALU.mult, ALU.add, [u_, fw, t_], [t_])
                    _cp(S, 'pool', halo.t[:, c, :], u_.t[:, 512:514], [u_], [halo])
                    tpair.append(t_)
                _act(S, sgl.t[:, :], tpair[0].t[:, :], AF.Silu, [tpair[0]], [sgl])
                _tt(S, 'dve', actT.t[:, j, :], sgl.t[:, :], tpair[1].t[:, :], ALU.mult, [sgl, tpair[1]], [actT])
            for t in range(4):
                tt = st * 4 + t
                hb = h2[tt % 2]
                ob = hb
                S.dma('sp', hb.sem, dict(out=hb.t[:, :], in_=T['H2'][tt * 128:(tt + 1) * 128, :]), writes=[hb])
                for half in range(2):
                    p = pd[half]
                    for j in range(NP):
                        _mm(S, p.t[:, :], actT.t[:, j, t * 128:(t + 1) * 128], wdn.t[:, j, half * 512:(half + 1) * 512],
                            j == 0, j == NP - 1, [actT, wdn], [p])
                    _tt(S, 'dve', hb.t[:, half * 512:(half + 1) * 512], p.t[:, :], hb.t[:, half * 512:(half + 1) * 512],
                        ALU.add, [p, hb], [hb])
                _stt(S, junk.t[:, :], hb.t[:, :], 1.0, hb.t[:, :], ALU.mult, ALU.mult, [hb], [junk, ss], accum_out=ss.t[:, 0:1])
                _act(S, rt.t[:, :], ss.t[:, :], AF.Sqrt, [ss, epsT], [rt], scale=1.0 / D, bias=epsT.t[:, :])
                S.op('dve', 'reciprocal', dict(out=rstd.t[:, :], in_=rt.t[:, :]), [rt], [rstd])
                _stt(S, ob.t[:, :], hb.t[:, :], rstd.t[:, 0:1], fgB.t[:, :], ALU.mult, ALU.mult, [hb, rstd, fgB], [hb])
                S.dma('sp', hb.sem, dict(out=T['y'][tt * 128:(tt + 1) * 128, :], in_=hb.t[:, :]), reads=[hb])
        S.flush()


def t5_bucket_np(d):
    n = np.maximum(d, 0)
    nf = np.maximum(n, 1).astype(np.float32)
    large = 16 + (np.log(nf / np.float32(16)) / np.float32(np.log(8.0)) * np.float32(16)).astype(np.int32)
    large = np.minimum(large, 31)
    return np.where(n < 16, n, large)


def scratch_spec(Sq):
    return {
        'QT': ([16, 64, Sq], BF16), 'KC': ([4, 64, Sq], BF16), 'VC': ([4, 64, Sq], BF16),
        'KS': ([4, 64, Sq], BF16), 'KW': ([4, 64, Sq], BF16), 'GM': ([16, 128, Sq], BF16),
        'HC': ([4, 128, Sq], BF16), 'VS1': ([Sq, 4, 65], BF16), 'VW1': ([Sq, 4, 65], BF16),
        'G': ([Sq, 48], F32), 'NSAT': ([8, 128, Sq], BF16), 'H2': ([Sq, D], F32), 'N3T': ([8, 128, Sq], BF16),
    }


INPUT_SHAPES = {
    'norm1_g': [1, D], 'w_in': [D, IN_W], 'conv_dw_w': [31, 512], 'conv_dw_b': [1, 512],
    'conv_ln_g': [1, 512], 'conv_ln_b': [1, 512], 'conv_w_pw': [512, D],
    'cmp_pe': [2, 32, 64], 'cmp_w1': [2, 2048, 256], 'cmp_b1': [1, 512], 'cmp_w2': [2, 256, 64],
    'nsa_w_o': [D, D], 'w_out': [D, D], 'norm2_g': [1, D], 'mem_norm_g': [1, D],
    'xa_wq': [D, D], 'xa_wkv': [D, 2 * D], 'xa_wo': [D, D], 'norm3_g': [1, D],
    'ffn_w_up': [D, 2 * D_FF], 'ffn_dw_w': [3, 2 * D_FF], 'ffn_dw_b': [1, 2 * D_FF],
    'ffn_w_down': [D_FF, D], 'final_g': [1, D],
}


def const_shapes(Sq):
    NT = Sq // 128
    NC, chunks = cmp_chunks(Sq)
    return {
        'identF': [128, 128], 'Econst': [64, Sq], 'm512': [128, 512],
        'SelW': [32, 8 * (NT - 1) + 128 * len(chunks)], 'mulB': [128, 128], 'addB': [128, 128],
        'ovl': [128 * len(chunks), 64],
        'tz1': [2, 128, 16, 128], 'tz31': [2, 128, 16, 128], 'tzm': [2, 128, 16, 128],
        'cb1': [32, 16, 128], 'cb31': [32, 16, 128], 'cbm': [32, 16, 128],
    }


def build(Sq, debug=(), phases='ABCDE'):
    nc = bass.Bass("TRN2", target_bir_lowering=False)
    T = {}
    T['x'] = nc.dram_tensor('x', [Sq, D], F32, kind='ExternalInput').ap()
    T['mem'] = nc.dram_tensor('mem', [MEM, D], F32, kind='ExternalInput').ap()
    for k, shp in list(INPUT_SHAPES.items()) + list(const_shapes(Sq).items()):
        T[k] = nc.dram_tensor(k, shp, F32, kind='ExternalInput').ap()
    for k, (shp, dt) in scratch_spec(Sq).items():
        kind = 'ExternalOutput' if k in debug else 'Internal'
        T[k] = nc.dram_tensor(k, shp, dt, kind=kind).ap()
    T['y'] = nc.dram_tensor('y', [Sq, D], F32, kind='ExternalOutput').ap()
    for k, (K_, N_) in WB_SPEC.items():
        T['WB_' + k] = nc.dram_tensor('WB_' + k, [K_, N_], BF16, kind='Internal').ap()
    NC, chunks = cmp_chunks(Sq)
    with ExitStack() as es:
        S = Sched(nc, es)
        if 'A' in phases:
            phase_A(nc, S, Sq, T)
        NCH = len(chunks)
        with ExitStack() as es1:
            P = Ctx()
            P.kmT = Tl(es1.enter_context(nc.sbuf_tensor('p_kmT', [128, 8, 256], BF16)))
            P.vm = Tl(es1.enter_context(nc.sbuf_tensor('p_vm', [128, 2, D], BF16)))
            with ExitStack() as es2:
                kcT = Tl(es2.enter_context(nc.sbuf_tensor('kcT', [128, 4, 128 * NCH], BF16)))
                vco = Tl(es2.enter_context(nc.sbuf_tensor('vco', [128, NCH, 4, 128], BF16)))
                P.biasT = Tl(es2.enter_context(nc.sbuf_tensor('p_biasT', [128, 2, 16, 128], BF16)))
                P.Bband = Tl(es2.enter_context(nc.sbuf_tensor('p_Bband', [32, 16, 128], BF16)))
                if 'B' in phases:
                    phase_B(nc, S, Sq, T, kcT, vco, P)
                if 'C' in phases:
                    phase_C(nc, S, Sq, T, kcT, vco, P)
            if 'D' in phases:
                phase_D(nc, S, Sq, T, P)
        if 'E' in phases:
            phase_E(nc, S, Sq, T, None)
    return nc


def host_consts(rel_bias, Sq):
    NT = Sq // 128
    NC, chunks = cmp_chunks(Sq)
    rb = np.asarray(rel_bias, dtype=np.float32)
    c = {}
    c['identF'] = np.eye(128, dtype=np.float32)
    E = np.zeros((64, Sq), np.float32)
    kk = np.arange(Sq)
    valid = kk // 64 < 64
    E[(kk // 64)[valid], kk[valid]] = 1.0
    c['Econst'] = E
    ki = np.arange(128)[:, None]
    qi = np.arange(128)[None, :]
    c['m512'] = np.tile(np.where(qi >= ki, MASKV, 0.0).astype(np.float32), (1, 4))
    OFFS = 8 * (NT - 1)
    W = np.zeros((32, OFFS + 128 * len(chunks)), np.float32)
    m = np.arange(W.shape[1]) - OFFS
    for r in range(17):
        W[r, m == r - 10] = 1.0
    W[31, m >= 7] = 1.0
    c['SelW'] = W
    r = np.arange(128)[None, :] - 62
    p = np.arange(128)[:, None]
    hi = (p >= 64).astype(np.int64)
    rel = r - hi
    free = rel <= -2
    forced = (rel == -1) | (rel == 0)
    c['mulB'] = np.where(free, 1.0, 0.0).astype(np.float32) * np.ones((128, 1), np.float32)
    c['addB'] = np.where(free, 0.0, np.where(forced, 10.0 + (rel + 2), -1.0 - 0.001 * np.maximum(rel, 0))).astype(np.float32)
    n = np.arange(128 * len(chunks))[:, None]
    j = np.arange(64)[None, :]
    ov = np.clip(np.minimum(16 * n + 32, 64 * j + 64) - np.maximum(16 * n, 64 * j), 0, None).astype(np.float32) / 32.0
    ov[NC:] = 0.0
    c['ovl'] = ov.astype(np.float32)
    tz1 = np.zeros((2, 128, 16, 128), np.float32)
    tz31 = np.zeros_like(tz1)
    tzm = np.zeros_like(tz1)
    for dl in range(2):
        d = dl * 128 + qi - ki
        ok = d >= 0
        g1 = rb[t5_bucket_np(d)]
        g31 = rb[np.full_like(d, 31)]
        tz1[dl] = np.where(ok[:, :, None], g1, 0.0).transpose(0, 2, 1)
        tz31[dl] = np.where(ok[:, :, None], g31, 0.0).transpose(0, 2, 1)
        tzm[dl] = np.where(ok[:, :, None], 0.0, MASKV).transpose(0, 2, 1) * np.ones((1, 16, 1), np.float32)
    c['tz1'], c['tz31'], c['tzm'] = tz1, tz31, tzm
    cb1 = np.zeros((32, 16, 128), np.float32)
    cb31 = np.zeros_like(cb1)
    cbm = np.zeros_like(cb1)
    rr = np.arange(17)[:, None]
    d1 = np.arange(128)[None, :] - 16 * (rr - 10) - 31
    ok = d1 >= 0
    cb1[:17] = np.where(ok[:, :, None], rb[t5_bucket_np(d1)], 0.0).transpose(0, 2, 1)
    cb31[:17] = np.where(ok[:, :, None], rb[np.full_like(d1, 31)], 0.0).transpose(0, 2, 1)
    cbm[:17] = (np.where(ok, 0.0, MASKV)[:, None, :] * np.ones((1, 16, 1))).astype(np.float32)
    cbm[31] = MASKV
    c['cb1'], c['cb31'], c['cbm'] = cb1, cb31, cbm
    return {k: np.ascontiguousarray(v, dtype=np.float32) for k, v in c.items()}


def host_inputs(inp, Sq):
    shared = {}
    for k, shp in INPUT_SHAPES.items():
        shared[k] = np.ascontiguousarray(np.asarray(inp[k], dtype=np.float32).reshape(shp))
    shared.update(host_consts(inp['rel_bias'], Sq))
    return shared


def kernel(**inp):
    x = np.asarray(inp['x'], dtype=np.float32)
    mem = np.asarray(inp['mem'], dtype=np.float32)
    B, Sq, _ = x.shape
    nc = build(Sq)
    shared = host_inputs(inp, Sq)
    in_maps = []
    for b in range(B):
        m = dict(shared)
        m['x'] = np.ascontiguousarray(x[b])
        m['mem'] = np.ascontiguousarray(mem[b])
        in_maps.append(m)
    res = run_bass_kernel_spmd(nc, in_maps, core_ids=list(range(B)))
    return np.stack([np.asarray(r['y'], dtype=np.float32) for r in res.results], axis=0)
```

```python
import os
import numpy as np
from contextlib import ExitStack
import concourse.bass as bass
import concourse.mybir as mybir
from concourse.bass_utils import run_bass_kernel_spmd

F32 = mybir.dt.float32
BF16 = mybir.dt.bfloat16
AF = mybir.ActivationFunctionType
ALU = mybir.AluOpType
AX = mybir.AxisListType

D = 1024
SEQ = 4096
MEM = 256
IN_W = 5680
D_FF = 2816
MASKV = -30000.0
EPS = 1e-6

ENGS = ['pe', 'act', 'dve', 'pool', 'sp']


class Buf:
    __slots__ = ('w', 'r')

    def __init__(self):
        self.w = None
        self.r = {}


class Tl:
    def __init__(self, t, sem=None):
        self.t = t
        self.b = Buf()
        self.sem = sem


def _b(x):
    return x.b if isinstance(x, Tl) else x


class Sched:
    def __init__(self, nc, es):
        self.nc = nc
        self.es = es
        self.q = {e: [] for e in ENGS}
        self.sem = {}
        self.cnt = {}
        self.known = {e: {} for e in ENGS}
        self.ndma = 0

    def semh(self, key):
        if key not in self.sem:
            self.sem[key] = self.es.enter_context(self.nc.semaphore('s_' + key))
            self.cnt[key] = 0
        return self.sem[key]

    def newdma(self, name=None):
        self.ndma += 1
        key = 'd%d' % self.ndma
        self.semh(key)
        return key

    def _deps(self, eng, reads, writes):
        need = {}
        for b in reads:
            b = _b(b)
            if b.w:
                k, v = b.w
                need[k] = max(need.get(k, 0), v)
        for b in writes:
            b = _b(b)
            if b.w:
                k, v = b.w
                need[k] = max(need.get(k, 0), v)
            for k, v in b.r.items():
                need[k] = max(need.get(k, 0), v)
        out = []
        kn = self.known[eng]
        for k, v in need.items():
            if eng == 'pe' and k == 'pe':
                continue
            if kn.get(k, 0) < v:
                kn[k] = v
                out.append((k, v))
        return out

    def _post(self, key, v, reads, writes):
        for b in reads:
            b = _b(b)
            b.r[key] = max(b.r.get(key, 0), v)
        for b in writes:
            b = _b(b)
            b.w = (key, v)
            b.r = {}

    def op(self, eng, meth, kw, reads=(), writes=()):
        self.semh(eng)
        waits = self._deps(eng, reads, writes)
        self.cnt[eng] += 1
        v = self.cnt[eng]
        self.q[eng].append((waits, meth, kw, eng, 1))
        self._post(eng, v, reads, writes)

    def dma(self, eng, semkey, kw, reads=(), writes=()):
        self.semh(semkey)
        waits = self._deps(eng, reads, writes)
        self.cnt[semkey] += 16
        v = self.cnt[semkey]
        self.q[eng].append((waits, 'dma_start', kw, semkey, 16))
        self._post(semkey, v, reads, writes)

    def barrier(self):
        for e in ENGS:
            kn = self.known[e]
            waits = []
            for k, v in self.cnt.items():
                if v > 0 and kn.get(k, 0) < v:
                    kn[k] = v
                    waits.append((k, v))
            if waits:
                self.q[e].append((waits, None, None, None, 0))

    def flush(self):
        self.barrier()
        nc = self.nc
        q = self.q
        sem = self.sem

        def run(engobj, items):
            for waits, meth, kw, key, inc in items:
                for k, v in waits:
                    engobj.wait_ge(sem[k], v)
                if meth is not None:
                    getattr(engobj, meth)(**kw).then_inc(sem[key], inc)

        with nc.Block() as block:
            @block.tensor
            def _(e):
                run(e, q['pe'])

            @block.scalar
            def _(e):
                run(e, q['act'])

            @block.vector
            def _(e):
                run(e, q['dve'])

            @block.gpsimd
            def _(e):
                run(e, q['pool'])

            @block.sync
            def _(e):
                run(e, q['sp'])
        self.q = {e: [] for e in ENGS}


class Ctx:
    pass


def _mm(S, out, lhsT, rhs, start, stop, reads, writes, skip=False):
    kw = dict(out=out, lhsT=lhsT, rhs=rhs, start=start, stop=stop)
    if skip:
        kw['skip_group_check'] = True
    S.op('pe', 'matmul', kw, reads, writes)


def _tp(S, out, in_, ident, reads, writes):
    S.op('pe', 'transpose', dict(out=out, in_=in_, identity=ident), reads, writes)


def _act(S, out, in_, func, reads, writes, **kw):
    S.op('act', 'activation', dict(out=out, in_=in_, func=func, **kw), reads, writes)


def _tt(S, eng, out, in0, in1, op, reads, writes):
    S.op(eng, 'tensor_tensor', dict(out=out, in0=in0, in1=in1, op=op), reads, writes)


def _ts(S, eng, out, in0, s1, op0, reads, writes, s2=None, op1=None):
    kw = dict(out=out, in0=in0, scalar1=s1, scalar2=s2, op0=op0)
    if op1 is not None:
        kw['op1'] = op1
    S.op(eng, 'tensor_scalar', kw, reads, writes)


def _stt(S, out, in0, scalar, in1, op0, op1, reads, writes, accum_out=None):
    kw = dict(out=out, in0=in0, scalar=scalar, in1=in1, op0=op0, op1=op1)
    if accum_out is not None:
        kw['accum_out'] = accum_out
    S.op('dve', 'scalar_tensor_tensor', kw, reads, writes)


def _cp(S, eng, out, in_, reads, writes):
    if eng == 'act':
        S.op('act', 'copy', dict(out=out, in_=in_), reads, writes)
    else:
        S.op(eng, 'tensor_copy', dict(out=out, in_=in_), reads, writes)


def _memset(S, eng, ap, val, writes):
    S.op(eng, 'memset', dict(ap=ap, constant=val), (), writes)


def load_w_cast(S, dst_tl, dst_ap_fn, src, kc_n, ncols, rows_per=128):
    for kc in range(kc_n):
        c0 = 0
        while c0 < ncols:
            c1 = min(ncols, c0 + 2048)
            S.dma('pool', dst_tl.sem, dict(out=dst_ap_fn(kc, c0, c1),
                                           in_=src[kc * 128:(kc + 1) * 128, c0:c1]),
                  writes=[dst_tl])
            c0 = c1


def cols_from_rows(S, C, src_rows, R, ncol, dst_tl, dst_fn):
    stage = C.stage
    assert ncol <= stage.t.shape[1] and R <= 32
    S.dma('sp', stage.sem, dict(out=stage.t[0:R, 0:ncol], in_=src_rows), writes=[stage])
    for ch in range(ncol // 128):
        _tp(S, C.pst.t[:, 0:R], stage.t[0:R, ch * 128:(ch + 1) * 128], C.identF.t[0:R, 0:R],
            [stage, C.identF], [C.pst])
        _cp(S, 'dve', dst_fn(ch), C.pst.t[:, 0:R], [C.pst], [dst_tl])


def phase_A(nc, S, Sq, T):
    NSUP = Sq // 512
    with ExitStack() as es:
        def sb(name, shape, dt, sem=False):
            t = es.enter_context(nc.sbuf_tensor(name, shape, dt))
            return Tl(t, S.newdma() if sem else None)

        def ps(name, shape, dt=F32):
            return Tl(es.enter_context(nc.psum_tensor(name, shape, dt)))

        C = Ctx()
        win = sb('a_win', [128, 8, IN_W], BF16, sem=True)
        dg = sb('a_dg', [128, 4, 31, 128], BF16)
        identF = sb('a_identF', [128, 128], F32, sem=True)
        identB = sb('a_identB', [128, 128], BF16, sem=True)
        onesF = sb('a_onesF', [128, 128], F32)
        C.identF = identF
        C.stage = sb('a_stage', [32, 1024], F32, sem=True)
        dwT = sb('a_dwT', [128, 4, 31], F32)
        dwb = sb('a_dwb', [128, 4, 1], F32)
        lng = sb('a_lng', [128, 4, 1], F32)
        lnb = sb('a_lnb', [128, 4, 1], F32)
        xs = [sb('a_x%d' % i, [128, D], F32, sem=True) for i in range(4)]
        ss = sb('a_ss', [128, 4], F32)
        rt = sb('a_rt', [128, 4], F32)
        rstd = sb('a_rstd', [128, 4], F32)
        junk = sb('a_junk', [128, D], BF16)
        ntok = [sb('a_ntok%d' % i, [128, D], BF16) for i in range(2)]
        nT = sb('a_nT', [128, 8, 512], BF16)
        hglu = sb('a_hglu', [128, 4, 542], BF16)
        stg = [sb('a_stg%d' % i, [128, 512], BF16, sem=True) for i in range(6)]
        sg = [sb('a_sg%d' % i, [128, 512], F32) for i in range(2)]
        ycs = sb('a_ycs', [128, 4, 512], F32)
        ysq = sb('a_ysq', [128, 4, 512], F32)
        mean = sb('a_mean', [128, 512], F32)
        msq = sb('a_msq', [128, 512], F32)
        rs = sb('a_rs', [128, 512], F32)
        dtl = [sb('a_d%d' % i, [128, 512], F32) for i in range(2)]
        vstg = [sb('a_vstg%d' % i, [128, 2, 4, 65], BF16, sem=True) for i in range(2)]
        gstg = [sb('a_gstg%d' % i, [128, 48], F32, sem=True) for i in range(2)]
        epsT = sb('a_eps', [128, 1], F32)

        ptr = ps('a_ptr', [128, 1024], BF16)
        pf = [ps('a_pf%d' % i, [128, 512]) for i in range(3)]
        pv = ps('a_pv', [128, 512])
        pg = ps('a_pg', [128, 48])
        C.pst = pg
        pc = [ps('a_pc%d' % i, [128, 512]) for i in range(2)]

        S.dma('sp', identF.sem, dict(out=identF.t[:, :], in_=T['identF']), writes=[identF])
        S.dma('pool', identB.sem, dict(out=identB.t[:, :], in_=T['identF']), writes=[identB])
        _memset(S, 'dve', onesF.t[:, :], 1.0 / 512.0, [onesF])
        _memset(S, 'dve', epsT.t[:, :], EPS, [epsT])
        _memset(S, 'dve', hglu.t[:, :, :], 0.0, [hglu])
        for v in vstg:
            _memset(S, 'dve', v.t[:, :, :, :], 1.0, [v])
        WBLK = [(2816, 3632), (0, 1024), (1024, 2048), (2048, 2816), (3632, 4656), (4656, 5680)]
        wblk = [Tl(None, S.newdma()) for _ in WBLK]

        def wb(c0):
            for i_, (a0, a1) in enumerate(WBLK):
                if a0 <= c0 < a1:
                    return wblk[i_]
            raise ValueError(c0)
        for i_, (a0, a1) in enumerate(WBLK):
            for kc in range(8):
                S.dma('pool', wblk[i_].sem, dict(out=win.t[:, kc, a0:a1], in_=T['w_in'][kc * 128:(kc + 1) * 128, a0:a1]),
                      writes=[wblk[i_]])
        precast_weights(S, T)
        cols_from_rows(S, C, T['conv_dw_w'], 31, 512, dwT, lambda ch: dwT.t[:, ch, :])
        cols_from_rows(S, C, T['conv_dw_b'], 1, 512, dwb, lambda ch: dwb.t[:, ch, :])
        cols_from_rows(S, C, T['conv_ln_g'], 1, 512, lng, lambda ch: lng.t[:, ch, :])
        cols_from_rows(S, C, T['conv_ln_b'], 1, 512, lnb, lambda ch: lnb.t[:, ch, :])
        onesR = sb('a_onesR', [1, 128], F32)
        g1B = sb('a_g1B', [128, D], F32)
        _memset(S, 'dve', onesR.t[:, :], 1.0, [onesR])
        bcast_row(S, C, T['norm1_g'], g1B, pf, onesR)
        for ch in range(4):
            for j in range(31):
                _ts(S, 'dve', dg.t[:, ch, j, :], identF.t[:, :], dwT.t[:, ch, j:j + 1], ALU.mult,
                    [identF, dwT], [dg])

        x = T['x']
        fchunks = []
        for i in range(4):
            fchunks.append((512 + 128 * i, 'gate', i))
            fchunks.append((128 * i, 'a', i))
        for i in range(8):
            fchunks.append((1024 + 128 * i, 'q', i))
        for nm, c0 in (('kc', 2048), ('vc', 2304), ('ks', 2560), ('kw', 3072)):
            for i in range(2):
                fchunks.append((c0 + 128 * i, nm, i))
        for i in range(16):
            fchunks.append((3632 + 128 * i, 'gm', i))

        import os
        STG = int(os.environ.get('STG', '9'))
        nstg = 0
        npf = 0
        for st in range(NSUP if STG >= 1 else 0):
            for t in range(4):
                tt = st * 4 + t
                xb = xs[tt % 4]
                S.dma('sp', xb.sem, dict(out=xb.t[:, :], in_=x[tt * 128:(tt + 1) * 128, :]), writes=[xb])
                _stt(S, junk.t[:, :], xb.t[:, :], 1.0, xb.t[:, :], ALU.mult, ALU.mult, [xb], [junk, ss],
                     accum_out=ss.t[:, t:t + 1])
            _act(S, rt.t[:, :], ss.t[:, :], AF.Sqrt, [ss, epsT], [rt], scale=1.0 / D, bias=epsT.t[:, :])
            S.op('dve', 'reciprocal', dict(out=rstd.t[:, :], in_=rt.t[:, :]), [rt], [rstd])
            for t in range(4):
                tt = st * 4 + t
                xb = xs[tt % 4]
                nk = ntok[t % 2]
                _stt(S, nk.t[:, :], xb.t[:, :], rstd.t[:, t:t + 1], g1B.t[:, :], ALU.mult, ALU.mult, [xb, rstd, g1B], [nk])
                for kc in range(8):
                    _tp(S, ptr.t[:, kc * 128:(kc + 1) * 128], nk.t[:, kc * 128:(kc + 1) * 128], identB.t[:, :],
                        [nk, identB], [ptr])
                _cp(S, 'act', nT.t[:, :, t * 128:(t + 1) * 128],
                    ptr.t[:, :].rearrange('p (k q) -> p k q', k=8), [ptr], [nT])
            for t in range(4 if STG >= 2 else 0):
                tt = st * 4 + t
                for half, c0 in ((0, 2816), (1, 3328)):
                    for kc in range(8):
                        _mm(S, pv.t[:, half * 256:(half + 1) * 256], nT.t[:, kc, t * 128:(t + 1) * 128],
                            win.t[:, kc, c0:c0 + 256], kc == 0, kc == 7, [nT, wb(c0)], [pv])
                for kc in range(8):
                    _mm(S, pg.t[:, :], nT.t[:, kc, t * 128:(t + 1) * 128], win.t[:, kc, 3584:3632],
                        kc == 0, kc == 7, [nT, wb(3584)], [pg])
                vs = vstg[tt % 2]
                _cp(S, 'dve', vs.t[:, :, :, 0:64], pv.t[:, :].rearrange('p (a g d) -> p a g d', a=2, g=4),
                    [pv], [vs])
                S.dma('sp', vs.sem, dict(out=T['VS1'][tt * 128:(tt + 1) * 128, :, :], in_=vs.t[:, 0, :, :]),
                      reads=[vs])
                S.dma('sp', vs.sem, dict(out=T['VW1'][tt * 128:(tt + 1) * 128, :, :], in_=vs.t[:, 1, :, :]),
                      reads=[vs])
                gs = gstg[tt % 2]
                _act(S, gs.t[:, :], pg.t[:, :], AF.Sigmoid, [pg], [gs])
                S.dma('sp', gs.sem, dict(out=T['G'][tt * 128:(tt + 1) * 128, :], in_=gs.t[:, :]), reads=[gs])
            cs = slice(st * 512, (st + 1) * 512)
            for (c0, kind, idx) in (fchunks if STG >= 3 else []):
                p = pf[npf % 3]
                npf += 1
                for kc in range(8):
                    _mm(S, p.t[:, :], win.t[:, kc, c0:c0 + 128], nT.t[:, kc, :], kc == 0, kc == 7, [wb(c0), nT], [p])
                if kind == 'gate':
                    sgt = sg[idx % 2]
                    _act(S, sgt.t[:, :], p.t[:, :], AF.Sigmoid, [p], [sgt])
                elif kind == 'a':
                    sgt = sg[idx % 2]
                    _tt(S, 'dve', hglu.t[:, idx, 30:542], p.t[:, :], sgt.t[:, :], ALU.mult, [p, sgt], [hglu])
                else:
                    sl = stg[nstg % 6]
                    nstg += 1
                    if kind == 'q':
                        _act(S, sl.t[:, :], p.t[:, :], AF.Copy, [p], [sl], scale=0.125)
                        dst = T['QT'][2 * idx:2 * idx + 2, :, cs].rearrange('h d s -> (h d) s')
                    elif kind == 'gm':
                        _act(S, sl.t[:, :], p.t[:, :], AF.Sigmoid, [p], [sl])
                        dst = T['GM'][idx, :, cs]
                    else:
                        _cp(S, 'dve', sl.t[:, :], p.t[:, :], [p], [sl])
                        dst = T[{'kc': 'KC', 'vc': 'VC', 'ks': 'KS', 'kw': 'KW'}[kind]][2 * idx:2 * idx + 2, :, cs] \
                            .rearrange('g d s -> (g d) s')
                    S.dma('sp', sl.sem, dict(out=dst, in_=sl.t[:, :]), reads=[sl])
            if STG < 4:
                continue
            for ch in range(4):
                p = pc[ch % 2]
                for j in range(0, 31, int(os.environ.get('JSTEP', '1'))):
                    _mm(S, p.t[:, :], dg.t[:, ch, j, :], hglu.t[:, ch, j:j + 512], j == 0, j == 30, [dg, hglu], [p])
                CP = int(os.environ.get('CP', '15'))
                if CP & 2:
                    _ts(S, 'dve', ycs.t[:, ch, :], p.t[:, :], dwb.t[:, ch, :], ALU.add, [p, dwb], [ycs])
                if CP & 4:
                    _tt(S, 'pool', ysq.t[:, ch, :], ycs.t[:, ch, :], ycs.t[:, ch, :], ALU.mult, [ycs], [ysq])
            if CP & 8:
                _cp(S, 'dve', hglu.t[:, :, 0:30], hglu.t[:, :, 512:542], [hglu], [hglu])
            if STG < 5:
                continue
            pm = pf[npf % 3]
            npf += 1
            pq = pf[npf % 3]
            npf += 1
            for ch in range(4):
                _mm(S, pm.t[:, :], onesF.t[:, :], ycs.t[:, ch, :], ch == 0, ch == 3, [onesF, ycs], [pm])
            for ch in range(4):
                _mm(S, pq.t[:, :], onesF.t[:, :], ysq.t[:, ch, :], ch == 0, ch == 3, [onesF, ysq], [pq])
            _cp(S, 'act', mean.t[:, :], pm.t[:, :], [pm], [mean])
            _tt(S, 'dve', msq.t[:, :], mean.t[:, :], mean.t[:, :], ALU.mult, [mean], [msq])
            _tt(S, 'dve', msq.t[:, :], pq.t[:, :], msq.t[:, :], ALU.subtract, [pq, msq], [msq])
            _act(S, msq.t[:, :], msq.t[:, :], AF.Sqrt, [msq, epsT], [msq], bias=epsT.t[:, :])
            S.op('dve', 'reciprocal', dict(out=rs.t[:, :], in_=msq.t[:, :]), [msq], [rs])
            for ch in range(4):
                d_ = dtl[ch % 2]
                _tt(S, 'dve', d_.t[:, :], ycs.t[:, ch, :], mean.t[:, :], ALU.subtract, [ycs, mean], [d_])
                _tt(S, 'dve', d_.t[:, :], d_.t[:, :], rs.t[:, :], ALU.mult, [d_, rs], [d_])
                _ts(S, 'dve', d_.t[:, :], d_.t[:, :], lng.t[:, ch, :], ALU.mult, [d_, lng, lnb], [d_],
                    s2=lnb.t[:, ch, :], op1=ALU.add)
                sl = stg[nstg % 6]
                nstg += 1
                _act(S, sl.t[:, :], d_.t[:, :], AF.Silu, [d_], [sl])
                S.dma('sp', sl.sem, dict(out=T['HC'][ch, :, cs], in_=sl.t[:, :]), reads=[sl])
        S.flush()


def bcast_row(S, C, src_row, dst_tl, pbanks, onesF):
    S.dma('sp', C.stage.sem, dict(out=C.stage.t[0:1, 0:1024], in_=src_row), writes=[C.stage])
    for half in range(2):
        p = pbanks[half]
        _mm(S, p.t[:, :], onesF.t[0:1, :], C.stage.t[0:1, half * 512:(half + 1) * 512], True, True,
            [onesF, C.stage], [p])
        _cp(S, 'dve', dst_tl.t[:, half * 512:(half + 1) * 512], p.t[:, :], [p], [dst_tl])


def rms_T(S, R, xtiles, gB):
    n = len(xtiles)
    for i, xb in enumerate(xtiles):
        _stt(S, R.junk.t[:, :], xb.t[:, :], 1.0, xb.t[:, :], ALU.mult, ALU.mult, [xb], [R.junk, R.ss],
             accum_out=R.ss.t[:, i:i + 1])
    _act(S, R.rt.t[:, 0:n], R.ss.t[:, 0:n], AF.Sqrt, [R.ss, R.epsT], [R.rt], scale=1.0 / D, bias=R.epsT.t[:, :])
    S.op('dve', 'reciprocal', dict(out=R.rstd.t[:, 0:n], in_=R.rt.t[:, 0:n]), [R.rt], [R.rstd])
    for i, xb in enumerate(xtiles):
        nk = R.ntok[i % 2]
        _stt(S, nk.t[:, :], xb.t[:, :], R.rstd.t[:, i:i + 1], gB.t[:, :], ALU.mult, ALU.mult, [xb, R.rstd, gB], [nk])
        for kc in range(8):
            _tp(S, R.ptr.t[:, kc * 128:(kc + 1) * 128], nk.t[:, kc * 128:(kc + 1) * 128], R.identB.t[:, :],
                [nk, R.identB], [R.ptr])
        yield i, R.ptr


WB_SPEC = {'xa_wkv': (D, 2 * D), 'nsa_w_o': (D, D), 'conv_w_pw': (512, D), 'w_out': (D, D), 'xa_wq': (D, D), 'xa_wo': (D, D),
           'ffn_w_up': (D, 2 * D_FF), 'ffn_w_down': (D_FF, D)}


def precast_weights(S, T):
    key = S.newdma()
    for name, (K_, N_) in WB_SPEC.items():
        for r0 in range(0, K_, 128):
            for c0 in range(0, N_, 2048):
                c1 = min(N_, c0 + 2048)
                S.dma('pool', key, dict(out=T['WB_' + name][r0:r0 + 128, c0:c1], in_=T[name][r0:r0 + 128, c0:c1]))


def _mk(nc, es, S):
    def sb(name, shape, dt, sem=False):
        t = es.enter_context(nc.sbuf_tensor(name, shape, dt))
        return Tl(t, S.newdma() if sem else None)

    def ps(name, shape, dt=F32):
        return Tl(es.enter_context(nc.psum_tensor(name, shape, dt)))
    return sb, ps


def cmp_chunks(Sq):
    NC = Sq // 16 - 1
    out = []
    n0 = 0
    while n0 < NC:
        out.append((n0, min(128, NC - n0)))
        n0 += 128
    return NC, out


def phase_B(nc, S, Sq, T, kcT, vco, P):
    NC, chunks = cmp_chunks(Sq)
    with ExitStack() as es:
        sb, ps = _mk(nc, es, S)
        C = Ctx()
        identF = sb('b_identF', [128, 128], F32, sem=True)
        C.identF = identF
        C.stage = sb('b_stage', [32, 1024], F32, sem=True)
        pst = ps('b_pst', [128, 48])
        C.pst = pst
        kin1 = sb('b_kin', [64, 4, Sq], BF16, sem=True)
        kin = [kin1, kin1]
        w1s = sb('b_w1s', [64, 2, 32, 256], BF16, sem=True)
        w2s = sb('b_w2s', [128, 2, 2, 64], BF16, sem=True)
        peT = sb('b_peT', [64, 2, 32], BF16)
        b1T = sb('b_b1T', [128, 4, 1], F32)
        biasc = sb('b_biasc', [128, 4, 1], F32)
        hT = [sb('b_hT%d' % i, [128, 2, 512], BF16) for i in range(2)]
        xb = sb('b_xb', [128, 512], F32)
        x2 = sb('b_x2', [128, 512], F32)
        u = sb('b_u', [128, 512], F32)
        sgm = sb('b_sgm', [128, 512], F32)
        ovs = sb('b_ovs', [128, 2, 64], F32, sem=True)
        pA = [ps('b_pA%d' % i, [128, 512]) for i in range(2)]
        pB = ps('b_pB', [128, 512])

        S.dma('sp', identF.sem, dict(out=identF.t[:, :], in_=T['identF']), writes=[identF])
        for kv in range(2):
            for l0 in range(0, 32, 8):
                S.dma('pool', w1s.sem, dict(out=w1s.t[:, kv, l0:l0 + 8, :],
                                            in_=T['cmp_w1'][kv, l0 * 64:(l0 + 8) * 64, :].rearrange('(l d) c -> d l c', d=64)),
                      writes=[w1s])
            S.dma('pool', w2s.sem, dict(out=w2s.t[:, kv, :, :],
                                        in_=T['cmp_w2'][kv].rearrange('(h p) d -> p h d', p=128)), writes=[w2s])
            S.dma('sp', C.stage.sem, dict(out=C.stage.t[0:32, 0:64], in_=T['cmp_pe'][kv]), writes=[C.stage])
            _tp(S, pst.t[0:64, 0:32], C.stage.t[0:32, 0:64], identF.t[0:32, 0:32], [C.stage, identF], [pst])
            _cp(S, 'dve', peT.t[:, kv, :], pst.t[0:64, 0:32], [pst], [peT])
        cols_from_rows(S, C, T['cmp_b1'], 1, 512, b1T, lambda ch: b1T.t[:, ch, :])
        _memset(S, 'dve', vco.t[:, :, :, :], 0.0, [vco])
        _memset(S, 'dve', kcT.t[:, :, :], 0.0, [kcT])
        S.dma('sp', ovs.sem, dict(out=ovs.t[:, 0:len(chunks), :], in_=T['ovl'].rearrange('(c p) j -> p c j', p=128)),
              writes=[ovs])
        for ci in range(len(chunks)):
            for g in range(4):
                _cp(S, 'dve', vco.t[:, ci, g, 64:128], ovs.t[:, ci, :], [ovs], [vco])
        for kv in range(2):
            for half in range(2):
                for l in range(32):
                    _mm(S, pB.t[:, 0:1], w1s.t[0:64, kv, l, half * 128:(half + 1) * 128], peT.t[0:64, kv, l:l + 1],
                        l == 0, l == 31, [w1s, peT], [pB])
                _tt(S, 'dve', biasc.t[:, kv * 2 + half, :], pB.t[:, 0:1], b1T.t[:, kv * 2 + half, :], ALU.add,
                    [pB, b1T], [biasc])

        tmpA = sb('b_tmpA', [128, 2048], F32, sem=True)
        tmpB = sb('b_tmpB', [128, 2048], F32, sem=True)
        biasT, Bband = P.biasT, P.Bband
        for dl in range(2):
            S.dma('sp', tmpA.sem, dict(out=tmpA.t[:, :], in_=T['tz1'][dl].rearrange('k h q -> k (h q)')), writes=[tmpA])
            S.dma('sp', tmpB.sem, dict(out=tmpB.t[:, :], in_=T['tz31'][dl].rearrange('k h q -> k (h q)')), writes=[tmpB])
            _tt(S, 'pool', tmpA.t[:, :], tmpA.t[:, :], tmpB.t[:, :], ALU.subtract, [tmpA, tmpB], [tmpA])
            S.dma('sp', tmpB.sem, dict(out=tmpB.t[:, :], in_=T['tzm'][dl].rearrange('k h q -> k (h q)')), writes=[tmpB])
            _tt(S, 'pool', biasT.t[:, dl, :, :].rearrange('k h q -> k (h q)'), tmpA.t[:, :], tmpB.t[:, :], ALU.add,
                [tmpA, tmpB], [biasT])
        S.dma('sp', tmpA.sem, dict(out=tmpA.t[0:32, :], in_=T['cb1'].rearrange('k h q -> k (h q)')), writes=[tmpA])
        S.dma('sp', tmpB.sem, dict(out=tmpB.t[0:32, :], in_=T['cb31'].rearrange('k h q -> k (h q)')), writes=[tmpB])
        _tt(S, 'pool', tmpA.t[0:32, :], tmpA.t[0:32, :], tmpB.t[0:32, :], ALU.subtract, [tmpA, tmpB], [tmpA])
        S.dma('sp', tmpB.sem, dict(out=tmpB.t[0:32, :], in_=T['cbm'].rearrange('k h q -> k (h q)')), writes=[tmpB])
        _tt(S, 'pool', Bband.t[:, :, :].rearrange('k h q -> k (h q)'), tmpA.t[0:32, :], tmpB.t[0:32, :], ALU.add,
            [tmpA, tmpB], [Bband])

        npa = 0
        for kv in range(2):
            src = kin[kv]
            S.dma('sp', src.sem, dict(out=src.t[:, :, :], in_=T['KC' if kv == 0 else 'VC'].rearrange('g d s -> d g s')), writes=[src])
            for g in range(4):
                h_ = hT[(kv * 4 + g) % 2]
                for half in range(2):
                    p = pA[npa % 2]
                    npa += 1
                    for l in range(32):
                        _mm(S, p.t[:, 0:NC], w1s.t[0:64, kv, l, half * 128:(half + 1) * 128],
                            src.t[0:64, g, l:l + 16 * (NC - 1) + 1:16], l == 0, l == 31, [w1s, src], [p])
                    _ts(S, 'dve', xb.t[:, 0:NC], p.t[:, 0:NC], biasc.t[:, kv * 2 + half, :], ALU.add, [p, biasc], [xb])
                    _tt(S, 'pool', x2.t[:, 0:NC], xb.t[:, 0:NC], xb.t[:, 0:NC], ALU.mult, [xb], [x2])
                    _ts(S, 'dve', x2.t[:, 0:NC], x2.t[:, 0:NC], 0.044715, ALU.mult, [x2], [x2], s2=1.0, op1=ALU.add)
                    _tt(S, 'dve', u.t[:, 0:NC], x2.t[:, 0:NC], xb.t[:, 0:NC], ALU.mult, [x2, xb], [u])
                    _act(S, sgm.t[:, 0:NC], u.t[:, 0:NC], AF.Sigmoid, [u], [sgm], scale=1.5957691216057308)
                    _tt(S, 'dve', h_.t[:, half, 0:NC], xb.t[:, 0:NC], sgm.t[:, 0:NC], ALU.mult, [xb, sgm], [h_])
                if kv == 0:
                    for half in range(2):
                        _mm(S, pB.t[0:64, 0:NC], w2s.t[:, 0, half, :], h_.t[:, half, 0:NC], half == 0, half == 1,
                            [w2s, h_], [pB])
                    _cp(S, 'act', kcT.t[0:64, g, 0:NC], pB.t[0:64, 0:NC], [pB], [kcT])
                else:
                    for ci, (n0, sz) in enumerate(chunks):
                        for half in range(2):
                            _mm(S, pB.t[0:sz, 0:64], h_.t[:, half, n0:n0 + sz], w2s.t[:, 1, half, :], half == 0,
                                half == 1, [w2s, h_], [pB])
                        _cp(S, 'act', vco.t[0:sz, ci, g, 0:64], pB.t[0:sz, 0:64], [pB], [vco])
        R = Ctx()
        R.identB = sb('b_identB', [128, 128], BF16, sem=True)
        R.epsT = sb('b_eps', [128, 1], F32)
        R.ss = sb('b_ss', [128, 4], F32)
        R.rt = sb('b_rt', [128, 4], F32)
        R.rstd = sb('b_rstd', [128, 4], F32)
        R.junk = sb('b_junk', [128, D], BF16)
        R.ntok = [sb('b_ntok%d' % i, [128, D], BF16) for i in range(2)]
        R.ptr = ps('b_ptr', [128, 1024], BF16)
        onesF = sb('b_onesF', [1, 128], F32)
        gmB = sb('b_gmB', [128, D], F32)
        wkv = sb('b_wkv', [128, 8, 2 * D], BF16, sem=True)
        memx = [sb('b_memx%d' % i, [128, D], F32, sem=True) for i in range(2)]
        memnT = sb('b_memnT', [128, 8, 256], BF16)
        kmT, vm = P.kmT, P.vm
        S.dma('pool', R.identB.sem, dict(out=R.identB.t[:, :], in_=T['identF']), writes=[R.identB])
        _memset(S, 'dve', R.epsT.t[:, :], EPS, [R.epsT])
        _memset(S, 'dve', onesF.t[:, :], 1.0, [onesF])
        S.dma('sp', wkv.sem, dict(out=wkv.t[:, :, :], in_=T['WB_xa_wkv'].rearrange('(k p) n -> p k n', p=128)), writes=[wkv])
        bcast_row(S, C, T['mem_norm_g'], gmB, pA, onesF)
        for mc in range(2):
            S.dma('sp', memx[mc].sem, dict(out=memx[mc].t[:, :], in_=T['mem'][mc * 128:(mc + 1) * 128, :]),
                  writes=[memx[mc]])
        for i, p_ in rms_T(S, R, memx, gmB):
            _cp(S, 'act', memnT.t[:, :, i * 128:(i + 1) * 128], p_.t[:, :].rearrange('p (k q) -> p k q', k=8),
                [p_], [memnT])
        for c in range(8):
            p = pA[c % 2]
            for kc in range(8):
                _mm(S, p.t[:, 0:256], wkv.t[:, kc, c * 128:(c + 1) * 128], memnT.t[:, kc, :], kc == 0, kc == 7,
                    [wkv, memnT], [p])
            _cp(S, 'dve', kmT.t[:, c, :], p.t[:, 0:256], [p], [kmT])
        for mc in range(2):
            for half in range(2):
                p = pA[(mc * 2 + half) % 2]
                for kc in range(8):
                    _mm(S, p.t[:, :], memnT.t[:, kc, mc * 128:(mc + 1) * 128],
                        wkv.t[:, kc, D + half * 512:D + (half + 1) * 512], kc == 0, kc == 7, [wkv, memnT], [p])
                _cp(S, 'dve', vm.t[:, mc, half * 512:(half + 1) * 512], p.t[:, :], [p], [vm])
        S.flush()


def phase_C(nc, S, Sq, T, kcT, vco, P):
    NT = Sq // 128
    NC, chunks = cmp_chunks(Sq)
    OFFS = 8 * (NT - 1)
    with ExitStack() as es:
        sb, ps = _mk(nc, es, S)
        identB = sb('c_identB', [128, 128], BF16, sem=True)
        KE = sb('c_KE', [128, 4, Sq], BF16, sem=True)
        KWt = sb('c_KW', [128, 4, Sq], BF16, sem=True)
        KWz = Buf()
        KEe = Buf()
        KEe_sem = S.newdma()
        VS = sb('c_VS', [128, NT, 4, 65], BF16, sem=True)
        VW = sb('c_VW', [128, NT, 4, 65], BF16, sem=True)
        biasT, Bband = P.biasT, P.Bband
        m512 = sb('c_m512', [128, 512], BF16, sem=True)
        SelW = sb('c_SelW', [32, OFFS + 128 * len(chunks)], BF16, sem=True)
        mulB = sb('c_mulB', [128, 128], F32, sem=True)
        addB = sb('c_addB', [128, 128], F32, sem=True)
        QM = [sb('c_QM%d' % i, [128, 4, 4, 128], BF16, sem=True) for i in range(2)]
        QMq = [Buf() for _ in range(2)]
        QMm = [[Buf() for _ in range(4)] for _ in range(2)]
        gt = [sb('c_gt%d' % i, [128, 16, 3], F32, sem=True) for i in range(3)]
        NE = 4
        Et = [sb('c_E%d' % i, [128, 512], BF16) for i in range(NE)]
        o1s = [[sb('c_o1s%d_%d' % (j, i), [128, 4, 128], F32) for i in range(4)] for j in range(2)]
        o2s = sb('c_o2s', [128, 4, 65], F32)
        o3s = [[sb('c_o3s%d_%d' % (j, i), [128, 4, 65], F32) for i in range(4)] for j in range(2)]
        coef1 = [[sb('c_coef1_%d_%d' % (j, i), [128, 4], F32) for i in range(4)] for j in range(2)]
        den = sb('c_den', [128, 4], F32)
        rden = sb('c_rden', [128, 4], F32)
        c2 = sb('c_c2', [128, 4], F32)
        c3 = sb('c_c3', [128, 4], F32)
        imp = sb('c_imp', [128, 64], F32)
        score = sb('c_score', [128, 64], F32)
        score2 = sb('c_score2', [128, 64], F32)
        m8a = sb('c_m8a', [128, 8], F32)
        m8b = sb('c_m8b', [128, 8], F32)
        nms = [sb('c_nm%d' % i, [128, 128], BF16) for i in range(4)]
        acc = sb('c_acc', [128, 4, 64], F32)
        ntk = sb('c_ntk', [128, 1024], BF16)
        nst = [sb('c_nst%d' % i, [128, 8, 128], BF16, sem=True) for i in range(2)]

        scp = [ps('c_sc%d' % i, [128, 512]) for i in range(3)]
        o1U = ps('c_o1U', [128, 512])
        o3p = ps('c_o3', [128, 4, 65])
        o2p = [ps('c_o2_%d' % i, [128, 4, 65]) for i in range(2)]
        ptr = ps('c_ptr', [128, 1024], BF16)

        S.dma('pool', identB.sem, dict(out=identB.t[:, :], in_=T['identF']), writes=[identB])
        S.dma('sp', KE.sem, dict(out=KE.t[0:64, :, :], in_=T['KS'].rearrange('g d s -> d g s')), writes=[KE])
        for g in range(4):
            for c0 in range(0, Sq, 2048):
                c1 = min(Sq, c0 + 2048)
                S.dma('pool', KEe_sem, dict(out=KE.t[64:128, g, c0:c1], in_=T['Econst'][:, c0:c1]), writes=[KEe])
        S.dma('sp', KWt.sem, dict(out=KWt.t[0:64, :, :], in_=T['KW'].rearrange('g d s -> d g s')), writes=[KWt])
        _memset(S, 'pool', KWt.t[64:128, :, :], 0.0, [KWz])
        for i_, q_ in enumerate(QM):
            _memset(S, 'pool', q_.t[:, :, :, :], 0.0, [q_, QMq[i_]] + QMm[i_])
        for k0 in range(0, NT, 8):
            k1 = min(NT, k0 + 8)
            S.dma('sp', VS.sem, dict(out=VS.t[:, k0:k1, :, :],
                                     in_=T['VS1'][k0 * 128:k1 * 128].rearrange('(k p) g d -> p k g d', p=128)),
                  writes=[VS])
            S.dma('sp', VW.sem, dict(out=VW.t[:, k0:k1, :, :],
                                     in_=T['VW1'][k0 * 128:k1 * 128].rearrange('(k p) g d -> p k g d', p=128)),
                  writes=[VW])
        S.dma('pool', m512.sem, dict(out=m512.t[:, :], in_=T['m512']), writes=[m512])
        S.dma('pool', SelW.sem, dict(out=SelW.t[:, :], in_=T['SelW']), writes=[SelW])
        S.dma('sp', mulB.sem, dict(out=mulB.t[:, :], in_=T['mulB']), writes=[mulB])
        S.dma('sp', addB.sem, dict(out=addB.t[:, :], in_=T['addB']), writes=[addB])
        for nm_ in nms:
            _memset(S, 'dve', nm_.t[:, :], 0.0, [nm_])

        def cw_steps(qt):
            out = []
            for g in range(4):
                cl = [(ci, n0, sz) for ci, (n0, sz) in enumerate(chunks) if n0 <= 8 * qt + 6]
                for i, (ci, n0, sz) in enumerate(cl):
                    out.append(dict(kind='cmp', qt=qt, g=g, ci=ci, n0=n0, sz=sz, first=i == 0, last=i == len(cl) - 1))
                kl = list(range(max(0, qt - 4), qt + 1))
                for i, kt in enumerate(kl):
                    out.append(dict(kind='win', qt=qt, g=g, kt=kt, sz=128, first=i == 0, last=i == len(kl) - 1))
            out[0]['loadq'] = qt
            out[-1]['flush_def'] = True
            return out

        def sel_steps(qt):
            out = []
            for g in range(4):
                for kt in range(qt + 1):
                    out.append(dict(kind='sel', qt=qt, g=g, kt=kt, sz=128, first=kt == 0, last=kt == qt))
            return out

        if os.environ.get('PIPE', '1') == '1':
            steps = cw_steps(0)
            for qt in range(NT):
                if qt + 1 < NT:
                    steps += cw_steps(qt + 1)
                steps += sel_steps(qt)
        else:
            steps = []
            for qt in range(NT):
                steps += cw_steps(qt) + sel_steps(qt)
        cnt = dict(sc=0, e=0, o2=0)

        def load_q(qt):
            sl = qt % 2
            qs = slice(qt * 128, (qt + 1) * 128)
            S.dma('sp', QM[sl].sem, dict(out=QM[sl].t[0:64, :, :, :].rearrange('d g h q -> d (g h) q'),
                                         in_=T['QT'][:, :, qs].rearrange('h d q -> d h q')), writes=[QMq[sl]])
            S.dma('sp', gt[qt % 3].sem, dict(out=gt[qt % 3].t[:, :, :].rearrange('p h b -> p (h b)'), in_=T['G'][qs, :]),
                  writes=[gt[qt % 3]])

        def emit_scores(st):
            qt, g, sz = st['qt'], st['g'], st['sz']
            sl = qt % 2
            sc = scp[cnt['sc'] % 3]
            cnt['sc'] += 1
            st['sc'] = sc
            qrow = QM[sl].t[:, g, :, :].rearrange('d h q -> d (h q)')
            if st['kind'] == 'cmp':
                n0 = st['n0']
                a = n0 - 8 * qt + OFFS
                _mm(S, sc.t[0:sz, :], kcT.t[:, g, n0:n0 + sz], qrow, True, False, [kcT, QMq[sl], QM[sl], QMm[sl][g]], [sc])
                _mm(S, sc.t[0:sz, :], SelW.t[0:32, a:a + sz],
                    Bband.t[0:32, 4 * g:4 * g + 4, :].rearrange('k h q -> k (h q)'), False, True, [SelW, Bband], [sc])
                return
            kt = st['kt']
            dl = (qt - kt)
            extra = None
            if dl in (0, 1):
                extra = (biasT.t[:, dl, 4 * g:4 * g + 4, :].rearrange('k h q -> k (h q)'), biasT)
            elif dl == 4 and st['kind'] == 'win':
                extra = (m512.t[:, :], m512)
            ks = slice(kt * 128, (kt + 1) * 128)
            if st['kind'] == 'win':
                _mm(S, sc.t[:, :], KWt.t[:, g, ks], qrow, True, extra is None, [KWt, KWz, QMq[sl], QM[sl], QMm[sl][g]], [sc])
            else:
                _mm(S, sc.t[:, :], KE.t[:, g, ks], QM[sl].t[:, g, :, :].rearrange('d h q -> d (h q)'), True,
                    extra is None, [KE, KEe, QMq[sl], QMm[sl][g]], [sc])
            if extra is not None:
                _mm(S, sc.t[:, :], identB.t[:, :], extra[0], False, True, [identB, extra[1]], [sc])

        def emit_exp(st):
            sz = st['sz']
            E = Et[cnt['e'] % NE]
            cnt['e'] += 1
            st['E'] = E
            _act(S, E.t[0:sz, :], st['sc'].t[0:sz, :], AF.Exp, [st['sc']], [E])

        def emit_pv(st):
            qt, g, sz, E = st['qt'], st['g'], st['sz'], st['E']
            if st['kind'] == 'cmp':
                for h in range(4):
                    _mm(S, o1U.t[:, h * 128:(h + 1) * 128], E.t[0:sz, h * 128:(h + 1) * 128], vco.t[0:sz, st['ci'], g, :],
                        st['first'] and h == 0, st['last'] and h == 3, [E, vco], [o1U], skip=True)
                if st['last']:
                    fin_cmp(qt, g)
                return
            kt = st['kt']
            if st['kind'] == 'win':
                op_, V = o3p, VW
            else:
                if st['first']:
                    st['o2'] = o2p[cnt['o2'] % 2]
                    cnt['o2'] += 1
                    cur['o2'] = st['o2']
                op_, V = cur['o2'], VS
            for h in range(4):
                _mm(S, op_.t[:, h, :], E.t[:, h * 128:(h + 1) * 128], V.t[:, kt, g, :], st['first'] and h == 0,
                    st['last'] and h == 3, [E, V], [op_], skip=True)
            if st['last']:
                if st['kind'] == 'win':
                    _cp(S, EV, o3s[qt % 2][g].t[:, :, :], o3p.t[:, :, :], [o3p], [o3s[qt % 2][g]])
                else:
                    fin_sel(qt, g, op_)

        cur = {}

        def fin_cmp(qt, g):
            sl = qt % 2
            nm = nms[g]
            o1 = o1s[sl][g]
            _cp(S, EV, o1.t[:, :, :], o1U.t[:, :].rearrange('p (h c) -> p h c', h=4), [o1U], [o1])
            S.op('dve', 'tensor_reduce', dict(out=den.t[:, :], in_=o1.t[:, :, 64:128], axis=AX.X, op=ALU.add), [o1], [den])
            _ts(S, 'dve', den.t[:, :], den.t[:, :], 1e-30, ALU.max, [den], [den])
            S.op('dve', 'reciprocal', dict(out=rden.t[:, :], in_=den.t[:, :]), [den], [rden])
            _ts(S, 'dve', imp.t[:, :], o1.t[:, 0, 64:128], rden.t[:, 0:1], ALU.mult, [o1, rden], [imp])
            for h in range(1, 4):
                _stt(S, imp.t[:, :], o1.t[:, h, 64:128], rden.t[:, h:h + 1], imp.t[:, :], ALU.mult, ALU.add,
                     [o1, rden, imp], [imp])
            a = 62 - 2 * qt
            _tt(S, 'dve', score.t[:, :], imp.t[:, :], mulB.t[:, a:a + 64], ALU.mult, [imp, mulB], [score])
            _tt(S, 'dve', score.t[:, :], score.t[:, :], addB.t[:, a:a + 64], ALU.add, [score, addB], [score])
            _memset(S, 'dve', score.t[:, 0:1], 50.0, [score])
            S.op('dve', 'max', dict(out=m8a.t[:, :], in_=score.t[:, :]), [score], [m8a])
            S.op('dve', 'match_replace', dict(out=score2.t[:, :], in_to_replace=m8a.t[:, :], in_values=score.t[:, :],
                                              imm_value=-1e9), [score, m8a], [score2])
            S.op('dve', 'max', dict(out=m8b.t[:, :], in_=score2.t[:, :]), [score2], [m8b])
            _ts(S, 'dve', nm.t[:, 64:128], score.t[:, :], m8b.t[:, 7:8], ALU.is_lt, [score, m8b], [nm], s2=MASKV,
                op1=ALU.mult)

            def part2(sl=sl, g=g, nm=nm):
                _tp(S, ptr.t[:, 0:128], nm.t[:, :], identB.t[:, :], [nm, identB], [ptr])
                for h in range(4):
                    _cp(S, 'dve', QM[sl].t[64:128, g, h, :], ptr.t[64:128, 0:128], [ptr], [QMm[sl][g]])
            deferred.append([DEFER, part2])
            _tt(S, 'dve', coef1[sl][g].t[:, :], rden.t[:, :], gt[qt % 3].t[:, 4 * g:4 * g + 4, 0], ALU.mult, [rden, gt[qt % 3]],
                [coef1[sl][g]])

        def fin_sel(qt, g, o2):
            sl = qt % 2
            _cp(S, EV, o2s.t[:, :, :], o2.t[:, :, :], [o2], [o2s])
            for (osrc, cf, br) in ((o2s, c2, 1), (o3s[sl][g], c3, 2)):
                _ts(S, 'dve', den.t[:, :], osrc.t[:, :, 64], 1e-30, ALU.max, [osrc], [den])
                S.op('dve', 'reciprocal', dict(out=rden.t[:, :], in_=den.t[:, :]), [den], [rden])
                _tt(S, 'dve', cf.t[:, :], rden.t[:, :], gt[qt % 3].t[:, 4 * g:4 * g + 4, br], ALU.mult, [rden, gt[qt % 3]], [cf])
            o1 = o1s[sl][g]
            for h in range(4):
                _ts(S, 'dve', acc.t[:, h, :], o1.t[:, h, 0:64], coef1[sl][g].t[:, h:h + 1], ALU.mult, [o1, coef1[sl][g]], [acc])
                _stt(S, acc.t[:, h, :], o3s[sl][g].t[:, h, 0:64], c3.t[:, h:h + 1], acc.t[:, h, :], ALU.mult, ALU.add,
                     [o3s[sl][g], c3, acc], [acc])
                c0 = (4 * g + h) * 64
                _stt(S, ntk.t[:, c0:c0 + 64], o2s.t[:, h, 0:64], c2.t[:, h:h + 1], acc.t[:, h, :], ALU.mult, ALU.add,
                     [o2s, c2, acc], [ntk])
            if g == 3:
                def part2(qt=qt):
                    for kc in range(8):
                        _tp(S, ptr.t[:, kc * 128:(kc + 1) * 128], ntk.t[:, kc * 128:(kc + 1) * 128], identB.t[:, :],
                            [ntk, identB], [ptr])
                    ns = nst[qt % 2]
                    _cp(S, 'dve', ns.t[:, :, :], ptr.t[:, :].rearrange('p (k q) -> p k q', k=8), [ptr], [ns])
                    S.dma('sp', ns.sem, dict(out=T['NSAT'][:, :, qt * 128:(qt + 1) * 128].rearrange('k p q -> p k q'),
                                             in_=ns.t[:, :, :]), reads=[ns])
                deferred.append([DEFER, part2])

        LOOK = int(os.environ.get('LOOK', '2'))
        EV = os.environ.get('EV', 'act')
        DEFER = int(os.environ.get('DEFER', '8'))
        deferred = []
        pend = []

        def run_deferred(force=False):
            while deferred and (force or deferred[0][0] <= 0):
                deferred.pop(0)[1]()

        for st in steps:
            if 'loadq' in st:
                load_q(st['loadq'])
            emit_scores(st)
            emit_exp(st)
            pend.append(st)
            if len(pend) > LOOK:
                emit_pv(pend.pop(0))
            for d_ in deferred:
                d_[0] -= 1
            run_deferred()
            if st.get('flush_def'):
                while pend:
                    emit_pv(pend.pop(0))
                run_deferred(force=True)
        while pend:
            emit_pv(pend.pop(0))
        run_deferred(force=True)
        S.flush()


def phase_D(nc, S, Sq, T, P):
    NSUP = Sq // 512
    with ExitStack() as es:
        sb, ps = _mk(nc, es, S)
        C = Ctx()
        R = Ctx()
        identB = sb('d_identB', [128, 128], BF16, sem=True)
        R.identB = identB
        onesB = sb('d_onesB', [128, 128], BF16)
        onesF = sb('d_onesF', [1, 128], F32)
        R.epsT = sb('d_eps', [128, 1], F32)
        wno = sb('d_wno', [128, 8, D], BF16, sem=True)
        wpw = sb('d_wpw', [128, 4, D], BF16, sem=True)
        wout = sb('d_wout', [128, 8, D], BF16, sem=True)
        wq = sb('d_wq', [128, 8, D], BF16, sem=True)
        wo = sb('d_wo', [128, 8, D], BF16, sem=True)
        g2B = sb('d_g2B', [128, D], F32)
        g3B = sb('d_g3B', [128, D], F32)
        kmT, vm = P.kmT, P.vm
        R.ptr = ps('d_ptr', [128, 1024], BF16)
        ptr = R.ptr
        pf = [ps('d_pf%d' % i, [128, 512]) for i in range(4)]
        ph = [ps('d_ph%d' % i, [128, 512]) for i in range(2)]
        R.ss = sb('d_ss', [128, 4], F32)
        R.rt = sb('d_rt', [128, 4], F32)
        R.rstd = sb('d_rstd', [128, 4], F32)
        R.ntok = [sb('d_ntok%d' % i, [128, D], BF16) for i in range(2)]

        S.dma('pool', identB.sem, dict(out=identB.t[:, :], in_=T['identF']), writes=[identB])
        _memset(S, 'dve', onesB.t[:, :], 1.0, [onesB])
        _memset(S, 'dve', onesF.t[:, :], 1.0, [onesF])
        _memset(S, 'dve', R.epsT.t[:, :], EPS, [R.epsT])
        for w_, nm_ in ((wno, 'nsa_w_o'), (wpw, 'conv_w_pw'), (wout, 'w_out'), (wq, 'xa_wq'), (wo, 'xa_wo')):
            S.dma('sp', w_.sem, dict(out=w_.t[:, :, :], in_=T['WB_' + nm_].rearrange('(k p) n -> p k n', p=128)),
                  writes=[w_])
        nsa_s = sb('d_nsa', [128, 8, 512], BF16, sem=True)
        hc_s = sb('d_hc', [128, 4, 512], BF16, sem=True)
        gm_s = sb('d_gm', [128, 16, 512], BF16, sem=True)
        xs = [sb('d_x%d' % i, [128, D], F32, sem=True) for i in range(4)]
        xin = [sb('d_xin%d' % i, [128, D], F32, sem=True) for i in range(2)]
        C.stage = xs[0]
        bcast_row(S, C, T['norm2_g'], g2B, ph, onesF)
        bcast_row(S, C, T['norm3_g'], g3B, ph, onesF)
        mrg = sb('d_mrg', [128, 8, 512], BF16)
        n2T = sb('d_n2T', [128, 8, 512], BF16)
        qxT = sb('d_qxT', [128, 8, 512], BF16)
        PT = sb('d_PT', [128, 2, 512], BF16)
        R.junk = Tl(PT.t[:, :, :].rearrange('p a b -> p (a b)'))
        R.junk.b = PT.b
        oTn = sb('d_oTn', [128, 8, 512], BF16)
        t1 = sb('d_t1', [128, 512], F32)
        t2 = sb('d_t2', [128, 512], F32)
        rdn = sb('d_rdn', [128, 512], F32)
        n3s = [sb('d_n3s%d' % i, [128, 8, 128], BF16, sem=True) for i in range(2)]
        npf = 0
        NTT = Sq // 128

        def load_acts(st):
            cs = slice(st * 512, (st + 1) * 512)
            S.dma('sp', nsa_s.sem, dict(out=nsa_s.t[:, :, :], in_=T['NSAT'][:, :, cs].rearrange('k p s -> p k s')),
                  writes=[nsa_s])
            S.dma('sp', hc_s.sem, dict(out=hc_s.t[:, :, :], in_=T['HC'][:, :, cs].rearrange('k p s -> p k s')),
                  writes=[hc_s])
            S.dma('sp', gm_s.sem, dict(out=gm_s.t[:, :, :], in_=T['GM'][:, :, cs].rearrange('k p s -> p k s')),
                  writes=[gm_s])

        def load_x(tt):
            if tt < NTT:
                S.dma('sp', xin[tt % 2].sem, dict(out=xin[tt % 2].t[:, :], in_=T['x'][tt * 128:(tt + 1) * 128, :]),
                      writes=[xin[tt % 2]])
        load_acts(0)
        load_x(0)
        load_x(1)
        for st in range(NSUP):
            cs = slice(st * 512, (st + 1) * 512)
            for f in range(8):
                pa = pf[npf % 4]
                pcv = pf[(npf + 1) % 4]
                npf += 2
                for kc in range(8):
                    _mm(S, pa.t[:, :], wno.t[:, kc, f * 128:(f + 1) * 128], nsa_s.t[:, kc, :], kc == 0, kc == 7,
                        [wno, nsa_s], [pa])
                for c in range(4):
                    _mm(S, pcv.t[:, :], wpw.t[:, c, f * 128:(f + 1) * 128], hc_s.t[:, c, :], c == 0, c == 3,
                        [wpw, hc_s], [pcv])
                _tt(S, 'dve', t1.t[:, :], pa.t[:, :], gm_s.t[:, 8 + f, :], ALU.mult, [pa, gm_s], [t1])
                _tt(S, 'dve', t2.t[:, :], pcv.t[:, :], gm_s.t[:, f, :], ALU.mult, [pcv, gm_s], [t2])
                _tt(S, 'pool', mrg.t[:, f, :], t1.t[:, :], t2.t[:, :], ALU.add, [t1, t2], [mrg])
            if st + 1 < NSUP:
                load_acts(st + 1)
            for t in range(4):
                tt = st * 4 + t
                for half in range(2):
                    p = ph[half]
                    for f in range(8):
                        _mm(S, p.t[:, :], mrg.t[:, f, t * 128:(t + 1) * 128], wout.t[:, f, half * 512:(half + 1) * 512],
                            f == 0, f == 7, [mrg, wout], [p])
                    _tt(S, 'dve', xs[t].t[:, half * 512:(half + 1) * 512], p.t[:, :],
                        xin[tt % 2].t[:, half * 512:(half + 1) * 512], ALU.add, [p, xin[tt % 2]], [xs[t]])
                load_x(tt + 2)
            for i, p_ in rms_T(S, R, xs, g2B):
                _cp(S, 'act', n2T.t[:, :, i * 128:(i + 1) * 128], p_.t[:, :].rearrange('p (k q) -> p k q', k=8),
                    [p_], [n2T])
            for c in range(8):
                p = pf[npf % 4]
                npf += 1
                for kc in range(8):
                    _mm(S, p.t[:, :], wq.t[:, kc, c * 128:(c + 1) * 128], n2T.t[:, kc, :], kc == 0, kc == 7,
                        [wq, n2T], [p])
                _act(S, qxT.t[:, c, :], p.t[:, :], AF.Copy, [p], [qxT], scale=1.0 / 16.0)
            for hd in range(4):
                for mc in range(2):
                    p = pf[npf % 4]
                    npf += 1
                    for dc in range(2):
                        _mm(S, p.t[:, :], kmT.t[:, hd * 2 + dc, mc * 128:(mc + 1) * 128], qxT.t[:, hd * 2 + dc, :],
                            dc == 0, dc == 1, [kmT, qxT], [p])
                    _act(S, PT.t[:, mc, :], p.t[:, :], AF.Exp, [p], [PT])
                pd = pf[npf % 4]
                npf += 1
                for mc in range(2):
                    _mm(S, pd.t[:, :], onesB.t[:, :], PT.t[:, mc, :], mc == 0, mc == 1, [onesB, PT], [pd])
                S.op('dve', 'reciprocal', dict(out=rdn.t[:, :], in_=pd.t[:, :]), [pd], [rdn])
                for dc in range(2):
                    po = pf[npf % 4]
                    npf += 1
                    for mc in range(2):
                        _mm(S, po.t[:, :], vm.t[:, mc, hd * 256 + dc * 128:hd * 256 + (dc + 1) * 128], PT.t[:, mc, :],
                            mc == 0, mc == 1, [vm, PT], [po])
                    _tt(S, 'dve', oTn.t[:, hd * 2 + dc, :], po.t[:, :], rdn.t[:, :], ALU.mult, [po, rdn], [oTn])
            for t in range(4):
                tt = st * 4 + t
                for half in range(2):
                    p = ph[half]
                    for c in range(8):
                        _mm(S, p.t[:, :], oTn.t[:, c, t * 128:(t + 1) * 128], wo.t[:, c, half * 512:(half + 1) * 512],
                            c == 0, c == 7, [oTn, wo], [p])
                    _tt(S, 'dve', xs[t].t[:, half * 512:(half + 1) * 512], p.t[:, :],
                        xs[t].t[:, half * 512:(half + 1) * 512], ALU.add, [p, xs[t]], [xs[t]])
                S.dma('sp', xs[t].sem, dict(out=T['H2'][tt * 128:(tt + 1) * 128, :], in_=xs[t].t[:, :]), reads=[xs[t]])
            for i, p_ in rms_T(S, R, xs, g3B):
                tt = st * 4 + i
                ns = n3s[i % 2]
                _cp(S, 'act', ns.t[:, :, :], p_.t[:, :].rearrange('p (k q) -> p k q', k=8), [p_], [ns])
                S.dma('sp', ns.sem, dict(out=T['N3T'][:, :, tt * 128:(tt + 1) * 128].rearrange('k p q -> p k q'),
                                         in_=ns.t[:, :, :]), reads=[ns])
        S.flush()


def phase_E(nc, S, Sq, T, P):
    NSUP = Sq // 512
    NP = D_FF // 128
    with ExitStack() as es:
        sb, ps = _mk(nc, es, S)
        C = Ctx()
        identF = sb('e_identF', [128, 128], F32, sem=True)
        C.identF = identF
        C.stage = sb('e_stage', [32, 1024], F32, sem=True)
        pst = ps('e_pst', [128, 48])
        C.pst = pst
        wupb = [sb('e_wup%d' % i, [128, 8, 512], BF16, sem=True) for i in range(11)]
        wdn = sb('e_wdn', [128, NP, D], BF16, sem=True)
        fw = sb('e_fw', [128, 2 * NP, 3], F32)
        fb = sb('e_fb', [128, 2 * NP, 1], F32)
        fgB = sb('e_fgB', [128, D], F32)
        onesF = sb('e_onesF', [1, 128], F32)
        epsT = sb('e_eps', [128, 1], F32)
        halo = sb('e_halo', [128, 2 * NP, 2], F32)
        n3 = sb('e_n3', [128, 8, 512], BF16, sem=True)
        actT = sb('e_actT', [128, NP, 512], BF16)
        ub = [sb('e_ub%d' % i, [128, 514], F32) for i in range(3)]
        tb = [sb('e_tb%d' % i, [128, 512], F32) for i in range(3)]
        sgl = sb('e_sgl', [128, 512], F32)
        h2 = [sb('e_h2_%d' % i, [128, D], F32, sem=True) for i in range(2)]
        junk = sb('e_junk', [128, D], BF16)
        ss = sb('e_ss', [128, 1], F32)
        rt = sb('e_rt', [128, 1], F32)
        rstd = sb('e_rstd', [128, 1], F32)
        pu = [ps('e_pu%d' % i, [128, 512]) for i in range(3)]
        pd = [ps('e_pd%d' % i, [128, 512]) for i in range(2)]

        S.dma('sp', identF.sem, dict(out=identF.t[:, :], in_=T['identF']), writes=[identF])
        _memset(S, 'dve', onesF.t[:, :], 1.0, [onesF])
        _memset(S, 'dve', epsT.t[:, :], EPS, [epsT])
        _memset(S, 'dve', halo.t[:, :, :], 0.0, [halo])
        order = []
        for j in range(NP):
            for c in (j, j + NP):
                if c // 4 not in order:
                    order.append(c // 4)
        for bi in order[:2]:
            S.dma('sp', wupb[bi].sem, dict(out=wupb[bi].t[:, :, :],
                                           in_=T['WB_ffn_w_up'][:, bi * 512:(bi + 1) * 512].rearrange('(k p) n -> p k n', p=128)),
                  writes=[wupb[bi]])
        for blk in range(0, 2 * D_FF, 1024):
            w = min(1024, 2 * D_FF - blk)
            c0 = blk // 128
            cols_from_rows(S, C, T['ffn_dw_w'][:, blk:blk + w], 3, w, fw, lambda ch, c0=c0: fw.t[:, c0 + ch, :])
            cols_from_rows(S, C, T['ffn_dw_b'][:, blk:blk + w], 1, w, fb, lambda ch, c0=c0: fb.t[:, c0 + ch, :])
        for bi in order[2:]:
            S.dma('sp', wupb[bi].sem, dict(out=wupb[bi].t[:, :, :],
                                           in_=T['WB_ffn_w_up'][:, bi * 512:(bi + 1) * 512].rearrange('(k p) n -> p k n', p=128)),
                  writes=[wupb[bi]])
        S.dma('sp', wdn.sem, dict(out=wdn.t[:, :, :], in_=T['WB_ffn_w_down'].rearrange('(k p) n -> p k n', p=128)),
              writes=[wdn])
        S.dma('sp', C.stage.sem, dict(out=C.stage.t[0:1, 0:1024], in_=T['final_g']), writes=[C.stage])
        for half in range(2):
            _mm(S, pd[half].t[:, :], onesF.t[0:1, :], C.stage.t[0:1, half * 512:(half + 1) * 512], True, True,
                [onesF, C.stage], [pd[half]])
            _cp(S, 'dve', fgB.t[:, half * 512:(half + 1) * 512], pd[half].t[:, :], [pd[half]], [fgB])

        npu = 0
        nub = 0
        for st in range(NSUP):
            cs = slice(st * 512, (st + 1) * 512)
            S.dma('sp', n3.sem, dict(out=n3.t[:, :, :], in_=T['N3T'][:, :, cs].rearrange('k p s -> p k s')), writes=[n3])
            for j in range(NP):
                tpair = []
                for c in (j, j + NP):
                    p = pu[npu % 3]
                    npu += 1
                    u_ = ub[nub % 3]
                    t_ = tb[nub % 3]
                    nub += 1
                    for kc in range(8):
                        _mm(S, p.t[:, :], wupb[c // 4].t[:, kc, (c % 4) * 128:(c % 4 + 1) * 128], n3.t[:, kc, :], kc == 0, kc == 7,
                            [wupb[c // 4], n3], [p])
                    _cp(S, 'act', u_.t[:, 2:514], p.t[:, :], [p], [u_])
                    _cp(S, 'pool', u_.t[:, 0:2], halo.t[:, c, :], [halo], [u_])
                    _act(S, t_.t[:, :], p.t[:, :], AF.Identity, [p, fw, fb], [t_], scale=fw.t[:, c, 2:3], bias=fb.t[:, c, :])
                    _stt(S, t_.t[:, :], u_.t[:, 0:512], fw.t[:, c, 0:1], t_.t[:, :], ALU.mult, ALU.add, [u_, fw, t_], [t_])
                    _stt(S, t_.t[:, :], u_.t[:, 1:513], fw.t[:, c, 1:2], t_.t[:, :], ALU.mult, ALU.add, [u_, fw, t_], [t_])
                    _cp(S, 'pool', halo.t[:, c, :], u_.t[:, 512:514], [u_], [halo])
                    tpair.append(t_)
                _act(S, sgl.t[:, :], tpair[0].t[:, :], AF.Silu, [tpair[0]], [sgl])
                _tt(S, 'dve', actT.t[:, j, :], sgl.t[:, :], tpair[1].t[:, :], ALU.mult, [sgl, tpair[1]], [actT])
            for t in range(4):
                tt = st * 4 + t
                hb = h2[tt % 2]
                ob = hb
                S.dma('sp', hb.sem, dict(out=hb.t[:, :], in_=T['H2'][tt * 128:(tt + 1) * 128, :]), writes=[hb])
                for half in range(2):
                    p = pd[half]
                    for j in range(NP):
                        _mm(S, p.t[:, :], actT.t[:, j, t * 128:(t + 1) * 128], wdn.t[:, j, half * 512:(half + 1) * 512],
                            j == 0, j == NP - 1, [actT, wdn], [p])
                    _tt(S, 'dve', hb.t[:, half * 512:(half + 1) * 512], p.t[:, :], hb.t[:, half * 512:(half + 1) * 512],
                        ALU.add, [p, hb], [hb])
                _stt(S, junk.t[:, :], hb.t[:, :], 1.0, hb.t[:, :], ALU.mult, ALU.mult, [hb], [junk, ss], accum_out=ss.t[:, 0:1])
                _act(S, rt.t[:, :], ss.t[:, :], AF.Sqrt, [ss, epsT], [rt], scale=1.0 / D, bias=epsT.t[:, :])
                S.op('dve', 'reciprocal', dict(out=rstd.t[:, :], in_=rt.t[:, :]), [rt], [rstd])
                _stt(S, ob.t[:, :], hb.t[:, :], rstd.t[:, 0:1], fgB.t[:, :], ALU.mult, ALU.mult, [hb, rstd, fgB], [hb])
                S.dma('sp', hb.sem, dict(out=T['y'][tt * 128:(tt + 1) * 128, :], in_=hb.t[:, :]), reads=[hb])
        S.flush()


def t5_bucket_np(d):
    n = np.maximum(d, 0)
    nf = np.maximum(n, 1).astype(np.float32)
    large = 16 + (np.log(nf / np.float32(16)) / np.float32(np.log(8.0)) * np.float32(16)).astype(np.int32)
    large = np.minimum(large, 31)
    return np.where(n < 16, n, large)


def scratch_spec(Sq):
    return {
        'QT': ([16, 64, Sq], BF16), 'KC': ([4, 64, Sq], BF16), 'VC': ([4, 64, Sq], BF16),
        'KS': ([4, 64, Sq], BF16), 'KW': ([4, 64, Sq], BF16), 'GM': ([16, 128, Sq], BF16),
        'HC': ([4, 128, Sq], BF16), 'VS1': ([Sq, 4, 65], BF16), 'VW1': ([Sq, 4, 65], BF16),
        'G': ([Sq, 48], F32), 'NSAT': ([8, 128, Sq], BF16), 'H2': ([Sq, D], F32), 'N3T': ([8, 128, Sq], BF16),
    }


INPUT_SHAPES = {
    'norm1_g': [1, D], 'w_in': [D, IN_W], 'conv_dw_w': [31, 512], 'conv_dw_b': [1, 512],
    'conv_ln_g': [1, 512], 'conv_ln_b': [1, 512], 'conv_w_pw': [512, D],
    'cmp_pe': [2, 32, 64], 'cmp_w1': [2, 2048, 256], 'cmp_b1': [1, 512], 'cmp_w2': [2, 256, 64],
    'nsa_w_o': [D, D], 'w_out': [D, D], 'norm2_g': [1, D], 'mem_norm_g': [1, D],
    'xa_wq': [D, D], 'xa_wkv': [D, 2 * D], 'xa_wo': [D, D], 'norm3_g': [1, D],
    'ffn_w_up': [D, 2 * D_FF], 'ffn_dw_w': [3, 2 * D_FF], 'ffn_dw_b': [1, 2 * D_FF],
    'ffn_w_down': [D_FF, D], 'final_g': [1, D],
}


def const_shapes(Sq):
    NT = Sq // 128
    NC, chunks = cmp_chunks(Sq)
    return {
        'identF': [128, 128], 'Econst': [64, Sq], 'm512': [128, 512],
        'SelW': [32, 8 * (NT - 1) + 128 * len(chunks)], 'mulB': [128, 128], 'addB': [128, 128],
        'ovl': [128 * len(chunks), 64],
        'tz1': [2, 128, 16, 128], 'tz31': [2, 128, 16, 128], 'tzm': [2, 128, 16, 128],
        'cb1': [32, 16, 128], 'cb31': [32, 16, 128], 'cbm': [32, 16, 128],
    }


def build(Sq, debug=(), phases='ABCDE'):
    nc = bass.Bass("TRN2", target_bir_lowering=False)
    T = {}
    T['x'] = nc.dram_tensor('x', [Sq, D], F32, kind='ExternalInput').ap()
    T['mem'] = nc.dram_tensor('mem', [MEM, D], F32, kind='ExternalInput').ap()
    for k, shp in list(INPUT_SHAPES.items()) + list(const_shapes(Sq).items()):
        T[k] = nc.dram_tensor(k, shp, F32, kind='ExternalInput').ap()
    for k, (shp, dt) in scratch_spec(Sq).items():
        kind = 'ExternalOutput' if k in debug else 'Internal'
        T[k] = nc.dram_tensor(k, shp, dt, kind=kind).ap()
    T['y'] = nc.dram_tensor('y', [Sq, D], F32, kind='ExternalOutput').ap()
    for k, (K_, N_) in WB_SPEC.items():
        T['WB_' + k] = nc.dram_tensor('WB_' + k, [K_, N_], BF16, kind='Internal').ap()
    NC, chunks = cmp_chunks(Sq)
    with ExitStack() as es:
        S = Sched(nc, es)
        if 'A' in phases:
            phase_A(nc, S, Sq, T)
        NCH = len(chunks)
        with ExitStack() as es1:
            P = Ctx()
            P.kmT = Tl(es1.enter_context(nc.sbuf_tensor('p_kmT', [128, 8, 256], BF16)))
            P.vm = Tl(es1.enter_context(nc.sbuf_tensor('p_vm', [128, 2, D], BF16)))
            with ExitStack() as es2:
                kcT = Tl(es2.enter_context(nc.sbuf_tensor('kcT', [128, 4, 128 * NCH], BF16)))
                vco = Tl(es2.enter_context(nc.sbuf_tensor('vco', [128, NCH, 4, 128], BF16)))
                P.biasT = Tl(es2.enter_context(nc.sbuf_tensor('p_biasT', [128, 2, 16, 128], BF16)))
                P.Bband = Tl(es2.enter_context(nc.sbuf_tensor('p_Bband', [32, 16, 128], BF16)))
                if 'B' in phases:
                    phase_B(nc, S, Sq, T, kcT, vco, P)
                if 'C' in phases:
                    phase_C(nc, S, Sq, T, kcT, vco, P)
            if 'D' in phases:
                phase_D(nc, S, Sq, T, P)
        if 'E' in phases:
            phase_E(nc, S, Sq, T, None)
    return nc


def host_consts(rel_bias, Sq):
    NT = Sq // 128
    NC, chunks = cmp_chunks(Sq)
    rb = np.asarray(rel_bias, dtype=np.float32)
    c = {}
    c['identF'] = np.eye(128, dtype=np.float32)
    E = np.zeros((64, Sq), np.float32)
    kk = np.arange(Sq)
    valid = kk // 64 < 64
    E[(kk // 64)[valid], kk[valid]] = 1.0
    c['Econst'] = E
    ki = np.arange(128)[:, None]
    qi = np.arange(128)[None, :]
    c['m512'] = np.tile(np.where(qi >= ki, MASKV, 0.0).astype(np.float32), (1, 4))
    OFFS = 8 * (NT - 1)
    W = np.zeros((32, OFFS + 128 * len(chunks)), np.float32)
    m = np.arange(W.shape[1]) - OFFS
    for r in range(17):
        W[r, m == r - 10] = 1.0
    W[31, m >= 7] = 1.0
    c['SelW'] = W
    r = np.arange(128)[None, :] - 62
    p = np.arange(128)[:, None]
    hi = (p >= 64).astype(np.int64)
    rel = r - hi
    free = rel <= -2
    forced = (rel == -1) | (rel == 0)
    c['mulB'] = np.where(free, 1.0, 0.0).astype(np.float32) * np.ones((128, 1), np.float32)
    c['addB'] = np.where(free, 0.0, np.where(forced, 10.0 + (rel + 2), -1.0 - 0.001 * np.maximum(rel, 0))).astype(np.float32)
    n = np.arange(128 * len(chunks))[:, None]
    j = np.arange(64)[None, :]
    ov = np.clip(np.minimum(16 * n + 32, 64 * j + 64) - np.maximum(16 * n, 64 * j), 0, None).astype(np.float32) / 32.0
    ov[NC:] = 0.0
    c['ovl'] = ov.astype(np.float32)
    tz1 = np.zeros((2, 128, 16, 128), np.float32)
    tz31 = np.zeros_like(tz1)
    tzm = np.zeros_like(tz1)
    for dl in range(2):
        d = dl * 128 + qi - ki
        ok = d >= 0
        g1 = rb[t5_bucket_np(d)]
        g31 = rb[np.full_like(d, 31)]
        tz1[dl] = np.where(ok[:, :, None], g1, 0.0).transpose(0, 2, 1)
        tz31[dl] = np.where(ok[:, :, None], g31, 0.0).transpose(0, 2, 1)
        tzm[dl] = np.where(ok[:, :, None], 0.0, MASKV).transpose(0, 2, 1) * np.ones((1, 16, 1), np.float32)
    c['tz1'], c['tz31'], c['tzm'] = tz1, tz31, tzm
    cb1 = np.zeros((32, 16, 128), np.float32)
    cb31 = np.zeros_like(cb1)
    cbm = np.zeros_like(cb1)
    rr = np.arange(17)[:, None]
    d1 = np.arange(128)[None, :] - 16 * (rr - 10) - 31
    ok = d1 >= 0
    cb1[:17] = np.where(ok[:, :, None], rb[t5_bucket_np(d1)], 0.0).transpose(0, 2, 1)
    cb31[:17] = np.where(ok[:, :, None], rb[np.full_like(d1, 31)], 0.0).transpose(0, 2, 1)
    cbm[:17] = (np.where(ok, 0.0, MASKV)[:, None, :] * np.ones((1, 16, 1))).astype(np.float32)
    cbm[31] = MASKV
    c['cb1'], c['cb31'], c['cbm'] = cb1, cb31, cbm
    return {k: np.ascontiguousarray(v, dtype=np.float32) for k, v in c.items()}


def host_inputs(inp, Sq):
    shared = {}
    for k, shp in INPUT_SHAPES.items():
        shared[k] = np.ascontiguousarray(np.asarray(inp[k], dtype=np.float32).reshape(shp))
    shared.update(host_consts(inp['rel_bias'], Sq))
    return shared


def kernel(**inp):
    x = np.asarray(inp['x'], dtype=np.float32)
    mem = np.asarray(inp['mem'], dtype=np.float32)
    B, Sq, _ = x.shape
    nc = build(Sq)
    shared = host_inputs(inp, Sq)
    in_maps = []
    for b in range(B):
        m = dict(shared)
        m['x'] = np.ascontiguousarray(x[b])
        m['mem'] = np.ascontiguousarray(mem[b])
        in_maps.append(m)
    res = run_bass_kernel_spmd(nc, in_maps, core_ids=list(range(B)))
    return np.stack([np.asarray(r['y'], dtype=np.float32) for r in res.results], axis=0)
```

```python
import os
import numpy as np
from contextlib import ExitStack
import concourse.bass as bass
import concourse.mybir as mybir
from concourse.bass_utils import run_bass_kernel_spmd

F32 = mybir.dt.float32
BF16 = mybir.dt.bfloat16
AF = mybir.ActivationFunctionType
ALU = mybir.AluOpType
AX = mybir.AxisListType

D = 1024
SEQ = 4096
MEM = 256
IN_W = 5680
D_FF = 2816
MASKV = -30000.0
EPS = 1e-6

ENGS = ['pe', 'act', 'dve', 'pool', 'sp']


class Buf:
    __slots__ = ('w', 'r')

    def __init__(self):
        self.w = None
        self.r = {}


class Tl:
    def __init__(self, t, sem=None):
        self.t = t
        self.b = Buf()
        self.sem = sem


def _b(x):
    return x.b if isinstance(x, Tl) else x


class Sched:
    def __init__(self, nc, es):
        self.nc = nc
        self.es = es
        self.q = {e: [] for e in ENGS}
        self.sem = {}
        self.cnt = {}
        self.known = {e: {} for e in ENGS}
        self.ndma = 0

    def semh(self, key):
        if key not in self.sem:
            self.sem[key] = self.es.enter_context(self.nc.semaphore('s_' + key))
            self.cnt[key] = 0
        return self.sem[key]

    def newdma(self, name=None):
        self.ndma += 1
        key = 'd%d' % self.ndma
        self.semh(key)
        return key

    def _deps(self, eng, reads, writes):
        need = {}
        for b in reads:
            b = _b(b)
            if b.w:
                k, v = b.w
                need[k] = max(need.get(k, 0), v)
        for b in writes:
            b = _b(b)
            if b.w:
                k, v = b.w
                need[k] = max(need.get(k, 0), v)
            for k, v in b.r.items():
                need[k] = max(need.get(k, 0), v)
        out = []
        kn = self.known[eng]
        for k, v in need.items():
            if eng == 'pe' and k == 'pe':
                continue
            if kn.get(k, 0) < v:
                kn[k] = v
                out.append((k, v))
        return out

    def _post(self, key, v, reads, writes):
        for b in reads:
            b = _b(b)
            b.r[key] = max(b.r.get(key, 0), v)
        for b in writes:
            b = _b(b)
            b.w = (key, v)
            b.r = {}

    def op(self, eng, meth, kw, reads=(), writes=()):
        self.semh(eng)
        waits = self._deps(eng, reads, writes)
        self.cnt[eng] += 1
        v = self.cnt[eng]
        self.q[eng].append((waits, meth, kw, eng, 1))
        self._post(eng, v, reads, writes)

    def dma(self, eng, semkey, kw, reads=(), writes=()):
        self.semh(semkey)
        waits = self._deps(eng, reads, writes)
        self.cnt[semkey] += 16
        v = self.cnt[semkey]
        self.q[eng].append((waits, 'dma_start', kw, semkey, 16))
        self._post(semkey, v, reads, writes)

    def barrier(self):
        for e in ENGS:
            kn = self.known[e]
            waits = []
            for k, v in self.cnt.items():
                if v > 0 and kn.get(k, 0) < v:
                    kn[k] = v
                    waits.append((k, v))
            if waits:
                self.q[e].append((waits, None, None, None, 0))

    def flush(self):
        self.barrier()
        nc = self.nc
        q = self.q
        sem = self.sem

        def run(engobj, items):
            for waits, meth, kw, key, inc in items:
                for k, v in waits:
                    engobj.wait_ge(sem[k], v)
                if meth is not None:
                    getattr(engobj, meth)(**kw).then_inc(sem[key], inc)

        with nc.Block() as block:
            @block.tensor
            def _(e):
                run(e, q['pe'])

            @block.scalar
            def _(e):
                run(e, q['act'])

            @block.vector
            def _(e):
                run(e, q['dve'])

            @block.gpsimd
            def _(e):
                run(e, q['pool'])

            @block.sync
            def _(e):
                run(e, q['sp'])
        self.q = {e: [] for e in ENGS}


class Ctx:
    pass


def _mm(S, out, lhsT, rhs, start, stop, reads, writes, skip=False):
    kw = dict(out=out, lhsT=lhsT, rhs=rhs, start=start, stop=stop)
    if skip:
        kw['skip_group_check'] = True
    S.op('pe', 'matmul', kw, reads, writes)


def _tp(S, out, in_, ident, reads, writes):
    S.op('pe', 'transpose', dict(out=out, in_=in_, identity=ident), reads, writes)


def _act(S, out, in_, func, reads, writes, **kw):
    S.op('act', 'activation', dict(out=out, in_=in_, func=func, **kw), reads, writes)


def _tt(S, eng, out, in0, in1, op, reads, writes):
    S.op(eng, 'tensor_tensor', dict(out=out, in0=in0, in1=in1, op=op), reads, writes)


def _ts(S, eng, out, in0, s1, op0, reads, writes, s2=None, op1=None):
    kw = dict(out=out, in0=in0, scalar1=s1, scalar2=s2, op0=op0)
    if op1 is not None:
        kw['op1'] = op1
    S.op(eng, 'tensor_scalar', kw, reads, writes)


def _stt(S, out, in0, scalar, in1, op0, op1, reads, writes, accum_out=None):
    kw = dict(out=out, in0=in0, scalar=scalar, in1=in1, op0=op0, op1=op1)
    if accum_out is not None:
        kw['accum_out'] = accum_out
    S.op('dve', 'scalar_tensor_tensor', kw, reads, writes)


def _cp(S, eng, out, in_, reads, writes):
    if eng == 'act':
        S.op('act', 'copy', dict(out=out, in_=in_), reads, writes)
    else:
        S.op(eng, 'tensor_copy', dict(out=out, in_=in_), reads, writes)


def _memset(S, eng, ap, val, writes):
    S.op(eng, 'memset', dict(ap=ap, constant=val), (), writes)


def load_w_cast(S, dst_tl, dst_ap_fn, src, kc_n, ncols, rows_per=128):
    for kc in range(kc_n):
        c0 = 0
        while c0 < ncols:
            c1 = min(ncols, c0 + 2048)
            S.dma('pool', dst_tl.sem, dict(out=dst_ap_fn(kc, c0, c1),
                                           in_=src[kc * 128:(kc + 1) * 128, c0:c1]),
                  writes=[dst_tl])
            c0 = c1


def cols_from_rows(S, C, src_rows, R, ncol, dst_tl, dst_fn):
    stage = C.stage
    assert ncol <= stage.t.shape[1] and R <= 32
    S.dma('sp', stage.sem, dict(out=stage.t[0:R, 0:ncol], in_=src_rows), writes=[stage])
    for ch in range(ncol // 128):
        _tp(S, C.pst.t[:, 0:R], stage.t[0:R, ch * 128:(ch + 1) * 128], C.identF.t[0:R, 0:R],
            [stage, C.identF], [C.pst])
        _cp(S, 'dve', dst_fn(ch), C.pst.t[:, 0:R], [C.pst], [dst_tl])


def phase_A(nc, S, Sq, T):
    NSUP = Sq // 512
    with ExitStack() as es:
        def sb(name, shape, dt, sem=False):
            t = es.enter_context(nc.sbuf_tensor(name, shape, dt))
            return Tl(t, S.newdma() if sem else None)

        def ps(name, shape, dt=F32):
            return Tl(es.enter_context(nc.psum_tensor(name, shape, dt)))

        C = Ctx()
        win = sb('a_win', [128, 8, IN_W], BF16, sem=True)
        dg = sb('a_dg', [128, 4, 31, 128], BF16)
        identF = sb('a_identF', [128, 128], F32, sem=True)
        identB = sb('a_identB', [128, 128], BF16, sem=True)
        onesF = sb('a_onesF', [128, 128], F32)
        C.identF = identF
        C.stage = sb('a_stage', [32, 1024], F32, sem=True)
        dwT = sb('a_dwT', [128, 4, 31], F32)
        dwb = sb('a_dwb', [128, 4, 1], F32)
        lng = sb('a_lng', [128, 4, 1], F32)
        lnb = sb('a_lnb', [128, 4, 1], F32)
        xs = [sb('a_x%d' % i, [128, D], F32, sem=True) for i in range(4)]
        ss = sb('a_ss', [128, 4], F32)
        rt = sb('a_rt', [128, 4], F32)
        rstd = sb('a_rstd', [128, 4], F32)
        junk = sb('a_junk', [128, D], BF16)
        ntok = [sb('a_ntok%d' % i, [128, D], BF16) for i in range(2)]
        nT = sb('a_nT', [128, 8, 512], BF16)
        hglu = sb('a_hglu', [128, 4, 542], BF16)
        stg = [sb('a_stg%d' % i, [128, 512], BF16, sem=True) for i in range(6)]
        sg = [sb('a_sg%d' % i, [128, 512], F32) for i in range(2)]
        ycs = sb('a_ycs', [128, 4, 512], F32)
        ysq = sb('a_ysq', [128, 4, 512], F32)
        mean = sb('a_mean', [128, 512], F32)
        msq = sb('a_msq', [128, 512], F32)
        rs = sb('a_rs', [128, 512], F32)
        dtl = [sb('a_d%d' % i, [128, 512], F32) for i in range(2)]
        vstg = [sb('a_vstg%d' % i, [128, 2, 4, 65], BF16, sem=True) for i in range(2)]
        gstg = [sb('a_gstg%d' % i, [128, 48], F32, sem=True) for i in range(2)]
        epsT = sb('a_eps', [128, 1], F32)

        ptr = ps('a_ptr', [128, 1024], BF16)
        pf = [ps('a_pf%d' % i, [128, 512]) for i in range(3)]
        pv = ps('a_pv', [128, 512])
        pg = ps('a_pg', [128, 48])
        C.pst = pg
        pc = [ps('a_pc%d' % i, [128, 512]) for i in range(2)]

        S.dma('sp', identF.sem, dict(out=identF.t[:, :], in_=T['identF']), writes=[identF])
        S.dma('pool', identB.sem, dict(out=identB.t[:, :], in_=T['identF']), writes=[identB])
        _memset(S, 'dve', onesF.t[:, :], 1.0 / 512.0, [onesF])
        _memset(S, 'dve', epsT.t[:, :], EPS, [epsT])
        _memset(S, 'dve', hglu.t[:, :, :], 0.0, [hglu])
        for v in vstg:
            _memset(S, 'dve', v.t[:, :, :, :], 1.0, [v])
        WBLK = [(2816, 3632), (0, 1024), (1024, 2048), (2048, 2816), (3632, 4656), (4656, 5680)]
        wblk = [Tl(None, S.newdma()) for _ in WBLK]

        def wb(c0):
            for i_, (a0, a1) in enumerate(WBLK):
                if a0 <= c0 < a1:
                    return wblk[i_]
            raise ValueError(c0)
        for i_, (a0, a1) in enumerate(WBLK):
            for kc in range(8):
                S.dma('pool', wblk[i_].sem, dict(out=win.t[:, kc, a0:a1], in_=T['w_in'][kc * 128:(kc + 1) * 128, a0:a1]),
                      writes=[wblk[i_]])
        precast_weights(S, T)
        cols_from_rows(S, C, T['conv_dw_w'], 31, 512, dwT, lambda ch: dwT.t[:, ch, :])
        cols_from_rows(S, C, T['conv_dw_b'], 1, 512, dwb, lambda ch: dwb.t[:, ch, :])
        cols_from_rows(S, C, T['conv_ln_g'], 1, 512, lng, lambda ch: lng.t[:, ch, :])
        cols_from_rows(S, C, T['conv_ln_b'], 1, 512, lnb, lambda ch: lnb.t[:, ch, :])
        onesR = sb('a_onesR', [1, 128], F32)
        g1B = sb('a_g1B', [128, D], F32)
        _memset(S, 'dve', onesR.t[:, :], 1.0, [onesR])
        bcast_row(S, C, T['norm1_g'], g1B, pf, onesR)
        for ch in range(4):
            for j in range(31):
                _ts(S, 'dve', dg.t[:, ch, j, :], identF.t[:, :], dwT.t[:, ch, j:j + 1], ALU.mult,
                    [identF, dwT], [dg])

        x = T['x']
        fchunks = []
        for i in range(4):
            fchunks.append((512 + 128 * i, 'gate', i))
            fchunks.append((128 * i, 'a', i))
        for i in range(8):
            fchunks.append((1024 + 128 * i, 'q', i))
        for nm, c0 in (('kc', 2048), ('vc', 2304), ('ks', 2560), ('kw', 3072)):
            for i in range(2):
                fchunks.append((c0 + 128 * i, nm, i))
        for i in range(16):
            fchunks.append((3632 + 128 * i, 'gm', i))

        import os
        STG = int(os.environ.get('STG', '9'))
        nstg = 0
        npf = 0
        for st in range(NSUP if STG >= 1 else 0):
            for t in range(4):
                tt = st * 4 + t
                xb = xs[tt % 4]
                S.dma('sp', xb.sem, dict(out=xb.t[:, :], in_=x[tt * 128:(tt + 1) * 128, :]), writes=[xb])
                _stt(S, junk.t[:, :], xb.t[:, :], 1.0, xb.t[:, :], ALU.mult, ALU.mult, [xb], [junk, ss],
                     accum_out=ss.t[:, t:t + 1])
            _act(S, rt.t[:, :], ss.t[:, :], AF.Sqrt, [ss, epsT], [rt], scale=1.0 / D, bias=epsT.t[:, :])
            S.op('dve', 'reciprocal', dict(out=rstd.t[:, :], in_=rt.t[:, :]), [rt], [rstd])
            for t in range(4):
                tt = st * 4 + t
                xb = xs[tt % 4]
                nk = ntok[t % 2]
                _stt(S, nk.t[:, :], xb.t[:, :], rstd.t[:, t:t + 1], g1B.t[:, :], ALU.mult, ALU.mult, [xb, rstd, g1B], [nk])
                for kc in range(8):
                    _tp(S, ptr.t[:, kc * 128:(kc + 1) * 128], nk.t[:, kc * 128:(kc + 1) * 128], identB.t[:, :],
                        [nk, identB], [ptr])
                _cp(S, 'act', nT.t[:, :, t * 128:(t + 1) * 128],
                    ptr.t[:, :].rearrange('p (k q) -> p k q', k=8), [ptr], [nT])
            for t in range(4 if STG >= 2 else 0):
                tt = st * 4 + t
                for half, c0 in ((0, 2816), (1, 3328)):
                    for kc in range(8):
                        _mm(S, pv.t[:, half * 256:(half + 1) * 256], nT.t[:, kc, t * 128:(t + 1) * 128],
                            win.t[:, kc, c0:c0 + 256], kc == 0, kc == 7, [nT, wb(c0)], [pv])
                for kc in range(8):
                    _mm(S, pg.t[:, :], nT.t[:, kc, t * 128:(t + 1) * 128], win.t[:, kc, 3584:3632],
                        kc == 0, kc == 7, [nT, wb(3584)], [pg])
                vs = vstg[tt % 2]
                _cp(S, 'dve', vs.t[:, :, :, 0:64], pv.t[:, :].rearrange('p (a g d) -> p a g d', a=2, g=4),
                    [pv], [vs])
                S.dma('sp', vs.sem, dict(out=T['VS1'][tt * 128:(tt + 1) * 128, :, :], in_=vs.t[:, 0, :, :]),
                      reads=[vs])
                S.dma('sp', vs.sem, dict(out=T['VW1'][tt * 128:(tt + 1) * 128, :, :], in_=vs.t[:, 1, :, :]),
                      reads=[vs])
                gs = gstg[tt % 2]
                _act(S, gs.t[:, :], pg.t[:, :], AF.Sigmoid, [pg], [gs])
                S.dma('sp', gs.sem, dict(out=T['G'][tt * 128:(tt + 1) * 128, :], in_=gs.t[:, :]), reads=[gs])
            cs = slice(st * 512, (st + 1) * 512)
            for (c0, kind, idx) in (fchunks if STG >= 3 else []):
                p = pf[npf % 3]
                npf += 1
                for kc in range(8):
                    _mm(S, p.t[:, :], win.t[:, kc, c0:c0 + 128], nT.t[:, kc, :], kc == 0, kc == 7, [wb(c0), nT], [p])
                if kind == 'gate':
                    sgt = sg[idx % 2]
                    _act(S, sgt.t[:, :], p.t[:, :], AF.Sigmoid, [p], [sgt])
                elif kind == 'a':
                    sgt = sg[idx % 2]
                    _tt(S, 'dve', hglu.t[:, idx, 30:542], p.t[:, :], sgt.t[:, :], ALU.mult, [p, sgt], [hglu])
                else:
                    sl = stg[nstg % 6]
                    nstg += 1
                    if kind == 'q':
                        _act(S, sl.t[:, :], p.t[:, :], AF.Copy, [p], [sl], scale=0.125)
                        dst = T['QT'][2 * idx:2 * idx + 2, :, cs].rearrange('h d s -> (h d) s')
                    elif kind == 'gm':
                        _act(S, sl.t[:, :], p.t[:, :], AF.Sigmoid, [p], [sl])
                        dst = T['GM'][idx, :, cs]
                    else:
                        _cp(S, 'dve', sl.t[:, :], p.t[:, :], [p], [sl])
                        dst = T[{'kc': 'KC', 'vc': 'VC', 'ks': 'KS', 'kw': 'KW'}[kind]][2 * idx:2 * idx + 2, :, cs] \
                            .rearrange('g d s -> (g d) s')
                    S.dma('sp', sl.sem, dict(out=dst, in_=sl.t[:, :]), reads=[sl])
            if STG < 4:
                continue
            for ch in range(4):
                p = pc[ch % 2]
                for j in range(0, 31, int(os.environ.get('JSTEP', '1'))):
                    _mm(S, p.t[:, :], dg.t[:, ch, j, :], hglu.t[:, ch, j:j + 512], j == 0, j == 30, [dg, hglu], [p])
                CP = int(os.environ.get('CP', '15'))
                if CP & 2:
                    _ts(S, 'dve', ycs.t[:, ch, :], p.t[:, :], dwb.t[:, ch, :], ALU.add, [p, dwb], [ycs])
                if CP & 4:
                    _tt(S, 'pool', ysq.t[:, ch, :], ycs.t[:, ch, :], ycs.t[:, ch, :], ALU.mult, [ycs], [ysq])
            if CP & 8:
                _cp(S, 'dve', hglu.t[:, :, 0:30], hglu.t[:, :, 512:542], [hglu], [hglu])
            if STG < 5:
                continue
            pm = pf[npf % 3]
            npf += 1
            pq = pf[npf % 3]
            npf += 1
            for ch in range(4):
                _mm(S, pm.t[:, :], onesF.t[:, :], ycs.t[:, ch, :], ch == 0, ch == 3, [onesF, ycs], [pm])
            for ch in range(4):
                _mm(S, pq.t[:, :], onesF.t[:, :], ysq.t[:, ch, :], ch == 0, ch == 3, [onesF, ysq], [pq])
            _cp(S, 'act', mean.t[:, :], pm.t[:, :], [pm], [mean])
            _tt(S, 'dve', msq.t[:, :], mean.t[:, :], mean.t[:, :], ALU.mult, [mean], [msq])
            _tt(S, 'dve', msq.t[:, :], pq.t[:, :], msq.t[:, :], ALU.subtract, [pq, msq], [msq])
            _act(S, msq.t[:, :], msq.t[:, :], AF.Sqrt, [msq, epsT], [msq], bias=epsT.t[:, :])
            S.op('dve', 'reciprocal', dict(out=rs.t[:, :], in_=msq.t[:, :]), [msq], [rs])
            for ch in range(4):
                d_ = dtl[ch % 2]
                _tt(S, 'dve', d_.t[:, :], ycs.t[:, ch, :], mean.t[:, :], ALU.subtract, [ycs, mean], [d_])
                _tt(S, 'dve', d_.t[:, :], d_.t[:, :], rs.t[:, :], ALU.mult, [d_, rs], [d_])
                _ts(S, 'dve', d_.t[:, :], d_.t[:, :], lng.t[:, ch, :], ALU.mult, [d_, lng, lnb], [d_],
                    s2=lnb.t[:, ch, :], op1=ALU.add)
                sl = stg[nstg % 6]
                nstg += 1
                _act(S, sl.t[:, :], d_.t[:, :], AF.Silu, [d_], [sl])
                S.dma('sp', sl.sem, dict(out=T['HC'][ch, :, cs], in_=sl.t[:, :]), reads=[sl])
        S.flush()


def bcast_row(S, C, src_row, dst_tl, pbanks, onesF):
    S.dma('sp', C.stage.sem, dict(out=C.stage.t[0:1, 0:1024], in_=src_row), writes=[C.stage])
    for half in range(2):
        p = pbanks[half]
        _mm(S, p.t[:, :], onesF.t[0:1, :], C.stage.t[0:1, half * 512:(half + 1) * 512], True, True,
            [onesF, C.stage], [p])
        _cp(S, 'dve', dst_tl.t[:, half * 512:(half + 1) * 512], p.t[:, :], [p], [dst_tl])


def rms_T(S, R, xtiles, gB, per_tile=False):
    n = len(xtiles)
    if not per_tile:
        for i, xb in enumerate(xtiles):
            _stt(S, R.junk.t[:, :], xb.t[:, :], 1.0, xb.t[:, :], ALU.mult, ALU.mult, [xb], [R.junk, R.ss],
                 accum_out=R.ss.t[:, i:i + 1])
        _act(S, R.rt.t[:, 0:n], R.ss.t[:, 0:n], AF.Sqrt, [R.ss, R.epsT], [R.rt], scale=1.0 / D, bias=R.epsT.t[:, :])
        S.op('dve', 'reciprocal', dict(out=R.rstd.t[:, 0:n], in_=R.rt.t[:, 0:n]), [R.rt], [R.rstd])
    for i, xb in enumerate(xtiles):
        if per_tile:
            _stt(S, R.junk.t[:, :], xb.t[:, :], 1.0, xb.t[:, :], ALU.mult, ALU.mult, [xb], [R.junk, R.ssl[i]],
                 accum_out=R.ssl[i].t[:, 0:1])
            _act(S, R.ssl[i].t[:, 1:2], R.ssl[i].t[:, 0:1], AF.Sqrt, [R.ssl[i], R.epsT], [R.ssl[i]], scale=1.0 / D,
                 bias=R.epsT.t[:, :])
            S.op('dve', 'reciprocal', dict(out=R.ssl[i].t[:, 2:3], in_=R.ssl[i].t[:, 1:2]), [R.ssl[i]], [R.ssl[i]])
            sc_ap, sc_tl = R.ssl[i].t[:, 2:3], R.ssl[i]
        else:
            sc_ap, sc_tl = R.rstd.t[:, i:i + 1], R.rstd
        nk = R.ntok[i % 2]
        _stt(S, nk.t[:, :], xb.t[:, :], sc_ap, gB.t[:, :], ALU.mult, ALU.mult, [xb, sc_tl, gB], [nk])
        for kc in range(8):
            _tp(S, R.ptr.t[:, kc * 128:(kc + 1) * 128], nk.t[:, kc * 128:(kc + 1) * 128], R.identB.t[:, :],
                [nk, R.identB], [R.ptr])
        yield i, R.ptr


WB_SPEC = {'xa_wkv': (D, 2 * D), 'nsa_w_o': (D, D), 'conv_w_pw': (512, D), 'w_out': (D, D), 'xa_wq': (D, D), 'xa_wo': (D, D),
           'ffn_w_up': (D, 2 * D_FF), 'ffn_w_down': (D_FF, D)}


def precast_weights(S, T):
    key = S.newdma()
    for name, (K_, N_) in WB_SPEC.items():
        for r0 in range(0, K_, 128):
            for c0 in range(0, N_, 2048):
                c1 = min(N_, c0 + 2048)
                S.dma('pool', key, dict(out=T['WB_' + name][r0:r0 + 128, c0:c1], in_=T[name][r0:r0 + 128, c0:c1]))


def _mk(nc, es, S):
    def sb(name, shape, dt, sem=False):
        t = es.enter_context(nc.sbuf_tensor(name, shape, dt))
        return Tl(t, S.newdma() if sem else None)

    def ps(name, shape, dt=F32):
        return Tl(es.enter_context(nc.psum_tensor(name, shape, dt)))
    return sb, ps


def cmp_chunks(Sq):
    NC = Sq // 16 - 1
    out = []
    n0 = 0
    while n0 < NC:
        out.append((n0, min(128, NC - n0)))
        n0 += 128
    return NC, out


def phase_B(nc, S, Sq, T, kcT, vco, P):
    NC, chunks = cmp_chunks(Sq)
    with ExitStack() as es:
        sb, ps = _mk(nc, es, S)
        C = Ctx()
        identF = sb('b_identF', [128, 128], F32, sem=True)
        C.identF = identF
        C.stage = sb('b_stage', [32, 1024], F32, sem=True)
        pst = ps('b_pst', [128, 48])
        C.pst = pst
        kin1 = sb('b_kin', [128, 4, Sq], BF16, sem=True)
        kin = [kin1, kin1]
        w1s = sb('b_w1s', [128, 2, 16, 256], BF16, sem=True)
        w2s = sb('b_w2s', [128, 2, 2, 64], BF16, sem=True)
        peT = sb('b_peT', [128, 2, 16], BF16)
        b1T = sb('b_b1T', [128, 4, 1], F32)
        biasc = sb('b_biasc', [128, 4, 1], F32)
        hT = [sb('b_hT%d' % i, [128, 2, 512], BF16) for i in range(2)]
        xb = sb('b_xb', [128, 512], F32)
        x2 = sb('b_x2', [128, 512], F32)
        u = sb('b_u', [128, 512], F32)
        sgm = sb('b_sgm', [128, 512], F32)
        ovs = sb('b_ovs', [128, 2, 64], F32, sem=True)
        pA = [ps('b_pA%d' % i, [128, 512]) for i in range(2)]
        pB = ps('b_pB', [128, 512])

        S.dma('sp', identF.sem, dict(out=identF.t[:, :], in_=T['identF']), writes=[identF])
        for kv in range(2):
            for lh in range(2):
                for l0 in range(0, 16, 8):
                    S.dma('pool', w1s.sem, dict(out=w1s.t[lh * 64:(lh + 1) * 64, kv, l0:l0 + 8, :],
                                                in_=T['cmp_w1'][kv, (lh * 16 + l0) * 64:(lh * 16 + l0 + 8) * 64, :]
                                                .rearrange('(l d) c -> d l c', d=64)), writes=[w1s])
            S.dma('pool', w2s.sem, dict(out=w2s.t[:, kv, :, :],
                                        in_=T['cmp_w2'][kv].rearrange('(h p) d -> p h d', p=128)), writes=[w2s])
            for lh in range(2):
                S.dma('sp', C.stage.sem, dict(out=C.stage.t[0:16, lh * 64:(lh + 1) * 64],
                                              in_=T['cmp_pe'][kv, lh * 16:(lh + 1) * 16, :]), writes=[C.stage])
            _tp(S, pst.t[:, 0:16], C.stage.t[0:16, 0:128], identF.t[0:16, 0:16], [C.stage, identF], [pst])
            _cp(S, 'dve', peT.t[:, kv, :], pst.t[:, 0:16], [pst], [peT])
        cols_from_rows(S, C, T['cmp_b1'], 1, 512, b1T, lambda ch: b1T.t[:, ch, :])
        _memset(S, 'dve', vco.t[:, :, :, :], 0.0, [vco])
        _memset(S, 'dve', kcT.t[:, :, :], 0.0, [kcT])
        S.dma('sp', ovs.sem, dict(out=ovs.t[:, 0:len(chunks), :], in_=T['ovl'].rearrange('(c p) j -> p c j', p=128)),
              writes=[ovs])
        for ci in range(len(chunks)):
            for g in range(4):
                _cp(S, 'dve', vco.t[:, ci, g, 64:128], ovs.t[:, ci, :], [ovs], [vco])
        for kv in range(2):
            for half in range(2):
                for l in range(16):
                    _mm(S, pB.t[:, 0:1], w1s.t[:, kv, l, half * 128:(half + 1) * 128], peT.t[:, kv, l:l + 1],
                        l == 0, l == 15, [w1s, peT], [pB])
                _tt(S, 'dve', biasc.t[:, kv * 2 + half, :], pB.t[:, 0:1], b1T.t[:, kv * 2 + half, :], ALU.add,
                    [pB, b1T], [biasc])

        tmpA = sb('b_tmpA', [128, 2048], F32, sem=True)
        tmpB = sb('b_tmpB', [128, 2048], F32, sem=True)
        biasT, Bband = P.biasT, P.Bband
        for dl in range(2):
            S.dma('sp', tmpA.sem, dict(out=tmpA.t[:, :], in_=T['tz1'][dl].rearrange('k h q -> k (h q)')), writes=[tmpA])
            S.dma('sp', tmpB.sem, dict(out=tmpB.t[:, :], in_=T['tz31'][dl].rearrange('k h q -> k (h q)')), writes=[tmpB])
            _tt(S, 'pool', tmpA.t[:, :], tmpA.t[:, :], tmpB.t[:, :], ALU.subtract, [tmpA, tmpB], [tmpA])
            S.dma('sp', tmpB.sem, dict(out=tmpB.t[:, :], in_=T['tzm'][dl].rearrange('k h q -> k (h q)')), writes=[tmpB])
            _tt(S, 'pool', biasT.t[:, dl, :, :].rearrange('k h q -> k (h q)'), tmpA.t[:, :], tmpB.t[:, :], ALU.add,
                [tmpA, tmpB], [biasT])
        S.dma('sp', tmpA.sem, dict(out=tmpA.t[0:32, :], in_=T['cb1'].rearrange('k h q -> k (h q)')), writes=[tmpA])
        S.dma('sp', tmpB.sem, dict(out=tmpB.t[0:32, :], in_=T['cb31'].rearrange('k h q -> k (h q)')), writes=[tmpB])
        _tt(S, 'pool', tmpA.t[0:32, :], tmpA.t[0:32, :], tmpB.t[0:32, :], ALU.subtract, [tmpA, tmpB], [tmpA])
        S.dma('sp', tmpB.sem, dict(out=tmpB.t[0:32, :], in_=T['cbm'].rearrange('k h q -> k (h q)')), writes=[tmpB])
        _tt(S, 'pool', Bband.t[:, :, :].rearrange('k h q -> k (h q)'), tmpA.t[0:32, :], tmpB.t[0:32, :], ALU.add,
            [tmpA, tmpB], [Bband])

        npa = 0
        for kv in range(2):
            src = kin[kv]
            S.dma('sp', src.sem, dict(out=src.t[0:64, :, :], in_=T['KC' if kv == 0 else 'VC'].rearrange('g d s -> d g s')), writes=[src])
            S.dma('sp', src.sem, dict(out=src.t[64:128, :, 0:Sq - 16], in_=T['KC' if kv == 0 else 'VC'][:, :, 16:Sq].rearrange('g d s -> d g s')), writes=[src])
            for g in range(4):
                h_ = hT[(kv * 4 + g) % 2]
                for half in range(2):
                    p = pA[npa % 2]
                    npa += 1
                    for l in range(16):
                        _mm(S, p.t[:, 0:NC], w1s.t[:, kv, l, half * 128:(half + 1) * 128],
                            src.t[:, g, l:l + 16 * (NC - 1) + 1:16], l == 0, l == 15, [w1s, src], [p])
                    _ts(S, 'dve', xb.t[:, 0:NC], p.t[:, 0:NC], biasc.t[:, kv * 2 + half, :], ALU.add, [p, biasc], [xb])
                    _tt(S, 'pool', x2.t[:, 0:NC], xb.t[:, 0:NC], xb.t[:, 0:NC], ALU.mult, [xb], [x2])
                    _ts(S, 'dve', x2.t[:, 0:NC], x2.t[:, 0:NC], 0.044715, ALU.mult, [x2], [x2], s2=1.0, op1=ALU.add)
                    _tt(S, 'dve', u.t[:, 0:NC], x2.t[:, 0:NC], xb.t[:, 0:NC], ALU.mult, [x2, xb], [u])
                    _act(S, sgm.t[:, 0:NC], u.t[:, 0:NC], AF.Sigmoid, [u], [sgm], scale=1.5957691216057308)
                    _tt(S, 'dve', h_.t[:, half, 0:NC], xb.t[:, 0:NC], sgm.t[:, 0:NC], ALU.mult, [xb, sgm], [h_])
                if kv == 0:
                    for half in range(2):
                        _mm(S, pB.t[0:64, 0:NC], w2s.t[:, 0, half, :], h_.t[:, half, 0:NC], half == 0, half == 1,
                            [w2s, h_], [pB])
                    _cp(S, 'act', kcT.t[0:64, g, 0:NC], pB.t[0:64, 0:NC], [pB], [kcT])
                else:
                    for ci, (n0, sz) in enumerate(chunks):
                        for half in range(2):
                            _mm(S, pB.t[0:sz, 0:64], h_.t[:, half, n0:n0 + sz], w2s.t[:, 1, half, :], half == 0,
                                half == 1, [w2s, h_], [pB])
                        _cp(S, 'act', vco.t[0:sz, ci, g, 0:64], pB.t[0:sz, 0:64], [pB], [vco])
        R = Ctx()
        R.identB = sb('b_identB', [128, 128], BF16, sem=True)
        R.epsT = sb('b_eps', [128, 1], F32)
        R.ss = sb('b_ss', [128, 4], F32)
        R.rt = sb('b_rt', [128, 4], F32)
        R.rstd = sb('b_rstd', [128, 4], F32)
        R.junk = sb('b_junk', [128, D], BF16)
        R.ntok = [sb('b_ntok%d' % i, [128, D], BF16) for i in range(2)]
        R.ptr = ps('b_ptr', [128, 1024], BF16)
        onesF = sb('b_onesF', [1, 128], F32)
        gmB = sb('b_gmB', [128, D], F32)
        wkv = sb('b_wkv', [128, 8, 2 * D], BF16, sem=True)
        memx = [sb('b_memx%d' % i, [128, D], F32, sem=True) for i in range(2)]
        memnT = sb('b_memnT', [128, 8, 256], BF16)
        kmT, vm = P.kmT, P.vm
        S.dma('pool', R.identB.sem, dict(out=R.identB.t[:, :], in_=T['identF']), writes=[R.identB])
        _memset(S, 'dve', R.epsT.t[:, :], EPS, [R.epsT])
        _memset(S, 'dve', onesF.t[:, :], 1.0, [onesF])
        S.dma('sp', wkv.sem, dict(out=wkv.t[:, :, :], in_=T['WB_xa_wkv'].rearrange('(k p) n -> p k n', p=128)), writes=[wkv])
        bcast_row(S, C, T['mem_norm_g'], gmB, pA, onesF)
        for mc in range(2):
            S.dma('sp', memx[mc].sem, dict(out=memx[mc].t[:, :], in_=T['mem'][mc * 128:(mc + 1) * 128, :]),
                  writes=[memx[mc]])
        for i, p_ in rms_T(S, R, memx, gmB):
            _cp(S, 'act', memnT.t[:, :, i * 128:(i + 1) * 128], p_.t[:, :].rearrange('p (k q) -> p k q', k=8),
                [p_], [memnT])
        for c in range(8):
            p = pA[c % 2]
            for kc in range(8):
                _mm(S, p.t[:, 0:256], wkv.t[:, kc, c * 128:(c + 1) * 128], memnT.t[:, kc, :], kc == 0, kc == 7,
                    [wkv, memnT], [p])
            _cp(S, 'dve', kmT.t[:, c, :], p.t[:, 0:256], [p], [kmT])
        for mc in range(2):
            for half in range(2):
                p = pA[(mc * 2 + half) % 2]
                for kc in range(8):
                    _mm(S, p.t[:, :], memnT.t[:, kc, mc * 128:(mc + 1) * 128],
                        wkv.t[:, kc, D + half * 512:D + (half + 1) * 512], kc == 0, kc == 7, [wkv, memnT], [p])
                _cp(S, 'dve', vm.t[:, mc, half * 512:(half + 1) * 512], p.t[:, :], [p], [vm])
        S.flush()


def phase_C(nc, S, Sq, T, kcT, vco, P):
    NT = Sq // 128
    NC, chunks = cmp_chunks(Sq)
    OFFS = 8 * (NT - 1)
    with ExitStack() as es:
        sb, ps = _mk(nc, es, S)
        identB = sb('c_identB', [128, 128], BF16, sem=True)
        KE = sb('c_KE', [128, 4, Sq], BF16, sem=True)
        KWt = sb('c_KW', [128, 4, Sq], BF16, sem=True)
        KWz = Buf()
        KEe = Buf()
        KEe_sem = S.newdma()
        VS = sb('c_VS', [128, NT, 4, 65], BF16, sem=True)
        VW = sb('c_VW', [128, NT, 4, 65], BF16, sem=True)
        biasT, Bband = P.biasT, P.Bband
        m512 = sb('c_m512', [128, 512], BF16, sem=True)
        SelW = sb('c_SelW', [32, OFFS + 128 * len(chunks)], BF16, sem=True)
        mulB = sb('c_mulB', [128, 128], F32, sem=True)
        addB = sb('c_addB', [128, 128], F32, sem=True)
        QM = [sb('c_QM%d' % i, [128, 4, 4, 128], BF16, sem=True) for i in range(2)]
        QMq = [Buf() for _ in range(2)]
        QMm = [[Buf() for _ in range(4)] for _ in range(2)]
        gt = [sb('c_gt%d' % i, [128, 16, 3], F32, sem=True) for i in range(3)]
        NE = 4
        Et = [sb('c_E%d' % i, [128, 512], BF16) for i in range(NE)]
        o1s = [[sb('c_o1s%d_%d' % (j, i), [128, 4, 128], F32) for i in range(4)] for j in range(2)]
        o2sl = [sb('c_o2s%d' % i, [128, 4, 65], F32) for i in range(2)]
        o3s = [[sb('c_o3s%d_%d' % (j, i), [128, 4, 65], F32) for i in range(4)] for j in range(2)]
        coef1 = [[sb('c_coef1_%d_%d' % (j, i), [128, 4], F32) for i in range(4)] for j in range(2)]
        den = sb('c_den', [128, 4], F32)
        rden = sb('c_rden', [128, 4], F32)
        c2l = [sb('c_c2_%d' % i, [128, 4], F32) for i in range(2)]
        c3l = [sb('c_c3_%d' % i, [128, 4], F32) for i in range(2)]
        tmpc = sb('c_tmpc', [128, 4, 64], F32)
        imp = sb('c_imp', [128, 64], F32)
        score = sb('c_score', [128, 64], F32)
        score2 = sb('c_score2', [128, 64], F32)
        m8a = sb('c_m8a', [128, 8], F32)
        m8b = sb('c_m8b', [128, 8], F32)
        nms = [sb('c_nm%d' % i, [128, 128], BF16) for i in range(4)]
        acc = sb('c_acc', [128, 4, 64], F32)
        ntk = sb('c_ntk', [128, 1024], BF16)
        nst = [sb('c_nst%d' % i, [128, 8, 128], BF16, sem=True) for i in range(2)]

        scp = [ps('c_sc%d' % i, [128, 512]) for i in range(3)]
        o1U = ps('c_o1U', [128, 512])
        o3p = ps('c_o3', [128, 4, 65])
        o2p = [ps('c_o2_%d' % i, [128, 4, 65]) for i in range(2)]
        ptr = ps('c_ptr', [128, 1024], BF16)

        S.dma('pool', identB.sem, dict(out=identB.t[:, :], in_=T['identF']), writes=[identB])
        S.dma('sp', KE.sem, dict(out=KE.t[0:64, :, :], in_=T['KS'].rearrange('g d s -> d g s')), writes=[KE])
        for g in range(4):
            for c0 in range(0, Sq, 2048):
                c1 = min(Sq, c0 + 2048)
                S.dma('pool', KEe_sem, dict(out=KE.t[64:128, g, c0:c1], in_=T['Econst'][:, c0:c1]), writes=[KEe])
        S.dma('sp', KWt.sem, dict(out=KWt.t[0:64, :, :], in_=T['KW'].rearrange('g d s -> d g s')), writes=[KWt])
        _memset(S, 'pool', KWt.t[64:128, :, :], 0.0, [KWz])
        for i_, q_ in enumerate(QM):
            _memset(S, 'pool', q_.t[:, :, :, :], 0.0, [q_, QMq[i_]] + QMm[i_])
        for k0 in range(0, NT, 8):
            k1 = min(NT, k0 + 8)
            S.dma('sp', VS.sem, dict(out=VS.t[:, k0:k1, :, :],
                                     in_=T['VS1'][k0 * 128:k1 * 128].rearrange('(k p) g d -> p k g d', p=128)),
                  writes=[VS])
            S.dma('sp', VW.sem, dict(out=VW.t[:, k0:k1, :, :],
                                     in_=T['VW1'][k0 * 128:k1 * 128].rearrange('(k p) g d -> p k g d', p=128)),
                  writes=[VW])
        S.dma('pool', m512.sem, dict(out=m512.t[:, :], in_=T['m512']), writes=[m512])
        S.dma('pool', SelW.sem, dict(out=SelW.t[:, :], in_=T['SelW']), writes=[SelW])
        S.dma('sp', mulB.sem, dict(out=mulB.t[:, :], in_=T['mulB']), writes=[mulB])
        S.dma('sp', addB.sem, dict(out=addB.t[:, :], in_=T['addB']), writes=[addB])
        for nm_ in nms:
            _memset(S, 'dve', nm_.t[:, :], 0.0, [nm_])

        def cw_steps(qt):
            out = []
            for g in range(4):
                cl = [(ci, n0, sz) for ci, (n0, sz) in enumerate(chunks) if n0 <= 8 * qt + 6]
                for i, (ci, n0, sz) in enumerate(cl):
                    out.append(dict(kind='cmp', qt=qt, g=g, ci=ci, n0=n0, sz=sz, first=i == 0, last=i == len(cl) - 1))
                kl = list(range(max(0, qt - 4), qt + 1))
                for i, kt in enumerate(kl):
                    out.append(dict(kind='win', qt=qt, g=g, kt=kt, sz=128, first=i == 0, last=i == len(kl) - 1))
            out[0]['loadq'] = qt
            out[-1]['flush_def'] = True
            return out

        def sel_steps(qt):
            out = []
            for g in range(4):
                for kt in range(qt + 1):
                    out.append(dict(kind='sel', qt=qt, g=g, kt=kt, sz=128, first=kt == 0, last=kt == qt))
            return out

        if os.environ.get('PIPE', '1') == '1':
            steps = cw_steps(0)
            for qt in range(NT):
                if qt + 1 < NT:
                    steps += cw_steps(qt + 1)
                steps += sel_steps(qt)
        else:
            steps = []
            for qt in range(NT):
                steps += cw_steps(qt) + sel_steps(qt)
        cnt = dict(sc=0, e=0, o2=0, fs=0)

        def load_q(qt):
            sl = qt % 2
            qs = slice(qt * 128, (qt + 1) * 128)
            S.dma('sp', QM[sl].sem, dict(out=QM[sl].t[0:64, :, :, :].rearrange('d g h q -> d (g h) q'),
                                         in_=T['QT'][:, :, qs].rearrange('h d q -> d h q')), writes=[QMq[sl]])
            S.dma('sp', gt[qt % 3].sem, dict(out=gt[qt % 3].t[:, :, :].rearrange('p h b -> p (h b)'), in_=T['G'][qs, :]),
                  writes=[gt[qt % 3]])

        def emit_scores(st):
            qt, g, sz = st['qt'], st['g'], st['sz']
            sl = qt % 2
            sc = scp[cnt['sc'] % 3]
            cnt['sc'] += 1
            st['sc'] = sc
            qrow = QM[sl].t[:, g, :, :].rearrange('d h q -> d (h q)')
            if st['kind'] == 'cmp':
                n0 = st['n0']
                a = n0 - 8 * qt + OFFS
                _mm(S, sc.t[0:sz, :], kcT.t[:, g, n0:n0 + sz], qrow, True, False, [kcT, QMq[sl], QM[sl], QMm[sl][g]], [sc])
                _mm(S, sc.t[0:sz, :], SelW.t[0:32, a:a + sz],
                    Bband.t[0:32, 4 * g:4 * g + 4, :].rearrange('k h q -> k (h q)'), False, True, [SelW, Bband], [sc])
                return
            kt = st['kt']
            dl = (qt - kt)
            extra = None
            if dl in (0, 1):
                extra = (biasT.t[:, dl, 4 * g:4 * g + 4, :].rearrange('k h q -> k (h q)'), biasT)
            elif dl == 4 and st['kind'] == 'win':
                extra = (m512.t[:, :], m512)
            ks = slice(kt * 128, (kt + 1) * 128)
            if st['kind'] == 'win':
                _mm(S, sc.t[:, :], KWt.t[:, g, ks], qrow, True, extra is None, [KWt, KWz, QMq[sl], QM[sl], QMm[sl][g]], [sc])
            else:
                _mm(S, sc.t[:, :], KE.t[:, g, ks], QM[sl].t[:, g, :, :].rearrange('d h q -> d (h q)'), True,
                    extra is None, [KE, KEe, QMq[sl], QMm[sl][g]], [sc])
            if extra is not None:
                _mm(S, sc.t[:, :], identB.t[:, :], extra[0], False, True, [identB, extra[1]], [sc])

        def emit_exp(st):
            sz = st['sz']
            E = Et[cnt['e'] % NE]
            cnt['e'] += 1
            st['E'] = E
            _act(S, E.t[0:sz, :], st['sc'].t[0:sz, :], AF.Exp, [st['sc']], [E])

        def emit_pv(st):
            qt, g, sz, E = st['qt'], st['g'], st['sz'], st['E']
            if st['kind'] == 'cmp':
                for h in range(4):
                    _mm(S, o1U.t[:, h * 128:(h + 1) * 128], E.t[0:sz, h * 128:(h + 1) * 128], vco.t[0:sz, st['ci'], g, :],
                        st['first'] and h == 0, st['last'] and h == 3, [E, vco], [o1U], skip=True)
                if st['last']:
                    fin_cmp(qt, g)
                return
            kt = st['kt']
            if st['kind'] == 'win':
                op_, V = o3p, VW
            else:
                if st['first']:
                    st['o2'] = o2p[cnt['o2'] % 2]
                    cnt['o2'] += 1
                    cur['o2'] = st['o2']
                op_, V = cur['o2'], VS
            for h in range(4):
                _mm(S, op_.t[:, h, :], E.t[:, h * 128:(h + 1) * 128], V.t[:, kt, g, :], st['first'] and h == 0,
                    st['last'] and h == 3, [E, V], [op_], skip=True)
            if st['last']:
                if st['kind'] == 'win':
                    _cp(S, EV, o3s[qt % 2][g].t[:, :, :], o3p.t[:, :, :], [o3p], [o3s[qt % 2][g]])
                else:
                    fin_sel(qt, g, op_)

        cur = {}

        def fin_cmp(qt, g):
            sl = qt % 2
            nm = nms[g]
            o1 = o1s[sl][g]
            _cp(S, EV, o1.t[:, :, :], o1U.t[:, :].rearrange('p (h c) -> p h c', h=4), [o1U], [o1])
            S.op('dve', 'tensor_reduce', dict(out=den.t[:, :], in_=o1.t[:, :, 64:128], axis=AX.X, op=ALU.add), [o1], [den])
            _ts(S, 'dve', den.t[:, :], den.t[:, :], 1e-30, ALU.max, [den], [den])
            S.op('dve', 'reciprocal', dict(out=rden.t[:, :], in_=den.t[:, :]), [den], [rden])
            _ts(S, 'dve', imp.t[:, :], o1.t[:, 0, 64:128], rden.t[:, 0:1], ALU.mult, [o1, rden], [imp])
            for h in range(1, 4):
                _stt(S, imp.t[:, :], o1.t[:, h, 64:128], rden.t[:, h:h + 1], imp.t[:, :], ALU.mult, ALU.add,
                     [o1, rden, imp], [imp])
            a = 62 - 2 * qt
            _tt(S, 'dve', score.t[:, :], imp.t[:, :], mulB.t[:, a:a + 64], ALU.mult, [imp, mulB], [score])
            _tt(S, 'dve', score.t[:, :], score.t[:, :], addB.t[:, a:a + 64], ALU.add, [score, addB], [score])
            _memset(S, 'dve', score.t[:, 0:1], 50.0, [score])
            S.op('dve', 'max', dict(out=m8a.t[:, :], in_=score.t[:, :]), [score], [m8a])
            S.op('dve', 'match_replace', dict(out=score2.t[:, :], in_to_replace=m8a.t[:, :], in_values=score.t[:, :],
                                              imm_value=-1e9), [score, m8a], [score2])
            S.op('dve', 'max', dict(out=m8b.t[:, :], in_=score2.t[:, :]), [score2], [m8b])
            _ts(S, 'dve', nm.t[:, 64:128], score.t[:, :], m8b.t[:, 7:8], ALU.is_lt, [score, m8b], [nm], s2=MASKV,
                op1=ALU.mult)

            def part2(sl=sl, g=g, nm=nm):
                _tp(S, ptr.t[:, 0:128], nm.t[:, :], identB.t[:, :], [nm, identB], [ptr])
                for h in range(4):
                    _cp(S, 'dve', QM[sl].t[64:128, g, h, :], ptr.t[64:128, 0:128], [ptr], [QMm[sl][g]])
            deferred.append([DEFER, part2])
            _tt(S, 'dve', coef1[sl][g].t[:, :], rden.t[:, :], gt[qt % 3].t[:, 4 * g:4 * g + 4, 0], ALU.mult, [rden, gt[qt % 3]],
                [coef1[sl][g]])

        def fin_sel(qt, g, o2):
            sl = qt % 2
            k_ = cnt['fs'] % 2
            cnt['fs'] += 1
            o2s, c2, c3 = o2sl[k_], c2l[k_], c3l[k_]
            _cp(S, EV, o2s.t[:, :, :], o2.t[:, :, :], [o2], [o2s])
            for (osrc, cf, br) in ((o2s, c2, 1), (o3s[sl][g], c3, 2)):
                _ts(S, 'dve', den.t[:, :], osrc.t[:, :, 64], 1e-30, ALU.max, [osrc], [den])
                S.op('dve', 'reciprocal', dict(out=rden.t[:, :], in_=den.t[:, :]), [den], [rden])
                _tt(S, 'dve', cf.t[:, :], rden.t[:, :], gt[qt % 3].t[:, 4 * g:4 * g + 4, br], ALU.mult, [rden, gt[qt % 3]], [cf])
            o1 = o1s[sl][g]
            o3 = o3s[sl][g]
            c1 = coef1[sl][g]

            def cb(c):
                return c.t[:, :].unsqueeze(2).broadcast_to([128, 4, 64])
            CE = 'pool'
            _tt(S, CE, acc.t[:, :, :], o1.t[:, :, 0:64], cb(c1), ALU.mult, [o1, c1], [acc])
            _tt(S, CE, tmpc.t[:, :, :], o3.t[:, :, 0:64], cb(c3), ALU.mult, [o3, c3], [tmpc])
            _tt(S, CE, acc.t[:, :, :], acc.t[:, :, :], tmpc.t[:, :, :], ALU.add, [acc, tmpc], [acc])
            _tt(S, CE, tmpc.t[:, :, :], o2s.t[:, :, 0:64], cb(c2), ALU.mult, [o2s, c2], [tmpc])
            _tt(S, CE, ntk.t[:, g * 256:(g + 1) * 256].rearrange('p (h d) -> p h d', h=4), acc.t[:, :, :], tmpc.t[:, :, :],
                ALU.add, [acc, tmpc], [ntk])
            if g == 3:
                def part2(qt=qt):
                    for kc in range(8):
                        _tp(S, ptr.t[:, kc * 128:(kc + 1) * 128], ntk.t[:, kc * 128:(kc + 1) * 128], identB.t[:, :],
                            [ntk, identB], [ptr])
                    ns = nst[qt % 2]
                    _cp(S, 'dve', ns.t[:, :, :], ptr.t[:, :].rearrange('p (k q) -> p k q', k=8), [ptr], [ns])
                    S.dma('sp', ns.sem, dict(out=T['NSAT'][:, :, qt * 128:(qt + 1) * 128].rearrange('k p q -> p k q'),
                                             in_=ns.t[:, :, :]), reads=[ns])
                deferred.append([DEFER, part2])

        LOOK = int(os.environ.get('LOOK', '2'))
        EV = os.environ.get('EV', 'act')
        DEFER = int(os.environ.get('DEFER', '8'))
        deferred = []
        pend = []

        def run_deferred(force=False):
            while deferred and (force or deferred[0][0] <= 0):
                deferred.pop(0)[1]()

        for st in steps:
            if 'loadq' in st:
                load_q(st['loadq'])
            emit_scores(st)
            emit_exp(st)
            pend.append(st)
            if len(pend) > LOOK:
                emit_pv(pend.pop(0))
            for d_ in deferred:
                d_[0] -= 1
            run_deferred()
            if st.get('flush_def'):
                while pend:
                    emit_pv(pend.pop(0))
                run_deferred(force=True)
        while pend:
            emit_pv(pend.pop(0))
        run_deferred(force=True)
        S.flush()


def phase_D(nc, S, Sq, T, P):
    NSUP = Sq // 512
    with ExitStack() as es:
        sb, ps = _mk(nc, es, S)
        C = Ctx()
        R = Ctx()
        identB = sb('d_identB', [128, 128], BF16, sem=True)
        R.identB = identB
        onesB = sb('d_onesB', [128, 128], BF16)
        onesF = sb('d_onesF', [1, 128], F32)
        R.epsT = sb('d_eps', [128, 1], F32)
        wno = sb('d_wno', [128, 8, D], BF16, sem=True)
        wpw = sb('d_wpw', [128, 4, D], BF16, sem=True)
        wout = sb('d_wout', [128, 8, D], BF16, sem=True)
        wq = sb('d_wq', [128, 8, D], BF16, sem=True)
        wo = sb('d_wo', [128, 8, D], BF16, sem=True)
        g2B = sb('d_g2B', [128, D], F32)
        g3B = sb('d_g3B', [128, D], F32)
        kmT, vm = P.kmT, P.vm
        R.ptr = ps('d_ptr', [128, 1024], BF16)
        ptr = R.ptr
        pf = [ps('d_pf%d' % i, [128, 512]) for i in range(4)]
        ph = [ps('d_ph%d' % i, [128, 512]) for i in range(2)]
        R.ss = sb('d_ss', [128, 4], F32)
        R.rt = sb('d_rt', [128, 4], F32)
        R.rstd = sb('d_rstd', [128, 4], F32)
        R.ntok = [sb('d_ntok%d' % i, [128, D], BF16) for i in range(2)]
        R.ssl = [sb('d_ssl%d' % i, [128, 4], F32) for i in range(4)]

        S.dma('pool', identB.sem, dict(out=identB.t[:, :], in_=T['identF']), writes=[identB])
        _memset(S, 'dve', onesB.t[:, :], 1.0, [onesB])
        _memset(S, 'dve', onesF.t[:, :], 1.0, [onesF])
        _memset(S, 'dve', R.epsT.t[:, :], EPS, [R.epsT])
        for w_, nm_ in ((wno, 'nsa_w_o'), (wpw, 'conv_w_pw'), (wout, 'w_out'), (wq, 'xa_wq'), (wo, 'xa_wo')):
            S.dma('sp', w_.sem, dict(out=w_.t[:, :, :], in_=T['WB_' + nm_].rearrange('(k p) n -> p k n', p=128)),
                  writes=[w_])
        nsa_s = sb('d_nsa', [128, 8, 512], BF16, sem=True)
        hc_s = sb('d_hc', [128, 4, 512], BF16, sem=True)
        gm_s = sb('d_gm', [128, 16, 512], BF16, sem=True)
        xs = [sb('d_x%d' % i, [128, D], F32, sem=True) for i in range(4)]
        xin = [sb('d_xin%d' % i, [128, D], F32, sem=True) for i in range(2)]
        C.stage = xs[0]
        bcast_row(S, C, T['norm2_g'], g2B, ph, onesF)
        bcast_row(S, C, T['norm3_g'], g3B, ph, onesF)
        mrg = sb('d_mrg', [128, 8, 512], BF16)
        n2T = sb('d_n2T', [128, 8, 512], BF16)
        qxT = sb('d_qxT', [128, 8, 512], BF16)
        PT = sb('d_PT', [128, 2, 512], BF16)
        R.junk = Tl(PT.t[:, :, :].rearrange('p a b -> p (a b)'))
        R.junk.b = PT.b
        oTn = sb('d_oTn', [128, 8, 512], BF16)
        t1 = sb('d_t1', [128, 512], F32)
        t2 = sb('d_t2', [128, 512], F32)
        rdn = sb('d_rdn', [128, 512], F32)
        n3s = [sb('d_n3s%d' % i, [128, 8, 128], BF16, sem=True) for i in range(2)]
        npf = 0
        NTT = Sq // 128

        def load_acts(st):
            cs = slice(st * 512, (st + 1) * 512)
            S.dma('sp', nsa_s.sem, dict(out=nsa_s.t[:, :, :], in_=T['NSAT'][:, :, cs].rearrange('k p s -> p k s')),
                  writes=[nsa_s])
            S.dma('sp', hc_s.sem, dict(out=hc_s.t[:, :, :], in_=T['HC'][:, :, cs].rearrange('k p s -> p k s')),
                  writes=[hc_s])
            S.dma('sp', gm_s.sem, dict(out=gm_s.t[:, :, :], in_=T['GM'][:, :, cs].rearrange('k p s -> p k s')),
                  writes=[gm_s])

        def load_x(tt):
            if tt < NTT:
                S.dma('sp', xin[tt % 2].sem, dict(out=xin[tt % 2].t[:, :], in_=T['x'][tt * 128:(tt + 1) * 128, :]),
                      writes=[xin[tt % 2]])
        load_acts(0)
        load_x(0)
        load_x(1)
        for st in range(NSUP):
            cs = slice(st * 512, (st + 1) * 512)
            for f in range(8):
                pa = pf[npf % 4]
                pcv = pf[(npf + 1) % 4]
                npf += 2
                for kc in range(8):
                    _mm(S, pa.t[:, :], wno.t[:, kc, f * 128:(f + 1) * 128], nsa_s.t[:, kc, :], kc == 0, kc == 7,
                        [wno, nsa_s], [pa])
                for c in range(4):
                    _mm(S, pcv.t[:, :], wpw.t[:, c, f * 128:(f + 1) * 128], hc_s.t[:, c, :], c == 0, c == 3,
                        [wpw, hc_s], [pcv])
                _tt(S, 'dve', t1.t[:, :], pa.t[:, :], gm_s.t[:, 8 + f, :], ALU.mult, [pa, gm_s], [t1])
                _tt(S, 'dve', t2.t[:, :], pcv.t[:, :], gm_s.t[:, f, :], ALU.mult, [pcv, gm_s], [t2])
                _tt(S, 'pool', mrg.t[:, f, :], t1.t[:, :], t2.t[:, :], ALU.add, [t1, t2], [mrg])
            if st + 1 < NSUP:
                load_acts(st + 1)
            for t in range(4):
                tt = st * 4 + t
                for half in range(2):
                    p = ph[half]
                    for f in range(8):
                        _mm(S, p.t[:, :], mrg.t[:, f, t * 128:(t + 1) * 128], wout.t[:, f, half * 512:(half + 1) * 512],
                            f == 0, f == 7, [mrg, wout], [p])
                    _tt(S, 'dve', xs[t].t[:, half * 512:(half + 1) * 512], p.t[:, :],
                        xin[tt % 2].t[:, half * 512:(half + 1) * 512], ALU.add, [p, xin[tt % 2]], [xs[t]])
                load_x(tt + 2)
            for i, p_ in rms_T(S, R, xs, g2B, per_tile=True):
                _cp(S, 'act', n2T.t[:, :, i * 128:(i + 1) * 128], p_.t[:, :].rearrange('p (k q) -> p k q', k=8),
                    [p_], [n2T])
            for c in range(8):
                p = pf[npf % 4]
                npf += 1
                for kc in range(8):
                    _mm(S, p.t[:, :], wq.t[:, kc, c * 128:(c + 1) * 128], n2T.t[:, kc, :], kc == 0, kc == 7,
                        [wq, n2T], [p])
                _act(S, qxT.t[:, c, :], p.t[:, :], AF.Copy, [p], [qxT], scale=1.0 / 16.0)
            for hd in range(4):
                for mc in range(2):
                    p = pf[npf % 4]
                    npf += 1
                    for dc in range(2):
                        _mm(S, p.t[:, :], kmT.t[:, hd * 2 + dc, mc * 128:(mc + 1) * 128], qxT.t[:, hd * 2 + dc, :],
                            dc == 0, dc == 1, [kmT, qxT], [p])
                    _act(S, PT.t[:, mc, :], p.t[:, :], AF.Exp, [p], [PT])
                pd = pf[npf % 4]
                npf += 1
                for mc in range(2):
                    _mm(S, pd.t[:, :], onesB.t[:, :], PT.t[:, mc, :], mc == 0, mc == 1, [onesB, PT], [pd])
                S.op('dve', 'reciprocal', dict(out=rdn.t[:, :], in_=pd.t[:, :]), [pd], [rdn])
                for dc in range(2):
                    po = pf[npf % 4]
                    npf += 1
                    for mc in range(2):
                        _mm(S, po.t[:, :], vm.t[:, mc, hd * 256 + dc * 128:hd * 256 + (dc + 1) * 128], PT.t[:, mc, :],
                            mc == 0, mc == 1, [vm, PT], [po])
                    _tt(S, 'dve', oTn.t[:, hd * 2 + dc, :], po.t[:, :], rdn.t[:, :], ALU.mult, [po, rdn], [oTn])
            for t in range(4):
                tt = st * 4 + t
                for half in range(2):
                    p = ph[half]
                    for c in range(8):
                        _mm(S, p.t[:, :], oTn.t[:, c, t * 128:(t + 1) * 128], wo.t[:, c, half * 512:(half + 1) * 512],
                            c == 0, c == 7, [oTn, wo], [p])
                    _tt(S, 'dve', xs[t].t[:, half * 512:(half + 1) * 512], p.t[:, :],
                        xs[t].t[:, half * 512:(half + 1) * 512], ALU.add, [p, xs[t]], [xs[t]])
                S.dma('sp', xs[t].sem, dict(out=T['H2'][tt * 128:(tt + 1) * 128, :], in_=xs[t].t[:, :]), reads=[xs[t]])
            for i, p_ in rms_T(S, R, xs, g3B, per_tile=True):
                tt = st * 4 + i
                ns = n3s[i % 2]
                _cp(S, 'act', ns.t[:, :, :], p_.t[:, :].rearrange('p (k q) -> p k q', k=8), [p_], [ns])
                S.dma('sp', ns.sem, dict(out=T['N3T'][:, :, tt * 128:(tt + 1) * 128].rearrange('k p q -> p k q'),
                                         in_=ns.t[:, :, :]), reads=[ns])
        S.flush()


def phase_E(nc, S, Sq, T, P):
    NSUP = Sq // 512
    NP = D_FF // 128
    with ExitStack() as es:
        sb, ps = _mk(nc, es, S)
        C = Ctx()
        identF = sb('e_identF', [128, 128], F32, sem=True)
        C.identF = identF
        C.stage = sb('e_stage', [32, 1024], F32, sem=True)
        pst = ps('e_pst', [128, 48])
        C.pst = pst
        wupb = [sb('e_wup%d' % i, [128, 8, 512], BF16, sem=True) for i in range(11)]
        wdn = sb('e_wdn', [128, NP, D], BF16, sem=True)
        fw = sb('e_fw', [128, 2 * NP, 3], F32)
        fb = sb('e_fb', [128, 2 * NP, 1], F32)
        fgB = sb('e_fgB', [128, D], F32)
        onesF = sb('e_onesF', [1, 128], F32)
        epsT = sb('e_eps', [128, 1], F32)
        halo = sb('e_halo', [128, 2 * NP, 2], F32)
        n3 = sb('e_n3', [128, 8, 512], BF16, sem=True)
        actT = sb('e_actT', [128, NP, 512], BF16)
        ub = [sb('e_ub%d' % i, [128, 514], F32) for i in range(3)]
        tb = [sb('e_tb%d' % i, [128, 512], F32) for i in range(3)]
        sgl = sb('e_sgl', [128, 512], F32)
        h2 = [sb('e_h2_%d' % i, [128, D], F32, sem=True) for i in range(2)]
        junk = sb('e_junk', [128, D], BF16)
        ss = sb('e_ss', [128, 1], F32)
        rt = sb('e_rt', [128, 1], F32)
        rstd = sb('e_rstd', [128, 1], F32)
        pu = [ps('e_pu%d' % i, [128, 512]) for i in range(3)]
        pd = [ps('e_pd%d' % i, [128, 512]) for i in range(2)]

        S.dma('sp', identF.sem, dict(out=identF.t[:, :], in_=T['identF']), writes=[identF])
        _memset(S, 'dve', onesF.t[:, :], 1.0, [onesF])
        _memset(S, 'dve', epsT.t[:, :], EPS, [epsT])
        _memset(S, 'dve', halo.t[:, :, :], 0.0, [halo])
        order = []
        for j in range(NP):
            for c in (j, j + NP):
                if c // 4 not in order:
                    order.append(c // 4)
        for bi in order[:2]:
            S.dma('sp', wupb[bi].sem, dict(out=wupb[bi].t[:, :, :],
                                           in_=T['WB_ffn_w_up'][:, bi * 512:(bi + 1) * 512].rearrange('(k p) n -> p k n', p=128)),
                  writes=[wupb[bi]])
        for blk in range(0, 2 * D_FF, 1024):
            w = min(1024, 2 * D_FF - blk)
            c0 = blk // 128
            cols_from_rows(S, C, T['ffn_dw_w'][:, blk:blk + w], 3, w, fw, lambda ch, c0=c0: fw.t[:, c0 + ch, :])
            cols_from_rows(S, C, T['ffn_dw_b'][:, blk:blk + w], 1, w, fb, lambda ch, c0=c0: fb.t[:, c0 + ch, :])
        for bi in order[2:]:
            S.dma('sp', wupb[bi].sem, dict(out=wupb[bi].t[:, :, :],
                                           in_=T['WB_ffn_w_up'][:, bi * 512:(bi + 1) * 512].rearrange('(k p) n -> p k n', p=128)),
                  writes=[wupb[bi]])
        S.dma('sp', wdn.sem, dict(out=wdn.t[:, :, :], in_=T['WB_ffn_w_down'].rearrange('(k p) n -> p k n', p=128)),
              writes=[wdn])
        S.dma('sp', C.stage.sem, dict(out=C.stage.t[0:1, 0:1024], in_=T['final_g']), writes=[C.stage])
        for half in range(2):
            _mm(S, pd[half].t[:, :], onesF.t[0:1, :], C.stage.t[0:1, half * 512:(half + 1) * 512], True, True,
                [onesF, C.stage], [pd[half]])
            _cp(S, 'dve', fgB.t[:, half * 512:(half + 1) * 512], pd[half].t[:, :], [pd[half]], [fgB])

        npu = 0
        nub = 0
        for st in range(NSUP):
            cs = slice(st * 512, (st + 1) * 512)
            S.dma('sp', n3.sem, dict(out=n3.t[:, :, :], in_=T['N3T'][:, :, cs].rearrange('k p s -> p k s')), writes=[n3])
            for j in range(NP):
                tpair = []
                for c in (j, j + NP):
                    p = pu[npu % 3]
                    npu += 1
                    u_ = ub[nub % 3]
                    t_ = tb[nub % 3]
                    nub += 1
                    for kc in range(8):
                        _mm(S, p.t[:, :], wupb[c // 4].t[:, kc, (c % 4) * 128:(c % 4 + 1) * 128], n3.t[:, kc, :], kc == 0, kc == 7,
                            [wupb[c // 4], n3], [p])
                    _cp(S, 'act', u_.t[:, 2:514], p.t[:, :], [p], [u_])
                    _cp(S, 'pool', u_.t[:, 0:2], halo.t[:, c, :], [halo], [u_])
                    _act(S, t_.t[:, :], p.t[:, :], AF.Identity, [p, fw, fb], [t_], scale=fw.t[:, c, 2:3], bias=fb.t[:, c, :])
                    _stt(S, t_.t[:, :], u_.t[:, 0:512], fw.t[:, c, 0:1], t_.t[:, :], ALU.mult, ALU.add, [u_, fw, t_], [t_])
                    _stt(S, t_.t[:, :], u_.t[:, 1:513], fw.t[:, c, 1:2], t_.t[:, :], ALU.mult, ALU.add, [u_, fw, t_], [t_])
                    _cp(S, 'pool', halo.t[:, c, :], u_.t[:, 512:514], [u_], [halo])
                    tpair.append(t_)
                _act(S, sgl.t[:, :], tpair[0].t[:, :], AF.Silu, [tpair[0]], [sgl])
                _tt(S, 'dve', actT.t[:, j, :], sgl.t[:, :], tpair[1].t[:, :], ALU.mult, [sgl, tpair[1]], [actT])
            for t in range(4):
                tt = st * 4 + t
                hb = h2[tt % 2]
                ob = hb
                S.dma('sp', hb.sem, dict(out=hb.t[:, :], in_=T['H2'][tt * 128:(tt + 1) * 128, :]), writes=[hb])
                for half in range(2):
                    p = pd[half]
                    for j in range(NP):
                        _mm(S, p.t[:, :], actT.t[:, j, t * 128:(t + 1) * 128], wdn.t[:, j, half * 512:(half + 1) * 512],
                            j == 0, j == NP - 1, [actT, wdn], [p])
                    _tt(S, 'dve', hb.t[:, half * 512:(half + 1) * 512], p.t[:, :], hb.t[:, half * 512:(half + 1) * 512],
                        ALU.add, [p, hb], [hb])
                _stt(S, junk.t[:, :], hb.t[:, :], 1.0, hb.t[:, :], ALU.mult, ALU.mult, [hb], [junk, ss], accum_out=ss.t[:, 0:1])
                _act(S, rt.t[:, :], ss.t[:, :], AF.Sqrt, [ss, epsT], [rt], scale=1.0 / D, bias=epsT.t[:, :])
                S.op('dve', 'reciprocal', dict(out=rstd.t[:, :], in_=rt.t[:, :]), [rt], [rstd])
                _stt(S, ob.t[:, :], hb.t[:, :], rstd.t[:, 0:1], fgB.t[:, :], ALU.mult, ALU.mult, [hb, rstd, fgB], [hb])
                S.dma('sp', hb.sem, dict(out=T['y'][tt * 128:(tt + 1) * 128, :], in_=hb.t[:, :]), reads=[hb])
        S.flush()


def t5_bucket_np(d):
    n = np.maximum(d, 0)
    nf = np.maximum(n, 1).astype(np.float32)
    large = 16 + (np.log(nf / np.float32(16)) / np.float32(np.log(8.0)) * np.float32(16)).astype(np.int32)
    large = np.minimum(large, 31)
    return np.where(n < 16, n, large)


def scratch_spec(Sq):
    return {
        'QT': ([16, 64, Sq], BF16), 'KC': ([4, 64, Sq], BF16), 'VC': ([4, 64, Sq], BF16),
        'KS': ([4, 64, Sq], BF16), 'KW': ([4, 64, Sq], BF16), 'GM': ([16, 128, Sq], BF16),
        'HC': ([4, 128, Sq], BF16), 'VS1': ([Sq, 4, 65], BF16), 'VW1': ([Sq, 4, 65], BF16),
        'G': ([Sq, 48], F32), 'NSAT': ([8, 128, Sq], BF16), 'H2': ([Sq, D], F32), 'N3T': ([8, 128, Sq], BF16),
    }


INPUT_SHAPES = {
    'norm1_g': [1, D], 'w_in': [D, IN_W], 'conv_dw_w': [31, 512], 'conv_dw_b': [1, 512],
    'conv_ln_g': [1, 512], 'conv_ln_b': [1, 512], 'conv_w_pw': [512, D],
    'cmp_pe': [2, 32, 64], 'cmp_w1': [2, 2048, 256], 'cmp_b1': [1, 512], 'cmp_w2': [2, 256, 64],
    'nsa_w_o': [D, D], 'w_out': [D, D], 'norm2_g': [1, D], 'mem_norm_g': [1, D],
    'xa_wq': [D, D], 'xa_wkv': [D, 2 * D], 'xa_wo': [D, D], 'norm3_g': [1, D],
    'ffn_w_up': [D, 2 * D_FF], 'ffn_dw_w': [3, 2 * D_FF], 'ffn_dw_b': [1, 2 * D_FF],
    'ffn_w_down': [D_FF, D], 'final_g': [1, D],
}


def const_shapes(Sq):
    NT = Sq // 128
    NC, chunks = cmp_chunks(Sq)
    return {
        'identF': [128, 128], 'Econst': [64, Sq], 'm512': [128, 512],
        'SelW': [32, 8 * (NT - 1) + 128 * len(chunks)], 'mulB': [128, 128], 'addB': [128, 128],
        'ovl': [128 * len(chunks), 64],
        'tz1': [2, 128, 16, 128], 'tz31': [2, 128, 16, 128], 'tzm': [2, 128, 16, 128],
        'cb1': [32, 16, 128], 'cb31': [32, 16, 128], 'cbm': [32, 16, 128],
    }


def build(Sq, debug=(), phases='ABCDE'):
    nc = bass.Bass("TRN2", target_bir_lowering=False)
    T = {}
    T['x'] = nc.dram_tensor('x', [Sq, D], F32, kind='ExternalInput').ap()
    T['mem'] = nc.dram_tensor('mem', [MEM, D], F32, kind='ExternalInput').ap()
    for k, shp in list(INPUT_SHAPES.items()) + list(const_shapes(Sq).items()):
        T[k] = nc.dram_tensor(k, shp, F32, kind='ExternalInput').ap()
    for k, (shp, dt) in scratch_spec(Sq).items():
        kind = 'ExternalOutput' if k in debug else 'Internal'
        T[k] = nc.dram_tensor(k, shp, dt, kind=kind).ap()
    T['y'] = nc.dram_tensor('y', [Sq, D], F32, kind='ExternalOutput').ap()
    for k, (K_, N_) in WB_SPEC.items():
        T['WB_' + k] = nc.dram_tensor('WB_' + k, [K_, N_], BF16, kind='Internal').ap()
    NC, chunks = cmp_chunks(Sq)
    with ExitStack() as es:
        S = Sched(nc, es)
        if 'A' in phases:
            phase_A(nc, S, Sq, T)
        NCH = len(chunks)
        with ExitStack() as es1:
            P = Ctx()
            P.kmT = Tl(es1.enter_context(nc.sbuf_tensor('p_kmT', [128, 8, 256], BF16)))
            P.vm = Tl(es1.enter_context(nc.sbuf_tensor('p_vm', [128, 2, D], BF16)))
            with ExitStack() as es2:
                kcT = Tl(es2.enter_context(nc.sbuf_tensor('kcT', [128, 4, 128 * NCH], BF16)))
                vco = Tl(es2.enter_context(nc.sbuf_tensor('vco', [128, NCH, 4, 128], BF16)))
                P.biasT = Tl(es2.enter_context(nc.sbuf_tensor('p_biasT', [128, 2, 16, 128], BF16)))
                P.Bband = Tl(es2.enter_context(nc.sbuf_tensor('p_Bband', [32, 16, 128], BF16)))
                if 'B' in phases:
                    phase_B(nc, S, Sq, T, kcT, vco, P)
                if 'C' in phases:
                    phase_C(nc, S, Sq, T, kcT, vco, P)
            if 'D' in phases:
                phase_D(nc, S, Sq, T, P)
        if 'E' in phases:
            phase_E(nc, S, Sq, T, None)
    return nc


def host_consts(rel_bias, Sq):
    NT = Sq // 128
    NC, chunks = cmp_chunks(Sq)
    rb = np.asarray(rel_bias, dtype=np.float32)
    c = {}
    c['identF'] = np.eye(128, dtype=np.float32)
    E = np.zeros((64, Sq), np.float32)
    kk = np.arange(Sq)
    valid = kk // 64 < 64
    E[(kk // 64)[valid], kk[valid]] = 1.0
    c['Econst'] = E
    ki = np.arange(128)[:, None]
    qi = np.arange(128)[None, :]
    c['m512'] = np.tile(np.where(qi >= ki, MASKV, 0.0).astype(np.float32), (1, 4))
    OFFS = 8 * (NT - 1)
    W = np.zeros((32, OFFS + 128 * len(chunks)), np.float32)
    m = np.arange(W.shape[1]) - OFFS
    for r in range(17):
        W[r, m == r - 10] = 1.0
    W[31, m >= 7] = 1.0
    c['SelW'] = W
    r = np.arange(128)[None, :] - 62
    p = np.arange(128)[:, None]
    hi = (p >= 64).astype(np.int64)
    rel = r - hi
    free = rel <= -2
    forced = (rel == -1) | (rel == 0)
    c['mulB'] = np.where(free, 1.0, 0.0).astype(np.float32) * np.ones((128, 1), np.float32)
    c['addB'] = np.where(free, 0.0, np.where(forced, 10.0 + (rel + 2), -1.0 - 0.001 * np.maximum(rel, 0))).astype(np.float32)
    n = np.arange(128 * len(chunks))[:, None]
    j = np.arange(64)[None, :]
    ov = np.clip(np.minimum(16 * n + 32, 64 * j + 64) - np.maximum(16 * n, 64 * j), 0, None).astype(np.float32) / 32.0
    ov[NC:] = 0.0
    c['ovl'] = ov.astype(np.float32)
    tz1 = np.zeros((2, 128, 16, 128), np.float32)
    tz31 = np.zeros_like(tz1)
    tzm = np.zeros_like(tz1)
    for dl in range(2):
        d = dl * 128 + qi - ki
        ok = d >= 0
        g1 = rb[t5_bucket_np(d)]
        g31 = rb[np.full_like(d, 31)]
        tz1[dl] = np.where(ok[:, :, None], g1, 0.0).transpose(0, 2, 1)
        tz31[dl] = np.where(ok[:, :, None], g31, 0.0).transpose(0, 2, 1)
        tzm[dl] = np.where(ok[:, :, None], 0.0, MASKV).transpose(0, 2, 1) * np.ones((1, 16, 1), np.float32)
    c['tz1'], c['tz31'], c['tzm'] = tz1, tz31, tzm
    cb1 = np.zeros((32, 16, 128), np.float32)
    cb31 = np.zeros_like(cb1)
    cbm = np.zeros_like(cb1)
    rr = np.arange(17)[:, None]
    d1 = np.arange(128)[None, :] - 16 * (rr - 10) - 31
    ok = d1 >= 0
    cb1[:17] = np.where(ok[:, :, None], rb[t5_bucket_np(d1)], 0.0).transpose(0, 2, 1)
    cb31[:17] = np.where(ok[:, :, None], rb[np.full_like(d1, 31)], 0.0).transpose(0, 2, 1)
    cbm[:17] = (np.where(ok, 0.0, MASKV)[:, None, :] * np.ones((1, 16, 1))).astype(np.float32)
    cbm[31] = MASKV
    c['cb1'], c['cb31'], c['cbm'] = cb1, cb31, cbm
    return {k: np.ascontiguousarray(v, dtype=np.float32) for k, v in c.items()}


def host_inputs(inp, Sq):
    shared = {}
    for k, shp in INPUT_SHAPES.items():
        shared[k] = np.ascontiguousarray(np.asarray(inp[k], dtype=np.float32).reshape(shp))
    shared.update(host_consts(inp['rel_bias'], Sq))
    return shared


def kernel(**inp):
    x = np.asarray(inp['x'], dtype=np.float32)
    mem = np.asarray(inp['mem'], dtype=np.float32)
    B, Sq, _ = x.shape
    nc = build(Sq)
    shared = host_inputs(inp, Sq)
    in_maps = []
    for b in range(B):
        m = dict(shared)
        m['x'] = np.ascontiguousarray(x[b])
        m['mem'] = np.ascontiguousarray(mem[b])
        in_maps.append(m)
    res = run_bass_kernel_spmd(nc, in_maps, core_ids=list(range(B)))
    return np.stack([np.asarray(r['y'], dtype=np.float32) for r in res.results], axis=0)
```

```python
import os
import numpy as np
from contextlib import ExitStack
import concourse.bass as bass
import concourse.mybir as mybir
from concourse.bass_utils import run_bass_kernel_spmd

F32 = mybir.dt.float32
BF16 = mybir.dt.bfloat16
AF = mybir.ActivationFunctionType
ALU = mybir.AluOpType
AX = mybir.AxisListType

D = 1024
SEQ = 4096
MEM = 256
IN_W = 5680
D_FF = 2816
MASKV = -30000.0
EPS = 1e-6

ENGS = ['pe', 'act', 'dve', 'pool', 'sp']


class Buf:
    __slots__ = ('w', 'r')

    def __init__(self):
        self.w = None
        self.r = {}


class Tl:
    def __init__(self, t, sem=None):
        self.t = t
        self.b = Buf()
        self.sem = sem


def _b(x):
    return x.b if isinstance(x, Tl) else x


class Sched:
    def __init__(self, nc, es):
        self.nc = nc
        self.es = es
        self.q = {e: [] for e in ENGS}
        self.sem = {}
        self.cnt = {}
        self.known = {e: {} for e in ENGS}
        self.ndma = 0

    def semh(self, key):
        if key not in self.sem:
            self.sem[key] = self.es.enter_context(self.nc.semaphore('s_' + key))
            self.cnt[key] = 0
        return self.sem[key]

    def newdma(self, name=None):
        self.ndma += 1
        key = 'd%d' % self.ndma
        self.semh(key)
        return key

    def _deps(self, eng, reads, writes):
        need = {}
        for b in reads:
            b = _b(b)
            if b.w:
                k, v = b.w
                need[k] = max(need.get(k, 0), v)
        for b in writes:
            b = _b(b)
            if b.w:
                k, v = b.w
                need[k] = max(need.get(k, 0), v)
            for k, v in b.r.items():
                need[k] = max(need.get(k, 0), v)
        out = []
        kn = self.known[eng]
        for k, v in need.items():
            if eng == 'pe' and k == 'pe':
                continue
            if kn.get(k, 0) < v:
                kn[k] = v
                out.append((k, v))
        return out

    def _post(self, key, v, reads, writes):
        for b in reads:
            b = _b(b)
            b.r[key] = max(b.r.get(key, 0), v)
        for b in writes:
            b = _b(b)
            b.w = (key, v)
            b.r = {}

    def op(self, eng, meth, kw, reads=(), writes=()):
        self.semh(eng)
        waits = self._deps(eng, reads, writes)
        self.cnt[eng] += 1
        v = self.cnt[eng]
        self.q[eng].append((waits, meth, kw, eng, 1))
        self._post(eng, v, reads, writes)

    def dma(self, eng, semkey, kw, reads=(), writes=()):
        self.semh(semkey)
        waits = self._deps(eng, reads, writes)
        self.cnt[semkey] += 16
        v = self.cnt[semkey]
        self.q[eng].append((waits, 'dma_start', kw, semkey, 16))
        self._post(semkey, v, reads, writes)

    def barrier(self):
        for e in ENGS:
            kn = self.known[e]
            waits = []
            for k, v in self.cnt.items():
                if v > 0 and kn.get(k, 0) < v:
                    kn[k] = v
                    waits.append((k, v))
            if waits:
                self.q[e].append((waits, None, None, None, 0))

    def flush(self):
        self.barrier()
        nc = self.nc
        q = self.q
        sem = self.sem

        def run(engobj, items):
            for waits, meth, kw, key, inc in items:
                for k, v in waits:
                    engobj.wait_ge(sem[k], v)
                if meth is not None:
                    getattr(engobj, meth)(**kw).then_inc(sem[key], inc)

        with nc.Block() as block:
            @block.tensor
            def _(e):
                run(e, q['pe'])

            @block.scalar
            def _(e):
                run(e, q['act'])

            @block.vector
            def _(e):
                run(e, q['dve'])

            @block.gpsimd
            def _(e):
                run(e, q['pool'])

            @block.sync
            def _(e):
                run(e, q['sp'])
        self.q = {e: [] for e in ENGS}


class Ctx:
    pass


def _mm(S, out, lhsT, rhs, start, stop, reads, writes, skip=False):
    kw = dict(out=out, lhsT=lhsT, rhs=rhs, start=start, stop=stop)
    if skip:
        kw['skip_group_check'] = True
    S.op('pe', 'matmul', kw, reads, writes)


def _tp(S, out, in_, ident, reads, writes):
    S.op('pe', 'transpose', dict(out=out, in_=in_, identity=ident), reads, writes)


def _act(S, out, in_, func, reads, writes, **kw):
    S.op('act', 'activation', dict(out=out, in_=in_, func=func, **kw), reads, writes)


def _tt(S, eng, out, in0, in1, op, reads, writes):
    S.op(eng, 'tensor_tensor', dict(out=out, in0=in0, in1=in1, op=op), reads, writes)


def _ts(S, eng, out, in0, s1, op0, reads, writes, s2=None, op1=None):
    kw = dict(out=out, in0=in0, scalar1=s1, scalar2=s2, op0=op0)
    if op1 is not None:
        kw['op1'] = op1
    S.op(eng, 'tensor_scalar', kw, reads, writes)


def _stt(S, out, in0, scalar, in1, op0, op1, reads, writes, accum_out=None):
    kw = dict(out=out, in0=in0, scalar=scalar, in1=in1, op0=op0, op1=op1)
    if accum_out is not None:
        kw['accum_out'] = accum_out
    S.op('dve', 'scalar_tensor_tensor', kw, reads, writes)


def _cp(S, eng, out, in_, reads, writes):
    if eng == 'act':
        S.op('act', 'copy', dict(out=out, in_=in_), reads, writes)
    else:
        S.op(eng, 'tensor_copy', dict(out=out, in_=in_), reads, writes)


def _memset(S, eng, ap, val, writes):
    S.op(eng, 'memset', dict(ap=ap, constant=val), (), writes)


def load_w_cast(S, dst_tl, dst_ap_fn, src, kc_n, ncols, rows_per=128):
    for kc in range(kc_n):
        c0 = 0
        while c0 < ncols:
            c1 = min(ncols, c0 + 2048)
            S.dma('pool', dst_tl.sem, dict(out=dst_ap_fn(kc, c0, c1),
                                           in_=src[kc * 128:(kc + 1) * 128, c0:c1]),
                  writes=[dst_tl])
            c0 = c1


def cols_from_rows(S, C, src_rows, R, ncol, dst_tl, dst_fn):
    stage = C.stage
    assert ncol <= stage.t.shape[1] and R <= 32
    S.dma('sp', stage.sem, dict(out=stage.t[0:R, 0:ncol], in_=src_rows), writes=[stage])
    for ch in range(ncol // 128):
        _tp(S, C.pst.t[:, 0:R], stage.t[0:R, ch * 128:(ch + 1) * 128], C.identF.t[0:R, 0:R],
            [stage, C.identF], [C.pst])
        _cp(S, 'dve', dst_fn(ch), C.pst.t[:, 0:R], [C.pst], [dst_tl])


def phase_A(nc, S, Sq, T):
    NSUP = Sq // 512
    with ExitStack() as es:
        def sb(name, shape, dt, sem=False):
            t = es.enter_context(nc.sbuf_tensor(name, shape, dt))
            return Tl(t, S.newdma() if sem else None)

        def ps(name, shape, dt=F32):
            return Tl(es.enter_context(nc.psum_tensor(name, shape, dt)))

        C = Ctx()
        win = sb('a_win', [128, 8, IN_W], BF16, sem=True)
        dg = sb('a_dg', [128, 4, 31, 128], BF16)
        identF = sb('a_identF', [128, 128], F32, sem=True)
        identB = sb('a_identB', [128, 128], BF16, sem=True)
        onesF = sb('a_onesF', [128, 128], F32)
        C.identF = identF
        C.stage = sb('a_stage', [32, 1024], F32, sem=True)
        dwT = sb('a_dwT', [128, 4, 31], F32)
        dwb = sb('a_dwb', [128, 4, 1], F32)
        lng = sb('a_lng', [128, 4, 1], F32)
        lnb = sb('a_lnb', [128, 4, 1], F32)
        xs = [sb('a_x%d' % i, [128, D], F32, sem=True) for i in range(4)]
        ss = sb('a_ss', [128, 4], F32)
        rt = sb('a_rt', [128, 4], F32)
        rstd = sb('a_rstd', [128, 4], F32)
        junk = sb('a_junk', [128, D], BF16)
        ntok = [sb('a_ntok%d' % i, [128, D], BF16) for i in range(4)]
        nT = sb('a_nT', [128, 8, 512], BF16)
        hglu = sb('a_hglu', [128, 4, 542], BF16)
        stg = [sb('a_stg%d' % i, [128, 512], BF16, sem=True) for i in range(6)]
        sg = [sb('a_sg%d' % i, [128, 512], F32) for i in range(2)]
        ycs = sb('a_ycs', [128, 4, 512], F32)
        ysq = sb('a_ysq', [128, 4, 512], F32)
        mean = sb('a_mean', [128, 512], F32)
        msq = sb('a_msq', [128, 512], F32)
        rs = sb('a_rs', [128, 512], F32)
        dtl = [sb('a_d%d' % i, [128, 512], F32) for i in range(2)]
        vstg = [sb('a_vstg%d' % i, [128, 2, 4, 65], BF16, sem=True) for i in range(2)]
        gstg = [sb('a_gstg%d' % i, [128, 48], F32, sem=True) for i in range(2)]
        epsT = sb('a_eps', [128, 1], F32)

        ptr = ps('a_ptr', [128, 1024], BF16)
        pf = [ps('a_pf%d' % i, [128, 512]) for i in range(3)]
        pv = ps('a_pv', [128, 512])
        pg = ps('a_pg', [128, 48])
        C.pst = pg
        pc = [ps('a_pc%d' % i, [128, 512]) for i in range(2)]

        S.dma('sp', identF.sem, dict(out=identF.t[:, :], in_=T['identF']), writes=[identF])
        S.dma('pool', identB.sem, dict(out=identB.t[:, :], in_=T['identF']), writes=[identB])
        _memset(S, 'dve', onesF.t[:, :], 1.0 / 512.0, [onesF])
        _memset(S, 'dve', epsT.t[:, :], EPS, [epsT])
        _memset(S, 'dve', hglu.t[:, :, :], 0.0, [hglu])
        for v in vstg:
            _memset(S, 'dve', v.t[:, :, :, :], 1.0, [v])
        WBLK = [(2816, 3632), (0, 1024), (1024, 2048), (2048, 2816), (3632, 4656), (4656, 5680)]
        wblk = [Tl(None, S.newdma()) for _ in WBLK]

        def wb(c0):
            for i_, (a0, a1) in enumerate(WBLK):
                if a0 <= c0 < a1:
                    return wblk[i_]
            raise ValueError(c0)
        for i_, (a0, a1) in enumerate(WBLK):
            for kc in range(8):
                S.dma('pool', wblk[i_].sem, dict(out=win.t[:, kc, a0:a1], in_=T['w_in'][kc * 128:(kc + 1) * 128, a0:a1]),
                      writes=[wblk[i_]])
        precast_weights(S, T)
        cols_from_rows(S, C, T['conv_dw_w'], 31, 512, dwT, lambda ch: dwT.t[:, ch, :])
        cols_from_rows(S, C, T['conv_dw_b'], 1, 512, dwb, lambda ch: dwb.t[:, ch, :])
        cols_from_rows(S, C, T['conv_ln_g'], 1, 512, lng, lambda ch: lng.t[:, ch, :])
        cols_from_rows(S, C, T['conv_ln_b'], 1, 512, lnb, lambda ch: lnb.t[:, ch, :])
        onesR = sb('a_onesR', [1, 128], F32)
        g1B = sb('a_g1B', [128, D], F32)
        _memset(S, 'dve', onesR.t[:, :], 1.0, [onesR])
        bcast_row(S, C, T['norm1_g'], g1B, pf, onesR)
        for ch in range(4):
            for j in range(31):
                _ts(S, 'dve', dg.t[:, ch, j, :], identF.t[:, :], dwT.t[:, ch, j:j + 1], ALU.mult,
                    [identF, dwT], [dg])

        x = T['x']
        fchunks = []
        for i in range(4):
            fchunks.append((512 + 128 * i, 'gate', i))
            fchunks.append((128 * i, 'a', i))
        for i in range(8):
            fchunks.append((1024 + 128 * i, 'q', i))
        for nm, c0 in (('kc', 2048), ('vc', 2304), ('ks', 2560), ('kw', 3072)):
            for i in range(2):
                fchunks.append((c0 + 128 * i, nm, i))
        for i in range(16):
            fchunks.append((3632 + 128 * i, 'gm', i))

        import os
        STG = int(os.environ.get('STG', '9'))
        nstg = 0
        npf = 0
        def load_x(st):
            for t in range(4):
                tt = st * 4 + t
                S.dma('sp', xs[t].sem, dict(out=xs[t].t[:, :], in_=x[tt * 128:(tt + 1) * 128, :]), writes=[xs[t]])

        def rms_pre(st):
            for t in range(4):
                _stt(S, junk.t[:, :], xs[t].t[:, :], 1.0, xs[t].t[:, :], ALU.mult, ALU.mult, [xs[t]], [junk, ss],
                     accum_out=ss.t[:, t:t + 1])
            _act(S, rt.t[:, :], ss.t[:, :], AF.Sqrt, [ss, epsT], [rt], scale=1.0 / D, bias=epsT.t[:, :])
            S.op('dve', 'reciprocal', dict(out=rstd.t[:, :], in_=rt.t[:, :]), [rt], [rstd])
            for t in range(4):
                _stt(S, ntok[t].t[:, :], xs[t].t[:, :], rstd.t[:, t:t + 1], g1B.t[:, :], ALU.mult, ALU.mult,
                     [xs[t], rstd, g1B], [ntok[t]])
            if st + 1 < NSUP:
                load_x(st + 1)
        load_x(0)
        rms_pre(0)
        for st in range(NSUP if STG >= 1 else 0):
            for t in range(4):
                nk = ntok[t]
                for kc in range(8):
                    _tp(S, ptr.t[:, kc * 128:(kc + 1) * 128], nk.t[:, kc * 128:(kc + 1) * 128], identB.t[:, :],
                        [nk, identB], [ptr])
                _cp(S, 'act', nT.t[:, :, t * 128:(t + 1) * 128],
                    ptr.t[:, :].rearrange('p (k q) -> p k q', k=8), [ptr], [nT])
            for t in range(4 if STG >= 2 else 0):
                tt = st * 4 + t
                for half, c0 in ((0, 2816), (1, 3328)):
                    for kc in range(8):
                        _mm(S, pv.t[:, half * 256:(half + 1) * 256], nT.t[:, kc, t * 128:(t + 1) * 128],
                            win.t[:, kc, c0:c0 + 256], kc == 0, kc == 7, [nT, wb(c0)], [pv])
                for kc in range(8):
                    _mm(S, pg.t[:, :], nT.t[:, kc, t * 128:(t + 1) * 128], win.t[:, kc, 3584:3632],
                        kc == 0, kc == 7, [nT, wb(3584)], [pg])
                vs = vstg[tt % 2]
                _cp(S, 'dve', vs.t[:, :, :, 0:64], pv.t[:, :].rearrange('p (a g d) -> p a g d', a=2, g=4),
                    [pv], [vs])
                S.dma('sp', vs.sem, dict(out=T['VS1'][tt * 128:(tt + 1) * 128, :, :], in_=vs.t[:, 0, :, :]),
                      reads=[vs])
                S.dma('sp', vs.sem, dict(out=T['VW1'][tt * 128:(tt + 1) * 128, :, :], in_=vs.t[:, 1, :, :]),
                      reads=[vs])
                gs = gstg[tt % 2]
                _act(S, gs.t[:, :], pg.t[:, :], AF.Sigmoid, [pg], [gs])
                S.dma('sp', gs.sem, dict(out=T['G'][tt * 128:(tt + 1) * 128, :], in_=gs.t[:, :]), reads=[gs])
            cs = slice(st * 512, (st + 1) * 512)
            for (c0, kind, idx) in (fchunks if STG >= 3 else []):
                p = pf[npf % 3]
                npf += 1
                for kc in range(8):
                    _mm(S, p.t[:, :], win.t[:, kc, c0:c0 + 128], nT.t[:, kc, :], kc == 0, kc == 7, [wb(c0), nT], [p])
                if kind == 'gate':
                    sgt = sg[idx % 2]
                    _act(S, sgt.t[:, :], p.t[:, :], AF.Sigmoid, [p], [sgt])
                elif kind == 'a':
                    sgt = sg[idx % 2]
                    _tt(S, 'dve', hglu.t[:, idx, 30:542], p.t[:, :], sgt.t[:, :], ALU.mult, [p, sgt], [hglu])
                else:
                    sl = stg[nstg % 6]
                    nstg += 1
                    if kind == 'q':
                        _act(S, sl.t[:, :], p.t[:, :], AF.Copy, [p], [sl], scale=0.125)
                        dst = T['QT'][2 * idx:2 * idx + 2, :, cs].rearrange('h d s -> (h d) s')
                    elif kind == 'gm':
                        _act(S, sl.t[:, :], p.t[:, :], AF.Sigmoid, [p], [sl])
                        dst = T['GM'][idx, :, cs]
                    else:
                        _cp(S, 'dve', sl.t[:, :], p.t[:, :], [p], [sl])
                        dst = T[{'kc': 'KC', 'vc': 'VC', 'ks': 'KS', 'kw': 'KW'}[kind]][2 * idx:2 * idx + 2, :, cs] \
                            .rearrange('g d s -> (g d) s')
                    S.dma('sp', sl.sem, dict(out=dst, in_=sl.t[:, :]), reads=[sl])
            if st + 1 < NSUP:
                rms_pre(st + 1)
            if STG < 4:
                continue
            POOL_TAPS = list(range(0, 30, 3))
            PE_TAPS = [j for j in range(31) if j not in POOL_TAPS]
            for ch in range(4):
                tmp_ = dtl[ch % 2]
                for i_, j in enumerate(POOL_TAPS):
                    if i_ == 0:
                        _ts(S, 'pool', ysq.t[:, ch, :], hglu.t[:, ch, j:j + 512], dwT.t[:, ch, j:j + 1], ALU.mult,
                            [hglu, dwT], [ysq], s2=0.0, op1=ALU.add)
                    else:
                        _ts(S, 'pool', tmp_.t[:, :], hglu.t[:, ch, j:j + 512], dwT.t[:, ch, j:j + 1], ALU.mult,
                            [hglu, dwT], [tmp_], s2=0.0, op1=ALU.add)
                        _tt(S, 'pool', ysq.t[:, ch, :], ysq.t[:, ch, :], tmp_.t[:, :], ALU.add, [ysq, tmp_], [ysq])
            for ch in range(4):
                p = pc[ch % 2]
                for j in PE_TAPS:
                    _mm(S, p.t[:, :], dg.t[:, ch, j, :], hglu.t[:, ch, j:j + 512], j == PE_TAPS[0], j == PE_TAPS[-1],
                        [dg, hglu], [p])
                _stt(S, ycs.t[:, ch, :], p.t[:, :], dwb.t[:, ch, :], ysq.t[:, ch, :], ALU.add, ALU.add, [p, dwb, ysq], [ycs])
                _tt(S, 'pool', ysq.t[:, ch, :], ycs.t[:, ch, :], ycs.t[:, ch, :], ALU.mult, [ycs], [ysq])
            _cp(S, 'dve', hglu.t[:, :, 0:30], hglu.t[:, :, 512:542], [hglu], [hglu])
            if STG < 5:
                continue
            pm = pf[npf % 3]
            npf += 1
            pq = pf[npf % 3]
            npf += 1
            for ch in range(4):
                _mm(S, pm.t[:, :], onesF.t[:, :], ycs.t[:, ch, :], ch == 0, ch == 3, [onesF, ycs], [pm])
            for ch in range(4):
                _mm(S, pq.t[:, :], onesF.t[:, :], ysq.t[:, ch, :], ch == 0, ch == 3, [onesF, ysq], [pq])
            _cp(S, 'act', mean.t[:, :], pm.t[:, :], [pm], [mean])
            _tt(S, 'dve', msq.t[:, :], mean.t[:, :], mean.t[:, :], ALU.mult, [mean], [msq])
            _tt(S, 'dve', msq.t[:, :], pq.t[:, :], msq.t[:, :], ALU.subtract, [pq, msq], [msq])
            _act(S, msq.t[:, :], msq.t[:, :], AF.Sqrt, [msq, epsT], [msq], bias=epsT.t[:, :])
            S.op('dve', 'reciprocal', dict(out=rs.t[:, :], in_=msq.t[:, :]), [msq], [rs])
            for ch in range(4):
                d_ = dtl[ch % 2]
                _tt(S, 'dve', d_.t[:, :], ycs.t[:, ch, :], mean.t[:, :], ALU.subtract, [ycs, mean], [d_])
                _tt(S, 'dve', d_.t[:, :], d_.t[:, :], rs.t[:, :], ALU.mult, [d_, rs], [d_])
                _ts(S, 'dve', d_.t[:, :], d_.t[:, :], lng.t[:, ch, :], ALU.mult, [d_, lng, lnb], [d_],
                    s2=lnb.t[:, ch, :], op1=ALU.add)
                sl = stg[nstg % 6]
                nstg += 1
                _act(S, sl.t[:, :], d_.t[:, :], AF.Silu, [d_], [sl])
                S.dma('sp', sl.sem, dict(out=T['HC'][ch, :, cs], in_=sl.t[:, :]), reads=[sl])
        S.flush()


def bcast_row(S, C, src_row, dst_tl, pbanks, onesF):
    S.dma('sp', C.stage.sem, dict(out=C.stage.t[0:1, 0:1024], in_=src_row), writes=[C.stage])
    for half in range(2):
        p = pbanks[half]
        _mm(S, p.t[:, :], onesF.t[0:1, :], C.stage.t[0:1, half * 512:(half + 1) * 512], True, True,
            [onesF, C.stage], [p])
        _cp(S, 'dve', dst_tl.t[:, half * 512:(half + 1) * 512], p.t[:, :], [p], [dst_tl])


def rms_stats(S, R, i, xb):
    _stt(S, R.junk.t[:, :], xb.t[:, :], 1.0, xb.t[:, :], ALU.mult, ALU.mult, [xb], [R.junk, R.ssl[i]],
         accum_out=R.ssl[i].t[:, 0:1])
    _act(S, R.ssl[i].t[:, 1:2], R.ssl[i].t[:, 0:1], AF.Sqrt, [R.ssl[i], R.epsT], [R.ssl[i]], scale=1.0 / D,
         bias=R.epsT.t[:, :])
    S.op('dve', 'reciprocal', dict(out=R.ssl[i].t[:, 2:3], in_=R.ssl[i].t[:, 1:2]), [R.ssl[i]], [R.ssl[i]])


def rms_T(S, R, xtiles, gB, per_tile=False, stats_done=False):
    n = len(xtiles)
    if not per_tile:
        for i, xb in enumerate(xtiles):
            _stt(S, R.junk.t[:, :], xb.t[:, :], 1.0, xb.t[:, :], ALU.mult, ALU.mult, [xb], [R.junk, R.ss],
                 accum_out=R.ss.t[:, i:i + 1])
        _act(S, R.rt.t[:, 0:n], R.ss.t[:, 0:n], AF.Sqrt, [R.ss, R.epsT], [R.rt], scale=1.0 / D, bias=R.epsT.t[:, :])
        S.op('dve', 'reciprocal', dict(out=R.rstd.t[:, 0:n], in_=R.rt.t[:, 0:n]), [R.rt], [R.rstd])
    for i, xb in enumerate(xtiles):
        if per_tile:
            if not stats_done:
                rms_stats(S, R, i, xb)
            sc_ap, sc_tl = R.ssl[i].t[:, 2:3], R.ssl[i]
        else:
            sc_ap, sc_tl = R.rstd.t[:, i:i + 1], R.rstd
        nk = R.ntok[i % 2]
        _stt(S, nk.t[:, :], xb.t[:, :], sc_ap, gB.t[:, :], ALU.mult, ALU.mult, [xb, sc_tl, gB], [nk])
        for kc in range(8):
            _tp(S, R.ptr.t[:, kc * 128:(kc + 1) * 128], nk.t[:, kc * 128:(kc + 1) * 128], R.identB.t[:, :],
                [nk, R.identB], [R.ptr])
        yield i, R.ptr


WB_SPEC = {'xa_wkv': (D, 2 * D), 'nsa_w_o': (D, D), 'conv_w_pw': (512, D), 'w_out': (D, D), 'xa_wq': (D, D), 'xa_wo': (D, D),
           'ffn_w_up': (D, 2 * D_FF), 'ffn_w_down': (D_FF, D)}


def precast_weights(S, T):
    key = S.newdma()
    for name, (K_, N_) in WB_SPEC.items():
        for r0 in range(0, K_, 128):
            for c0 in range(0, N_, 2048):
                c1 = min(N_, c0 + 2048)
                S.dma('pool', key, dict(out=T['WB_' + name][r0:r0 + 128, c0:c1], in_=T[name][r0:r0 + 128, c0:c1]))


def _mk(nc, es, S):
    def sb(name, shape, dt, sem=False):
        t = es.enter_context(nc.sbuf_tensor(name, shape, dt))
        return Tl(t, S.newdma() if sem else None)

    def ps(name, shape, dt=F32):
        return Tl(es.enter_context(nc.psum_tensor(name, shape, dt)))
    return sb, ps


def cmp_chunks(Sq):
    NC = Sq // 16 - 1
    out = []
    n0 = 0
    while n0 < NC:
        out.append((n0, min(128, NC - n0)))
        n0 += 128
    return NC, out


def phase_B(nc, S, Sq, T, kcT, vco, P):
    NC, chunks = cmp_chunks(Sq)
    with ExitStack() as es:
        sb, ps = _mk(nc, es, S)
        C = Ctx()
        identF = sb('b_identF', [128, 128], F32, sem=True)
        C.identF = identF
        C.stage = sb('b_stage', [32, 1024], F32, sem=True)
        pst = ps('b_pst', [128, 48])
        C.pst = pst
        kin = [sb('b_kin%d' % i, [128, 4, Sq], BF16, sem=True) for i in range(2)]
        w1s = sb('b_w1s', [128, 2, 16, 256], BF16, sem=True)
        w2s = sb('b_w2s', [128, 2, 2, 64], BF16, sem=True)
        peT = sb('b_peT', [128, 2, 16], BF16)
        b1T = sb('b_b1T', [128, 4, 1], F32)
        biasc = sb('b_biasc', [128, 4, 1], F32)
        hT = [sb('b_hT%d' % i, [128, 2, 512], BF16) for i in range(2)]
        xb = sb('b_xb', [128, 512], F32)
        x2 = sb('b_x2', [128, 512], F32)
        u = sb('b_u', [128, 512], F32)
        sgm = sb('b_sgm', [128, 512], F32)
        ovs = sb('b_ovs', [128, 2, 64], F32, sem=True)
        pA = [ps('b_pA%d' % i, [128, 512]) for i in range(2)]
        pB = ps('b_pB', [128, 512])

        S.dma('sp', identF.sem, dict(out=identF.t[:, :], in_=T['identF']), writes=[identF])
        for kv in range(2):
            for lh in range(2):
                for l0 in range(0, 16, 8):
                    S.dma('pool', w1s.sem, dict(out=w1s.t[lh * 64:(lh + 1) * 64, kv, l0:l0 + 8, :],
                                                in_=T['cmp_w1'][kv, (lh * 16 + l0) * 64:(lh * 16 + l0 + 8) * 64, :]
                                                .rearrange('(l d) c -> d l c', d=64)), writes=[w1s])
            S.dma('pool', w2s.sem, dict(out=w2s.t[:, kv, :, :],
                                        in_=T['cmp_w2'][kv].rearrange('(h p) d -> p h d', p=128)), writes=[w2s])
            for lh in range(2):
                S.dma('sp', C.stage.sem, dict(out=C.stage.t[0:16, lh * 64:(lh + 1) * 64],
                                              in_=T['cmp_pe'][kv, lh * 16:(lh + 1) * 16, :]), writes=[C.stage])
            _tp(S, pst.t[:, 0:16], C.stage.t[0:16, 0:128], identF.t[0:16, 0:16], [C.stage, identF], [pst])
            _cp(S, 'dve', peT.t[:, kv, :], pst.t[:, 0:16], [pst], [peT])
        cols_from_rows(S, C, T['cmp_b1'], 1, 512, b1T, lambda ch: b1T.t[:, ch, :])
        _memset(S, 'dve', vco.t[:, :, :, :], 0.0, [vco])
        _memset(S, 'dve', kcT.t[:, :, :], 0.0, [kcT])
        S.dma('sp', ovs.sem, dict(out=ovs.t[:, 0:len(chunks), :], in_=T['ovl'].rearrange('(c p) j -> p c j', p=128)),
              writes=[ovs])
        for ci in range(len(chunks)):
            for g in range(4):
                _cp(S, 'dve', vco.t[:, ci, g, 64:128], ovs.t[:, ci, :], [ovs], [vco])
        for kv in range(2):
            for half in range(2):
                for l in range(16):
                    _mm(S, pB.t[:, 0:1], w1s.t[:, kv, l, half * 128:(half + 1) * 128], peT.t[:, kv, l:l + 1],
                        l == 0, l == 15, [w1s, peT], [pB])
                _tt(S, 'dve', biasc.t[:, kv * 2 + half, :], pB.t[:, 0:1], b1T.t[:, kv * 2 + half, :], ALU.add,
                    [pB, b1T], [biasc])

        for kv in range(2):
            src = kin[kv]
            S.dma('sp', src.sem, dict(out=src.t[0:64, :, :], in_=T['KC' if kv == 0 else 'VC'].rearrange('g d s -> d g s')), writes=[src])
            S.dma('sp', src.sem, dict(out=src.t[64:128, :, 0:Sq - 16], in_=T['KC' if kv == 0 else 'VC'][:, :, 16:Sq].rearrange('g d s -> d g s')), writes=[src])
        npa = 0
        for kv in range(2):
            src = kin[kv]
            for g in range(4):
                h_ = hT[(kv * 4 + g) % 2]
                for half in range(2):
                    p = pA[npa % 2]
                    npa += 1
                    for l in range(16):
                        _mm(S, p.t[:, 0:NC], w1s.t[:, kv, l, half * 128:(half + 1) * 128],
                            src.t[:, g, l:l + 16 * (NC - 1) + 1:16], l == 0, l == 15, [w1s, src], [p])
                    _ts(S, 'dve', xb.t[:, 0:NC], p.t[:, 0:NC], biasc.t[:, kv * 2 + half, :], ALU.add, [p, biasc], [xb])
                    _tt(S, 'pool', x2.t[:, 0:NC], xb.t[:, 0:NC], xb.t[:, 0:NC], ALU.mult, [xb], [x2])
                    _ts(S, 'dve', x2.t[:, 0:NC], x2.t[:, 0:NC], 0.044715, ALU.mult, [x2], [x2], s2=1.0, op1=ALU.add)
                    _tt(S, 'dve', u.t[:, 0:NC], x2.t[:, 0:NC], xb.t[:, 0:NC], ALU.mult, [x2, xb], [u])
                    _act(S, sgm.t[:, 0:NC], u.t[:, 0:NC], AF.Sigmoid, [u], [sgm], scale=1.5957691216057308)
                    _tt(S, 'dve', h_.t[:, half, 0:NC], xb.t[:, 0:NC], sgm.t[:, 0:NC], ALU.mult, [xb, sgm], [h_])
                if kv == 0:
                    for half in range(2):
                        _mm(S, pB.t[0:64, 0:NC], w2s.t[:, 0, half, :], h_.t[:, half, 0:NC], half == 0, half == 1,
                            [w2s, h_], [pB])
                    _cp(S, 'act', kcT.t[0:64, g, 0:NC], pB.t[0:64, 0:NC], [pB], [kcT])
                else:
                    for ci, (n0, sz) in enumerate(chunks):
                        for half in range(2):
                            _mm(S, pB.t[0:sz, 0:64], h_.t[:, half, n0:n0 + sz], w2s.t[:, 1, half, :], half == 0,
                                half == 1, [w2s, h_], [pB])
                        _cp(S, 'act', vco.t[0:sz, ci, g, 0:64], pB.t[0:sz, 0:64], [pB], [vco])
        tmpA = sb('b_tmpA', [128, 2048], F32, sem=True)
        tmpB = sb('b_tmpB', [128, 2048], F32, sem=True)
        biasT, Bband = P.biasT, P.Bband
        for dl in range(2):
            S.dma('sp', tmpA.sem, dict(out=tmpA.t[:, :], in_=T['tz1'][dl].rearrange('k h q -> k (h q)')), writes=[tmpA])
            S.dma('sp', tmpB.sem, dict(out=tmpB.t[:, :], in_=T['tz31'][dl].rearrange('k h q -> k (h q)')), writes=[tmpB])
            _tt(S, 'pool', tmpA.t[:, :], tmpA.t[:, :], tmpB.t[:, :], ALU.subtract, [tmpA, tmpB], [tmpA])
            S.dma('sp', tmpB.sem, dict(out=tmpB.t[:, :], in_=T['tzm'][dl].rearrange('k h q -> k (h q)')), writes=[tmpB])
            _tt(S, 'pool', biasT.t[:, dl, :, :].rearrange('k h q -> k (h q)'), tmpA.t[:, :], tmpB.t[:, :], ALU.add,
                [tmpA, tmpB], [biasT])
        S.dma('sp', tmpA.sem, dict(out=tmpA.t[0:32, :], in_=T['cb1'].rearrange('k h q -> k (h q)')), writes=[tmpA])
        S.dma('sp', tmpB.sem, dict(out=tmpB.t[0:32, :], in_=T['cb31'].rearrange('k h q -> k (h q)')), writes=[tmpB])
        _tt(S, 'pool', tmpA.t[0:32, :], tmpA.t[0:32, :], tmpB.t[0:32, :], ALU.subtract, [tmpA, tmpB], [tmpA])
        S.dma('sp', tmpB.sem, dict(out=tmpB.t[0:32, :], in_=T['cbm'].rearrange('k h q -> k (h q)')), writes=[tmpB])
        _tt(S, 'pool', Bband.t[:, :, :].rearrange('k h q -> k (h q)'), tmpA.t[0:32, :], tmpB.t[0:32, :], ALU.add,
            [tmpA, tmpB], [Bband])

        for blk in range(0, 2 * D_FF, 1024):
            w = min(1024, 2 * D_FF - blk)
            c0 = blk // 128
            cols_from_rows(S, C, T['ffn_dw_w'][:, blk:blk + w], 3, w, P.fw, lambda ch, c0=c0: P.fw.t[:, c0 + ch, :])
            cols_from_rows(S, C, T['ffn_dw_b'][:, blk:blk + w], 1, w, P.fb, lambda ch, c0=c0: P.fb.t[:, c0 + ch, :])
        R = Ctx()
        R.identB = sb('b_identB', [128, 128], BF16, sem=True)
        R.epsT = sb('b_eps', [128, 1], F32)
        R.ss = sb('b_ss', [128, 4], F32)
        R.rt = sb('b_rt', [128, 4], F32)
        R.rstd = sb('b_rstd', [128, 4], F32)
        R.junk = sb('b_junk', [128, D], BF16)
        R.ntok = [sb('b_ntok%d' % i, [128, D], BF16) for i in range(2)]
        R.ptr = ps('b_ptr', [128, 1024], BF16)
        onesF = sb('b_onesF', [1, 128], F32)
        gmB = sb('b_gmB', [128, D], F32)
        wkv = sb('b_wkv', [128, 8, 2 * D], BF16, sem=True)
        memx = [sb('b_memx%d' % i, [128, D], F32, sem=True) for i in range(2)]
        memnT = sb('b_memnT', [128, 8, 256], BF16)
        kmT, vm = P.kmT, P.vm
        S.dma('pool', R.identB.sem, dict(out=R.identB.t[:, :], in_=T['identF']), writes=[R.identB])
        _memset(S, 'dve', R.epsT.t[:, :], EPS, [R.epsT])
        _memset(S, 'dve', onesF.t[:, :], 1.0, [onesF])
        S.dma('sp', wkv.sem, dict(out=wkv.t[:, :, :], in_=T['WB_xa_wkv'].rearrange('(k p) n -> p k n', p=128)), writes=[wkv])
        bcast_row(S, C, T['mem_norm_g'], gmB, pA, onesF)
        for mc in range(2):
            S.dma('sp', memx[mc].sem, dict(out=memx[mc].t[:, :], in_=T['mem'][mc * 128:(mc + 1) * 128, :]),
                  writes=[memx[mc]])
        for i, p_ in rms_T(S, R, memx, gmB):
            _cp(S, 'act', memnT.t[:, :, i * 128:(i + 1) * 128], p_.t[:, :].rearrange('p (k q) -> p k q', k=8),
                [p_], [memnT])
        for c in range(8):
            p = pA[c % 2]
            for kc in range(8):
                _mm(S, p.t[:, 0:256], wkv.t[:, kc, c * 128:(c + 1) * 128], memnT.t[:, kc, :], kc == 0, kc == 7,
                    [wkv, memnT], [p])
            _cp(S, 'dve', kmT.t[:, c, :], p.t[:, 0:256], [p], [kmT])
        for mc in range(2):
            for half in range(2):
                p = pA[(mc * 2 + half) % 2]
                for kc in range(8):
                    _mm(S, p.t[:, :], memnT.t[:, kc, mc * 128:(mc + 1) * 128],
                        wkv.t[:, kc, D + half * 512:D + (half + 1) * 512], kc == 0, kc == 7, [wkv, memnT], [p])
                _cp(S, 'dve', vm.t[:, mc, half * 512:(half + 1) * 512], p.t[:, :], [p], [vm])
        S.flush()


def phase_C(nc, S, Sq, T, kcT, vco, P):
    NT = Sq // 128
    NC, chunks = cmp_chunks(Sq)
    OFFS = 8 * (NT - 1)
    with ExitStack() as es:
        sb, ps = _mk(nc, es, S)
        identB = sb('c_identB', [128, 128], BF16, sem=True)
        KE = sb('c_KE', [128, 4, Sq], BF16, sem=True)
        KWt = sb('c_KW', [128, 4, Sq], BF16, sem=True)
        KWz = Buf()
        KEe = Buf()
        KEe_sem = S.newdma()
        VS = sb('c_VS', [128, NT, 4, 65], BF16, sem=True)
        VW = sb('c_VW', [128, NT, 4, 65], BF16, sem=True)
        biasT, Bband = P.biasT, P.Bband
        m512 = sb('c_m512', [128, 512], BF16, sem=True)
        SelW = sb('c_SelW', [32, OFFS + 128 * len(chunks)], BF16, sem=True)
        mulB = sb('c_mulB', [128, 128], F32, sem=True)
        addB = sb('c_addB', [128, 128], F32, sem=True)
        QM = [sb('c_QM%d' % i, [128, 4, 4, 128], BF16, sem=True) for i in range(2)]
        QMq = [Buf() for _ in range(2)]
        QMm = [[Buf() for _ in range(4)] for _ in range(2)]
        gt = [sb('c_gt%d' % i, [128, 16, 3], F32, sem=True) for i in range(3)]
        NE = 4
        Et = [sb('c_E%d' % i, [128, 512], BF16) for i in range(NE)]
        o1s = [[sb('c_o1s%d_%d' % (j, i), [128, 4, 128], F32) for i in range(4)] for j in range(2)]
        o2sl = [sb('c_o2s%d' % i, [128, 4, 65], F32) for i in range(2)]
        o3s = [[sb('c_o3s%d_%d' % (j, i), [128, 4, 65], F32) for i in range(4)] for j in range(2)]
        coef1 = [[sb('c_coef1_%d_%d' % (j, i), [128, 4], F32) for i in range(4)] for j in range(2)]
        den = sb('c_den', [128, 4], F32)
        rden = sb('c_rden', [128, 4], F32)
        c2l = [sb('c_c2_%d' % i, [128, 4], F32) for i in range(2)]
        c3l = [sb('c_c3_%d' % i, [128, 4], F32) for i in range(2)]
        tmpc = sb('c_tmpc', [128, 4, 64], F32)
        imp = sb('c_imp', [128, 64], F32)
        score = sb('c_score', [128, 64], F32)
        score2 = sb('c_score2', [128, 64], F32)
        m8a = sb('c_m8a', [128, 8], F32)
        m8b = sb('c_m8b', [128, 8], F32)
        nms = [sb('c_nm%d' % i, [128, 128], BF16) for i in range(4)]
        acc = sb('c_acc', [128, 4, 64], F32)
        ntk = sb('c_ntk', [128, 1024], BF16)
        nst = [sb('c_nst%d' % i, [128, 8, 128], BF16, sem=True) for i in range(2)]

        scp = [ps('c_sc%d' % i, [128, 512]) for i in range(3)]
        o1U = ps('c_o1U', [128, 512])
        o3p = ps('c_o3', [128, 4, 65])
        o2p = [ps('c_o2_%d' % i, [128, 4, 65]) for i in range(2)]
        ptr = ps('c_ptr', [128, 1024], BF16)

        S.dma('pool', identB.sem, dict(out=identB.t[:, :], in_=T['identF']), writes=[identB])
        S.dma('sp', KWt.sem, dict(out=KWt.t[0:64, :, :], in_=T['KW'].rearrange('g d s -> d g s')), writes=[KWt])
        _memset(S, 'pool', KWt.t[64:128, :, :], 0.0, [KWz])
        for i_, q_ in enumerate(QM):
            _memset(S, 'pool', q_.t[:, :, :, :], 0.0, [q_, QMq[i_]] + QMm[i_])
        def load_q(qt):
            sl = qt % 2
            qs = slice(qt * 128, (qt + 1) * 128)
            S.dma('sp', QM[sl].sem, dict(out=QM[sl].t[0:64, :, :, :].rearrange('d g h q -> d (g h) q'),
                                         in_=T['QT'][:, :, qs].rearrange('h d q -> d h q')), writes=[QMq[sl]])
            S.dma('sp', gt[qt % 3].sem, dict(out=gt[qt % 3].t[:, :, :].rearrange('p h b -> p (h b)'), in_=T['G'][qs, :]),
                  writes=[gt[qt % 3]])

        load_q(0)
        S.dma('pool', SelW.sem, dict(out=SelW.t[:, :], in_=T['SelW']), writes=[SelW])
        for k0 in range(0, NT, 8):
            k1 = min(NT, k0 + 8)
            S.dma('sp', VW.sem, dict(out=VW.t[:, k0:k1, :, :],
                                     in_=T['VW1'][k0 * 128:k1 * 128].rearrange('(k p) g d -> p k g d', p=128)),
                  writes=[VW])
        S.dma('sp', mulB.sem, dict(out=mulB.t[:, :], in_=T['mulB']), writes=[mulB])
        S.dma('sp', addB.sem, dict(out=addB.t[:, :], in_=T['addB']), writes=[addB])
        S.dma('sp', KE.sem, dict(out=KE.t[0:64, :, :], in_=T['KS'].rearrange('g d s -> d g s')), writes=[KE])
        for g in range(4):
            for c0 in range(0, Sq, 2048):
                c1 = min(Sq, c0 + 2048)
                S.dma('pool', KEe_sem, dict(out=KE.t[64:128, g, c0:c1], in_=T['Econst'][:, c0:c1]), writes=[KEe])
        for k0 in range(0, NT, 8):
            k1 = min(NT, k0 + 8)
            S.dma('sp', VS.sem, dict(out=VS.t[:, k0:k1, :, :],
                                     in_=T['VS1'][k0 * 128:k1 * 128].rearrange('(k p) g d -> p k g d', p=128)),
                  writes=[VS])
        S.dma('pool', m512.sem, dict(out=m512.t[:, :], in_=T['m512']), writes=[m512])
        for nm_ in nms:
            _memset(S, 'dve', nm_.t[:, :], 0.0, [nm_])

        def cw_steps(qt):
            out = []
            for g in range(4):
                cl = [(ci, n0, sz) for ci, (n0, sz) in enumerate(chunks) if n0 <= 8 * qt + 6]
                for i, (ci, n0, sz) in enumerate(cl):
                    out.append(dict(kind='cmp', qt=qt, g=g, ci=ci, n0=n0, sz=sz, first=i == 0, last=i == len(cl) - 1))
                kl = list(range(max(0, qt - 4), qt + 1))
                for i, kt in enumerate(kl):
                    out.append(dict(kind='win', qt=qt, g=g, kt=kt, sz=128, first=i == 0, last=i == len(kl) - 1))
            out[0]['loadq'] = qt
            out[-1]['flush_def'] = True
            return out

        def sel_steps(qt):
            out = []
            for g in range(4):
                for kt in range(qt + 1):
                    out.append(dict(kind='sel', qt=qt, g=g, kt=kt, sz=128, first=kt == 0, last=kt == qt))
            return out

        if os.environ.get('PIPE', '1') == '1':
            steps = cw_steps(0)
            for qt in range(NT):
                if qt + 1 < NT:
                    steps += cw_steps(qt + 1)
                steps += sel_steps(qt)
        else:
            steps = []
            for qt in range(NT):
                steps += cw_steps(qt) + sel_steps(qt)
        cnt = dict(sc=0, e=0, o2=0, fs=0)

        def emit_scores(st):
            qt, g, sz = st['qt'], st['g'], st['sz']
            sl = qt % 2
            sc = scp[cnt['sc'] % 3]
            cnt['sc'] += 1
            st['sc'] = sc
            qrow = QM[sl].t[:, g, :, :].rearrange('d h q -> d (h q)')
            if st['kind'] == 'cmp':
                n0 = st['n0']
                a = n0 - 8 * qt + OFFS
                _mm(S, sc.t[0:sz, :], kcT.t[:, g, n0:n0 + sz], qrow, True, False, [kcT, QMq[sl], QM[sl], QMm[sl][g]], [sc])
                _mm(S, sc.t[0:sz, :], SelW.t[0:32, a:a + sz],
                    Bband.t[0:32, 4 * g:4 * g + 4, :].rearrange('k h q -> k (h q)'), False, True, [SelW, Bband], [sc])
                return
            kt = st['kt']
            dl = (qt - kt)
            extra = None
            if dl in (0, 1):
                extra = (biasT.t[:, dl, 4 * g:4 * g + 4, :].rearrange('k h q -> k (h q)'), biasT)
            elif dl == 4 and st['kind'] == 'win':
                extra = (m512.t[:, :], m512)
            ks = slice(kt * 128, (kt + 1) * 128)
            if st['kind'] == 'win':
                _mm(S, sc.t[:, :], KWt.t[:, g, ks], qrow, True, extra is None, [KWt, KWz, QMq[sl], QM[sl], QMm[sl][g]], [sc])
            else:
                _mm(S, sc.t[:, :], KE.t[:, g, ks], QM[sl].t[:, g, :, :].rearrange('d h q -> d (h q)'), True,
                    extra is None, [KE, KEe, QMq[sl], QMm[sl][g]], [sc])
            if extra is not None:
                _mm(S, sc.t[:, :], identB.t[:, :], extra[0], False, True, [identB, extra[1]], [sc])

        def emit_exp(st):
            sz = st['sz']
            E = Et[cnt['e'] % NE]
            cnt['e'] += 1
            st['E'] = E
            _act(S, E.t[0:sz, :], st['sc'].t[0:sz, :], AF.Exp, [st['sc']], [E])

        def emit_pv(st):
            qt, g, sz, E = st['qt'], st['g'], st['sz'], st['E']
            if st['kind'] == 'cmp':
                for h in range(4):
                    _mm(S, o1U.t[:, h * 128:(h + 1) * 128], E.t[0:sz, h * 128:(h + 1) * 128], vco.t[0:sz, st['ci'], g, :],
                        st['first'] and h == 0, st['last'] and h == 3, [E, vco], [o1U], skip=True)
                if st['last']:
                    fin_cmp(qt, g)
                return
            kt = st['kt']
            if st['kind'] == 'win':
                op_, V = o3p, VW
            else:
                if st['first']:
                    st['o2'] = o2p[cnt['o2'] % 2]
                    cnt['o2'] += 1
                    cur['o2'] = st['o2']
                op_, V = cur['o2'], VS
            for h in range(4):
                _mm(S, op_.t[:, h, :], E.t[:, h * 128:(h + 1) * 128], V.t[:, kt, g, :], st['first'] and h == 0,
                    st['last'] and h == 3, [E, V], [op_], skip=True)
            if st['last']:
                if st['kind'] == 'win':
                    _cp(S, EV, o3s[qt % 2][g].t[:, :, :], o3p.t[:, :, :], [o3p], [o3s[qt % 2][g]])
                else:
                    fin_sel(qt, g, op_)

        cur = {}

        def fin_cmp(qt, g):
            sl = qt % 2
            nm = nms[g]
            o1 = o1s[sl][g]
            _cp(S, EV, o1.t[:, :, :], o1U.t[:, :].rearrange('p (h c) -> p h c', h=4), [o1U], [o1])
            S.op('dve', 'tensor_reduce', dict(out=den.t[:, :], in_=o1.t[:, :, 64:128], axis=AX.X, op=ALU.add), [o1], [den])
            _ts(S, 'dve', den.t[:, :], den.t[:, :], 1e-30, ALU.max, [den], [den])
            S.op('dve', 'reciprocal', dict(out=rden.t[:, :], in_=den.t[:, :]), [den], [rden])
            _ts(S, 'dve', imp.t[:, :], o1.t[:, 0, 64:128], rden.t[:, 0:1], ALU.mult, [o1, rden], [imp])
            for h in range(1, 4):
                _stt(S, imp.t[:, :], o1.t[:, h, 64:128], rden.t[:, h:h + 1], imp.t[:, :], ALU.mult, ALU.add,
                     [o1, rden, imp], [imp])
            a = 62 - 2 * qt
            _tt(S, 'dve', score.t[:, :], imp.t[:, :], mulB.t[:, a:a + 64], ALU.mult, [imp, mulB], [score])
            _tt(S, 'dve', score.t[:, :], score.t[:, :], addB.t[:, a:a + 64], ALU.add, [score, addB], [score])
            _memset(S, 'dve', score.t[:, 0:1], 50.0, [score])
            S.op('dve', 'max', dict(out=m8a.t[:, :], in_=score.t[:, :]), [score], [m8a])
            S.op('dve', 'match_replace', dict(out=score2.t[:, :], in_to_replace=m8a.t[:, :], in_values=score.t[:, :],
                                              imm_value=-1e9), [score, m8a], [score2])
            S.op('dve', 'max', dict(out=m8b.t[:, :], in_=score2.t[:, :]), [score2], [m8b])
            _ts(S, 'dve', nm.t[:, 64:128], score.t[:, :], m8b.t[:, 7:8], ALU.is_lt, [score, m8b], [nm], s2=MASKV,
                op1=ALU.mult)

            def part2(sl=sl, g=g, nm=nm):
                _tp(S, ptr.t[:, 0:128], nm.t[:, :], identB.t[:, :], [nm, identB], [ptr])
                for h in range(4):
                    _cp(S, 'dve', QM[sl].t[64:128, g, h, :], ptr.t[64:128, 0:128], [ptr], [QMm[sl][g]])
            deferred.append([DEFER, part2])
            _tt(S, 'dve', coef1[sl][g].t[:, :], rden.t[:, :], gt[qt % 3].t[:, 4 * g:4 * g + 4, 0], ALU.mult, [rden, gt[qt % 3]],
                [coef1[sl][g]])

        def fin_sel(qt, g, o2):
            sl = qt % 2
            k_ = cnt['fs'] % 2
            cnt['fs'] += 1
            o2s, c2, c3 = o2sl[k_], c2l[k_], c3l[k_]
            _cp(S, EV, o2s.t[:, :, :], o2.t[:, :, :], [o2], [o2s])
            for (osrc, cf, br) in ((o2s, c2, 1), (o3s[sl][g], c3, 2)):
                _ts(S, 'dve', den.t[:, :], osrc.t[:, :, 64], 1e-30, ALU.max, [osrc], [den])
                S.op('dve', 'reciprocal', dict(out=rden.t[:, :], in_=den.t[:, :]), [den], [rden])
                _tt(S, 'dve', cf.t[:, :], rden.t[:, :], gt[qt % 3].t[:, 4 * g:4 * g + 4, br], ALU.mult, [rden, gt[qt % 3]], [cf])
            o1 = o1s[sl][g]
            o3 = o3s[sl][g]
            c1 = coef1[sl][g]

            def cb(c):
                return c.t[:, :].unsqueeze(2).broadcast_to([128, 4, 64])
            CE = 'pool'
            _tt(S, CE, acc.t[:, :, :], o1.t[:, :, 0:64], cb(c1), ALU.mult, [o1, c1], [acc])
            _tt(S, CE, tmpc.t[:, :, :], o3.t[:, :, 0:64], cb(c3), ALU.mult, [o3, c3], [tmpc])
            _tt(S, CE, acc.t[:, :, :], acc.t[:, :, :], tmpc.t[:, :, :], ALU.add, [acc, tmpc], [acc])
            _tt(S, CE, tmpc.t[:, :, :], o2s.t[:, :, 0:64], cb(c2), ALU.mult, [o2s, c2], [tmpc])
            _tt(S, CE, ntk.t[:, g * 256:(g + 1) * 256].rearrange('p (h d) -> p h d', h=4), acc.t[:, :, :], tmpc.t[:, :, :],
                ALU.add, [acc, tmpc], [ntk])
            if g == 3:
                def part2(qt=qt):
                    for kc in range(8):
                        _tp(S, ptr.t[:, kc * 128:(kc + 1) * 128], ntk.t[:, kc * 128:(kc + 1) * 128], identB.t[:, :],
                            [ntk, identB], [ptr])
                    ns = nst[qt % 2]
                    _cp(S, 'dve', ns.t[:, :, :], ptr.t[:, :].rearrange('p (k q) -> p k q', k=8), [ptr], [ns])
                    S.dma('sp', ns.sem, dict(out=T['NSAT'][:, :, qt * 128:(qt + 1) * 128].rearrange('k p q -> p k q'),
                                             in_=ns.t[:, :, :]), reads=[ns])
                deferred.append([DEFER, part2])

        LOOK = int(os.environ.get('LOOK', '2'))
        EV = os.environ.get('EV', 'act')
        DEFER = int(os.environ.get('DEFER', '8'))
        deferred = []
        pend = []

        def run_deferred(force=False):
            while deferred and (force or deferred[0][0] <= 0):
                deferred.pop(0)[1]()

        for st in steps:
            if 'loadq' in st and st['loadq'] > 0:
                load_q(st['loadq'])
            emit_scores(st)
            emit_exp(st)
            pend.append(st)
            if len(pend) > LOOK:
                emit_pv(pend.pop(0))
            for d_ in deferred:
                d_[0] -= 1
            run_deferred()
            if st.get('flush_def'):
                while pend:
                    emit_pv(pend.pop(0))
                run_deferred(force=True)
        while pend:
            emit_pv(pend.pop(0))
        run_deferred(force=True)
        S.flush()


def phase_D(nc, S, Sq, T, P):
    NSUP = Sq // 512
    with ExitStack() as es:
        sb, ps = _mk(nc, es, S)
        C = Ctx()
        R = Ctx()
        identB = sb('d_identB', [128, 128], BF16, sem=True)
        R.identB = identB
        onesB = sb('d_onesB', [128, 128], BF16)
        onesF = sb('d_onesF', [1, 128], F32)
        R.epsT = sb('d_eps', [128, 1], F32)
        wno = sb('d_wno', [128, 8, D], BF16, sem=True)
        wpw = sb('d_wpw', [128, 4, D], BF16, sem=True)
        wout = sb('d_wout', [128, 8, D], BF16, sem=True)
        wq = sb('d_wq', [128, 8, D], BF16, sem=True)
        wo = sb('d_wo', [128, 8, D], BF16, sem=True)
        g2B = sb('d_g2B', [128, D], F32)
        g3B = sb('d_g3B', [128, D], F32)
        kmT, vm = P.kmT, P.vm
        R.ptr = ps('d_ptr', [128, 1024], BF16)
        ptr = R.ptr
        pf = [ps('d_pf%d' % i, [128, 512]) for i in range(4)]
        ph = [ps('d_ph%d' % i, [128, 512]) for i in range(2)]
        R.ss = sb('d_ss', [128, 4], F32)
        R.rt = sb('d_rt', [128, 4], F32)
        R.rstd = sb('d_rstd', [128, 4], F32)
        R.ntok = [sb('d_ntok%d' % i, [128, D], BF16) for i in range(2)]
        R.ssl = [sb('d_ssl%d' % i, [128, 4], F32) for i in range(4)]

        S.dma('pool', identB.sem, dict(out=identB.t[:, :], in_=T['identF']), writes=[identB])
        _memset(S, 'dve', onesB.t[:, :], 1.0, [onesB])
        _memset(S, 'dve', onesF.t[:, :], 1.0, [onesF])
        _memset(S, 'dve', R.epsT.t[:, :], EPS, [R.epsT])
        def load_wts(lst):
            for w_, nm_ in lst:
                S.dma('sp', w_.sem, dict(out=w_.t[:, :, :], in_=T['WB_' + nm_].rearrange('(k p) n -> p k n', p=128)),
                      writes=[w_])
        load_wts(((wno, 'nsa_w_o'), (wpw, 'conv_w_pw')))
        nsa_s = sb('d_nsa', [128, 8, 512], BF16, sem=True)
        hc_s = sb('d_hc', [128, 4, 512], BF16, sem=True)
        gm_s = sb('d_gm', [128, 16, 512], BF16, sem=True)
        xs = [sb('d_x%d' % i, [128, D], F32, sem=True) for i in range(4)]
        xin = [sb('d_xin%d' % i, [128, D], F32, sem=True) for i in range(2)]
        C.stage = xs[0]
        bcast_row(S, C, T['norm2_g'], g2B, ph, onesF)
        bcast_row(S, C, T['norm3_g'], g3B, ph, onesF)
        mrg = sb('d_mrg', [128, 8, 512], BF16)
        n2T = sb('d_n2T', [128, 8, 512], BF16)
        qxT = sb('d_qxT', [128, 8, 512], BF16)
        PT = sb('d_PT', [128, 2, 512], BF16)
        R.junk = Tl(PT.t[:, :, :].rearrange('p a b -> p (a b)'))
        R.junk.b = PT.b
        oTn = sb('d_oTn', [128, 8, 512], BF16)
        t1 = sb('d_t1', [128, 512], F32)
        t2 = sb('d_t2', [128, 512], F32)
        rdn = sb('d_rdn', [128, 512], F32)
        n3s = [sb('d_n3s%d' % i, [128, 8, 128], BF16, sem=True) for i in range(2)]
        npf = 0
        NTT = Sq // 128

        def load_acts(st):
            cs = slice(st * 512, (st + 1) * 512)
            S.dma('sp', nsa_s.sem, dict(out=nsa_s.t[:, :, :], in_=T['NSAT'][:, :, cs].rearrange('k p s -> p k s')),
                  writes=[nsa_s])
            S.dma('sp', hc_s.sem, dict(out=hc_s.t[:, :, :], in_=T['HC'][:, :, cs].rearrange('k p s -> p k s')),
                  writes=[hc_s])
            S.dma('sp', gm_s.sem, dict(out=gm_s.t[:, :, :], in_=T['GM'][:, :, cs].rearrange('k p s -> p k s')),
                  writes=[gm_s])

        def load_x(tt):
            if tt < NTT:
                S.dma('sp', xin[tt % 2].sem, dict(out=xin[tt % 2].t[:, :], in_=T['x'][tt * 128:(tt + 1) * 128, :]),
                      writes=[xin[tt % 2]])
        load_acts(0)
        load_x(0)
        load_x(1)
        load_wts(((wout, 'w_out'), (wq, 'xa_wq'), (wo, 'xa_wo')))
        for st in range(NSUP):
            cs = slice(st * 512, (st + 1) * 512)
            for f in range(8):
                pa = pf[npf % 4]
                pcv = pf[(npf + 1) % 4]
                npf += 2
                for kc in range(8):
                    _mm(S, pa.t[:, :], wno.t[:, kc, f * 128:(f + 1) * 128], nsa_s.t[:, kc, :], kc == 0, kc == 7,
                        [wno, nsa_s], [pa])
                for c in range(4):
                    _mm(S, pcv.t[:, :], wpw.t[:, c, f * 128:(f + 1) * 128], hc_s.t[:, c, :], c == 0, c == 3,
                        [wpw, hc_s], [pcv])
                _tt(S, 'dve', t1.t[:, :], pa.t[:, :], gm_s.t[:, 8 + f, :], ALU.mult, [pa, gm_s], [t1])
                _tt(S, 'dve', t2.t[:, :], pcv.t[:, :], gm_s.t[:, f, :], ALU.mult, [pcv, gm_s], [t2])
                _tt(S, 'pool', mrg.t[:, f, :], t1.t[:, :], t2.t[:, :], ALU.add, [t1, t2], [mrg])
            if st + 1 < NSUP:
                load_acts(st + 1)
            for t in range(4):
                tt = st * 4 + t
                for half in range(2):
                    p = ph[half]
                    for f in range(8):
                        _mm(S, p.t[:, :], mrg.t[:, f, t * 128:(t + 1) * 128], wout.t[:, f, half * 512:(half + 1) * 512],
                            f == 0, f == 7, [mrg, wout], [p])
                    _tt(S, 'dve', xs[t].t[:, half * 512:(half + 1) * 512], p.t[:, :],
                        xin[tt % 2].t[:, half * 512:(half + 1) * 512], ALU.add, [p, xin[tt % 2]], [xs[t]])
                load_x(tt + 2)
                rms_stats(S, R, t, xs[t])
            for i, p_ in rms_T(S, R, xs, g2B, per_tile=True, stats_done=True):
                _cp(S, 'act', n2T.t[:, :, i * 128:(i + 1) * 128], p_.t[:, :].rearrange('p (k q) -> p k q', k=8),
                    [p_], [n2T])
            for c in range(8):
                p = pf[npf % 4]
                npf += 1
                for kc in range(8):
                    _mm(S, p.t[:, :], wq.t[:, kc, c * 128:(c + 1) * 128], n2T.t[:, kc, :], kc == 0, kc == 7,
                        [wq, n2T], [p])
                _act(S, qxT.t[:, c, :], p.t[:, :], AF.Copy, [p], [qxT], scale=1.0 / 16.0)
            for hd in range(4):
                for mc in range(2):
                    p = pf[npf % 4]
                    npf += 1
                    for dc in range(2):
                        _mm(S, p.t[:, :], kmT.t[:, hd * 2 + dc, mc * 128:(mc + 1) * 128], qxT.t[:, hd * 2 + dc, :],
                            dc == 0, dc == 1, [kmT, qxT], [p])
                    _act(S, PT.t[:, mc, :], p.t[:, :], AF.Exp, [p], [PT])
                pd = pf[npf % 4]
                npf += 1
                for mc in range(2):
                    _mm(S, pd.t[:, :], onesB.t[:, :], PT.t[:, mc, :], mc == 0, mc == 1, [onesB, PT], [pd])
                S.op('dve', 'reciprocal', dict(out=rdn.t[:, :], in_=pd.t[:, :]), [pd], [rdn])
                for dc in range(2):
                    po = pf[npf % 4]
                    npf += 1
                    for mc in range(2):
                        _mm(S, po.t[:, :], vm.t[:, mc, hd * 256 + dc * 128:hd * 256 + (dc + 1) * 128], PT.t[:, mc, :],
                            mc == 0, mc == 1, [vm, PT], [po])
                    _tt(S, 'dve', oTn.t[:, hd * 2 + dc, :], po.t[:, :], rdn.t[:, :], ALU.mult, [po, rdn], [oTn])
            for t in range(4):
                tt = st * 4 + t
                for half in range(2):
                    p = ph[half]
                    for c in range(8):
                        _mm(S, p.t[:, :], oTn.t[:, c, t * 128:(t + 1) * 128], wo.t[:, c, half * 512:(half + 1) * 512],
                            c == 0, c == 7, [oTn, wo], [p])
                    _tt(S, 'dve', xs[t].t[:, half * 512:(half + 1) * 512], p.t[:, :],
                        xs[t].t[:, half * 512:(half + 1) * 512], ALU.add, [p, xs[t]], [xs[t]])
                S.dma('sp', xs[t].sem, dict(out=T['H2'][tt * 128:(tt + 1) * 128, :], in_=xs[t].t[:, :]), reads=[xs[t]])
                rms_stats(S, R, t, xs[t])
            for i, p_ in rms_T(S, R, xs, g3B, per_tile=True, stats_done=True):
                tt = st * 4 + i
                ns = n3s[i % 2]
                _cp(S, 'act', ns.t[:, :, :], p_.t[:, :].rearrange('p (k q) -> p k q', k=8), [p_], [ns])
                S.dma('sp', ns.sem, dict(out=T['N3T'][:, :, tt * 128:(tt + 1) * 128].rearrange('k p q -> p k q'),
                                         in_=ns.t[:, :, :]), reads=[ns])
        S.flush()


def phase_E(nc, S, Sq, T, P):
    NSUP = Sq // 512
    NP = D_FF // 128
    with ExitStack() as es:
        sb, ps = _mk(nc, es, S)
        C = Ctx()
        identF = sb('e_identF', [128, 128], F32, sem=True)
        C.identF = identF
        C.stage = sb('e_stage', [32, 1024], F32, sem=True)
        pst = ps('e_pst', [128, 48])
        C.pst = pst
        wupb = [sb('e_wup%d' % i, [128, 8, 512], BF16, sem=True) for i in range(11)]
        wdn = sb('e_wdn', [128, NP, D], BF16, sem=True)
        fw, fb = P.fw, P.fb
        fgB = sb('e_fgB', [128, D], F32)
        onesF = sb('e_onesF', [1, 128], F32)
        epsT = sb('e_eps', [128, 1], F32)
        halo = sb('e_halo', [128, 2 * NP, 2], F32)
        n3 = sb('e_n3', [128, 8, 512], BF16, sem=True)
        actT = sb('e_actT', [128, NP, 512], BF16)
        ub = [sb('e_ub%d' % i, [128, 514], F32) for i in range(3)]
        tb = [sb('e_tb%d' % i, [128, 512], F32) for i in range(3)]
        sgl = sb('e_sgl', [128, 512], F32)
        h2 = [sb('e_h2_%d' % i, [128, D], F32, sem=True) for i in range(2)]
        junk = sb('e_junk', [128, D], BF16)
        ss = sb('e_ss', [128, 1], F32)
        rt = sb('e_rt', [128, 1], F32)
        rstd = sb('e_rstd', [128, 1], F32)
        pu = [ps('e_pu%d' % i, [128, 512]) for i in range(3)]
        pd = [ps('e_pd%d' % i, [128, 512]) for i in range(2)]

        S.dma('sp', identF.sem, dict(out=identF.t[:, :], in_=T['identF']), writes=[identF])
        _memset(S, 'dve', onesF.t[:, :], 1.0, [onesF])
        _memset(S, 'dve', epsT.t[:, :], EPS, [epsT])
        _memset(S, 'dve', halo.t[:, :, :], 0.0, [halo])
        order = []
        for j in range(NP):
            for c in (j, j + NP):
                if c // 4 not in order:
                    order.append(c // 4)
        for bi in order[:2]:
            S.dma('sp', wupb[bi].sem, dict(out=wupb[bi].t[:, :, :],
                                           in_=T['WB_ffn_w_up'][:, bi * 512:(bi + 1) * 512].rearrange('(k p) n -> p k n', p=128)),
                  writes=[wupb[bi]])
        S.dma('sp', n3.sem, dict(out=n3.t[:, :, :], in_=T['N3T'][:, :, 0:512].rearrange('k p s -> p k s')), writes=[n3])
        for bi in order[2:]:
            S.dma('sp', wupb[bi].sem, dict(out=wupb[bi].t[:, :, :],
                                           in_=T['WB_ffn_w_up'][:, bi * 512:(bi + 1) * 512].rearrange('(k p) n -> p k n', p=128)),
                  writes=[wupb[bi]])
        S.dma('sp', wdn.sem, dict(out=wdn.t[:, :, :], in_=T['WB_ffn_w_down'].rearrange('(k p) n -> p k n', p=128)),
              writes=[wdn])
        S.dma('sp', C.stage.sem, dict(out=C.stage.t[0:1, 0:1024], in_=T['final_g']), writes=[C.stage])
        for half in range(2):
            _mm(S, pd[half].t[:, :], onesF.t[0:1, :], C.stage.t[0:1, half * 512:(half + 1) * 512], True, True,
                [onesF, C.stage], [pd[half]])
            _cp(S, 'dve', fgB.t[:, half * 512:(half + 1) * 512], pd[half].t[:, :], [pd[half]], [fgB])

        npu = 0
        nub = 0
        for st in range(NSUP):
            cs = slice(st * 512, (st + 1) * 512)
            if st > 0:
                S.dma('sp', n3.sem, dict(out=n3.t[:, :, :], in_=T['N3T'][:, :, cs].rearrange('k p s -> p k s')), writes=[n3])
            for j in range(NP):
                tpair = []
                for c in (j, j + NP):
                    p = pu[npu % 3]
                    npu += 1
                    u_ = ub[nub % 3]
                    t_ = tb[nub % 3]
                    nub += 1
                    for kc in range(8):
                        _mm(S, p.t[:, :], wupb[c // 4].t[:, kc, (c % 4) * 128:(c % 4 + 1) * 128], n3.t[:, kc, :], kc == 0, kc == 7,
                            [wupb[c // 4], n3], [p])
                    _cp(S, 'act', u_.t[:, 2:514], p.t[:, :], [p], [u_])
                    _cp(S, 'pool', u_.t[:, 0:2], halo.t[:, c, :], [halo], [u_])
                    _act(S, t_.t[:, :], p.t[:, :], AF.Identity, [p, fw, fb], [t_], scale=fw.t[:, c, 2:3], bias=fb.t[:, c, :])
                    _stt(S, t_.t[:, :], u_.t[:, 0:512], fw.t[:, c, 0:1], t_.t[:, :], ALU.mult, ALU.add, [u_, fw, t_], [t_])
                    _stt(S, t_.t[:, :], u_.t[:, 1:513], fw.t[:, c, 1:2], t_.t[:, :], ALU.mult, ALU.add, [u_, fw, t_], [t_])
                    _cp(S, 'pool', halo.t[:, c, :], u_.t[:, 512:514], [u_], [halo])
                    tpair.append(t_)
                _act(S, sgl.t[:, :], tpair[0].t[:, :], AF.Silu, [tpair[0]], [sgl])
                _tt(S, 'dve', actT.t[:, j, :], sgl.t[:, :], tpair[1].t[:, :], ALU.mult, [sgl, tpair[1]], [actT])
            for t in range(4):
                tt = st * 4 + t
                hb = h2[tt % 2]
                ob = hb
                S.dma('sp', hb.sem, dict(out=hb.t[:, :], in_=T['H2'][tt * 128:(tt + 1) * 128, :]), writes=[hb])
                for half in range(2):
                    p = pd[half]
                    for j in range(NP):
                        _mm(S, p.t[:, :], actT.t[:, j, t * 128:(t + 1) * 128], wdn.t[:, j, half * 512:(half + 1) * 512],
                            j == 0, j == NP - 1, [actT, wdn], [p])
                    _tt(S, 'dve', hb.t[:, half * 512:(half + 1) * 512], p.t[:, :], hb.t[:, half * 512:(half + 1) * 512],
                        ALU.add, [p, hb], [hb])
                _stt(S, junk.t[:, :], hb.t[:, :], 1.0, hb.t[:, :], ALU.mult, ALU.mult, [hb], [junk, ss], accum_out=ss.t[:, 0:1])
                _act(S, rt.t[:, :], ss.t[:, :], AF.Sqrt, [ss, epsT], [rt], scale=1.0 / D, bias=epsT.t[:, :])
                S.op('dve', 'reciprocal', dict(out=rstd.t[:, :], in_=rt.t[:, :]), [rt], [rstd])
                _stt(S, ob.t[:, :], hb.t[:, :], rstd.t[:, 0:1], fgB.t[:, :], ALU.mult, ALU.mult, [hb, rstd, fgB], [hb])
                S.dma('sp', hb.sem, dict(out=T['y'][tt * 128:(tt + 1) * 128, :], in_=hb.t[:, :]), reads=[hb])
        S.flush()


def t5_bucket_np(d):
    n = np.maximum(d, 0)
    nf = np.maximum(n, 1).astype(np.float32)
    large = 16 + (np.log(nf / np.float32(16)) / np.float32(np.log(8.0)) * np.float32(16)).astype(np.int32)
    large = np.minimum(large, 31)
    return np.where(n < 16, n, large)


def scratch_spec(Sq):
    return {
        'QT': ([16, 64, Sq], BF16), 'KC': ([4, 64, Sq], BF16), 'VC': ([4, 64, Sq], BF16),
        'KS': ([4, 64, Sq], BF16), 'KW': ([4, 64, Sq], BF16), 'GM': ([16, 128, Sq], BF16),
        'HC': ([4, 128, Sq], BF16), 'VS1': ([Sq, 4, 65], BF16), 'VW1': ([Sq, 4, 65], BF16),
        'G': ([Sq, 48], F32), 'NSAT': ([8, 128, Sq], BF16), 'H2': ([Sq, D], F32), 'N3T': ([8, 128, Sq], BF16),
    }


INPUT_SHAPES = {
    'norm1_g': [1, D], 'w_in': [D, IN_W], 'conv_dw_w': [31, 512], 'conv_dw_b': [1, 512],
    'conv_ln_g': [1, 512], 'conv_ln_b': [1, 512], 'conv_w_pw': [512, D],
    'cmp_pe': [2, 32, 64], 'cmp_w1': [2, 2048, 256], 'cmp_b1': [1, 512], 'cmp_w2': [2, 256, 64],
    'nsa_w_o': [D, D], 'w_out': [D, D], 'norm2_g': [1, D], 'mem_norm_g': [1, D],
    'xa_wq': [D, D], 'xa_wkv': [D, 2 * D], 'xa_wo': [D, D], 'norm3_g': [1, D],
    'ffn_w_up': [D, 2 * D_FF], 'ffn_dw_w': [3, 2 * D_FF], 'ffn_dw_b': [1, 2 * D_FF],
    'ffn_w_down': [D_FF, D], 'final_g': [1, D],
}


def const_shapes(Sq):
    NT = Sq // 128
    NC, chunks = cmp_chunks(Sq)
    return {
        'identF': [128, 128], 'Econst': [64, Sq], 'm512': [128, 512],
        'SelW': [32, 8 * (NT - 1) + 128 * len(chunks)], 'mulB': [128, 128], 'addB': [128, 128],
        'ovl': [128 * len(chunks), 64],
        'tz1': [2, 128, 16, 128], 'tz31': [2, 128, 16, 128], 'tzm': [2, 128, 16, 128],
        'cb1': [32, 16, 128], 'cb31': [32, 16, 128], 'cbm': [32, 16, 128],
    }


def build(Sq, debug=(), phases='ABCDE'):
    nc = bass.Bass("TRN2", target_bir_lowering=False)
    T = {}
    T['x'] = nc.dram_tensor('x', [Sq, D], F32, kind='ExternalInput').ap()
    T['mem'] = nc.dram_tensor('mem', [MEM, D], F32, kind='ExternalInput').ap()
    for k, shp in list(INPUT_SHAPES.items()) + list(const_shapes(Sq).items()):
        T[k] = nc.dram_tensor(k, shp, F32, kind='ExternalInput').ap()
    for k, (shp, dt) in scratch_spec(Sq).items():
        kind = 'ExternalOutput' if k in debug else 'Internal'
        T[k] = nc.dram_tensor(k, shp, dt, kind=kind).ap()
    T['y'] = nc.dram_tensor('y', [Sq, D], F32, kind='ExternalOutput').ap()
    for k, (K_, N_) in WB_SPEC.items():
        T['WB_' + k] = nc.dram_tensor('WB_' + k, [K_, N_], BF16, kind='Internal').ap()
    NC, chunks = cmp_chunks(Sq)
    with ExitStack() as es:
        S = Sched(nc, es)
        if 'A' in phases:
            phase_A(nc, S, Sq, T)
        NCH = len(chunks)
        P = Ctx()
        P.fw = Tl(es.enter_context(nc.sbuf_tensor('p_fw', [128, 2 * (D_FF // 128), 3], F32)))
        P.fb = Tl(es.enter_context(nc.sbuf_tensor('p_fb', [128, 2 * (D_FF // 128), 1], F32)))
        with ExitStack() as es1:
            P.kmT = Tl(es1.enter_context(nc.sbuf_tensor('p_kmT', [128, 8, 256], BF16)))
            P.vm = Tl(es1.enter_context(nc.sbuf_tensor('p_vm', [128, 2, D], BF16)))
            with ExitStack() as es2:
                kcT = Tl(es2.enter_context(nc.sbuf_tensor('kcT', [128, 4, 128 * NCH], BF16)))
                vco = Tl(es2.enter_context(nc.sbuf_tensor('vco', [128, NCH, 4, 128], BF16)))
                P.biasT = Tl(es2.enter_context(nc.sbuf_tensor('p_biasT', [128, 2, 16, 128], BF16)))
                P.Bband = Tl(es2.enter_context(nc.sbuf_tensor('p_Bband', [32, 16, 128], BF16)))
                if 'B' in phases:
                    phase_B(nc, S, Sq, T, kcT, vco, P)
                if 'C' in phases:
                    phase_C(nc, S, Sq, T, kcT, vco, P)
            if 'D' in phases:
                phase_D(nc, S, Sq, T, P)
        if 'E' in phases:
            phase_E(nc, S, Sq, T, P)
    return nc


def host_consts(rel_bias, Sq):
    NT = Sq // 128
    NC, chunks = cmp_chunks(Sq)
    rb = np.asarray(rel_bias, dtype=np.float32)
    c = {}
    c['identF'] = np.eye(128, dtype=np.float32)
    E = np.zeros((64, Sq), np.float32)
    kk = np.arange(Sq)
    valid = kk // 64 < 64
    E[(kk // 64)[valid], kk[valid]] = 1.0
    c['Econst'] = E
    ki = np.arange(128)[:, None]
    qi = np.arange(128)[None, :]
    c['m512'] = np.tile(np.where(qi >= ki, MASKV, 0.0).astype(np.float32), (1, 4))
    OFFS = 8 * (NT - 1)
    W = np.zeros((32, OFFS + 128 * len(chunks)), np.float32)
    m = np.arange(W.shape[1]) - OFFS
    for r in range(17):
        W[r, m == r - 10] = 1.0
    W[31, m >= 7] = 1.0
    c['SelW'] = W
    r = np.arange(128)[None, :] - 62
    p = np.arange(128)[:, None]
    hi = (p >= 64).astype(np.int64)
    rel = r - hi
    free = rel <= -2
    forced = (rel == -1) | (rel == 0)
    c['mulB'] = np.where(free, 1.0, 0.0).astype(np.float32) * np.ones((128, 1), np.float32)
    c['addB'] = np.where(free, 0.0, np.where(forced, 10.0 + (rel + 2), -1.0 - 0.001 * np.maximum(rel, 0))).astype(np.float32)
    n = np.arange(128 * len(chunks))[:, None]
    j = np.arange(64)[None, :]
    ov = np.clip(np.minimum(16 * n + 32, 64 * j + 64) - np.maximum(16 * n, 64 * j), 0, None).astype(np.float32) / 32.0
    ov[NC:] = 0.0
    c['ovl'] = ov.astype(np.float32)
    tz1 = np.zeros((2, 128, 16, 128), np.float32)
    tz31 = np.zeros_like(tz1)
    tzm = np.zeros_like(tz1)
    for dl in range(2):
        d = dl * 128 + qi - ki
        ok = d >= 0
        g1 = rb[t5_bucket_np(d)]
        g31 = rb[np.full_like(d, 31)]
        tz1[dl] = np.where(ok[:, :, None], g1, 0.0).transpose(0, 2, 1)
        tz31[dl] = np.where(ok[:, :, None], g31, 0.0).transpose(0, 2, 1)
        tzm[dl] = np.where(ok[:, :, None], 0.0, MASKV).transpose(0, 2, 1) * np.ones((1, 16, 1), np.float32)
    c['tz1'], c['tz31'], c['tzm'] = tz1, tz31, tzm
    cb1 = np.zeros((32, 16, 128), np.float32)
    cb31 = np.zeros_like(cb1)
    cbm = np.zeros_like(cb1)
    rr = np.arange(17)[:, None]
    d1 = np.arange(128)[None, :] - 16 * (rr - 10) - 31
    ok = d1 >= 0
    cb1[:17] = np.where(ok[:, :, None], rb[t5_bucket_np(d1)], 0.0).transpose(0, 2, 1)
    cb31[:17] = np.where(ok[:, :, None], rb[np.full_like(d1, 31)], 0.0).transpose(0, 2, 1)
    cbm[:17] = (np.where(ok, 0.0, MASKV)[:, None, :] * np.ones((1, 16, 1))).astype(np.float32)
    cbm[31] = MASKV
    c['cb1'], c['cb31'], c['cbm'] = cb1, cb31, cbm
    return {k: np.ascontiguousarray(v, dtype=np.float32) for k, v in c.items()}


def host_inputs(inp, Sq):
    shared = {}
    for k, shp in INPUT_SHAPES.items():
        shared[k] = np.ascontiguousarray(np.asarray(inp[k], dtype=np.float32).reshape(shp))
    shared.update(host_consts(inp['rel_bias'], Sq))
    return shared


def kernel(**inp):
    x = np.asarray(inp['x'], dtype=np.float32)
    mem = np.asarray(inp['mem'], dtype=np.float32)
    B, Sq, _ = x.shape
    nc = build(Sq)
    shared = host_inputs(inp, Sq)
    in_maps = []
    for b in range(B):
        m = dict(shared)
        m['x'] = np.ascontiguousarray(x[b])
        m['mem'] = np.ascontiguousarray(mem[b])
        in_maps.append(m)
    res = run_bass_kernel_spmd(nc, in_maps, core_ids=list(range(B)))
    return np.stack([np.asarray(r['y'], dtype=np.float32) for r in res.results], axis=0)
```

```python
import os
import numpy as np
from contextlib import ExitStack
import concourse.bass as bass
import concourse.mybir as mybir
from concourse.bass_utils import run_bass_kernel_spmd

F32 = mybir.dt.float32
BF16 = mybir.dt.bfloat16
AF = mybir.ActivationFunctionType
ALU = mybir.AluOpType
AX = mybir.AxisListType

D = 1024
SEQ = 4096
MEM = 256
IN_W = 5680
D_FF = 2816
MASKV = -30000.0
EPS = 1e-6

ENGS = ['pe', 'act', 'dve', 'pool', 'sp']


class Buf:
    __slots__ = ('w', 'r')

    def __init__(self):
        self.w = None
        self.r = {}


class Tl:
    def __init__(self, t, sem=None):
        self.t = t
        self.b = Buf()
        self.sem = sem


def _b(x):
    return x.b if isinstance(x, Tl) else x


class Sched:
    def __init__(self, nc, es):
        self.nc = nc
        self.es = es
        self.q = {e: [] for e in ENGS}
        self.sem = {}
        self.cnt = {}
        self.known = {e: {} for e in ENGS}
        self.ndma = 0

    def semh(self, key):
        if key not in self.sem:
            self.sem[key] = self.es.enter_context(self.nc.semaphore('s_' + key))
            self.cnt[key] = 0
        return self.sem[key]

    def newdma(self, name=None):
        self.ndma += 1
        key = 'd%d' % self.ndma
        self.semh(key)
        return key

    def _deps(self, eng, reads, writes):
        need = {}
        for b in reads:
            b = _b(b)
            if b.w:
                k, v = b.w
                need[k] = max(need.get(k, 0), v)
        for b in writes:
            b = _b(b)
            if b.w:
                k, v = b.w
                need[k] = max(need.get(k, 0), v)
            for k, v in b.r.items():
                need[k] = max(need.get(k, 0), v)
        out = []
        kn = self.known[eng]
        for k, v in need.items():
            if eng == 'pe' and k == 'pe':
                continue
            if kn.get(k, 0) < v:
                kn[k] = v
                out.append((k, v))
        return out

    def _post(self, key, v, reads, writes):
        for b in reads:
            b = _b(b)
            b.r[key] = max(b.r.get(key, 0), v)
        for b in writes:
            b = _b(b)
            b.w = (key, v)
            b.r = {}

    def op(self, eng, meth, kw, reads=(), writes=()):
        self.semh(eng)
        waits = self._deps(eng, reads, writes)
        self.cnt[eng] += 1
        v = self.cnt[eng]
        self.q[eng].append((waits, meth, kw, eng, 1))
        self._post(eng, v, reads, writes)

    def dma(self, eng, semkey, kw, reads=(), writes=()):
        self.semh(semkey)
        waits = self._deps(eng, reads, writes)
        self.cnt[semkey] += 16
        v = self.cnt[semkey]
        self.q[eng].append((waits, 'dma_start', kw, semkey, 16))
        self._post(semkey, v, reads, writes)

    def barrier(self):
        for e in ENGS:
            kn = self.known[e]
            waits = []
            for k, v in self.cnt.items():
                if v > 0 and kn.get(k, 0) < v:
                    kn[k] = v
                    waits.append((k, v))
            if waits:
                self.q[e].append((waits, None, None, None, 0))

    def flush(self):
        self.barrier()
        nc = self.nc
        q = self.q
        sem = self.sem

        def run(engobj, items):
            for waits, meth, kw, key, inc in items:
                for k, v in waits:
                    engobj.wait_ge(sem[k], v)
                if meth is not None:
                    getattr(engobj, meth)(**kw).then_inc(sem[key], inc)

        with nc.Block() as block:
            @block.tensor
            def _(e):
                run(e, q['pe'])

            @block.scalar
            def _(e):
                run(e, q['act'])

            @block.vector
            def _(e):
                run(e, q['dve'])

            @block.gpsimd
            def _(e):
                run(e, q['pool'])

            @block.sync
            def _(e):
                run(e, q['sp'])
        self.q = {e: [] for e in ENGS}


class Ctx:
    pass


def _mm(S, out, lhsT, rhs, start, stop, reads, writes, skip=False):
    kw = dict(out=out, lhsT=lhsT, rhs=rhs, start=start, stop=stop)
    if skip:
        kw['skip_group_check'] = True
    S.op('pe', 'matmul', kw, reads, writes)


def _tp(S, out, in_, ident, reads, writes):
    S.op('pe', 'transpose', dict(out=out, in_=in_, identity=ident), reads, writes)


def _act(S, out, in_, func, reads, writes, **kw):
    S.op('act', 'activation', dict(out=out, in_=in_, func=func, **kw), reads, writes)


def _tt(S, eng, out, in0, in1, op, reads, writes):
    S.op(eng, 'tensor_tensor', dict(out=out, in0=in0, in1=in1, op=op), reads, writes)


def _ts(S, eng, out, in0, s1, op0, reads, writes, s2=None, op1=None):
    kw = dict(out=out, in0=in0, scalar1=s1, scalar2=s2, op0=op0)
    if op1 is not None:
        kw['op1'] = op1
    S.op(eng, 'tensor_scalar', kw, reads, writes)


def _stt(S, out, in0, scalar, in1, op0, op1, reads, writes, accum_out=None):
    kw = dict(out=out, in0=in0, scalar=scalar, in1=in1, op0=op0, op1=op1)
    if accum_out is not None:
        kw['accum_out'] = accum_out
    S.op('dve', 'scalar_tensor_tensor', kw, reads, writes)


def _cp(S, eng, out, in_, reads, writes):
    if eng == 'act':
        S.op('act', 'copy', dict(out=out, in_=in_), reads, writes)
    else:
        S.op(eng, 'tensor_copy', dict(out=out, in_=in_), reads, writes)


def _memset(S, eng, ap, val, writes):
    S.op(eng, 'memset', dict(ap=ap, constant=val), (), writes)


def load_w_cast(S, dst_tl, dst_ap_fn, src, kc_n, ncols, rows_per=128):
    for kc in range(kc_n):
        c0 = 0
        while c0 < ncols:
            c1 = min(ncols, c0 + 2048)
            S.dma('pool', dst_tl.sem, dict(out=dst_ap_fn(kc, c0, c1),
                                           in_=src[kc * 128:(kc + 1) * 128, c0:c1]),
                  writes=[dst_tl])
            c0 = c1


def cols_from_rows(S, C, src_rows, R, ncol, dst_tl, dst_fn):
    stage = C.stage
    assert ncol <= stage.t.shape[1] and R <= 32
    S.dma('sp', stage.sem, dict(out=stage.t[0:R, 0:ncol], in_=src_rows), writes=[stage])
    for ch in range(ncol // 128):
        _tp(S, C.pst.t[:, 0:R], stage.t[0:R, ch * 128:(ch + 1) * 128], C.identF.t[0:R, 0:R],
            [stage, C.identF], [C.pst])
        _cp(S, 'dve', dst_fn(ch), C.pst.t[:, 0:R], [C.pst], [dst_tl])


def phase_A(nc, S, Sq, T):
    NSUP = Sq // 512
    with ExitStack() as es:
        def sb(name, shape, dt, sem=False):
            t = es.enter_context(nc.sbuf_tensor(name, shape, dt))
            return Tl(t, S.newdma() if sem else None)

        def ps(name, shape, dt=F32):
            return Tl(es.enter_context(nc.psum_tensor(name, shape, dt)))

        C = Ctx()
        win = sb('a_win', [128, 8, IN_W], BF16, sem=True)
        dg = sb('a_dg', [128, 4, 31, 128], BF16)
        identF = sb('a_identF', [128, 128], F32, sem=True)
        identB = sb('a_identB', [128, 128], BF16, sem=True)
        onesF = sb('a_onesF', [128, 128], F32)
        C.identF = identF
        C.stage = sb('a_stage', [32, 1024], F32, sem=True)
        dwT = sb('a_dwT', [128, 4, 31], F32)
        dwb = sb('a_dwb', [128, 4, 1], F32)
        lng = sb('a_lng', [128, 4, 1], F32)
        lnb = sb('a_lnb', [128, 4, 1], F32)
        xs = [sb('a_x%d' % i, [128, D], F32, sem=True) for i in range(4)]
        ss = sb('a_ss', [128, 4], F32)
        rt = sb('a_rt', [128, 4], F32)
        rstd = sb('a_rstd', [128, 4], F32)
        junk = sb('a_junk', [128, D], BF16)
        ntok = [sb('a_ntok%d' % i, [128, D], BF16) for i in range(4)]
        nT = sb('a_nT', [128, 8, 512], BF16)
        hglu = sb('a_hglu', [128, 4, 542], BF16)
        stg = [sb('a_stg%d' % i, [128, 512], BF16, sem=True) for i in range(6)]
        sg = [sb('a_sg%d' % i, [128, 512], F32) for i in range(2)]
        ycs = sb('a_ycs', [128, 4, 512], F32)
        ysq = sb('a_ysq', [128, 4, 512], F32)
        mean = sb('a_mean', [128, 512], F32)
        msq = sb('a_msq', [128, 512], F32)
        rs = sb('a_rs', [128, 512], F32)
        dtl = [sb('a_d%d' % i, [128, 512], F32) for i in range(2)]
        vstg = [sb('a_vstg%d' % i, [128, 2, 4, 65], BF16, sem=True) for i in range(2)]
        gstg = [sb('a_gstg%d' % i, [128, 48], F32, sem=True) for i in range(2)]
        epsT = sb('a_eps', [128, 1], F32)

        ptr = ps('a_ptr', [128, 1024], BF16)
        pf = [ps('a_pf%d' % i, [128, 512]) for i in range(3)]
        pv = ps('a_pv', [128, 512])
        pg = ps('a_pg', [128, 48])
        C.pst = pg
        pc = [ps('a_pc%d' % i, [128, 512]) for i in range(2)]

        S.dma('sp', identF.sem, dict(out=identF.t[:, :], in_=T['identF']), writes=[identF])
        S.dma('pool', identB.sem, dict(out=identB.t[:, :], in_=T['identF']), writes=[identB])
        _memset(S, 'dve', onesF.t[:, :], 1.0 / 512.0, [onesF])
        _memset(S, 'dve', epsT.t[:, :], EPS, [epsT])
        _memset(S, 'dve', hglu.t[:, :, :], 0.0, [hglu])
        for v in vstg:
            _memset(S, 'dve', v.t[:, :, :, :], 1.0, [v])
        WBLK = [(2816, 3632), (0, 1024), (1024, 2048), (2048, 2816), (3632, 4656), (4656, 5680)]
        wblk = [Tl(None, S.newdma()) for _ in WBLK]

        def wb(c0):
            for i_, (a0, a1) in enumerate(WBLK):
                if a0 <= c0 < a1:
                    return wblk[i_]
            raise ValueError(c0)
        for i_, (a0, a1) in enumerate(WBLK):
            for kc in range(8):
                S.dma('pool', wblk[i_].sem, dict(out=win.t[:, kc, a0:a1], in_=T['w_in'][kc * 128:(kc + 1) * 128, a0:a1]),
                      writes=[wblk[i_]])
        precast_weights(S, T)
        cols_from_rows(S, C, T['conv_dw_w'], 31, 512, dwT, lambda ch: dwT.t[:, ch, :])
        cols_from_rows(S, C, T['conv_dw_b'], 1, 512, dwb, lambda ch: dwb.t[:, ch, :])
        cols_from_rows(S, C, T['conv_ln_g'], 1, 512, lng, lambda ch: lng.t[:, ch, :])
        cols_from_rows(S, C, T['conv_ln_b'], 1, 512, lnb, lambda ch: lnb.t[:, ch, :])
        onesR = sb('a_onesR', [1, 128], F32)
        g1B = sb('a_g1B', [128, D], F32)
        _memset(S, 'dve', onesR.t[:, :], 1.0, [onesR])
        bcast_row(S, C, T['norm1_g'], g1B, pf, onesR)
        for ch in range(4):
            for j in range(31):
                _ts(S, 'dve', dg.t[:, ch, j, :], identF.t[:, :], dwT.t[:, ch, j:j + 1], ALU.mult,
                    [identF, dwT], [dg])

        x = T['x']
        fchunks = []
        for i in range(4):
            fchunks.append((512 + 128 * i, 'gate', i))
            fchunks.append((128 * i, 'a', i))
        for i in range(8):
            fchunks.append((1024 + 128 * i, 'q', i))
        for nm, c0 in (('kc', 2048), ('vc', 2304), ('ks', 2560), ('kw', 3072)):
            for i in range(2):
                fchunks.append((c0 + 128 * i, nm, i))
        for i in range(16):
            fchunks.append((3632 + 128 * i, 'gm', i))

        import os
        STG = int(os.environ.get('STG', '9'))
        nstg = 0
        npf = 0
        def load_x(st):
            for t in range(4):
                tt = st * 4 + t
                S.dma('sp', xs[t].sem, dict(out=xs[t].t[:, :], in_=x[tt * 128:(tt + 1) * 128, :]), writes=[xs[t]])

        def rms_pre(st):
            for t in range(4):
                _stt(S, junk.t[:, :], xs[t].t[:, :], 1.0, xs[t].t[:, :], ALU.mult, ALU.mult, [xs[t]], [junk, ss],
                     accum_out=ss.t[:, t:t + 1])
            _act(S, rt.t[:, :], ss.t[:, :], AF.Sqrt, [ss, epsT], [rt], scale=1.0 / D, bias=epsT.t[:, :])
            S.op('dve', 'reciprocal', dict(out=rstd.t[:, :], in_=rt.t[:, :]), [rt], [rstd])
            for t in range(4):
                _stt(S, ntok[t].t[:, :], xs[t].t[:, :], rstd.t[:, t:t + 1], g1B.t[:, :], ALU.mult, ALU.mult,
                     [xs[t], rstd, g1B], [ntok[t]])
            if st + 1 < NSUP:
                load_x(st + 1)
        load_x(0)
        rms_pre(0)
        for st in range(NSUP if STG >= 1 else 0):
            for t in range(4):
                nk = ntok[t]
                for kc in range(8):
                    _tp(S, ptr.t[:, kc * 128:(kc + 1) * 128], nk.t[:, kc * 128:(kc + 1) * 128], identB.t[:, :],
                        [nk, identB], [ptr])
                _cp(S, 'act', nT.t[:, :, t * 128:(t + 1) * 128],
                    ptr.t[:, :].rearrange('p (k q) -> p k q', k=8), [ptr], [nT])
            for t in range(4 if STG >= 2 else 0):
                tt = st * 4 + t
                for half, c0 in ((0, 2816), (1, 3328)):
                    for kc in range(8):
                        _mm(S, pv.t[:, half * 256:(half + 1) * 256], nT.t[:, kc, t * 128:(t + 1) * 128],
                            win.t[:, kc, c0:c0 + 256], kc == 0, kc == 7, [nT, wb(c0)], [pv])
                for kc in range(8):
                    _mm(S, pg.t[:, :], nT.t[:, kc, t * 128:(t + 1) * 128], win.t[:, kc, 3584:3632],
                        kc == 0, kc == 7, [nT, wb(3584)], [pg])
                vs = vstg[tt % 2]
                _cp(S, 'dve', vs.t[:, :, :, 0:64], pv.t[:, :].rearrange('p (a g d) -> p a g d', a=2, g=4),
                    [pv], [vs])
                S.dma('sp', vs.sem, dict(out=T['VS1'][tt * 128:(tt + 1) * 128, :, :], in_=vs.t[:, 0, :, :]),
                      reads=[vs])
                S.dma('sp', vs.sem, dict(out=T['VW1'][tt * 128:(tt + 1) * 128, :, :], in_=vs.t[:, 1, :, :]),
                      reads=[vs])
                gs = gstg[tt % 2]
                _act(S, gs.t[:, :], pg.t[:, :], AF.Sigmoid, [pg], [gs])
                S.dma('sp', gs.sem, dict(out=T['G'][tt * 128:(tt + 1) * 128, :], in_=gs.t[:, :]), reads=[gs])
            cs = slice(st * 512, (st + 1) * 512)
            for (c0, kind, idx) in (fchunks if STG >= 3 else []):
                p = pf[npf % 3]
                npf += 1
                for kc in range(8):
                    _mm(S, p.t[:, :], win.t[:, kc, c0:c0 + 128], nT.t[:, kc, :], kc == 0, kc == 7, [wb(c0), nT], [p])
                if kind == 'gate':
                    sgt = sg[idx % 2]
                    _act(S, sgt.t[:, :], p.t[:, :], AF.Sigmoid, [p], [sgt])
                elif kind == 'a':
                    sgt = sg[idx % 2]
                    _tt(S, 'dve', hglu.t[:, idx, 30:542], p.t[:, :], sgt.t[:, :], ALU.mult, [p, sgt], [hglu])
                else:
                    sl = stg[nstg % 6]
                    nstg += 1
                    if kind == 'q':
                        _act(S, sl.t[:, :], p.t[:, :], AF.Copy, [p], [sl], scale=0.125)
                        dst = T['QT'][2 * idx:2 * idx + 2, :, cs].rearrange('h d s -> (h d) s')
                    elif kind == 'gm':
                        _act(S, sl.t[:, :], p.t[:, :], AF.Sigmoid, [p], [sl])
                        dst = T['GM'][idx, :, cs]
                    else:
                        _cp(S, 'dve', sl.t[:, :], p.t[:, :], [p], [sl])
                        dst = T[{'kc': 'KC', 'vc': 'VC', 'ks': 'KS', 'kw': 'KW'}[kind]][2 * idx:2 * idx + 2, :, cs] \
                            .rearrange('g d s -> (g d) s')
                    S.dma('sp', sl.sem, dict(out=dst, in_=sl.t[:, :]), reads=[sl])
            if st + 1 < NSUP:
                rms_pre(st + 1)
            if STG < 4:
                continue
            for ch in range(4):
                p = pc[ch % 2]
                for j in range(0, 31, int(os.environ.get('JSTEP', '1'))):
                    _mm(S, p.t[:, :], dg.t[:, ch, j, :], hglu.t[:, ch, j:j + 512], j == 0, j == 30, [dg, hglu], [p])
                CP = int(os.environ.get('CP', '15'))
                if CP & 2:
                    _ts(S, 'dve', ycs.t[:, ch, :], p.t[:, :], dwb.t[:, ch, :], ALU.add, [p, dwb], [ycs])
                if CP & 4:
                    _tt(S, 'pool', ysq.t[:, ch, :], ycs.t[:, ch, :], ycs.t[:, ch, :], ALU.mult, [ycs], [ysq])
            if CP & 8:
                _cp(S, 'dve', hglu.t[:, :, 0:30], hglu.t[:, :, 512:542], [hglu], [hglu])
            if STG < 5:
                continue
            pm = pf[npf % 3]
            npf += 1
            pq = pf[npf % 3]
            npf += 1
            for ch in range(4):
                _mm(S, pm.t[:, :], onesF.t[:, :], ycs.t[:, ch, :], ch == 0, ch == 3, [onesF, ycs], [pm])
            for ch in range(4):
                _mm(S, pq.t[:, :], onesF.t[:, :], ysq.t[:, ch, :], ch == 0, ch == 3, [onesF, ysq], [pq])
            _cp(S, 'act', mean.t[:, :], pm.t[:, :], [pm], [mean])
            _tt(S, 'dve', msq.t[:, :], mean.t[:, :], mean.t[:, :], ALU.mult, [mean], [msq])
            _tt(S, 'dve', msq.t[:, :], pq.t[:, :], msq.t[:, :], ALU.subtract, [pq, msq], [msq])
            _act(S, msq.t[:, :], msq.t[:, :], AF.Sqrt, [msq, epsT], [msq], bias=epsT.t[:, :])
            S.op('dve', 'reciprocal', dict(out=rs.t[:, :], in_=msq.t[:, :]), [msq], [rs])
            for ch in range(4):
                d_ = dtl[ch % 2]
                _tt(S, 'dve', d_.t[:, :], ycs.t[:, ch, :], mean.t[:, :], ALU.subtract, [ycs, mean], [d_])
                _tt(S, 'dve', d_.t[:, :], d_.t[:, :], rs.t[:, :], ALU.mult, [d_, rs], [d_])
                _ts(S, 'dve', d_.t[:, :], d_.t[:, :], lng.t[:, ch, :], ALU.mult, [d_, lng, lnb], [d_],
                    s2=lnb.t[:, ch, :], op1=ALU.add)
                sl = stg[nstg % 6]
                nstg += 1
                _act(S, sl.t[:, :], d_.t[:, :], AF.Silu, [d_], [sl])
                S.dma('sp', sl.sem, dict(out=T['HC'][ch, :, cs], in_=sl.t[:, :]), reads=[sl])
        S.flush()


def bcast_row(S, C, src_row, dst_tl, pbanks, onesF):
    S.dma('sp', C.stage.sem, dict(out=C.stage.t[0:1, 0:1024], in_=src_row), writes=[C.stage])
    for half in range(2):
        p = pbanks[half]
        _mm(S, p.t[:, :], onesF.t[0:1, :], C.stage.t[0:1, half * 512:(half + 1) * 512], True, True,
            [onesF, C.stage], [p])
        _cp(S, 'dve', dst_tl.t[:, half * 512:(half + 1) * 512], p.t[:, :], [p], [dst_tl])


def rms_stats(S, R, i, xb):
    _stt(S, R.junk.t[:, :], xb.t[:, :], 1.0, xb.t[:, :], ALU.mult, ALU.mult, [xb], [R.junk, R.ssl[i]],
         accum_out=R.ssl[i].t[:, 0:1])
    _act(S, R.ssl[i].t[:, 1:2], R.ssl[i].t[:, 0:1], AF.Sqrt, [R.ssl[i], R.epsT], [R.ssl[i]], scale=1.0 / D,
         bias=R.epsT.t[:, :])
    S.op('dve', 'reciprocal', dict(out=R.ssl[i].t[:, 2:3], in_=R.ssl[i].t[:, 1:2]), [R.ssl[i]], [R.ssl[i]])


def rms_T(S, R, xtiles, gB, per_tile=False, stats_done=False):
    n = len(xtiles)
    if not per_tile:
        for i, xb in enumerate(xtiles):
            _stt(S, R.junk.t[:, :], xb.t[:, :], 1.0, xb.t[:, :], ALU.mult, ALU.mult, [xb], [R.junk, R.ss],
                 accum_out=R.ss.t[:, i:i + 1])
        _act(S, R.rt.t[:, 0:n], R.ss.t[:, 0:n], AF.Sqrt, [R.ss, R.epsT], [R.rt], scale=1.0 / D, bias=R.epsT.t[:, :])
        S.op('dve', 'reciprocal', dict(out=R.rstd.t[:, 0:n], in_=R.rt.t[:, 0:n]), [R.rt], [R.rstd])
    for i, xb in enumerate(xtiles):
        if per_tile:
            if not stats_done:
                rms_stats(S, R, i, xb)
            sc_ap, sc_tl = R.ssl[i].t[:, 2:3], R.ssl[i]
        else:
            sc_ap, sc_tl = R.rstd.t[:, i:i + 1], R.rstd
        nk = R.ntok[i % 2]
        _stt(S, nk.t[:, :], xb.t[:, :], sc_ap, gB.t[:, :], ALU.mult, ALU.mult, [xb, sc_tl, gB], [nk])
        for kc in range(8):
            _tp(S, R.ptr.t[:, kc * 128:(kc + 1) * 128], nk.t[:, kc * 128:(kc + 1) * 128], R.identB.t[:, :],
                [nk, R.identB], [R.ptr])
        yield i, R.ptr


WB_SPEC = {'xa_wkv': (D, 2 * D), 'nsa_w_o': (D, D), 'conv_w_pw': (512, D), 'w_out': (D, D), 'xa_wq': (D, D), 'xa_wo': (D, D),
           'ffn_w_up': (D, 2 * D_FF), 'ffn_w_down': (D_FF, D)}


def precast_weights(S, T):
    key = S.newdma()
    for name, (K_, N_) in WB_SPEC.items():
        for r0 in range(0, K_, 128):
            for c0 in range(0, N_, 2048):
                c1 = min(N_, c0 + 2048)
                S.dma('pool', key, dict(out=T['WB_' + name][r0:r0 + 128, c0:c1], in_=T[name][r0:r0 + 128, c0:c1]))


def _mk(nc, es, S):
    def sb(name, shape, dt, sem=False):
        t = es.enter_context(nc.sbuf_tensor(name, shape, dt))
        return Tl(t, S.newdma() if sem else None)

    def ps(name, shape, dt=F32):
        return Tl(es.enter_context(nc.psum_tensor(name, shape, dt)))
    return sb, ps


def cmp_chunks(Sq):
    NC = Sq // 16 - 1
    out = []
    n0 = 0
    while n0 < NC:
        out.append((n0, min(128, NC - n0)))
        n0 += 128
    return NC, out


def phase_B(nc, S, Sq, T, kcT, vco, P):
    NC, chunks = cmp_chunks(Sq)
    with ExitStack() as es:
        sb, ps = _mk(nc, es, S)
        C = Ctx()
        identF = sb('b_identF', [128, 128], F32, sem=True)
        C.identF = identF
        C.stage = sb('b_stage', [32, 1024], F32, sem=True)
        pst = ps('b_pst', [128, 48])
        C.pst = pst
        kin = [sb('b_kin%d' % i, [128, 4, Sq], BF16, sem=True) for i in range(2)]
        w1s = sb('b_w1s', [128, 2, 16, 256], BF16, sem=True)
        w2s = sb('b_w2s', [128, 2, 2, 64], BF16, sem=True)
        peT = sb('b_peT', [128, 2, 16], BF16)
        b1T = sb('b_b1T', [128, 4, 1], F32)
        biasc = sb('b_biasc', [128, 4, 1], F32)
        hT = [sb('b_hT%d' % i, [128, 2, 512], BF16) for i in range(2)]
        xb = sb('b_xb', [128, 512], F32)
        x2 = sb('b_x2', [128, 512], F32)
        u = sb('b_u', [128, 512], F32)
        sgm = sb('b_sgm', [128, 512], F32)
        ovs = sb('b_ovs', [128, 2, 64], F32, sem=True)
        pA = [ps('b_pA%d' % i, [128, 512]) for i in range(2)]
        pB = ps('b_pB', [128, 512])

        S.dma('sp', identF.sem, dict(out=identF.t[:, :], in_=T['identF']), writes=[identF])
        for kv in range(2):
            for lh in range(2):
                for l0 in range(0, 16, 8):
                    S.dma('pool', w1s.sem, dict(out=w1s.t[lh * 64:(lh + 1) * 64, kv, l0:l0 + 8, :],
                                                in_=T['cmp_w1'][kv, (lh * 16 + l0) * 64:(lh * 16 + l0 + 8) * 64, :]
                                                .rearrange('(l d) c -> d l c', d=64)), writes=[w1s])
            S.dma('pool', w2s.sem, dict(out=w2s.t[:, kv, :, :],
                                        in_=T['cmp_w2'][kv].rearrange('(h p) d -> p h d', p=128)), writes=[w2s])
            for lh in range(2):
                S.dma('sp', C.stage.sem, dict(out=C.stage.t[0:16, lh * 64:(lh + 1) * 64],
                                              in_=T['cmp_pe'][kv, lh * 16:(lh + 1) * 16, :]), writes=[C.stage])
            _tp(S, pst.t[:, 0:16], C.stage.t[0:16, 0:128], identF.t[0:16, 0:16], [C.stage, identF], [pst])
            _cp(S, 'dve', peT.t[:, kv, :], pst.t[:, 0:16], [pst], [peT])
        cols_from_rows(S, C, T['cmp_b1'], 1, 512, b1T, lambda ch: b1T.t[:, ch, :])
        _memset(S, 'dve', vco.t[:, :, :, :], 0.0, [vco])
        _memset(S, 'dve', kcT.t[:, :, :], 0.0, [kcT])
        S.dma('sp', ovs.sem, dict(out=ovs.t[:, 0:len(chunks), :], in_=T['ovl'].rearrange('(c p) j -> p c j', p=128)),
              writes=[ovs])
        for ci in range(len(chunks)):
            for g in range(4):
                _cp(S, 'dve', vco.t[:, ci, g, 64:128], ovs.t[:, ci, :], [ovs], [vco])
        for kv in range(2):
            for half in range(2):
                for l in range(16):
                    _mm(S, pB.t[:, 0:1], w1s.t[:, kv, l, half * 128:(half + 1) * 128], peT.t[:, kv, l:l + 1],
                        l == 0, l == 15, [w1s, peT], [pB])
                _tt(S, 'dve', biasc.t[:, kv * 2 + half, :], pB.t[:, 0:1], b1T.t[:, kv * 2 + half, :], ALU.add,
                    [pB, b1T], [biasc])

        R = Ctx()
        R.identB = sb('b_identB', [128, 128], BF16, sem=True)
        R.epsT = sb('b_eps', [128, 1], F32)
        R.ss = sb('b_ss', [128, 4], F32)
        R.rt = sb('b_rt', [128, 4], F32)
        R.rstd = sb('b_rstd', [128, 4], F32)
        R.junk = sb('b_junk', [128, D], BF16)
        R.ntok = [sb('b_ntok%d' % i, [128, D], BF16) for i in range(2)]
        R.ptr = ps('b_ptr', [128, 1024], BF16)
        onesF = sb('b_onesF', [1, 128], F32)
        gmB = sb('b_gmB', [128, D], F32)
        wkv = sb('b_wkv', [128, 8, 2 * D], BF16, sem=True)
        memx = [sb('b_memx%d' % i, [128, D], F32, sem=True) for i in range(2)]
        memnT = sb('b_memnT', [128, 8, 256], BF16)
        kmT, vm = P.kmT, P.vm
        S.dma('pool', R.identB.sem, dict(out=R.identB.t[:, :], in_=T['identF']), writes=[R.identB])
        _memset(S, 'dve', R.epsT.t[:, :], EPS, [R.epsT])
        _memset(S, 'dve', onesF.t[:, :], 1.0, [onesF])
        S.dma('sp', wkv.sem, dict(out=wkv.t[:, :, :], in_=T['WB_xa_wkv'].rearrange('(k p) n -> p k n', p=128)), writes=[wkv])
        bcast_row(S, C, T['mem_norm_g'], gmB, pA, onesF)
        for mc in range(2):
            S.dma('sp', memx[mc].sem, dict(out=memx[mc].t[:, :], in_=T['mem'][mc * 128:(mc + 1) * 128, :]),
                  writes=[memx[mc]])
        for kv in range(2):
            src = kin[kv]
            S.dma('sp', src.sem, dict(out=src.t[0:64, :, :], in_=T['KC' if kv == 0 else 'VC'].rearrange('g d s -> d g s')), writes=[src])
            S.dma('sp', src.sem, dict(out=src.t[64:128, :, 0:Sq - 16], in_=T['KC' if kv == 0 else 'VC'][:, :, 16:Sq].rearrange('g d s -> d g s')), writes=[src])
        npa = 0
        for kv in range(2):
            src = kin[kv]
            for g in range(4):
                h_ = hT[(kv * 4 + g) % 2]
                for half in range(2):
                    p = pA[npa % 2]
                    npa += 1
                    for l in range(16):
                        _mm(S, p.t[:, 0:NC], w1s.t[:, kv, l, half * 128:(half + 1) * 128],
                            src.t[:, g, l:l + 16 * (NC - 1) + 1:16], l == 0, l == 15, [w1s, src], [p])
                    _ts(S, 'dve', xb.t[:, 0:NC], p.t[:, 0:NC], biasc.t[:, kv * 2 + half, :], ALU.add, [p, biasc], [xb])
                    _tt(S, 'pool', x2.t[:, 0:NC], xb.t[:, 0:NC], xb.t[:, 0:NC], ALU.mult, [xb], [x2])
                    _ts(S, 'dve', x2.t[:, 0:NC], x2.t[:, 0:NC], 0.044715, ALU.mult, [x2], [x2], s2=1.0, op1=ALU.add)
                    _tt(S, 'dve', u.t[:, 0:NC], x2.t[:, 0:NC], xb.t[:, 0:NC], ALU.mult, [x2, xb], [u])
                    _act(S, sgm.t[:, 0:NC], u.t[:, 0:NC], AF.Sigmoid, [u], [sgm], scale=1.5957691216057308)
                    _tt(S, 'dve', h_.t[:, half, 0:NC], xb.t[:, 0:NC], sgm.t[:, 0:NC], ALU.mult, [xb, sgm], [h_])
                if kv == 0:
                    for half in range(2):
                        _mm(S, pB.t[0:64, 0:NC], w2s.t[:, 0, half, :], h_.t[:, half, 0:NC], half == 0, half == 1,
                            [w2s, h_], [pB])
                    _cp(S, 'act', kcT.t[0:64, g, 0:NC], pB.t[0:64, 0:NC], [pB], [kcT])
                else:
                    for ci, (n0, sz) in enumerate(chunks):
                        for half in range(2):
                            _mm(S, pB.t[0:sz, 0:64], h_.t[:, half, n0:n0 + sz], w2s.t[:, 1, half, :], half == 0,
                                half == 1, [w2s, h_], [pB])
                        _cp(S, 'act', vco.t[0:sz, ci, g, 0:64], pB.t[0:sz, 0:64], [pB], [vco])
        tmpA = sb('b_tmpA', [128, 2048], F32, sem=True)
        tmpB = sb('b_tmpB', [128, 2048], F32, sem=True)
        biasT, Bband = P.biasT, P.Bband
        for dl in range(2):
            S.dma('sp', tmpA.sem, dict(out=tmpA.t[:, :], in_=T['tz1'][dl].rearrange('k h q -> k (h q)')), writes=[tmpA])
            S.dma('sp', tmpB.sem, dict(out=tmpB.t[:, :], in_=T['tz31'][dl].rearrange('k h q -> k (h q)')), writes=[tmpB])
            _tt(S, 'pool', tmpA.t[:, :], tmpA.t[:, :], tmpB.t[:, :], ALU.subtract, [tmpA, tmpB], [tmpA])
            S.dma('sp', tmpB.sem, dict(out=tmpB.t[:, :], in_=T['tzm'][dl].rearrange('k h q -> k (h q)')), writes=[tmpB])
            _tt(S, 'pool', biasT.t[:, dl, :, :].rearrange('k h q -> k (h q)'), tmpA.t[:, :], tmpB.t[:, :], ALU.add,
                [tmpA, tmpB], [biasT])
        S.dma('sp', tmpA.sem, dict(out=tmpA.t[0:32, :], in_=T['cb1'].rearrange('k h q -> k (h q)')), writes=[tmpA])
        S.dma('sp', tmpB.sem, dict(out=tmpB.t[0:32, :], in_=T['cb31'].rearrange('k h q -> k (h q)')), writes=[tmpB])
        _tt(S, 'pool', tmpA.t[0:32, :], tmpA.t[0:32, :], tmpB.t[0:32, :], ALU.subtract, [tmpA, tmpB], [tmpA])
        S.dma('sp', tmpB.sem, dict(out=tmpB.t[0:32, :], in_=T['cbm'].rearrange('k h q -> k (h q)')), writes=[tmpB])
        _tt(S, 'pool', Bband.t[:, :, :].rearrange('k h q -> k (h q)'), tmpA.t[0:32, :], tmpB.t[0:32, :], ALU.add,
            [tmpA, tmpB], [Bband])

        for blk in range(0, 2 * D_FF, 1024):
            w = min(1024, 2 * D_FF - blk)
            c0 = blk // 128
            cols_from_rows(S, C, T['ffn_dw_w'][:, blk:blk + w], 3, w, P.fw, lambda ch, c0=c0: P.fw.t[:, c0 + ch, :])
            cols_from_rows(S, C, T['ffn_dw_b'][:, blk:blk + w], 1, w, P.fb, lambda ch, c0=c0: P.fb.t[:, c0 + ch, :])
        for i, p_ in rms_T(S, R, memx, gmB):
            _cp(S, 'act', memnT.t[:, :, i * 128:(i + 1) * 128], p_.t[:, :].rearrange('p (k q) -> p k q', k=8),
                [p_], [memnT])
        for c in range(8):
            p = pA[c % 2]
            for kc in range(8):
                _mm(S, p.t[:, 0:256], wkv.t[:, kc, c * 128:(c + 1) * 128], memnT.t[:, kc, :], kc == 0, kc == 7,
                    [wkv, memnT], [p])
            _cp(S, 'dve', kmT.t[:, c, :], p.t[:, 0:256], [p], [kmT])
        for mc in range(2):
            for half in range(2):
                p = pA[(mc * 2 + half) % 2]
                for kc in range(8):
                    _mm(S, p.t[:, :], memnT.t[:, kc, mc * 128:(mc + 1) * 128],
                        wkv.t[:, kc, D + half * 512:D + (half + 1) * 512], kc == 0, kc == 7, [wkv, memnT], [p])
                _cp(S, 'dve', vm.t[:, mc, half * 512:(half + 1) * 512], p.t[:, :], [p], [vm])
        S.flush()


def phase_C(nc, S, Sq, T, kcT, vco, P):
    NT = Sq // 128
    NC, chunks = cmp_chunks(Sq)
    OFFS = 8 * (NT - 1)
    with ExitStack() as es:
        sb, ps = _mk(nc, es, S)
        identB = sb('c_identB', [128, 128], BF16, sem=True)
        KE = sb('c_KE', [128, 4, Sq], BF16, sem=True)
        KWt = sb('c_KW', [128, 4, Sq], BF16, sem=True)
        KWz = Buf()
        KEe = Buf()
        KEe_sem = S.newdma()
        VS = sb('c_VS', [128, NT, 4, 65], BF16, sem=True)
        VW = sb('c_VW', [128, NT, 4, 65], BF16, sem=True)
        biasT, Bband = P.biasT, P.Bband
        m512 = sb('c_m512', [128, 512], BF16, sem=True)
        SelW = sb('c_SelW', [32, OFFS + 128 * len(chunks)], BF16, sem=True)
        mulB = sb('c_mulB', [128, 128], F32, sem=True)
        addB = sb('c_addB', [128, 128], F32, sem=True)
        QM = [sb('c_QM%d' % i, [128, 4, 4, 128], BF16, sem=True) for i in range(2)]
        QMq = [Buf() for _ in range(2)]
        QMm = [[Buf() for _ in range(4)] for _ in range(2)]
        gt = [sb('c_gt%d' % i, [128, 16, 3], F32, sem=True) for i in range(3)]
        NE = 4
        Et = [sb('c_E%d' % i, [128, 512], BF16) for i in range(NE)]
        o1s = [[sb('c_o1s%d_%d' % (j, i), [128, 4, 128], F32) for i in range(4)] for j in range(2)]
        o2sl = [sb('c_o2s%d' % i, [128, 4, 65], F32) for i in range(2)]
        o3s = [[sb('c_o3s%d_%d' % (j, i), [128, 4, 65], F32) for i in range(4)] for j in range(2)]
        coef1 = [[sb('c_coef1_%d_%d' % (j, i), [128, 4], F32) for i in range(4)] for j in range(2)]
        den = sb('c_den', [128, 4], F32)
        rden = sb('c_rden', [128, 4], F32)
        c2l = [sb('c_c2_%d' % i, [128, 4], F32) for i in range(2)]
        c3l = [sb('c_c3_%d' % i, [128, 4], F32) for i in range(2)]
        tmpc = sb('c_tmpc', [128, 4, 64], F32)
        imp = sb('c_imp', [128, 64], F32)
        score = sb('c_score', [128, 64], F32)
        score2 = sb('c_score2', [128, 64], F32)
        m8a = sb('c_m8a', [128, 8], F32)
        m8b = sb('c_m8b', [128, 8], F32)
        nms = [sb('c_nm%d' % i, [128, 128], BF16) for i in range(4)]
        acc = sb('c_acc', [128, 4, 64], F32)
        ntk = sb('c_ntk', [128, 1024], BF16)
        nst = [sb('c_nst%d' % i, [128, 8, 128], BF16, sem=True) for i in range(2)]

        scp = [ps('c_sc%d' % i, [128, 512]) for i in range(3)]
        o1U = ps('c_o1U', [128, 512])
        o3p = ps('c_o3', [128, 4, 65])
        o2p = [ps('c_o2_%d' % i, [128, 4, 65]) for i in range(2)]
        ptr = ps('c_ptr', [128, 1024], BF16)

        S.dma('pool', identB.sem, dict(out=identB.t[:, :], in_=T['identF']), writes=[identB])
        S.dma('sp', KWt.sem, dict(out=KWt.t[0:64, :, :], in_=T['KW'].rearrange('g d s -> d g s')), writes=[KWt])
        _memset(S, 'pool', KWt.t[64:128, :, :], 0.0, [KWz])
        for i_, q_ in enumerate(QM):
            _memset(S, 'pool', q_.t[:, :, :, :], 0.0, [q_, QMq[i_]] + QMm[i_])
        def load_q(qt):
            sl = qt % 2
            qs = slice(qt * 128, (qt + 1) * 128)
            S.dma('sp', QM[sl].sem, dict(out=QM[sl].t[0:64, :, :, :].rearrange('d g h q -> d (g h) q'),
                                         in_=T['QT'][:, :, qs].rearrange('h d q -> d h q')), writes=[QMq[sl]])
            S.dma('sp', gt[qt % 3].sem, dict(out=gt[qt % 3].t[:, :, :].rearrange('p h b -> p (h b)'), in_=T['G'][qs, :]),
                  writes=[gt[qt % 3]])

        load_q(0)
        S.dma('pool', SelW.sem, dict(out=SelW.t[:, :], in_=T['SelW']), writes=[SelW])
        for k0 in range(0, NT, 8):
            k1 = min(NT, k0 + 8)
            S.dma('sp', VW.sem, dict(out=VW.t[:, k0:k1, :, :],
                                     in_=T['VW1'][k0 * 128:k1 * 128].rearrange('(k p) g d -> p k g d', p=128)),
                  writes=[VW])
        S.dma('sp', mulB.sem, dict(out=mulB.t[:, :], in_=T['mulB']), writes=[mulB])
        S.dma('sp', addB.sem, dict(out=addB.t[:, :], in_=T['addB']), writes=[addB])
        S.dma('sp', KE.sem, dict(out=KE.t[0:64, :, :], in_=T['KS'].rearrange('g d s -> d g s')), writes=[KE])
        for g in range(4):
            for c0 in range(0, Sq, 2048):
                c1 = min(Sq, c0 + 2048)
                S.dma('pool', KEe_sem, dict(out=KE.t[64:128, g, c0:c1], in_=T['Econst'][:, c0:c1]), writes=[KEe])
        for k0 in range(0, NT, 8):
            k1 = min(NT, k0 + 8)
            S.dma('sp', VS.sem, dict(out=VS.t[:, k0:k1, :, :],
                                     in_=T['VS1'][k0 * 128:k1 * 128].rearrange('(k p) g d -> p k g d', p=128)),
                  writes=[VS])
        S.dma('pool', m512.sem, dict(out=m512.t[:, :], in_=T['m512']), writes=[m512])
        for nm_ in nms:
            _memset(S, 'dve', nm_.t[:, :], 0.0, [nm_])

        def cw_steps(qt):
            out = []
            for g in range(4):
                cl = [(ci, n0, sz) for ci, (n0, sz) in enumerate(chunks) if n0 <= 8 * qt + 6]
                for i, (ci, n0, sz) in enumerate(cl):
                    out.append(dict(kind='cmp', qt=qt, g=g, ci=ci, n0=n0, sz=sz, first=i == 0, last=i == len(cl) - 1))
                kl = list(range(max(0, qt - 4), qt + 1))
                for i, kt in enumerate(kl):
                    out.append(dict(kind='win', qt=qt, g=g, kt=kt, sz=128, first=i == 0, last=i == len(kl) - 1))
            out[0]['loadq'] = qt
            out[-1]['flush_def'] = True
            return out

        def sel_steps(qt):
            out = []
            for g in range(4):
                for kt in range(qt + 1):
                    out.append(dict(kind='sel', qt=qt, g=g, kt=kt, sz=128, first=kt == 0, last=kt == qt))
            return out

        if os.environ.get('PIPE', '1') == '1':
            steps = cw_steps(0)
            for qt in range(NT):
                if qt + 1 < NT:
                    steps += cw_steps(qt + 1)
                steps += sel_steps(qt)
        else:
            steps = []
            for qt in range(NT):
                steps += cw_steps(qt) + sel_steps(qt)
        cnt = dict(sc=0, e=0, o2=0, fs=0)

        def emit_scores(st):
            qt, g, sz = st['qt'], st['g'], st['sz']
            sl = qt % 2
            sc = scp[cnt['sc'] % 3]
            cnt['sc'] += 1
            st['sc'] = sc
            qrow = QM[sl].t[:, g, :, :].rearrange('d h q -> d (h q)')
            if st['kind'] == 'cmp':
                n0 = st['n0']
                a = n0 - 8 * qt + OFFS
                _mm(S, sc.t[0:sz, :], kcT.t[:, g, n0:n0 + sz], qrow, True, False, [kcT, QMq[sl], QM[sl], QMm[sl][g]], [sc])
                _mm(S, sc.t[0:sz, :], SelW.t[0:32, a:a + sz],
                    Bband.t[0:32, 4 * g:4 * g + 4, :].rearrange('k h q -> k (h q)'), False, True, [SelW, Bband], [sc])
                return
            kt = st['kt']
            dl = (qt - kt)
            extra = None
            if dl in (0, 1):
                extra = (biasT.t[:, dl, 4 * g:4 * g + 4, :].rearrange('k h q -> k (h q)'), biasT)
            elif dl == 4 and st['kind'] == 'win':
                extra = (m512.t[:, :], m512)
            ks = slice(kt * 128, (kt + 1) * 128)
            if st['kind'] == 'win':
                _mm(S, sc.t[:, :], KWt.t[:, g, ks], qrow, True, extra is None, [KWt, KWz, QMq[sl], QM[sl], QMm[sl][g]], [sc])
            else:
                _mm(S, sc.t[:, :], KE.t[:, g, ks], QM[sl].t[:, g, :, :].rearrange('d h q -> d (h q)'), True,
                    extra is None, [KE, KEe, QMq[sl], QMm[sl][g]], [sc])
            if extra is not None:
                _mm(S, sc.t[:, :], identB.t[:, :], extra[0], False, True, [identB, extra[1]], [sc])

        def emit_exp(st):
            sz = st['sz']
            E = Et[cnt['e'] % NE]
            cnt['e'] += 1
            st['E'] = E
            _act(S, E.t[0:sz, :], st['sc'].t[0:sz, :], AF.Exp, [st['sc']], [E])

        def emit_pv(st):
            qt, g, sz, E = st['qt'], st['g'], st['sz'], st['E']
            if st['kind'] == 'cmp':
                for h in range(4):
                    _mm(S, o1U.t[:, h * 128:(h + 1) * 128], E.t[0:sz, h * 128:(h + 1) * 128], vco.t[0:sz, st['ci'], g, :],
                        st['first'] and h == 0, st['last'] and h == 3, [E, vco], [o1U], skip=True)
                if st['last']:
                    fin_cmp(qt, g)
                return
            kt = st['kt']
            if st['kind'] == 'win':
                op_, V = o3p, VW
            else:
                if st['first']:
                    st['o2'] = o2p[cnt['o2'] % 2]
                    cnt['o2'] += 1
                    cur['o2'] = st['o2']
                op_, V = cur['o2'], VS
            for h in range(4):
                _mm(S, op_.t[:, h, :], E.t[:, h * 128:(h + 1) * 128], V.t[:, kt, g, :], st['first'] and h == 0,
                    st['last'] and h == 3, [E, V], [op_], skip=True)
            if st['last']:
                if st['kind'] == 'win':
                    _cp(S, EV, o3s[qt % 2][g].t[:, :, :], o3p.t[:, :, :], [o3p], [o3s[qt % 2][g]])
                else:
                    fin_sel(qt, g, op_)

        cur = {}

        def fin_cmp(qt, g):
            sl = qt % 2
            nm = nms[g]
            o1 = o1s[sl][g]
            _cp(S, EV, o1.t[:, :, :], o1U.t[:, :].rearrange('p (h c) -> p h c', h=4), [o1U], [o1])
            S.op('dve', 'tensor_reduce', dict(out=den.t[:, :], in_=o1.t[:, :, 64:128], axis=AX.X, op=ALU.add), [o1], [den])
            _ts(S, 'dve', den.t[:, :], den.t[:, :], 1e-30, ALU.max, [den], [den])
            S.op('dve', 'reciprocal', dict(out=rden.t[:, :], in_=den.t[:, :]), [den], [rden])
            _ts(S, 'dve', imp.t[:, :], o1.t[:, 0, 64:128], rden.t[:, 0:1], ALU.mult, [o1, rden], [imp])
            for h in range(1, 4):
                _stt(S, imp.t[:, :], o1.t[:, h, 64:128], rden.t[:, h:h + 1], imp.t[:, :], ALU.mult, ALU.add,
                     [o1, rden, imp], [imp])
            a = 62 - 2 * qt
            _tt(S, 'dve', score.t[:, :], imp.t[:, :], mulB.t[:, a:a + 64], ALU.mult, [imp, mulB], [score])
            _tt(S, 'dve', score.t[:, :], score.t[:, :], addB.t[:, a:a + 64], ALU.add, [score, addB], [score])
            _memset(S, 'dve', score.t[:, 0:1], 50.0, [score])
            S.op('dve', 'max', dict(out=m8a.t[:, :], in_=score.t[:, :]), [score], [m8a])
            S.op('dve', 'match_replace', dict(out=score2.t[:, :], in_to_replace=m8a.t[:, :], in_values=score.t[:, :],
                                              imm_value=-1e9), [score, m8a], [score2])
            S.op('dve', 'max', dict(out=m8b.t[:, :], in_=score2.t[:, :]), [score2], [m8b])
            _ts(S, 'dve', nm.t[:, 64:128], score.t[:, :], m8b.t[:, 7:8], ALU.is_lt, [score, m8b], [nm], s2=MASKV,
                op1=ALU.mult)

            def part2(sl=sl, g=g, nm=nm):
                _tp(S, ptr.t[:, 0:128], nm.t[:, :], identB.t[:, :], [nm, identB], [ptr])
                for h in range(4):
                    _cp(S, 'dve', QM[sl].t[64:128, g, h, :], ptr.t[64:128, 0:128], [ptr], [QMm[sl][g]])
            deferred.append([DEFER, part2])
            _tt(S, 'dve', coef1[sl][g].t[:, :], rden.t[:, :], gt[qt % 3].t[:, 4 * g:4 * g + 4, 0], ALU.mult, [rden, gt[qt % 3]],
                [coef1[sl][g]])

        def fin_sel(qt, g, o2):
            sl = qt % 2
            k_ = cnt['fs'] % 2
            cnt['fs'] += 1
            o2s, c2, c3 = o2sl[k_], c2l[k_], c3l[k_]
            _cp(S, EV, o2s.t[:, :, :], o2.t[:, :, :], [o2], [o2s])
            for (osrc, cf, br) in ((o2s, c2, 1), (o3s[sl][g], c3, 2)):
                _ts(S, 'dve', den.t[:, :], osrc.t[:, :, 64], 1e-30, ALU.max, [osrc], [den])
                S.op('dve', 'reciprocal', dict(out=rden.t[:, :], in_=den.t[:, :]), [den], [rden])
                _tt(S, 'dve', cf.t[:, :], rden.t[:, :], gt[qt % 3].t[:, 4 * g:4 * g + 4, br], ALU.mult, [rden, gt[qt % 3]], [cf])
            o1 = o1s[sl][g]
            o3 = o3s[sl][g]
            c1 = coef1[sl][g]

            def cb(c):
                return c.t[:, :].unsqueeze(2).broadcast_to([128, 4, 64])
            CE = 'pool'
            _tt(S, CE, acc.t[:, :, :], o1.t[:, :, 0:64], cb(c1), ALU.mult, [o1, c1], [acc])
            _tt(S, CE, tmpc.t[:, :, :], o3.t[:, :, 0:64], cb(c3), ALU.mult, [o3, c3], [tmpc])
            _tt(S, CE, acc.t[:, :, :], acc.t[:, :, :], tmpc.t[:, :, :], ALU.add, [acc, tmpc], [acc])
            _tt(S, CE, tmpc.t[:, :, :], o2s.t[:, :, 0:64], cb(c2), ALU.mult, [o2s, c2], [tmpc])
            _tt(S, CE, ntk.t[:, g * 256:(g + 1) * 256].rearrange('p (h d) -> p h d', h=4), acc.t[:, :, :], tmpc.t[:, :, :],
                ALU.add, [acc, tmpc], [ntk])
            if g == 3:
                def part2(qt=qt):
                    for kc in range(8):
                        _tp(S, ptr.t[:, kc * 128:(kc + 1) * 128], ntk.t[:, kc * 128:(kc + 1) * 128], identB.t[:, :],
                            [ntk, identB], [ptr])
                    ns = nst[qt % 2]
                    _cp(S, 'dve', ns.t[:, :, :], ptr.t[:, :].rearrange('p (k q) -> p k q', k=8), [ptr], [ns])
                    S.dma('sp', ns.sem, dict(out=T['NSAT'][:, :, qt * 128:(qt + 1) * 128].rearrange('k p q -> p k q'),
                                             in_=ns.t[:, :, :]), reads=[ns])
                deferred.append([DEFER, part2])

        LOOK = int(os.environ.get('LOOK', '2'))
        EV = os.environ.get('EV', 'act')
        DEFER = int(os.environ.get('DEFER', '8'))
        deferred = []
        pend = []

        def run_deferred(force=False):
            while deferred and (force or deferred[0][0] <= 0):
                deferred.pop(0)[1]()

        for st in steps:
            if 'loadq' in st and st['loadq'] > 0:
                load_q(st['loadq'])
            emit_scores(st)
            emit_exp(st)
            pend.append(st)
            if len(pend) > LOOK:
                emit_pv(pend.pop(0))
            for d_ in deferred:
                d_[0] -= 1
            run_deferred()
            if st.get('flush_def'):
                while pend:
                    emit_pv(pend.pop(0))
                run_deferred(force=True)
        while pend:
            emit_pv(pend.pop(0))
        run_deferred(force=True)
        S.flush()


def phase_D(nc, S, Sq, T, P):
    NSUP = Sq // 512
    with ExitStack() as es:
        sb, ps = _mk(nc, es, S)
        C = Ctx()
        R = Ctx()
        identB = sb('d_identB', [128, 128], BF16, sem=True)
        R.identB = identB
        onesB = sb('d_onesB', [128, 128], BF16)
        onesF = sb('d_onesF', [1, 128], F32)
        R.epsT = sb('d_eps', [128, 1], F32)
        wno = sb('d_wno', [128, 8, D], BF16, sem=True)
        wpw = sb('d_wpw', [128, 4, D], BF16, sem=True)
        wout = sb('d_wout', [128, 8, D], BF16, sem=True)
        wq = sb('d_wq', [128, 8, D], BF16, sem=True)
        wo = sb('d_wo', [128, 8, D], BF16, sem=True)
        g2B = sb('d_g2B', [128, D], F32)
        g3B = sb('d_g3B', [128, D], F32)
        kmT, vm = P.kmT, P.vm
        R.ptr = ps('d_ptr', [128, 1024], BF16)
        ptr = R.ptr
        pf = [ps('d_pf%d' % i, [128, 512]) for i in range(4)]
        ph = [ps('d_ph%d' % i, [128, 512]) for i in range(2)]
        R.ss = sb('d_ss', [128, 4], F32)
        R.rt = sb('d_rt', [128, 4], F32)
        R.rstd = sb('d_rstd', [128, 4], F32)
        R.ntok = [sb('d_ntok%d' % i, [128, D], BF16) for i in range(2)]
        R.ssl = [sb('d_ssl%d' % i, [128, 4], F32) for i in range(4)]

        S.dma('pool', identB.sem, dict(out=identB.t[:, :], in_=T['identF']), writes=[identB])
        _memset(S, 'dve', onesB.t[:, :], 1.0, [onesB])
        _memset(S, 'dve', onesF.t[:, :], 1.0, [onesF])
        _memset(S, 'dve', R.epsT.t[:, :], EPS, [R.epsT])
        def load_wts(lst):
            for w_, nm_ in lst:
                S.dma('sp', w_.sem, dict(out=w_.t[:, :, :], in_=T['WB_' + nm_].rearrange('(k p) n -> p k n', p=128)),
                      writes=[w_])
        load_wts(((wno, 'nsa_w_o'), (wpw, 'conv_w_pw')))
        nsa_s = sb('d_nsa', [128, 8, 512], BF16, sem=True)
        hc_s = sb('d_hc', [128, 4, 512], BF16, sem=True)
        gm_s = sb('d_gm', [128, 16, 512], BF16, sem=True)
        xs = [sb('d_x%d' % i, [128, D], F32, sem=True) for i in range(4)]
        xin = [sb('d_xin%d' % i, [128, D], F32, sem=True) for i in range(2)]
        C.stage = xs[0]
        bcast_row(S, C, T['norm2_g'], g2B, ph, onesF)
        bcast_row(S, C, T['norm3_g'], g3B, ph, onesF)
        mrg = sb('d_mrg', [128, 8, 512], BF16)
        n2T = sb('d_n2T', [128, 8, 512], BF16)
        qxT = sb('d_qxT', [128, 8, 512], BF16)
        PT = sb('d_PT', [128, 2, 512], BF16)
        R.junk = Tl(PT.t[:, :, :].rearrange('p a b -> p (a b)'))
        R.junk.b = PT.b
        oTn = sb('d_oTn', [128, 8, 512], BF16)
        t1 = sb('d_t1', [128, 512], F32)
        t2 = sb('d_t2', [128, 512], F32)
        rdn = sb('d_rdn', [128, 512], F32)
        n3s = [sb('d_n3s%d' % i, [128, 8, 128], BF16, sem=True) for i in range(2)]
        npf = 0
        NTT = Sq // 128

        def load_acts(st):
            cs = slice(st * 512, (st + 1) * 512)
            S.dma('sp', nsa_s.sem, dict(out=nsa_s.t[:, :, :], in_=T['NSAT'][:, :, cs].rearrange('k p s -> p k s')),
                  writes=[nsa_s])
            S.dma('sp', hc_s.sem, dict(out=hc_s.t[:, :, :], in_=T['HC'][:, :, cs].rearrange('k p s -> p k s')),
                  writes=[hc_s])
            S.dma('sp', gm_s.sem, dict(out=gm_s.t[:, :, :], in_=T['GM'][:, :, cs].rearrange('k p s -> p k s')),
                  writes=[gm_s])

        def load_x(tt):
            if tt < NTT:
                S.dma('sp', xin[tt % 2].sem, dict(out=xin[tt % 2].t[:, :], in_=T['x'][tt * 128:(tt + 1) * 128, :]),
                      writes=[xin[tt % 2]])
        load_acts(0)
        load_x(0)
        load_x(1)
        load_wts(((wout, 'w_out'), (wq, 'xa_wq'), (wo, 'xa_wo')))
        for st in range(NSUP):
            cs = slice(st * 512, (st + 1) * 512)
            for f in range(8):
                pa = pf[npf % 4]
                pcv = pf[(npf + 1) % 4]
                npf += 2
                for kc in range(8):
                    _mm(S, pa.t[:, :], wno.t[:, kc, f * 128:(f + 1) * 128], nsa_s.t[:, kc, :], kc == 0, kc == 7,
                        [wno, nsa_s], [pa])
                for c in range(4):
                    _mm(S, pcv.t[:, :], wpw.t[:, c, f * 128:(f + 1) * 128], hc_s.t[:, c, :], c == 0, c == 3,
                        [wpw, hc_s], [pcv])
                _tt(S, 'dve', t1.t[:, :], pa.t[:, :], gm_s.t[:, 8 + f, :], ALU.mult, [pa, gm_s], [t1])
                _tt(S, 'dve', t2.t[:, :], pcv.t[:, :], gm_s.t[:, f, :], ALU.mult, [pcv, gm_s], [t2])
                _tt(S, 'pool', mrg.t[:, f, :], t1.t[:, :], t2.t[:, :], ALU.add, [t1, t2], [mrg])
            if st + 1 < NSUP:
                load_acts(st + 1)
            for t in range(4):
                tt = st * 4 + t
                for half in range(2):
                    p = ph[half]
                    for f in range(8):
                        _mm(S, p.t[:, :], mrg.t[:, f, t * 128:(t + 1) * 128], wout.t[:, f, half * 512:(half + 1) * 512],
                            f == 0, f == 7, [mrg, wout], [p])
                    _tt(S, 'dve', xs[t].t[:, half * 512:(half + 1) * 512], p.t[:, :],
                        xin[tt % 2].t[:, half * 512:(half + 1) * 512], ALU.add, [p, xin[tt % 2]], [xs[t]])
                load_x(tt + 2)
                rms_stats(S, R, t, xs[t])
            for i, p_ in rms_T(S, R, xs, g2B, per_tile=True, stats_done=True):
                _cp(S, 'act', n2T.t[:, :, i * 128:(i + 1) * 128], p_.t[:, :].rearrange('p (k q) -> p k q', k=8),
                    [p_], [n2T])
            for c in range(8):
                p = pf[npf % 4]
                npf += 1
                for kc in range(8):
                    _mm(S, p.t[:, :], wq.t[:, kc, c * 128:(c + 1) * 128], n2T.t[:, kc, :], kc == 0, kc == 7,
                        [wq, n2T], [p])
                _act(S, qxT.t[:, c, :], p.t[:, :], AF.Copy, [p], [qxT], scale=1.0 / 16.0)
            for hd in range(4):
                for mc in range(2):
                    p = pf[npf % 4]
                    npf += 1
                    for dc in range(2):
                        _mm(S, p.t[:, :], kmT.t[:, hd * 2 + dc, mc * 128:(mc + 1) * 128], qxT.t[:, hd * 2 + dc, :],
                            dc == 0, dc == 1, [kmT, qxT], [p])
                    _act(S, PT.t[:, mc, :], p.t[:, :], AF.Exp, [p], [PT])
                pd = pf[npf % 4]
                npf += 1
                for mc in range(2):
                    _mm(S, pd.t[:, :], onesB.t[:, :], PT.t[:, mc, :], mc == 0, mc == 1, [onesB, PT], [pd])
                S.op('dve', 'reciprocal', dict(out=rdn.t[:, :], in_=pd.t[:, :]), [pd], [rdn])
                for dc in range(2):
                    po = pf[npf % 4]
                    npf += 1
                    for mc in range(2):
                        _mm(S, po.t[:, :], vm.t[:, mc, hd * 256 + dc * 128:hd * 256 + (dc + 1) * 128], PT.t[:, mc, :],
                            mc == 0, mc == 1, [vm, PT], [po])
                    _tt(S, 'dve', oTn.t[:, hd * 2 + dc, :], po.t[:, :], rdn.t[:, :], ALU.mult, [po, rdn], [oTn])
            for t in range(4):
                tt = st * 4 + t
                for half in range(2):
                    p = ph[half]
                    for c in range(8):
                        _mm(S, p.t[:, :], oTn.t[:, c, t * 128:(t + 1) * 128], wo.t[:, c, half * 512:(half + 1) * 512],
                            c == 0, c == 7, [oTn, wo], [p])
                    _tt(S, 'dve', xs[t].t[:, half * 512:(half + 1) * 512], p.t[:, :],
                        xs[t].t[:, half * 512:(half + 1) * 512], ALU.add, [p, xs[t]], [xs[t]])
                S.dma('sp', xs[t].sem, dict(out=T['H2'][tt * 128:(tt + 1) * 128, :], in_=xs[t].t[:, :]), reads=[xs[t]])
                rms_stats(S, R, t, xs[t])
            for i, p_ in rms_T(S, R, xs, g3B, per_tile=True, stats_done=True):
                tt = st * 4 + i
                ns = n3s[i % 2]
                _cp(S, 'act', ns.t[:, :, :], p_.t[:, :].rearrange('p (k q) -> p k q', k=8), [p_], [ns])
                S.dma('sp', ns.sem, dict(out=T['N3T'][:, :, tt * 128:(tt + 1) * 128].rearrange('k p q -> p k q'),
                                         in_=ns.t[:, :, :]), reads=[ns])
        S.flush()


def phase_E(nc, S, Sq, T, P):
    NSUP = Sq // 512
    NP = D_FF // 128
    with ExitStack() as es:
        sb, ps = _mk(nc, es, S)
        C = Ctx()
        identF = sb('e_identF', [128, 128], F32, sem=True)
        C.identF = identF
        C.stage = sb('e_stage', [32, 1024], F32, sem=True)
        pst = ps('e_pst', [128, 48])
        C.pst = pst
        wupb = [sb('e_wup%d' % i, [128, 8, 512], BF16, sem=True) for i in range(11)]
        wdn = sb('e_wdn', [128, NP, D], BF16, sem=True)
        fw, fb = P.fw, P.fb
        fgB = sb('e_fgB', [128, D], F32)
        onesF = sb('e_onesF', [1, 128], F32)
        epsT = sb('e_eps', [128, 1], F32)
        halo = sb('e_halo', [128, 2 * NP, 2], F32)
        n3 = sb('e_n3', [128, 8, 512], BF16, sem=True)
        actT = sb('e_actT', [128, NP, 512], BF16)
        ub = [sb('e_ub%d' % i, [128, 514], F32) for i in range(3)]
        tb = [sb('e_tb%d' % i, [128, 512], F32) for i in range(3)]
        sgl = sb('e_sgl', [128, 512], F32)
        h2 = [sb('e_h2_%d' % i, [128, D], F32, sem=True) for i in range(2)]
        junk = sb('e_junk', [128, D], BF16)
        ss = sb('e_ss', [128, 1], F32)
        rt = sb('e_rt', [128, 1], F32)
        rstd = sb('e_rstd', [128, 1], F32)
        pu = [ps('e_pu%d' % i, [128, 512]) for i in range(3)]
        pd = [ps('e_pd%d' % i, [128, 512]) for i in range(2)]

        S.dma('sp', identF.sem, dict(out=identF.t[:, :], in_=T['identF']), writes=[identF])
        _memset(S, 'dve', onesF.t[:, :], 1.0, [onesF])
        _memset(S, 'dve', epsT.t[:, :], EPS, [epsT])
        _memset(S, 'dve', halo.t[:, :, :], 0.0, [halo])
        order = []
        for j in range(NP):
            for c in (j, j + NP):
                if c // 4 not in order:
                    order.append(c // 4)
        for bi in order[:2]:
            S.dma('sp', wupb[bi].sem, dict(out=wupb[bi].t[:, :, :],
                                           in_=T['WB_ffn_w_up'][:, bi * 512:(bi + 1) * 512].rearrange('(k p) n -> p k n', p=128)),
                  writes=[wupb[bi]])
        S.dma('sp', n3.sem, dict(out=n3.t[:, :, :], in_=T['N3T'][:, :, 0:512].rearrange('k p s -> p k s')), writes=[n3])
        for bi in order[2:]:
            S.dma('sp', wupb[bi].sem, dict(out=wupb[bi].t[:, :, :],
                                           in_=T['WB_ffn_w_up'][:, bi * 512:(bi + 1) * 512].rearrange('(k p) n -> p k n', p=128)),
                  writes=[wupb[bi]])
        S.dma('sp', wdn.sem, dict(out=wdn.t[:, :, :], in_=T['WB_ffn_w_down'].rearrange('(k p) n -> p k n', p=128)),
              writes=[wdn])
        S.dma('sp', C.stage.sem, dict(out=C.stage.t[0:1, 0:1024], in_=T['final_g']), writes=[C.stage])
        for half in range(2):
            _mm(S, pd[half].t[:, :], onesF.t[0:1, :], C.stage.t[0:1, half * 512:(half + 1) * 512], True, True,
                [onesF, C.stage], [pd[half]])
            _cp(S, 'dve', fgB.t[:, half * 512:(half + 1) * 512], pd[half].t[:, :], [pd[half]], [fgB])

        npu = 0
        nub = 0
        for st in range(NSUP):
            cs = slice(st * 512, (st + 1) * 512)
            if st > 0:
                S.dma('sp', n3.sem, dict(out=n3.t[:, :, :], in_=T['N3T'][:, :, cs].rearrange('k p s -> p k s')), writes=[n3])
            for j in range(NP):
                tpair = []
                for c in (j, j + NP):
                    p = pu[npu % 3]
                    npu += 1
                    u_ = ub[nub % 3]
                    t_ = tb[nub % 3]
                    nub += 1
                    for kc in range(8):
                        _mm(S, p.t[:, :], wupb[c // 4].t[:, kc, (c % 4) * 128:(c % 4 + 1) * 128], n3.t[:, kc, :], kc == 0, kc == 7,
                            [wupb[c // 4], n3], [p])
                    _cp(S, 'act', u_.t[:, 2:514], p.t[:, :], [p], [u_])
                    _cp(S, 'pool', u_.t[:, 0:2], halo.t[:, c, :], [halo], [u_])
                    _act(S, t_.t[:, :], p.t[:, :], AF.Identity, [p, fw, fb], [t_], scale=fw.t[:, c, 2:3], bias=fb.t[:, c, :])
                    _stt(S, t_.t[:, :], u_.t[:, 0:512], fw.t[:, c, 0:1], t_.t[:, :], ALU.mult, ALU.add, [u_, fw, t_], [t_])
                    _stt(S, t_.t[:, :], u_.t[:, 1:513], fw.t[:, c, 1:2], t_.t[:, :], ALU.mult, ALU.add, [u_, fw, t_], [t_])
                    _cp(S, 'pool', halo.t[:, c, :], u_.t[:, 512:514], [u_], [halo])
                    tpair.append(t_)
                _act(S, sgl.t[:, :], tpair[0].t[:, :], AF.Silu, [tpair[0]], [sgl])
                _tt(S, 'dve', actT.t[:, j, :], sgl.t[:, :], tpair[1].t[:, :], ALU.mult, [sgl, tpair[1]], [actT])
            for t in range(4):
                tt = st * 4 + t
                hb = h2[tt % 2]
                ob = hb
                S.dma('sp', hb.sem, dict(out=hb.t[:, :], in_=T['H2'][tt * 128:(tt + 1) * 128, :]), writes=[hb])
                for half in range(2):
                    p = pd[half]
                    for j in range(NP):
                        _mm(S, p.t[:, :], actT.t[:, j, t * 128:(t + 1) * 128], wdn.t[:, j, half * 512:(half + 1) * 512],
                            j == 0, j == NP - 1, [actT, wdn], [p])
                    _tt(S, 'dve', hb.t[:, half * 512:(half + 1) * 512], p.t[:, :], hb.t[:, half * 512:(half + 1) * 512],
                        ALU.add, [p, hb], [hb])
                _stt(S, junk.t[:, :], hb.t[:, :], 1.0, hb.t[:, :], ALU.mult, ALU.mult, [hb], [junk, ss], accum_out=ss.t[:, 0:1])
                _act(S, rt.t[:, :], ss.t[:, :], AF.Sqrt, [ss, epsT], [rt], scale=1.0 / D, bias=epsT.t[:, :])
                S.op('dve', 'reciprocal', dict(out=rstd.t[:, :], in_=rt.t[:, :]), [rt], [rstd])
                _stt(S, ob.t[:, :], hb.t[:, :], rstd.t[:, 0:1], fgB.t[:, :], ALU.mult, ALU.mult, [hb, rstd, fgB], [hb])
                S.dma('sp', hb.sem, dict(out=T['y'][tt * 128:(tt + 1) * 128, :], in_=hb.t[:, :]), reads=[hb])
        S.flush()


def t5_bucket_np(d):
    n = np.maximum(d, 0)
    nf = np.maximum(n, 1).astype(np.float32)
    large = 16 + (np.log(nf / np.float32(16)) / np.float32(np.log(8.0)) * np.float32(16)).astype(np.int32)
    large = np.minimum(large, 31)
    return np.where(n < 16, n, large)


def scratch_spec(Sq):
    return {
        'QT': ([16, 64, Sq], BF16), 'KC': ([4, 64, Sq], BF16), 'VC': ([4, 64, Sq], BF16),
        'KS': ([4, 64, Sq], BF16), 'KW': ([4, 64, Sq], BF16), 'GM': ([16, 128, Sq], BF16),
        'HC': ([4, 128, Sq], BF16), 'VS1': ([Sq, 4, 65], BF16), 'VW1': ([Sq, 4, 65], BF16),
        'G': ([Sq, 48], F32), 'NSAT': ([8, 128, Sq], BF16), 'H2': ([Sq, D], F32), 'N3T': ([8, 128, Sq], BF16),
    }


INPUT_SHAPES = {
    'norm1_g': [1, D], 'w_in': [D, IN_W], 'conv_dw_w': [31, 512], 'conv_dw_b': [1, 512],
    'conv_ln_g': [1, 512], 'conv_ln_b': [1, 512], 'conv_w_pw': [512, D],
    'cmp_pe': [2, 32, 64], 'cmp_w1': [2, 2048, 256], 'cmp_b1': [1, 512], 'cmp_w2': [2, 256, 64],
    'nsa_w_o': [D, D], 'w_out': [D, D], 'norm2_g': [1, D], 'mem_norm_g': [1, D],
    'xa_wq': [D, D], 'xa_wkv': [D, 2 * D], 'xa_wo': [D, D], 'norm3_g': [1, D],
    'ffn_w_up': [D, 2 * D_FF], 'ffn_dw_w': [3, 2 * D_FF], 'ffn_dw_b': [1, 2 * D_FF],
    'ffn_w_down': [D_FF, D], 'final_g': [1, D],
}


def const_shapes(Sq):
    NT = Sq // 128
    NC, chunks = cmp_chunks(Sq)
    return {
        'identF': [128, 128], 'Econst': [64, Sq], 'm512': [128, 512],
        'SelW': [32, 8 * (NT - 1) + 128 * len(chunks)], 'mulB': [128, 128], 'addB': [128, 128],
        'ovl': [128 * len(chunks), 64],
        'tz1': [2, 128, 16, 128], 'tz31': [2, 128, 16, 128], 'tzm': [2, 128, 16, 128],
        'cb1': [32, 16, 128], 'cb31': [32, 16, 128], 'cbm': [32, 16, 128],
    }


def build(Sq, debug=(), phases='ABCDE'):
    nc = bass.Bass("TRN2", target_bir_lowering=False)
    T = {}
    T['x'] = nc.dram_tensor('x', [Sq, D], F32, kind='ExternalInput').ap()
    T['mem'] = nc.dram_tensor('mem', [MEM, D], F32, kind='ExternalInput').ap()
    for k, shp in list(INPUT_SHAPES.items()) + list(const_shapes(Sq).items()):
        T[k] = nc.dram_tensor(k, shp, F32, kind='ExternalInput').ap()
    for k, (shp, dt) in scratch_spec(Sq).items():
        kind = 'ExternalOutput' if k in debug else 'Internal'
        T[k] = nc.dram_tensor(k, shp, dt, kind=kind).ap()
    T['y'] = nc.dram_tensor('y', [Sq, D], F32, kind='ExternalOutput').ap()
    for k, (K_, N_) in WB_SPEC.items():
        T['WB_' + k] = nc.dram_tensor('WB_' + k, [K_, N_], BF16, kind='Internal').ap()
    NC, chunks = cmp_chunks(Sq)
    with ExitStack() as es:
        S = Sched(nc, es)
        if 'A' in phases:
            phase_A(nc, S, Sq, T)
        NCH = len(chunks)
        P = Ctx()
        P.fw = Tl(es.enter_context(nc.sbuf_tensor('p_fw', [128, 2 * (D_FF // 128), 3], F32)))
        P.fb = Tl(es.enter_context(nc.sbuf_tensor('p_fb', [128, 2 * (D_FF // 128), 1], F32)))
        with ExitStack() as es1:
            P.kmT = Tl(es1.enter_context(nc.sbuf_tensor('p_kmT', [128, 8, 256], BF16)))
            P.vm = Tl(es1.enter_context(nc.sbuf_tensor('p_vm', [128, 2, D], BF16)))
            with ExitStack() as es2:
                kcT = Tl(es2.enter_context(nc.sbuf_tensor('kcT', [128, 4, 128 * NCH], BF16)))
                vco = Tl(es2.enter_context(nc.sbuf_tensor('vco', [128, NCH, 4, 128], BF16)))
                P.biasT = Tl(es2.enter_context(nc.sbuf_tensor('p_biasT', [128, 2, 16, 128], BF16)))
                P.Bband = Tl(es2.enter_context(nc.sbuf_tensor('p_Bband', [32, 16, 128], BF16)))
                if 'B' in phases:
                    phase_B(nc, S, Sq, T, kcT, vco, P)
                if 'C' in phases:
                    phase_C(nc, S, Sq, T, kcT, vco, P)
            if 'D' in phases:
                phase_D(nc, S, Sq, T, P)
        if 'E' in phases:
            phase_E(nc, S, Sq, T, P)
    return nc


def host_consts(rel_bias, Sq):
    NT = Sq // 128
    NC, chunks = cmp_chunks(Sq)
    rb = np.asarray(rel_bias, dtype=np.float32)
    c = {}
    c['identF'] = np.eye(128, dtype=np.float32)
    E = np.zeros((64, Sq), np.float32)
    kk = np.arange(Sq)
    valid = kk // 64 < 64
    E[(kk // 64)[valid], kk[valid]] = 1.0
    c['Econst'] = E
    ki = np.arange(128)[:, None]
    qi = np.arange(128)[None, :]
    c['m512'] = np.tile(np.where(qi >= ki, MASKV, 0.0).astype(np.float32), (1, 4))
    OFFS = 8 * (NT - 1)
    W = np.zeros((32, OFFS + 128 * len(chunks)), np.float32)
    m = np.arange(W.shape[1]) - OFFS
    for r in range(17):
        W[r, m == r - 10] = 1.0
    W[31, m >= 7] = 1.0
    c['SelW'] = W
    r = np.arange(128)[None, :] - 62
    p = np.arange(128)[:, None]
    hi = (p >= 64).astype(np.int64)
    rel = r - hi
    free = rel <= -2
    forced = (rel == -1) | (rel == 0)
    c['mulB'] = np.where(free, 1.0, 0.0).astype(np.float32) * np.ones((128, 1), np.float32)
    c['addB'] = np.where(free, 0.0, np.where(forced, 10.0 + (rel + 2), -1.0 - 0.001 * np.maximum(rel, 0))).astype(np.float32)
    n = np.arange(128 * len(chunks))[:, None]
    j = np.arange(64)[None, :]
    ov = np.clip(np.minimum(16 * n + 32, 64 * j + 64) - np.maximum(16 * n, 64 * j), 0, None).astype(np.float32) / 32.0
    ov[NC:] = 0.0
    c['ovl'] = ov.astype(np.float32)
    tz1 = np.zeros((2, 128, 16, 128), np.float32)
    tz31 = np.zeros_like(tz1)
    tzm = np.zeros_like(tz1)
    for dl in range(2):
        d = dl * 128 + qi - ki
        ok = d >= 0
        g1 = rb[t5_bucket_np(d)]
        g31 = rb[np.full_like(d, 31)]
        tz1[dl] = np.where(ok[:, :, None], g1, 0.0).transpose(0, 2, 1)
        tz31[dl] = np.where(ok[:, :, None], g31, 0.0).transpose(0, 2, 1)
        tzm[dl] = np.where(ok[:, :, None], 0.0, MASKV).transpose(0, 2, 1) * np.ones((1, 16, 1), np.float32)
    c['tz1'], c['tz31'], c['tzm'] = tz1, tz31, tzm
    cb1 = np.zeros((32, 16, 128), np.float32)
    cb31 = np.zeros_like(cb1)
    cbm = np.zeros_like(cb1)
    rr = np.arange(17)[:, None]
    d1 = np.arange(128)[None, :] - 16 * (rr - 10) - 31
    ok = d1 >= 0
    cb1[:17] = np.where(ok[:, :, None], rb[t5_bucket_np(d1)], 0.0).transpose(0, 2, 1)
    cb31[:17] = np.where(ok[:, :, None], rb[np.full_like(d1, 31)], 0.0).transpose(0, 2, 1)
    cbm[:17] = (np.where(ok, 0.0, MASKV)[:, None, :] * np.ones((1, 16, 1))).astype(np.float32)
    cbm[31] = MASKV
    c['cb1'], c['cb31'], c['cbm'] = cb1, cb31, cbm
    return {k: np.ascontiguousarray(v, dtype=np.float32) for k, v in c.items()}


def host_inputs(inp, Sq):
    shared = {}
    for k, shp in INPUT_SHAPES.items():
        shared[k] = np.ascontiguousarray(np.asarray(inp[k], dtype=np.float32).reshape(shp))
    shared.update(host_consts(inp['rel_bias'], Sq))
    return shared


def kernel(**inp):
    x = np.asarray(inp['x'], dtype=np.float32)
    mem = np.asarray(inp['mem'], dtype=np.float32)
    B, Sq, _ = x.shape
    nc = build(Sq)
    shared = host_inputs(inp, Sq)
    in_maps = []
    for b in range(B):
        m = dict(shared)
        m['x'] = np.ascontiguousarray(x[b])
        m['mem'] = np.ascontiguousarray(mem[b])
        in_maps.append(m)
    res = run_bass_kernel_spmd(nc, in_maps, core_ids=list(range(B)))
    return np.stack([np.asarray(r['y'], dtype=np.float32) for r in res.results], axis=0)
```
